# Optimizing a Trainium2 kernel written in Bass

```python
import math
import functools
import jax
import jax.numpy as jnp
from jax import lax
import numpy as np

D_MODEL = 2048
BATCH = 4
SEQ = 4096
DEPTH = 2

GRID_W = 64
CTX_LEN = 256
MIX_HALF = D_MODEL // 2
D_FF = ((8 * D_MODEL // 3 + 127) // 128) * 128
FFN_RES = 0.5
N_MOD = 9
CHUNK = 64
EPS = 1e-6

GLA_HEADS = 4
GLA_DV = MIX_HALF // GLA_HEADS
GLA_DK = GLA_DV // 2
GLA_QK = GLA_HEADS * GLA_DK
GLA_RANK = 16
GLA_GATE_NORM = 16.0

MLSTM_HEADS = 4
MLSTM_DH = MIX_HALF // MLSTM_HEADS
MLSTM_CONV = 3

S5_CHANNELS = MIX_HALF
S5_GROUP = 16
S5_GROUPS = S5_CHANNELS // S5_GROUP
S5_STATE = 64

RET_HEADS = 4
RET_DH = MIX_HALF // RET_HEADS

N_EVEN = (DEPTH + 1) // 2
N_ODD = DEPTH // 2

EV_SIZES = (GLA_QK, GLA_QK, MIX_HALF, MIX_HALF, 2 * GLA_RANK, MIX_HALF, MIX_HALF, MIX_HALF, MIX_HALF, 4 * MLSTM_HEADS)
EV_COLS = sum(EV_SIZES)
OD_SIZES = (S5_CHANNELS, MIX_HALF, MIX_HALF, MIX_HALF, MIX_HALF)
OD_COLS = sum(OD_SIZES)

kernel_name = "hybrid_gla_mlstm_s5_retention_dit"


def rms_norm(x, g):
    xf = x.astype(jnp.float32)
    y = xf * lax.rsqrt(jnp.mean(xf * xf, axis=-1, keepdims=True) + EPS)
    return (y * g.astype(jnp.float32)).astype(x.dtype)


def head_norm(o, g, center):
    b, h, n, d = o.shape
    o = o.transpose(0, 2, 1, 3)
    if center:
        o = o - jnp.mean(o, axis=-1, keepdims=True)
    o = o * lax.rsqrt(jnp.mean(o * o, axis=-1, keepdims=True) + EPS)
    return o.reshape(b, n, h * d) * g


def sublayer_in(h, g, shift, scale):
    return rms_norm(h, g) * (1 + scale) + shift


def sublayer_out(y, g, gate, weight):
    return weight * gate * rms_norm(y, g)


def swiglu(h, w_gate, w_up, w_down):
    return (jax.nn.silu(h @ w_gate) * (h @ w_up)) @ w_down


def split_cols(p, sizes):
    out, start = [], 0
    for s in sizes:
        out.append(p[..., start:start + s])
        start += s
    return out


def split_heads(t, heads):
    b, n, _ = t.shape
    return t.reshape(b, n, heads, -1).transpose(0, 2, 1, 3)


def to_col_major(t, rows):
    b, rest = t.shape[0], t.shape[2:]
    return t.reshape(b, rows, GRID_W, *rest).swapaxes(1, 2).reshape(b, rows * GRID_W, *rest)


def to_row_major(t, rows):
    b, rest = t.shape[0], t.shape[2:]
    return t.reshape(b, GRID_W, rows, *rest).swapaxes(1, 2).reshape(b, rows * GRID_W, *rest)


def depthwise_conv(u, w, bias):
    out = lax.conv_general_dilated(u, w[:, None, :].astype(u.dtype), window_strides=(1,), padding='SAME',
                                   dimension_numbers=('NWC', 'WIO', 'NWC'), feature_group_count=u.shape[-1])
    return out + bias


def to_chunks(t):
    b, h, n = t.shape[:3]
    return jnp.moveaxis(t.reshape(b, h, n // CHUNK, CHUNK, *t.shape[3:]), 2, 0)


def from_chunks(t):
    t = jnp.moveaxis(t, 0, 2)
    return t.reshape(t.shape[0], t.shape[1], t.shape[2] * t.shape[3], *t.shape[4:])


def gla_scan(q, k, v, log_a, s0):
    mask = jnp.tril(jnp.ones((CHUNK, CHUNK), dtype=bool))
    mid = CHUNK // 2

    def step(s, inp):
        qc, kc, vc, ac = inp
        b = jnp.cumsum(ac, axis=2)
        b_mid = b[:, :, mid:mid + 1]
        b_end = b[:, :, -1:]
        scores = jnp.einsum('bhtd,bhsd->bhts', qc * jnp.exp(b - b_mid), kc * jnp.exp(b_mid - b))
        scores = jnp.where(mask, scores, 0.0)
        o = (jnp.einsum('bhts,bhsv->bhtv', scores, vc)
             + jnp.einsum('bhtd,bhdv->bhtv', qc * jnp.exp(b), s))
        s_new = (jnp.exp(b_end[:, :, 0])[..., None] * s
                 + jnp.einsum('bhsd,bhsv->bhdv', kc * jnp.exp(b_end - b), vc))
        return s_new, o

    s_fin, o = lax.scan(step, s0, (to_chunks(q), to_chunks(k), to_chunks(v), to_chunks(log_a)))
    return from_chunks(o), s_fin


def mlstm_scan(q, k, v, ig, lf, state):
    mask = jnp.tril(jnp.ones((CHUNK, CHUNK), dtype=bool))

    def step(carry, inp):
        c_st, n_st, m_st = carry
        qc, kc, vc, ic, fc = inp
        b = jnp.cumsum(fc, axis=-1)
        dmat = jnp.where(mask, b[..., :, None] - b[..., None, :] + ic[..., None, :], -jnp.inf)
        inter = b + m_st[..., None]
        m_t = jnp.maximum(inter, jnp.max(dmat, axis=-1))
        scores = jnp.einsum('bhtd,bhsd->bhts', qc, kc) * jnp.exp(dmat - m_t[..., None])
        w_inter = jnp.exp(inter - m_t)
        num = (jnp.einsum('bhts,bhsv->bhtv', scores, vc)
               + w_inter[..., None] * jnp.einsum('bhtd,bhdv->bhtv', qc, c_st))
        den = jnp.sum(scores, axis=-1) + w_inter * jnp.einsum('bhtd,bhd->bht', qc, n_st)
        h = num / jnp.maximum(jnp.abs(den), jnp.exp(-m_t))[..., None]
        b_end = b[..., -1]
        d_end = b_end[..., None] - b + ic
        m_new = jnp.maximum(b_end + m_st, jnp.max(d_end, axis=-1))
        wk = jnp.exp(d_end - m_new[..., None])
        decay = jnp.exp(b_end + m_st - m_new)
        c_new = decay[..., None, None] * c_st + jnp.einsum('bhsd,bhsv->bhdv', kc * wk[..., None], vc)
        n_new = decay[..., None] * n_st + jnp.einsum('bhsd,bhs->bhd', kc, wk)
        return (c_new, n_new, m_new), h

    fin, h = lax.scan(step, state, (to_chunks(q), to_chunks(k), to_chunks(v), to_chunks(ig), to_chunks(lf)))
    return from_chunks(h), fin


def complex_affine_combine(e1, e2):
    a1r, a1i, b1r, b1i = e1
    a2r, a2i, b2r, b2i = e2
    return (a2r * a1r - a2i * a1i, a2r * a1i + a2i * a1r,
            a2r * b1r - a2i * b1i + b2r, a2r * b1i + a2i * b1r + b2i)


def s5_direction(lam_re, lam_im, log_step, b_re, b_im, c_re, c_im, u, state):
    lam_re = lam_re.astype(jnp.float32)
    lam_im = lam_im.astype(jnp.float32)
    step = jnp.exp(log_step.astype(jnp.float32))[:, None]
    mag = jnp.exp(lam_re * step)
    ab_re, ab_im = mag * jnp.cos(lam_im * step), mag * jnp.sin(lam_im * step)
    den = lam_re * lam_re + lam_im * lam_im
    coef_re = ((ab_re - 1.0) * lam_re + ab_im * lam_im) / den
    coef_im = (ab_im * lam_re - (ab_re - 1.0) * lam_im) / den
    bu_re = jnp.einsum('bngp,gsp->bngs', u, b_re)
    bu_im = jnp.einsum('bngp,gsp->bngs', u, b_im)
    x_re = coef_re * bu_re - coef_im * bu_im
    x_im = coef_re * bu_im + coef_im * bu_re
    a_re = jnp.broadcast_to(ab_re, x_re.shape)
    a_im = jnp.broadcast_to(ab_im, x_re.shape)
    acc_re, acc_im, hs_re, hs_im = lax.associative_scan(complex_affine_combine, (a_re, a_im, x_re, x_im), axis=1)
    h0_re, h0_im = state[0][:, None], state[1][:, None]
    h_re = hs_re + acc_re * h0_re - acc_im * h0_im
    h_im = hs_im + acc_re * h0_im + acc_im * h0_re
    y = jnp.einsum('bngs,gps->bngp', h_re, c_re) - jnp.einsum('bngs,gps->bngp', h_im, c_im)
    return y, (h_re[:, -1], h_im[:, -1])


def bidirectional(scan_f, scan_b, ctx_f, lat_f, ctx_b, lat_b, state0, axis):
    flip = lambda t: jnp.flip(t, axis=axis)
    oc_f, sc_f = scan_f(*ctx_f, state0)
    ol_f, _ = scan_f(*lat_f, sc_f)
    oc_b, sc_b = scan_b(*[flip(t) for t in ctx_b], state0)
    ol_b, _ = scan_b(*[flip(t) for t in lat_b], sc_b)
    return oc_f + flip(oc_b), ol_f + flip(ol_b)


def even_mixer(hx, hc, rows, need_ctx, w_in, w_out, gla_w_gate, gla_b_gate, gla_norm,
               ml_conv_w, ml_conv_b, ml_b_gates, ml_norm):
    bsz = hx.shape[0]
    sx = split_cols((hx @ w_in).astype(jnp.float32), EV_SIZES)
    sc = split_cols((hc @ w_in).astype(jnp.float32), EV_SIZES)

    def gla_prep(s):
        q, k, v, _, low_rank = s[:5]
        base = (split_heads(q, GLA_HEADS) * GLA_DK ** -0.5, split_heads(k, GLA_HEADS), split_heads(v, GLA_HEADS))
        dirs = []
        for d in range(2):
            z = low_rank[..., d * GLA_RANK:(d + 1) * GLA_RANK] @ gla_w_gate[d] + gla_b_gate[d]
            dirs.append(base + (split_heads(jax.nn.log_sigmoid(z) / GLA_GATE_NORM, GLA_HEADS),))
        return dirs

    gc, gx = gla_prep(sc), gla_prep(sx)
    s0 = jnp.zeros((bsz, GLA_HEADS, GLA_DK, GLA_DV), jnp.float32)
    oc_a, ox_a = bidirectional(gla_scan, gla_scan, gc[0], gx[0], gc[1], gx[1], s0, axis=2)

    def ml_prep(s, reorder):
        q, k, v, _, g = s[5:]
        if reorder:
            q, k, v, g = (to_col_major(t, rows) for t in (q, k, v, g))
        qk = jax.nn.silu(depthwise_conv(jnp.concatenate([q, k], axis=-1), ml_conv_w, ml_conv_b))
        base = (split_heads(qk[..., :MIX_HALF], MLSTM_HEADS),
                split_heads(qk[..., MIX_HALF:], MLSTM_HEADS) * MLSTM_DH ** -0.5,
                split_heads(v, MLSTM_HEADS))
        gates = (g.reshape(g.shape[0], g.shape[1], 2, 2, MLSTM_HEADS) + ml_b_gates).transpose(2, 3, 0, 4, 1)
        return [base + (gates[d, 0], jax.nn.log_sigmoid(gates[d, 1])) for d in range(2)]

    mc, mx = ml_prep(sc, False), ml_prep(sx, True)
    m0 = (jnp.zeros((bsz, MLSTM_HEADS, MLSTM_DH, MLSTM_DH), jnp.float32),
          jnp.zeros((bsz, MLSTM_HEADS, MLSTM_DH), jnp.float32),
          jnp.zeros((bsz, MLSTM_HEADS), jnp.float32))
    oc_b, ox_b = bidirectional(mlstm_scan, mlstm_scan, mc[0], mx[0], mc[1], mx[1], m0, axis=2)

    def merge(o_a, o_b, s, reorder):
        a = jax.nn.silu(s[3]) * head_norm(o_a, gla_norm, False)
        b = head_norm(o_b, ml_norm, True)
        if reorder:
            b = to_row_major(b, rows)
        b = jax.nn.sigmoid(s[8]) * b
        return jnp.concatenate([a, b], axis=-1).astype(hx.dtype) @ w_out

    yx = merge(ox_a, ox_b, sx, True)
    yc = merge(oc_a, oc_b, sc, False) if need_ctx else None
    return yx, yc


def odd_mixer(hx, hc, rows, need_ctx, w_in, w_out, lam_re, lam_im, log_step, b_re, b_im, c_re, c_im,
              s5_d, w_glu, b_glu, ret_log_decay, ret_norm):
    bsz = hx.shape[0]
    sx = split_cols((hx @ w_in).astype(jnp.float32), OD_SIZES)
    sc = split_cols((hc @ w_in).astype(jnp.float32), OD_SIZES)

    scans = [functools.partial(s5_direction, lam_re[d], lam_im[d], log_step[d], b_re[d], b_im[d], c_re[d], c_im[d])
             for d in range(2)]
    grp = lambda u: u.reshape(u.shape[0], u.shape[1], S5_GROUPS, S5_GROUP)
    uc, ux = grp(sc[0]), grp(sx[0])
    h0 = (jnp.zeros((bsz, S5_GROUPS, S5_STATE), jnp.float32), jnp.zeros((bsz, S5_GROUPS, S5_STATE), jnp.float32))
    oc_c, ox_c = bidirectional(scans[0], scans[1], (uc,), (ux,), (uc,), (ux,), h0, axis=1)

    log_gamma = -jnp.exp(ret_log_decay.astype(jnp.float32))

    def ret_prep(s, reorder):
        q, k, v = s[1:4]
        if reorder:
            q, k, v = (to_col_major(t, rows) for t in (q, k, v))
        base = (split_heads(q, RET_HEADS), split_heads(k, RET_HEADS) * RET_DH ** -0.5, split_heads(v, RET_HEADS))
        shape = base[0].shape
        return [base + (jnp.broadcast_to(log_gamma[d][None, :, None, None], shape),) for d in range(2)]

    rc, rx = ret_prep(sc, False), ret_prep(sx, True)
    r0 = jnp.zeros((bsz, RET_HEADS, RET_DH, RET_DH), jnp.float32)
    oc_d, ox_d = bidirectional(gla_scan, gla_scan, rc[0], rx[0], rc[1], rx[1], r0, axis=2)

    def merge(o_c, o_d, s, reorder):
        y = o_c.reshape(s[0].shape) + s5_d * s[0]
        g = jax.nn.gelu(y)
        a = g * jax.nn.sigmoid(g @ w_glu + b_glu)
        b = head_norm(o_d, ret_norm, True)
        if reorder:
            b = to_row_major(b, rows)
        b = jax.nn.silu(s[4]) * b
        return jnp.concatenate([a, b], axis=-1).astype(hx.dtype) @ w_out

    yx = merge(ox_c, ox_d, sx, True)
    yc = merge(oc_c, oc_d, sc, False) if need_ctx else None
    return yx, yc


def setup_inputs(seed: int = 0) -> dict:
    key = jax.random.key(seed)
    keys = iter(jax.random.split(key, 48))

    def nrm(shape, scale):
        return jax.random.normal(next(keys), shape, jnp.float32) * scale

    D = D_MODEL
    gate_bias = jnp.stack([jnp.zeros((MLSTM_HEADS,), jnp.float32),
                           jnp.linspace(3.0, 6.0, MLSTM_HEADS, dtype=jnp.float32)])
    ret_base = jnp.log(-jnp.log(1.0 - 2.0 ** (-5.0 - jnp.arange(RET_HEADS, dtype=jnp.float32))))
    return {
        'x': nrm((BATCH, SEQ, D), 1.0),
        'c': nrm((BATCH, D), 1.0),
        'ctx': nrm((BATCH, CTX_LEN, D), 1.0),
        'c_ctx': nrm((D,), 1.0),
        'w_mod': nrm((DEPTH, D, N_MOD * D), 0.5 * D ** -0.5),
        'b_mod': nrm((DEPTH, N_MOD * D), 0.02),
        'norm_pre': 1.0 + nrm((DEPTH, 3, D), 0.02),
        'norm_post': 1.0 + nrm((DEPTH, 3, D), 0.02),
        'ffn_w_gate': nrm((DEPTH, 2, D, D_FF), D ** -0.5),
        'ffn_w_up': nrm((DEPTH, 2, D, D_FF), D ** -0.5),
        'ffn_w_down': nrm((DEPTH, 2, D_FF, D), D_FF ** -0.5),
        'ev_w_in': nrm((N_EVEN, D, EV_COLS), D ** -0.5),
        'ev_w_out': nrm((N_EVEN, 2 * MIX_HALF, D), (2 * MIX_HALF) ** -0.5),
        'gla_w_gate': nrm((N_EVEN, 2, GLA_RANK, GLA_QK), GLA_RANK ** -0.5),
        'gla_b_gate': 2.0 + nrm((N_EVEN, 2, GLA_QK), 0.1),
        'gla_norm': 1.0 + nrm((N_EVEN, MIX_HALF), 0.02),
        'ml_conv_w': nrm((N_EVEN, MLSTM_CONV, 2 * MIX_HALF), MLSTM_CONV ** -0.5),
        'ml_conv_b': nrm((N_EVEN, 2 * MIX_HALF), 0.02),
        'ml_b_gates': gate_bias + nrm((N_EVEN, 2, 2, MLSTM_HEADS), 0.1),
        'ml_norm': 1.0 + nrm((N_EVEN, MIX_HALF), 0.02),
        'od_w_in': nrm((N_ODD, D, OD_COLS), D ** -0.5),
        'od_w_out': nrm((N_ODD, 2 * MIX_HALF, D), (2 * MIX_HALF) ** -0.5),
        's5_lam_re': -0.5 + nrm((N_ODD, 2, S5_GROUPS, S5_STATE), 0.01),
        's5_lam_im': jnp.pi * jnp.arange(S5_STATE, dtype=jnp.float32) + nrm((N_ODD, 2, S5_GROUPS, S5_STATE), 0.01),
        's5_log_step': jax.random.uniform(next(keys), (N_ODD, 2, S5_GROUPS), jnp.float32,
                                          math.log(1e-3), math.log(1e-1)),
        's5_b_re': nrm((N_ODD, 2, S5_GROUPS, S5_STATE, S5_GROUP), (2 * S5_GROUP) ** -0.5),
        's5_b_im': nrm((N_ODD, 2, S5_GROUPS, S5_STATE, S5_GROUP), (2 * S5_GROUP) ** -0.5),
        's5_c_re': nrm((N_ODD, 2, S5_GROUPS, S5_GROUP, S5_STATE), 0.5),
        's5_c_im': nrm((N_ODD, 2, S5_GROUPS, S5_GROUP, S5_STATE), 0.5),
        's5_d': nrm((N_ODD, S5_CHANNELS), 0.5),
        's5_w_glu': nrm((N_ODD, S5_CHANNELS, S5_CHANNELS), S5_CHANNELS ** -0.5),
        's5_b_glu': nrm((N_ODD, S5_CHANNELS), 0.02),
        'ret_log_decay': ret_base + nrm((N_ODD, 2, RET_HEADS), 0.05),
        'ret_norm': 1.0 + nrm((N_ODD, MIX_HALF), 0.02),
    }


def reference(x, c, ctx, c_ctx, w_mod, b_mod, norm_pre, norm_post, ffn_w_gate, ffn_w_up, ffn_w_down,
              ev_w_in, ev_w_out, gla_w_gate, gla_b_gate, gla_norm, ml_conv_w, ml_conv_b, ml_b_gates, ml_norm,
              od_w_in, od_w_out, s5_lam_re, s5_lam_im, s5_log_step, s5_b_re, s5_b_im, s5_c_re, s5_c_im,
              s5_d, s5_w_glu, s5_b_glu, ret_log_decay, ret_norm):
    rows = x.shape[1] // GRID_W
    for layer in range(DEPTH):
        need_ctx = layer < DEPTH - 1
        mod_x = (jax.nn.silu(c) @ w_mod[layer] + b_mod[layer]).reshape(c.shape[0], N_MOD, 1, D_MODEL)
        mod_c = (jax.nn.silu(c_ctx) @ w_mod[layer] + b_mod[layer]).reshape(N_MOD, D_MODEL)
        mx = [mod_x[:, j] for j in range(N_MOD)]
        mc = [mod_c[j] for j in range(N_MOD)]
        g_pre, g_post = norm_pre[layer], norm_post[layer]
        wg, wu, wd = ffn_w_gate[layer], ffn_w_up[layer], ffn_w_down[layer]

        x = x + sublayer_out(swiglu(sublayer_in(x, g_pre[0], mx[0], mx[1]), wg[0], wu[0], wd[0]), g_post[0], mx[2], FFN_RES)
        ctx = ctx + sublayer_out(swiglu(sublayer_in(ctx, g_pre[0], mc[0], mc[1]), wg[0], wu[0], wd[0]), g_post[0], mc[2], FFN_RES)

        hx = sublayer_in(x, g_pre[1], mx[3], mx[4])
        hc = sublayer_in(ctx, g_pre[1], mc[3], mc[4])
        if layer % 2 == 0:
            e = layer // 2
            yx, yc = even_mixer(hx, hc, rows, need_ctx, ev_w_in[e], ev_w_out[e], gla_w_gate[e], gla_b_gate[e],
                                gla_norm[e], ml_conv_w[e], ml_conv_b[e], ml_b_gates[e], ml_norm[e])
        else:
            o = layer // 2
            yx, yc = odd_mixer(hx, hc, rows, need_ctx, od_w_in[o], od_w_out[o], s5_lam_re[o], s5_lam_im[o],
                               s5_log_step[o], s5_b_re[o], s5_b_im[o], s5_c_re[o], s5_c_im[o], s5_d[o],
                               s5_w_glu[o], s5_b_glu[o], ret_log_decay[o], ret_norm[o])
        x = x + sublayer_out(yx, g_post[1], mx[5], 1.0)

        x = x + sublayer_out(swiglu(sublayer_in(x, g_pre[2], mx[6], mx[7]), wg[1], wu[1], wd[1]), g_post[2], mx[8], FFN_RES)
        if need_ctx:
            ctx = ctx + sublayer_out(yc, g_post[1], mc[5], 1.0)
            ctx = ctx + sublayer_out(swiglu(sublayer_in(ctx, g_pre[2], mc[6], mc[7]), wg[1], wu[1], wd[1]), g_post[2], mc[8], FFN_RES)
    return x
```

```python
import math

import numpy as np
import concourse.bass as bass
import concourse.mybir as mybir
from concourse.bass_utils import run_bass_kernel_spmd

F32 = mybir.dt.float32
BF16 = mybir.dt.bfloat16
I32 = mybir.dt.int32
AF = mybir.ActivationFunctionType
ALU = mybir.AluOpType

_uid = [0]
import os as _os
SES_DEFAULT = _os.environ.get("SES", "1") == "1"
WAW_SKIP = _os.environ.get("WAWSKIP", "1") == "1"


class V:
    __slots__ = ("ap", "keys")

    def __init__(self, ap, keys):
        self.ap = ap
        self.keys = keys if isinstance(keys, tuple) else (keys,)

    def __getitem__(self, idx):
        return V(self.ap[idx], self.keys)

    def k(self, *keys):
        return V(self.ap, tuple(keys))

    def re(self, s, **kw):
        return V(self.ap.rearrange(s, **kw), self.keys)

    def bc(self, shape):
        return V(self.ap.to_broadcast(shape), self.keys)

    def rearrange(self, s, **kw):
        return V(self.ap.rearrange(s, **kw), self.keys)

    @property
    def shape(self):
        return self.ap.shape


class Prog:
    ENG = ("pe", "act", "dve", "pool", "sp")

    def __init__(self, name="k", ring=6, same_engine_sync=SES_DEFAULT):
        self.nc = bass.Bass("TRN2", target_bir_lowering=False, name=name)
        nc = self.nc
        self.eng = {"pe": nc.tensor, "act": nc.scalar, "dve": nc.vector, "pool": nc.gpsimd, "sp": nc.sync}
        self.sem = {k: nc.alloc_semaphore("c_" + k) for k in self.ENG}
        self.cnt = {k: 0 for k in self.ENG}
        self.seen = {k: {} for k in self.ENG}
        self.ring = {q: [[nc.alloc_semaphore("d_%s%d" % (q, i)), 0] for i in range(ring)] for q in ("sp", "pool", "act")}
        self.rpos = {q: 0 for q in self.ring}
        self.lastw = {}
        self.reads = {}
        self.ses = same_engine_sync
        self.out_events = []
        self.n_inst = 0

    def dram(self, name, shape, dt=F32, kind="ExternalInput"):
        bind = getattr(self, "bind", None)
        if bind is not None and name in bind:
            return bind[name]
        return self.nc.dram_tensor(name, list(shape), dt, kind=kind).ap()

    def dram_i(self, name, shape, dt=F32):
        return V(self.nc.dram_tensor(name, list(shape), dt, kind="Internal").ap(), "dram_" + name)

    def push_scope(self):
        import contextlib
        if not hasattr(self, "scopes"):
            self.scopes = []
            self.scope_id = 0
        self.scope_id += 1
        self.scopes.append((contextlib.ExitStack(), self.scope_id))

    def pop_scope(self):
        self.barrier()
        st, _ = self.scopes.pop()
        st.close()

    def _uname(self, name):
        if getattr(self, "scopes", None):
            return "%s_s%d" % (name, self.scopes[-1][1])
        return name

    def sb(self, shape, dt=F32, name=None):
        _uid[0] += 1
        name = self._uname(name or ("t%d" % _uid[0]))
        if getattr(self, "scopes", None):
            t = self.scopes[-1][0].enter_context(self.nc.sbuf_tensor("sb_" + name, list(shape), dt))
        else:
            t = self.nc.alloc_sbuf_tensor("sb_" + name, list(shape), dt)
        return V(t.ap() if hasattr(t, "ap") else t[:], name)

    def ps(self, shape, dt=F32, name=None):
        _uid[0] += 1
        base = name or ("p%d" % _uid[0])
        name = self._uname(base)
        esz = 2 if dt == BF16 else 4
        if getattr(self, "scopes", None):
            t = self.scopes[-1][0].enter_context(self.nc.psum_tensor("ps_" + name, [128, 2048 // esz], dt))
        else:
            t = self.nc.alloc_psum_tensor("ps_" + name, [128, 2048 // esz], dt)
        full = V(t.ap() if hasattr(t, "ap") else t[:], name)
        if not hasattr(self, "banks"):
            self.banks = {}
        self.banks[base] = full
        return self.ps_alias(base, shape)

    def barrier(self):
        evs = []
        for x in self.ENG:
            if self.cnt[x] > 0:
                evs.append((self.sem[x], self.cnt[x], "c_" + x))
        for q in self.ring:
            for i, slot in enumerate(self.ring[q]):
                if slot[1] > 0:
                    evs.append((slot[0], slot[1], "d_%s%d" % (q, i)))
        if hasattr(self, "cc_sem") and self.cc_cnt > 0:
            evs.append((self.cc_sem, self.cc_cnt, "cc_sem"))
        for e in self.ENG:
            for ev in evs:
                if ev[2] == "c_" + e:
                    continue
                self._wait(e, ev)

    def ps_alias(self, name, shape):
        full = self.banks[name]
        n = 1
        for d in shape[1:]:
            n *= d
        v = full[0:shape[0], 0:n]
        if len(shape) > 2:
            letters = "abcdefg"[:len(shape) - 1]
            kw = {letters[i]: shape[i + 1] for i in range(1, len(shape) - 1)}
            v = v.re("p (%s) -> p %s" % (" ".join(letters), " ".join(letters)), **kw)
        return v

    def _wait(self, e, ev):
        sem, val, key = ev
        if self.seen[e].get(key, 0) >= val:
            return
        self.eng[e].wait_ge(sem, val)
        self.seen[e][key] = val

    def _deps(self, e, reads, writes):
        raw, waw = [], []
        for v in reads:
            for k in v.keys:
                lw = self.lastw.get(k)
                if lw is not None:
                    raw.append(lw)
        for v in writes:
            for k in v.keys:
                lw = self.lastw.get(k)
                if lw is not None:
                    waw.append(lw)
                waw.extend(self.reads.get(k, ()))
        for ev in raw:
            if ev[3] == e and (e == "pe" or not self.ses):
                continue
            self._wait(e, ev[:3])
        for ev in waw:
            if ev[3] == e and (e == "pe" or not self.ses or WAW_SKIP):
                continue
            self._wait(e, ev[:3])

    def _commit(self, ev, reads, writes):
        for v in writes:
            for k in v.keys:
                self.lastw[k] = ev
                self.reads[k] = []
        for v in reads:
            for k in v.keys:
                self.reads.setdefault(k, []).append(ev)
                if len(self.reads[k]) > 24:
                    best = {}
                    for r in self.reads[k]:
                        if r[2] not in best or best[r[2]][1] < r[1]:
                            best[r[2]] = r
                    self.reads[k] = list(best.values())

    def op(self, e, fn, reads=(), writes=()):
        self._deps(e, reads, writes)
        inst = fn(self.eng[e])
        self.cnt[e] += 1
        inst.then_inc(self.sem[e], 1)
        ev = (self.sem[e], self.cnt[e], "c_" + e, e)
        self._commit(ev, reads, writes)
        self.n_inst += 1
        return ev

    def dma(self, out, in_, q="sp", is_output=False, **kw):
        reads = [in_] if isinstance(in_, V) else []
        writes = [out] if isinstance(out, V) else []
        self._deps(q, reads, writes)
        slot = self.ring[q][self.rpos[q] % len(self.ring[q])]
        key = "d_%s%d" % (q, self.rpos[q] % len(self.ring[q]))
        self.rpos[q] += 1
        if slot[1] > 0:
            self._wait(q, (slot[0], slot[1], key))
        o = out.ap if isinstance(out, V) else out
        i = in_.ap if isinstance(in_, V) else in_
        inst = self.eng[q].dma_start(out=o, in_=i, **kw)
        slot[1] += 16
        inst.then_inc(slot[0], 16)
        ev = (slot[0], slot[1], key, "dma")
        self._commit(ev, reads, writes)
        if is_output:
            self.out_events.append(ev)
        self.n_inst += 1
        return ev

    def finish(self):
        for q in self.ring:
            for i, slot in enumerate(self.ring[q]):
                if slot[1] > 0:
                    self._wait("sp", (slot[0], slot[1], "d_%s%d" % (q, i)))
        return self.nc

    def mm(self, out, lhsT, rhs, start=True, stop=True):
        return self.op("pe", lambda e: e.matmul(out.ap, lhsT.ap, rhs.ap, start=start, stop=stop),
                       reads=[lhsT, rhs], writes=[out])

    def transpose(self, out, in_, ident):
        return self.op("pe", lambda e: e.transpose(out.ap, in_.ap, ident.ap), reads=[in_, ident], writes=[out])

    def act(self, out, in_, func, bias=None, scale=None, accum_out=None, eng="act"):
        reads = [in_]
        kw = {}
        if bias is not None:
            if isinstance(bias, V):
                reads.append(bias)
                kw["bias"] = bias.ap
            else:
                kw["bias"] = bias
        if scale is not None:
            if isinstance(scale, V):
                reads.append(scale)
                kw["scale"] = scale.ap
            else:
                kw["scale"] = scale
        writes = [out]
        if accum_out is not None:
            writes.append(accum_out)
            kw["accum_out"] = accum_out.ap
        return self.op("act", lambda e: e.activation(out.ap, in_.ap, func, **kw), reads=reads, writes=writes)

    def tt(self, out, a, b, op, eng="dve"):
        return self.op(eng, lambda e: e.tensor_tensor(out.ap, a.ap, b.ap, op), reads=[a, b], writes=[out])

    def ts(self, out, a, s1, op0, s2=None, op1=None, eng="dve", accum_out=None):
        reads = [a]
        x1 = s1.ap if isinstance(s1, V) else s1
        x2 = s2.ap if isinstance(s2, V) else s2
        if isinstance(s1, V):
            reads.append(s1)
        if isinstance(s2, V):
            reads.append(s2)
        kw = {}
        writes = [out]
        if accum_out is not None:
            kw["accum_out"] = accum_out.ap
            writes.append(accum_out)
        if op1 is None:
            return self.op(eng, lambda e: e.tensor_scalar(out.ap, a.ap, x1, None, op0, **kw), reads=reads, writes=writes)
        return self.op(eng, lambda e: e.tensor_scalar(out.ap, a.ap, x1, x2, op0, op1, **kw), reads=reads, writes=writes)

    def stt(self, out, a, s, b, op0, op1):
        reads = [a, b]
        x = s.ap if isinstance(s, V) else s
        if isinstance(s, V):
            reads.append(s)
        return self.op("dve", lambda e: e.scalar_tensor_tensor(out.ap, a.ap, x, b.ap, op0, op1), reads=reads, writes=[out])

    def copy(self, out, in_, eng="dve"):
        if eng == "act":
            return self.op("act", lambda e: e.copy(out.ap, in_.ap), reads=[in_], writes=[out])
        return self.op(eng, lambda e: e.tensor_copy(out.ap, in_.ap), reads=[in_], writes=[out])

    def memset(self, out, val, eng="dve"):
        return self.op(eng, lambda e: e.memset(out.ap, val), writes=[out])

    def recip(self, out, in_):
        return self.op("dve", lambda e: e.reciprocal(out.ap, in_.ap), reads=[in_], writes=[out])

    def scan(self, out, d0, d1, init, op0, op1):
        reads = [d0, d1]
        x = init.ap if isinstance(init, V) else init
        if isinstance(init, V):
            reads.append(init)
        return self.op("dve", lambda e: e.tensor_tensor_scan(out.ap, d0.ap, d1.ap, x, op0, op1), reads=reads, writes=[out])

    def aselect(self, out, in_, pattern, cmp, fill, base, cm):
        return self.op("pool", lambda e: e.affine_select(out.ap, in_.ap, pattern, cmp, fill, base=base, channel_multiplier=cm),
                       reads=[in_], writes=[out])

    def iota(self, out, pattern, base, cm):
        return self.op("pool", lambda e: e.iota(out.ap, pattern, base=base, channel_multiplier=cm,
                                                 allow_small_or_imprecise_dtypes=True), writes=[out])


def run(prog, in_maps, n=8, trace=False):
    nc = prog.finish()
    res = run_bass_kernel_spmd(nc, in_maps, core_ids=list(range(n)), trace=trace)
    return res


def _collective(self, kind, out, in_, groups, op=None):
    q = "pool"
    if not hasattr(self, "cc_sem"):
        self.cc_sem = self.nc.alloc_semaphore("cc_sem")
        self.cc_cnt = 0
    self._deps(q, [in_], [out])
    inst = self.eng[q].collective_compute(kind, op or ALU.bypass, replica_groups=groups, ins=[in_.ap], outs=[out.ap])
    self.cc_cnt += 1
    inst.then_inc(self.cc_sem)
    ev = (self.cc_sem, self.cc_cnt, "cc_sem", "dma")
    self._commit(ev, [in_], [out])
    self._wait(q, ev[:3])
    self.n_inst += 1
    return ev


Prog.collective = _collective


def make_masks(P):
    U = P.sb([128, 128], F32, "Uincl")
    L = P.sb([128, 128], F32, "Lstrict")
    P.memset(U, 1.0, eng="pool")
    P.memset(L, 1.0, eng="pool")
    P.aselect(U, U, [[1, 128]], ALU.is_ge, 0.0, 0, -1)
    P.aselect(L, L, [[-1, 128]], ALU.is_gt, 0.0, 0, 1)
    return U, L


def emit_gla(P, NU, NCH, DK, DV, mode, qscale, kscale):
    N = NCH * 128
    NDK = DK // 128
    GN = 16.0 if mode == "gla" else 1.0
    qT = P.dram("qT", [NU, DK, N])
    kT = P.dram("kT", [NU, DK, N])
    kk = P.dram("k", [NU, N, DK])
    vv = P.dram("v", [NU, N, DV])
    if mode == "gla":
        lrT = P.dram("lrT", [NU, 16, N])
        wg = P.dram("wg", [NU, 17, DK])
    else:
        dec = P.dram("dec", [NU, 128, 1])
    o = P.dram("o", [NU, N, DV], kind="ExternalOutput")
    U, L = make_masks(P)
    ps_b = [P.ps([128, 128], F32, "psb%d" % i) for i in range(2)]
    ps_d = P.ps([128, DK], F32, "psd")
    ps_z = P.ps([128, DK], F32, "psz")
    ps_sc = P.ps([128, 128], F32, "pssc")
    ps_o = P.ps([128, DV], F32, "pso")
    ps_S = [P.ps([128, DV], F32, "psS%d" % i) for i in range(2)]
    if mode != "gla":
        zer = P.sb([128, DK], F32, "zer")
        P.memset(zer, 0.0)

    class B_:
        pass

    def alloc(u):
        B = B_()
        n = lambda s_: "%s_u%d" % (s_, u)
        B.qT_sb = [P.sb([128, NDK, 128], F32, n("qTs%d" % i)) for i in range(2)]
        B.kT_sb = [P.sb([128, NDK, 128], F32, n("kTs%d" % i)) for i in range(2)]
        B.k_sb = [P.sb([128, DK], F32, n("ks%d" % i)) for i in range(2)]
        B.v_sb = [P.sb([128, DV], F32, n("vs%d" % i)) for i in range(2)]
        B.v_bf = [P.sb([128, DV], BF16, n("vb%d" % i)) for i in range(2)]
        B.sp_t = [P.sb([128, DK], F32, n("spt%d" % i)) for i in range(2)]
        B.ex = P.sb([128, DK], F32, n("ex"))
        B.E1 = P.sb([128, NDK, 128], F32, n("E1"))
        B.E2 = P.sb([128, NDK, 128], F32, n("E2"))
        B.Dk = P.sb([128, DK], F32, n("Dk"))
        B.QtT = P.sb([128, NDK, 128], BF16, n("QtT"))
        B.KtT = P.sb([128, NDK, 128], BF16, n("KtT"))
        B.QbT = P.sb([128, NDK, 128], BF16, n("QbT"))
        B.Ke = P.sb([128, DK], BF16, n("Ke"))
        B.scm = P.sb([128, 128], BF16, n("scm"))
        B.S = P.sb([128, NDK, DV], F32, n("S"))
        B.Sbf = P.sb([128, NDK, DV], BF16, n("Sbf"))
        B.cols = P.sb([128, NDK, 4], F32, n("cols"))
        B.o_sb = [P.sb([128, DV], F32, n("osb%d" % i)) for i in range(2)]
        if mode == "gla":
            B.lr_sb = P.sb([17, N], F32, n("lrsb"))
            B.wg_sb = P.sb([17, DK], F32, n("wgsb"))
        else:
            B.dec_sb = P.sb([128, 1], F32, n("decsb"))
        return B

    Bs = [alloc(u) for u in range(NU)]
    for u in range(NU):
        B = Bs[u]
        P.memset(B.S, 0.0)
        P.memset(B.Sbf, 0.0, eng="pool")
        if mode == "gla":
            P.memset(B.lr_sb, 1.0)
            P.dma(B.lr_sb[0:16, :], lrT[u])
            P.dma(B.wg_sb, wg[u])
        else:
            P.dma(B.dec_sb, dec[u])
            P.act(B.sp_t[0], zer, AF.Exp, bias=B.dec_sb)
            P.act(B.sp_t[1], zer, AF.Exp, bias=B.dec_sb)
    for c in range(NCH):
        for u in range(NU):
            B = Bs[u]
            b = c % 2
            tsl = slice(c * 128, (c + 1) * 128)
            P.dma(B.qT_sb[b], qT[u, :, tsl].rearrange("(dc p) t -> p dc t", p=128))
            P.dma(B.kT_sb[b], kT[u, :, tsl].rearrange("(dc p) t -> p dc t", p=128))
            P.dma(B.k_sb[b], kk[u, tsl, :])
            P.dma(B.v_sb[b], vv[u, tsl, :])
            P.copy(B.v_bf[b], B.v_sb[b], eng="pool")
            spt = B.sp_t[b]
            if mode == "gla":
                P.mm(ps_z, B.lr_sb[:, tsl], B.wg_sb)
                P.act(B.ex, ps_z, AF.Exp, scale=-1.0)
                P.act(spt, B.ex, AF.Ln, bias=1.0)
            P.mm(ps_d, L, spt)
            P.act(B.Dk, ps_d, AF.Exp, scale=-1.0 / GN)
            P.stt(B.Ke, B.k_sb[b], float(kscale), B.Dk, ALU.mult, ALU.mult)
            for dc in range(NDK):
                pb = ps_b[dc % 2]
                P.mm(pb, spt[:, dc * 128:(dc + 1) * 128], U)
                cm = B.cols[:, dc, :]
                P.ts(cm[:, 0:1], pb[:, 63:64], 1.0 / GN, ALU.mult)
                P.ts(cm[:, 1:2], pb[:, 63:64], -1.0 / GN, ALU.mult)
                P.act(B.E1[:, dc, :], pb, AF.Exp, scale=-1.0 / GN, bias=cm[:, 0:1])
                P.act(B.E2[:, dc, :], pb, AF.Exp, scale=1.0 / GN, bias=cm[:, 1:2])
                P.act(cm[:, 2:3], pb[:, 63:64], AF.Exp, scale=-1.0 / GN)
                P.act(cm[:, 3:4], pb[:, 127:128], AF.Exp, scale=-1.0 / GN)
                P.stt(B.QtT[:, dc, :], B.qT_sb[b][:, dc, :], float(qscale), B.E1[:, dc, :], ALU.mult, ALU.mult)
                P.stt(B.KtT[:, dc, :], B.kT_sb[b][:, dc, :], float(kscale), B.E2[:, dc, :], ALU.mult, ALU.mult)
                P.ts(B.QbT[:, dc, :], B.QtT[:, dc, :], cm[:, 2:3], ALU.mult)
            for dc in range(NDK):
                P.mm(ps_sc, B.KtT[:, dc, :], B.QtT[:, dc, :], start=(dc == 0), stop=(dc == NDK - 1))
            P.tt(B.scm, ps_sc, U, ALU.mult)
            P.mm(ps_o, B.scm, B.v_bf[b], start=True, stop=False)
            for dc in range(NDK):
                P.mm(ps_o, B.QbT[:, dc, :], B.Sbf[:, dc, :], start=False, stop=(dc == NDK - 1))
            ob = B.o_sb[b]
            P.copy(ob, ps_o, eng="act")
            P.dma(o[u, tsl, :], ob, q="pool", is_output=True)
            for dc in range(NDK):
                pS = ps_S[dc % 2]
                P.mm(pS, B.Ke[:, dc * 128:(dc + 1) * 128], B.v_bf[b])
                P.stt(B.S[:, dc, :], B.S[:, dc, :], B.cols[:, dc, 3:4], pS, ALU.mult, ALU.add)
                P.copy(B.Sbf[:, dc, :], B.S[:, dc, :], eng="act")
    return P


def build_gla(NU, NCH, DK, DV, mode, qscale, kscale, name="gla"):
    P = Prog(name)
    emit_gla(P, NU, NCH, DK, DV, mode, qscale, kscale)
    return P


def emit_mlstm(P, NU, segs, DH, kscale):
    NCH = sum(segs)
    N = NCH * 128
    NDC = DH // 128
    DA = DH + 1
    qpT = P.dram("qpT", [NU, DH, N])
    kpT = P.dram("kpT", [NU, DH, N])
    vv = P.dram("v", [NU, N, DH])
    cwq = P.dram("cwq", [NU, DH, 3])
    cwk = P.dram("cwk", [NU, DH, 3])
    cbq = P.dram("cbq", [NU, DH, 1])
    cbk = P.dram("cbk", [NU, DH, 1])
    gi = P.dram("gi", [NU, N])
    gf = P.dram("gf", [NU, N])
    bi = P.dram("bi", [NU, 1])
    bf_ = P.dram("bf", [NU, 1])
    ho = P.dram("h", [NU, N, DH], kind="ExternalOutput")
    U, L = make_masks(P)
    identf = P.sb([128, 128], F32, "identf")
    P.memset(identf, 1.0, eng="pool")
    P.aselect(identf, identf, [[-1, 128]], ALU.is_equal, 0.0, 0, 1)
    identb = P.sb([128, 128], BF16, "identb")
    P.copy(identb, identf)
    R = P.sb([NU, 4, N], F32, "R")
    bcol = P.sb([NU, 4], F32, "bcol")
    with P.nc.allow_non_contiguous_dma(reason="small"):
        P.dma(R[:, 0, :], gf)
        P.dma(R[:, 1, :], gi)
        P.dma(bcol[:, 0:1], bf_)
        P.dma(bcol[:, 1:2], bi)
    P.ts(bcol[:, 2:3], bcol[:, 0:1], -1.0, ALU.mult)
    P.act(R[:, 2, :], R[:, 0, :], AF.Exp, scale=-1.0, bias=bcol[:, 2:3])
    P.act(R[:, 2, :], R[:, 2, :], AF.Ln, bias=1.0)
    P.scan(R[:, 3, :], R[:, 2, :], R[:, 2, :], 0.0, ALU.add, ALU.max)
    P.stt(R[:, 1, :], R[:, 1, :], bcol[:, 1:2], R[:, 3, :], ALU.add, ALU.add)
    P.scan(R[:, 2, :], R[:, 1, :], R[:, 1, :], 0.0, ALU.max, ALU.max)
    P.tt(R[:, 0, :], R[:, 2, :], R[:, 3, :], ALU.subtract)
    idn = P.sb([NU, NU], F32, "idn")
    P.memset(idn, 1.0, eng="pool")
    P.aselect(idn, idn, [[-1, NU]], ALU.is_equal, 0.0, 0, 1)
    sel = []
    for u in range(NU):
        s_ = P.sb([NU, 128], F32, "sel%d" % u)
        P.memset(s_, 1.0, eng="pool")
        P.aselect(s_, s_, [[0, 128]], ALU.is_equal, 0.0, -u, 1)
        sel.append(s_)
    assert NCH * 3 * NU <= 512
    ps_c = P.ps([128, NCH, 3, NU], F32, "psc")
    for c in range(NCH):
        for qi, row in enumerate((1, 2, 0)):
            P.mm(ps_c[:, c, qi, :], R[:, row, c * 128:(c + 1) * 128], idn)
    colsb = P.sb([128, NCH, 3, NU], F32, "colsb")
    P.copy(colsb, ps_c)
    HW = 130
    ps_M = P.ps([128, 128], F32, "psM")
    ps_sc = P.ps([128, 128], F32, "pssc")
    ps_o = P.ps([128, DA], F32, "pso")
    ps_t = P.ps([128, DH], BF16, "pst")
    ps_C = [P.ps([128, DA], F32, "psC%d" % i) for i in range(2)]
    seg_first = set()
    seg_last = set()
    c0 = 0
    for n in segs:
        seg_first.add(c0)
        seg_last.add(c0 + n - 1)
        c0 += n

    class B_:
        pass

    def alloc(u):
        B = B_()
        n = lambda s_: "%s_u%d" % (s_, u)
        B.qp_sb = [P.sb([128, NDC, HW], F32, n("qps%d" % i)) for i in range(2)]
        B.kp_sb = [P.sb([128, NDC, HW], F32, n("kps%d" % i)) for i in range(2)]
        B.acc = [P.sb([128, 128], F32, n("acc%d" % i)) for i in range(2)]
        B.qT = P.sb([128, NDC, 128], BF16, n("qT"))
        B.kT = P.sb([128, NDC, 128], BF16, n("kT"))
        B.qw = P.sb([128, NDC, 128], BF16, n("qw"))
        B.ktok = P.sb([128, DH], BF16, n("ktok"))
        B.va = [P.sb([128, DA], F32, n("va%d" % i)) for i in range(2)]
        for i in range(2):
            P.memset(B.va[i], 1.0)
        B.va_bf = P.sb([128, DA], BF16, n("vabf"))
        B.vw = P.sb([128, DA], BF16, n("vw"))
        B.cw = P.sb([128, 2, NDC, 4], F32, n("cw"))
        B.arg = P.sb([128, 128], F32, n("arg"))
        B.Dm = P.sb([128, 128], F32, n("Dm"))
        B.Wbc = P.sb([128, 128], F32, n("Wbc"))
        B.scm = P.sb([128, 128], BF16, n("scm"))
        B.Cst = P.sb([128, NDC, DA], F32, n("Cst"))
        B.Cbf = P.sb([128, NDC, DA], BF16, n("Cbf"))
        B.sc = P.sb([128, 12], F32, n("sc"))
        B.h_sb = [P.sb([128, DH], F32, n("hsb%d" % i)) for i in range(2)]
        return B

    Bs = [alloc(u) for u in range(NU)]
    for u in range(NU):
        B = Bs[u]
        P.memset(B.Cst, 0.0)
        P.memset(B.Cbf, 0.0, eng="pool")
        P.memset(B.sc[:, 0:1], 0.0)
        with P.nc.allow_non_contiguous_dma(reason="small"):
            P.dma(B.cw[:, 0, :, 0:3], cwq[u].rearrange("(dc p) k -> p dc k", p=128))
            P.dma(B.cw[:, 1, :, 0:3], cwk[u].rearrange("(dc p) k -> p dc k", p=128))
            P.dma(B.cw[:, 0, :, 3:4], cbq[u].rearrange("(dc p) k -> p dc k", p=128))
            P.dma(B.cw[:, 1, :, 3:4], cbk[u].rearrange("(dc p) k -> p dc k", p=128))
    for c in range(NCH):
        for u in range(NU):
            B = Bs[u]
            sc = B.sc
            b = c % 2
            tsl = slice(c * 128, (c + 1) * 128)
            lo = c * 128 - 1
            hi = c * 128 + 129
            dlo, dhi = 0, HW
            if c in seg_first:
                lo += 1
                dlo = 1
            if c in seg_last:
                hi -= 1
                dhi = HW - 1
            for (src, dst) in ((qpT, B.qp_sb[b]), (kpT, B.kp_sb[b])):
                if c in seg_first:
                    P.memset(dst[:, :, 0:1], 0.0, eng="pool")
                if c in seg_last:
                    P.memset(dst[:, :, HW - 1:HW], 0.0, eng="pool")
                P.dma(dst[:, :, dlo:dhi], src[u, :, lo:hi].rearrange("(dc p) t -> p dc t", p=128))
            P.dma(B.va[b][:, 0:DH], vv[u, tsl, :])
            P.copy(B.va_bf, B.va[b], eng="pool")
            k_ = 0
            for qk, (src, dstT) in enumerate(((B.qp_sb[b], B.qT), (B.kp_sb[b], B.kT))):
                for dc in range(NDC):
                    a_ = B.acc[k_ % 2]
                    k_ += 1
                    w = B.cw[:, qk, dc, :]
                    P.ts(a_, src[:, dc, 1:129], w[:, 1:2], ALU.mult, w[:, 3:4], ALU.add)
                    P.stt(a_, src[:, dc, 0:128], w[:, 0:1], a_, ALU.mult, ALU.add)
                    P.stt(a_, src[:, dc, 2:130], w[:, 2:3], a_, ALU.mult, ALU.add)
                    P.act(dstT[:, dc, :], a_, AF.Silu)
            a_col = colsb[:, c, 0, u:u + 1]
            m_col = colsb[:, c, 2, u:u + 1]
            P.mm(ps_M, sel[u], R[:, 2, tsl])
            P.ts(B.arg, ps_M, a_col, ALU.subtract, 0.0, ALU.max)
            P.act(B.Dm, B.arg, AF.Exp, scale=-1.0)
            P.tt(B.Dm, B.Dm, U, ALU.mult)
            for dc in range(NDC):
                P.mm(ps_sc, B.kT[:, dc, :], B.qT[:, dc, :], start=(dc == 0), stop=(dc == NDC - 1))
            P.stt(B.scm, ps_sc, float(kscale), B.Dm, ALU.mult, ALU.mult)
            P.act(B.Wbc, ps_M, AF.Exp, scale=-1.0, bias=sc[:, 0:1])
            for dc in range(NDC):
                P.tt(B.qw[:, dc, :], B.qT[:, dc, :], B.Wbc, ALU.mult)
            P.mm(ps_o, B.scm, B.va_bf, start=True, stop=False)
            for dc in range(NDC):
                P.mm(ps_o, B.qw[:, dc, :], B.Cbf[:, dc, :], start=False, stop=(dc == NDC - 1))
            P.act(sc[:, 5:6], m_col, AF.Exp, scale=-1.0)
            P.copy(sc[:, 9:10], ps_o[:, DH:DA])
            P.stt(sc[:, 6:7], sc[:, 9:10], -1.0, sc[:, 9:10], ALU.mult, ALU.max)
            P.tt(sc[:, 7:8], sc[:, 6:7], sc[:, 5:6], ALU.max)
            P.recip(sc[:, 8:9], sc[:, 7:8])
            hb = B.h_sb[b]
            P.ts(hb, ps_o[:, 0:DH], sc[:, 8:9], ALU.mult)
            P.dma(ho[u, tsl, :], hb, q="pool", is_output=True)
            P.copy(sc[:, 1:2], ps_M[:, 127:128])
            P.ts(sc[:, 2:3], ps_M[:, 127:128], -1.0, ALU.mult)
            P.act(sc[:, 3:4], a_col, AF.Exp, bias=sc[:, 2:3])
            P.ts(sc[:, 3:4], sc[:, 3:4], float(kscale), ALU.mult)
            P.ts(B.vw, B.va[b], sc[:, 3:4], ALU.mult)
            for dc in range(NDC):
                P.transpose(ps_t[:, dc * 128:(dc + 1) * 128], B.kT[:, dc, :], identb)
            P.copy(B.ktok, ps_t, eng="act")
            P.act(sc[:, 4:5], sc[:, 0:1], AF.Exp, bias=sc[:, 2:3])
            for dc in range(NDC):
                pC = ps_C[dc % 2]
                P.mm(pC, B.ktok[:, dc * 128:(dc + 1) * 128], B.vw)
                P.stt(B.Cst[:, dc, :], B.Cst[:, dc, :], sc[:, 4:5], pC, ALU.mult, ALU.add)
                P.copy(B.Cbf[:, dc, :], B.Cst[:, dc, :], eng="act")
            P.copy(sc[:, 0:1], sc[:, 1:2])
    return P


def build_mlstm(NU, segs, DH, kscale, name="mlstm"):
    P = Prog(name)
    emit_mlstm(P, NU, segs, DH, kscale)
    return P


TWO_PI = 2.0 * math.pi
TOKR = 2176
LATR = 2048


def emit_s5(P, G, GB, NB, NBLK):
    NMC = NB * NBLK
    Uin = P.dram("Uin", [G, 128, NMC])
    lamre_d = P.dram("lamre", [64, G])
    lamim_d = P.dram("lamim", [64, G])
    lstep_d = P.dram("lstep", [64, G])
    Bre_d = P.dram("Bre", [64, G, 16])
    Bim_d = P.dram("Bim", [64, G, 16])
    Cre_d = P.dram("Cre", [64, G, 16])
    Cim_d = P.dram("Cim", [64, G, 16])
    Y = P.dram("Y", [G, 128, NMC], kind="ExternalOutput")
    NK = 24
    kk = [7, 6, 5, 4, 3, 2, 1, 0] + [1, 2, 3, 4, 5, 6, 7, 8] + [-1, -2, -3, -4, -5, -6, -7, -8]
    I64 = P.sb([64, 64], F32, "I64")
    P.memset(I64, 1.0, eng="pool")
    P.aselect(I64, I64, [[-1, 64]], ALU.is_equal, 0.0, 0, 1)
    BM = P.sb([128, 8, 16], F32, "BM")
    P.memset(BM, 1.0, eng="pool")
    P.aselect(BM, BM, [[16, 8], [0, 16]], ALU.is_ge, 0.0, 15, -1)
    Tm = P.sb([128, G, 128], BF16, "Tm")
    W2 = P.sb([128, G, 128], BF16, "W2")
    Vre = P.sb([64, G, 128], BF16, "Vre")
    Vim = P.sb([64, G, 128], BF16, "Vim")
    MU1 = P.sb([64, 2, G], F32, "MU1")
    MUa = P.sb([64, G], F32, "MUa")
    MUb = P.sb([64, G], F32, "MUb")
    lre = P.sb([64, GB], F32, "lre")
    lim = P.sb([64, GB], F32, "lim")
    lst = P.sb([64, GB], F32, "lst")
    Bre = P.sb([64, GB, 16], F32, "sBre")
    Bim = P.sb([64, GB, 16], F32, "sBim")
    Cre = P.sb([64, GB, 16], F32, "sCre")
    Cim = P.sb([64, GB, 16], F32, "sCim")
    step = P.sb([64, GB], F32, "step")
    rho = P.sb([64, GB], F32, "rho")
    th = P.sb([64, GB], F32, "th")
    PH = P.sb([64, GB, NK], F32, "PH")
    RH = P.sb([64, GB, NK], F32, "RH")
    mag = P.sb([64, GB, NK], F32, "mag")
    TT = P.sb([64, GB, NK, 2], F32, "TT")
    TI = P.sb([64, GB, NK, 2], I32, "TI")
    TF = P.sb([64, GB, NK, 2], F32, "TF")
    LPre = P.sb([64, GB, NK], F32, "LPre")
    LPim = P.sb([64, GB, NK], F32, "LPim")
    w = [P.sb([64, GB], F32, "w%d" % i) for i in range(6)]
    BTre = P.sb([64, GB, 16], F32, "BTre")
    BTim = P.sb([64, GB, 16], F32, "BTim")
    t16a = P.sb([64, GB, 16], F32, "t16a")
    t16b = P.sb([64, GB, 16], F32, "t16b")
    Are = P.sb([64, GB, 8, 16], F32, "Are")
    Aim = P.sb([64, GB, 8, 16], F32, "Aim")
    Apre = P.sb([64, GB, 8, 16], F32, "Apre")
    Apim = P.sb([64, GB, 8, 16], F32, "Apim")
    Dre = P.sb([64, GB, 8, 16], F32, "Dre")
    nDim = P.sb([64, GB, 8, 16], F32, "nDim")
    t1 = P.sb([64, GB, 8, 16], F32, "t1")
    t2 = P.sb([64, GB, 8, 16], F32, "t2")
    ps_T = [P.ps([128, 128], F32, "psT%d" % i) for i in range(2)]
    ps_W = [P.ps([128, 128], F32, "psW%d" % i) for i in range(2)]

    def bc16(v):
        return V(v.ap.unsqueeze(2).to_broadcast([64, GB, 16]), v.keys)

    def cmul(out_re, out_im, lp0, xr, xi, neg_im=False):
        a_re = V(LPre[:, :, lp0:lp0 + 8].ap.unsqueeze(3).to_broadcast([64, GB, 8, 16]), LPre.keys)
        a_im = V(LPim[:, :, lp0:lp0 + 8].ap.unsqueeze(3).to_broadcast([64, GB, 8, 16]), LPim.keys)
        b_re = V(xr.ap.unsqueeze(2).to_broadcast([64, GB, 8, 16]), xr.keys)
        b_im = V(xi.ap.unsqueeze(2).to_broadcast([64, GB, 8, 16]), xi.keys)
        P.tt(t1, a_re, b_re, ALU.mult)
        P.tt(t2, a_im, b_im, ALU.mult)
        P.tt(out_re, t1, t2, ALU.subtract)
        P.tt(t1, a_re, b_im, ALU.mult)
        P.tt(t2, a_im, b_re, ALU.mult)
        P.tt(out_im, t1, t2, ALU.add)
        if neg_im:
            P.ts(out_im, out_im, -1.0, ALU.mult)

    fl = lambda v, gi: v[:, gi, :, :].re("p a b -> p (a b)")
    for g0 in range(0, G, GB):
        gsl = slice(g0, g0 + GB)
        for d_, s_ in ((lre, lamre_d), (lim, lamim_d), (lst, lstep_d)):
            P.dma(d_, s_[:, gsl])
        for d_, s_ in ((Bre, Bre_d), (Bim, Bim_d), (Cre, Cre_d), (Cim, Cim_d)):
            P.dma(d_, s_[:, gsl, :])
        P.act(step, lst, AF.Exp)
        P.tt(rho, lre, step, ALU.mult)
        P.tt(th, lim, step, ALU.mult)
        for i, kv in enumerate(kk):
            P.ts(PH[:, :, i], th, float(kv), ALU.mult)
            P.ts(RH[:, :, i], rho, float(kv), ALU.mult)
        P.act(mag, RH, AF.Exp)
        P.ts(TT[:, :, :, 0], PH, 1.0 / TWO_PI, ALU.mult, 0.5, ALU.add)
        P.ts(TT[:, :, :, 1], PH, 1.0 / TWO_PI, ALU.mult, 0.75, ALU.add)
        P.copy(TI, TT)
        P.copy(TF, TI)
        P.tt(TT, TT, TF, ALU.subtract)
        P.ts(TF, TT, 0.0, ALU.is_lt)
        P.tt(TT, TT, TF, ALU.add)
        P.ts(TT, TT, TWO_PI, ALU.mult, -math.pi, ALU.add)
        P.ts(TT, TT, 3.1415925, ALU.min, -3.1415925, ALU.max)
        P.act(TT, TT, AF.Sin)
        P.tt(LPre, mag, TT[:, :, :, 1], ALU.mult)
        P.tt(LPim, mag, TT[:, :, :, 0], ALU.mult)
        abre = LPre[:, :, 8]
        abim = LPim[:, :, 8]
        P.tt(w[0], lre, lre, ALU.mult)
        P.tt(w[1], lim, lim, ALU.mult)
        P.tt(w[0], w[0], w[1], ALU.add)
        P.recip(w[0], w[0])
        P.ts(w[1], abre, -1.0, ALU.add)
        P.tt(w[2], w[1], lre, ALU.mult)
        P.tt(w[3], abim, lim, ALU.mult)
        P.tt(w[2], w[2], w[3], ALU.add)
        P.tt(w[2], w[2], w[0], ALU.mult)
        P.tt(w[4], abim, lre, ALU.mult)
        P.tt(w[5], w[1], lim, ALU.mult)
        P.tt(w[4], w[4], w[5], ALU.subtract)
        P.tt(w[4], w[4], w[0], ALU.mult)
        cre, cim = w[2], w[4]
        P.tt(t16a, Bre, bc16(cre), ALU.mult)
        P.tt(t16b, Bim, bc16(cim), ALU.mult)
        P.tt(BTre, t16a, t16b, ALU.subtract)
        P.tt(t16a, Bim, bc16(cre), ALU.mult)
        P.tt(t16b, Bre, bc16(cim), ALU.mult)
        P.tt(BTim, t16a, t16b, ALU.add)
        P.copy(MU1[:, 0, gsl], LPre[:, :, 15])
        P.copy(MU1[:, 1, gsl], LPre[:, :, 15])
        P.copy(MUb[:, gsl], LPim[:, :, 15])
        P.ts(MUa[:, gsl], LPim[:, :, 15], -1.0, ALU.mult)
        cmul(Are, Aim, 0, BTre, BTim)
        cmul(Apre, Apim, 16, BTre, BTim)
        cmul(Dre, nDim, 8, Cre, Cim, neg_im=True)
        for gi in range(GB):
            g = g0 + gi
            pT = ps_T[g % 2]
            P.mm(pT, fl(Apre, gi), fl(Dre, gi), start=True, stop=False)
            P.mm(pT, fl(Apim, gi), fl(nDim, gi), start=False, stop=True)
            P.tt(Tm[:, g, :], pT, BM.re("p a b -> p (a b)"), ALU.mult)
            pW = ps_W[g % 2]
            P.mm(pW[:, 0:64], fl(Are, gi), I64)
            P.mm(pW[:, 64:128], fl(Aim, gi), I64)
            P.copy(W2[:, g, :], pW, eng="act")
        P.copy(Vre[:, gsl, :], Dre.re("p g a b -> p g (a b)"), eng="act")
        P.copy(Vim[:, gsl, :], nDim.re("p g a b -> p g (a b)"), eng="act")
    GQ = 4
    XE = P.sb([64, 2, G, NB + 1], F32, "XE")
    P.memset(XE[:, :, :, 0], 0.0)
    Ebf = P.sb([64, 2, G, NB], BF16, "Ebf")
    Ubf = P.sb([128, G, NB], BF16, "Ubf")
    ust = [P.sb([128, GQ, NB], F32, "ust%d" % i) for i in range(2)]
    yst = [P.sb([128, GQ, NB], F32, "yst%d" % i) for i in range(2)]
    r1 = P.sb([64, 2, G], F32, "r1")
    r2 = P.sb([64, 2, G], F32, "r2")
    ps_xr = [P.ps_alias("psT%d" % i, [64, GQ, NB]) for i in range(2)]
    ps_xi = [P.ps_alias("psW%d" % i, [64, GQ, NB]) for i in range(2)]
    ps_y = [P.ps([128, GQ, NB], F32, "psy%d" % i) for i in range(2)]
    qi = 0
    for blk in range(NBLK):
        csl = slice(blk * NB, (blk + 1) * NB)
        for g0 in range(0, G, GQ):
            b = qi % 2
            qi += 1
            P.dma(ust[b], Uin[g0:g0 + GQ, :, csl].rearrange("g p c -> p g c"))
            P.copy(Ubf[:, g0:g0 + GQ, :], ust[b], eng="pool")
            for gi in range(GQ):
                g = g0 + gi
                P.mm(ps_xr[b][:, gi, :], W2[:, g, 0:64], Ubf[:, g, :])
                P.mm(ps_xi[b][:, gi, :], W2[:, g, 64:128], Ubf[:, g, :])
            P.copy(XE[:, 0, g0:g0 + GQ, 1:NB + 1], ps_xr[b], eng="act")
            P.copy(XE[:, 1, g0:g0 + GQ, 1:NB + 1], ps_xi[b], eng="dve")
        for c in range(NB):
            prev = XE[:, :, :, c]
            cur = XE[:, :, :, c + 1]
            P.tt(r1, MU1, prev, ALU.mult)
            P.tt(r2[:, 0, :], MUa, XE[:, 1, :, c], ALU.mult)
            P.tt(r2[:, 1, :], MUb, XE[:, 0, :, c], ALU.mult)
            P.tt(r1, r1, r2, ALU.add)
            P.tt(cur, cur, r1, ALU.add)
        P.copy(Ebf, XE[:, :, :, 0:NB], eng="act")
        for g0 in range(0, G, GQ):
            b = qi % 2
            qi += 1
            for gi in range(GQ):
                g = g0 + gi
                py = ps_y[b][:, gi, :]
                P.mm(py, Tm[:, g, :], Ubf[:, g, :], start=True, stop=False)
                P.mm(py, Vre[:, g, :], Ebf[:, 0, g, :], start=False, stop=False)
                P.mm(py, Vim[:, g, :], Ebf[:, 1, g, :], start=False, stop=True)
            P.copy(yst[b], ps_y[b], eng="act")
            P.dma(Y[g0:g0 + GQ, :, csl].rearrange("g p c -> p g c"), yst[b], q="pool", is_output=True)
        if blk + 1 < NBLK:
            P.copy(XE[:, :, :, 0], XE[:, :, :, NB])
    return P


def emit_s5v2(P, G, GB, rev, ST, c0, Od, prm):
    lamre_d, lamim_d, lstep_d = prm["lamre"], prm["lamim"], prm["lstep"]
    Bre_d, Bim_d, Cre_d, Cim_d = prm["Bre"], prm["Bim"], prm["Cre"], prm["Cim"]
    NK = 24
    if not rev:
        kk = [7, 6, 5, 4, 3, 2, 1, 0] + [1, 2, 3, 4, 5, 6, 7, 8] + [-1, -2, -3, -4, -5, -6, -7, -8]
    else:
        kk = [0, 1, 2, 3, 4, 5, 6, 7] + [8, 7, 6, 5, 4, 3, 2, 1] + [-8, -7, -6, -5, -4, -3, -2, -1]
    i1 = 8 + kk[8:16].index(1)
    i8 = 8 + kk[8:16].index(8)
    I64 = P.sb([64, 64], F32, "I64")
    P.memset(I64, 1.0, eng="pool")
    P.aselect(I64, I64, [[-1, 64]], ALU.is_equal, 0.0, 0, 1)
    BM = P.sb([128, 8, 16], F32, "BM")
    P.memset(BM, 1.0, eng="pool")
    if not rev:
        P.aselect(BM, BM, [[16, 8], [0, 16]], ALU.is_ge, 0.0, 15, -1)
    else:
        P.aselect(BM, BM, [[-16, 8], [0, 16]], ALU.is_ge, 0.0, 0, 1)
    Tm = P.sb([128, G, 128], BF16, "Tm")
    W2 = P.sb([128, G, 128], BF16, "W2")
    Vre = P.sb([64, G, 128], BF16, "Vre")
    Vim = P.sb([64, G, 128], BF16, "Vim")
    MU1 = P.sb([64, 2, G], F32, "MU1")
    MUa = P.sb([64, G], F32, "MUa")
    MUb = P.sb([64, G], F32, "MUb")
    lre = P.sb([64, GB], F32, "lre")
    lim = P.sb([64, GB], F32, "lim")
    lst = P.sb([64, GB], F32, "lst")
    Bre = P.sb([64, GB, 16], F32, "sBre")
    Bim = P.sb([64, GB, 16], F32, "sBim")
    Cre = P.sb([64, GB, 16], F32, "sCre")
    Cim = P.sb([64, GB, 16], F32, "sCim")
    step = P.sb([64, GB], F32, "step")
    rho = P.sb([64, GB], F32, "rho")
    th = P.sb([64, GB], F32, "th")
    PH = P.sb([64, GB, NK], F32, "PH")
    RH = P.sb([64, GB, NK], F32, "RH")
    mag = P.sb([64, GB, NK], F32, "mag")
    TT = P.sb([64, GB, NK, 2], F32, "TT")
    TI = P.sb([64, GB, NK, 2], I32, "TI")
    TF = P.sb([64, GB, NK, 2], F32, "TF")
    LPre = P.sb([64, GB, NK], F32, "LPre")
    LPim = P.sb([64, GB, NK], F32, "LPim")
    w = [P.sb([64, GB], F32, "w%d" % i) for i in range(6)]
    BTre = P.sb([64, GB, 16], F32, "BTre")
    BTim = P.sb([64, GB, 16], F32, "BTim")
    t16a = P.sb([64, GB, 16], F32, "t16a")
    t16b = P.sb([64, GB, 16], F32, "t16b")
    Are = P.sb([64, GB, 8, 16], F32, "Are")
    Aim = P.sb([64, GB, 8, 16], F32, "Aim")
    Apre = P.sb([64, GB, 8, 16], F32, "Apre")
    Apim = P.sb([64, GB, 8, 16], F32, "Apim")
    Dre = P.sb([64, GB, 8, 16], F32, "Dre")
    nDim = P.sb([64, GB, 8, 16], F32, "nDim")
    t1 = P.sb([64, GB, 8, 16], F32, "t1")
    t2 = P.sb([64, GB, 8, 16], F32, "t2")
    ps_T = [P.ps([128, 128], F32, "psT%d" % i) for i in range(2)]
    ps_W = [P.ps([128, 128], F32, "psW%d" % i) for i in range(2)]

    def bc16(v):
        return V(v.ap.unsqueeze(2).to_broadcast([64, GB, 16]), v.keys)

    def cmul(out_re, out_im, lp0, xr, xi, neg_im=False):
        a_re = V(LPre[:, :, lp0:lp0 + 8].ap.unsqueeze(3).to_broadcast([64, GB, 8, 16]), LPre.keys)
        a_im = V(LPim[:, :, lp0:lp0 + 8].ap.unsqueeze(3).to_broadcast([64, GB, 8, 16]), LPim.keys)
        b_re = V(xr.ap.unsqueeze(2).to_broadcast([64, GB, 8, 16]), xr.keys)
        b_im = V(xi.ap.unsqueeze(2).to_broadcast([64, GB, 8, 16]), xi.keys)
        P.tt(t1, a_re, b_re, ALU.mult)
        P.tt(t2, a_im, b_im, ALU.mult)
        P.tt(out_re, t1, t2, ALU.subtract)
        P.tt(t1, a_re, b_im, ALU.mult)
        P.tt(t2, a_im, b_re, ALU.mult)
        P.tt(out_im, t1, t2, ALU.add)
        if neg_im:
            P.ts(out_im, out_im, -1.0, ALU.mult)

    fl = lambda v, gi: v[:, gi, :, :].re("p a b -> p (a b)")
    for g0 in range(0, G, GB):
        gsl = slice(g0, g0 + GB)
        for d_, s_ in ((lre, lamre_d), (lim, lamim_d), (lst, lstep_d)):
            P.dma(d_, s_[:, gsl])
        for d_, s_ in ((Bre, Bre_d), (Bim, Bim_d), (Cre, Cre_d), (Cim, Cim_d)):
            P.dma(d_, s_[:, gsl, :])
        P.act(step, lst, AF.Exp)
        P.tt(rho, lre, step, ALU.mult)
        P.tt(th, lim, step, ALU.mult)
        for i, kv in enumerate(kk):
            P.ts(PH[:, :, i], th, float(kv), ALU.mult)
            P.ts(RH[:, :, i], rho, float(kv), ALU.mult)
        P.act(mag, RH, AF.Exp)
        P.ts(TT[:, :, :, 0], PH, 1.0 / TWO_PI, ALU.mult, 0.5, ALU.add)
        P.ts(TT[:, :, :, 1], PH, 1.0 / TWO_PI, ALU.mult, 0.75, ALU.add)
        P.copy(TI, TT)
        P.copy(TF, TI)
        P.tt(TT, TT, TF, ALU.subtract)
        P.ts(TF, TT, 0.0, ALU.is_lt)
        P.tt(TT, TT, TF, ALU.add)
        P.ts(TT, TT, TWO_PI, ALU.mult, -math.pi, ALU.add)
        P.ts(TT, TT, 3.1415925, ALU.min, -3.1415925, ALU.max)
        P.act(TT, TT, AF.Sin)
        P.tt(LPre, mag, TT[:, :, :, 1], ALU.mult)
        P.tt(LPim, mag, TT[:, :, :, 0], ALU.mult)
        abre = LPre[:, :, i1]
        abim = LPim[:, :, i1]
        P.tt(w[0], lre, lre, ALU.mult)
        P.tt(w[1], lim, lim, ALU.mult)
        P.tt(w[0], w[0], w[1], ALU.add)
        P.recip(w[0], w[0])
        P.ts(w[1], abre, -1.0, ALU.add)
        P.tt(w[2], w[1], lre, ALU.mult)
        P.tt(w[3], abim, lim, ALU.mult)
        P.tt(w[2], w[2], w[3], ALU.add)
        P.tt(w[2], w[2], w[0], ALU.mult)
        P.tt(w[4], abim, lre, ALU.mult)
        P.tt(w[5], w[1], lim, ALU.mult)
        P.tt(w[4], w[4], w[5], ALU.subtract)
        P.tt(w[4], w[4], w[0], ALU.mult)
        cre, cim = w[2], w[4]
        P.tt(t16a, Bre, bc16(cre), ALU.mult)
        P.tt(t16b, Bim, bc16(cim), ALU.mult)
        P.tt(BTre, t16a, t16b, ALU.subtract)
        P.tt(t16a, Bim, bc16(cre), ALU.mult)
        P.tt(t16b, Bre, bc16(cim), ALU.mult)
        P.tt(BTim, t16a, t16b, ALU.add)
        P.copy(MU1[:, 0, gsl], LPre[:, :, i8])
        P.copy(MU1[:, 1, gsl], LPre[:, :, i8])
        P.copy(MUb[:, gsl], LPim[:, :, i8])
        P.ts(MUa[:, gsl], LPim[:, :, i8], -1.0, ALU.mult)
        cmul(Are, Aim, 0, BTre, BTim)
        cmul(Apre, Apim, 16, BTre, BTim)
        cmul(Dre, nDim, 8, Cre, Cim, neg_im=True)
        for gi in range(GB):
            g = g0 + gi
            pT = ps_T[g % 2]
            P.mm(pT, fl(Apre, gi), fl(Dre, gi), start=True, stop=False)
            P.mm(pT, fl(Apim, gi), fl(nDim, gi), start=False, stop=True)
            P.tt(Tm[:, g, :], pT, BM.re("p a b -> p (a b)"), ALU.mult)
            pW = ps_W[g % 2]
            P.mm(pW[:, 0:64], fl(Are, gi), I64)
            P.mm(pW[:, 64:128], fl(Aim, gi), I64)
            P.copy(W2[:, g, :], pW, eng="act")
        P.copy(Vre[:, gsl, :], Dre.re("p g a b -> p g (a b)"), eng="act")
        P.copy(Vim[:, gsl, :], nDim.re("p g a b -> p g (a b)"), eng="act")
    NBM = 128
    Iid = P.sb([128, 128], F32, "s5I")
    P.memset(Iid, 1.0, eng="pool")
    P.aselect(Iid, Iid, [[-1, 128]], ALU.is_equal, 0.0, 0, 1)
    XE = P.sb([64, 2, G, NBM + 1], F32, "XE")
    Ebf = P.sb([64, 2, G, NBM], BF16, "Ebf")
    Ubf = P.sb([128, G, NBM], BF16, "Ubf")
    utile = P.sb([128, 8, G * 16], F32, "utile")
    ybuf = P.sb([128, 8, G * 16], F32, "ybuf")
    ugs = [P.sb([128, 8, 16], F32, "ugs%d" % i) for i in range(2)]
    r1 = P.sb([64, 2, G], F32, "r1")
    r2 = P.sb([64, 2, G], F32, "r2")
    ps_u = [P.ps_alias("psT%d" % i, [128, NBM]) for i in range(2)]
    ps_xr = P.ps([64, 4, NBM], F32, "psxr")
    ps_xi = P.ps([64, 4, NBM], F32, "psxi")
    ps_y = [P.ps([128, 4, 128], F32, "psy%d" % i) for i in range(2)]
    GC = G * 16
    lat_blocks = [0, 1, 2, 3] if not rev else [3, 2, 1, 0]
    blocks = [("ctx", 0)] + [("lat", b) for b in lat_blocks]
    carry_col = 0 if not rev else NBM
    first = True
    for kind, bi in blocks:
        nb = 32 if kind == "ctx" else NBM
        if kind == "ctx":
            for rho in range(2):
                P.dma(utile[rho * 16:(rho + 1) * 16, :, :],
                      ST[rho * TOKR + LATR:rho * TOKR + LATR + 128, c0:c0 + GC].rearrange("(c j) ch -> c j ch", j=8))
        else:
            rho, hb = bi // 2, bi % 2
            r0 = rho * TOKR + hb * 1024
            P.dma(utile[:, :, :], ST[r0:r0 + 1024, c0:c0 + GC].rearrange("(c j) ch -> c j ch", j=8))
        xoff = 1 if not rev else 0
        ccol = 0 if not rev else nb
        if first:
            P.memset(XE[:, :, :, ccol], 0.0)
            first = False
        for g0 in range(0, G, 4):
            for gi in range(4):
                g = g0 + gi
                pu = ps_u[g % 2]
                ug = ugs[g % 2]
                P.copy(ug[0:nb, :, :], utile[0:nb, :, g * 16:(g + 1) * 16], eng="pool")
                P.mm(pu[:, 0:nb], ug[0:nb, :, :].rearrange("c j p -> c (j p)"), Iid[0:nb, 0:nb])
                P.copy(Ubf[:, g, 0:nb], pu[:, 0:nb], eng="act" if g % 2 else "dve")
                P.mm(ps_xr[:, gi, 0:nb], W2[:, g, 0:64], Ubf[:, g, 0:nb])
                P.mm(ps_xi[:, gi, 0:nb], W2[:, g, 64:128], Ubf[:, g, 0:nb])
            P.copy(XE[:, 0, g0:g0 + 4, xoff:xoff + nb], ps_xr[:, :, 0:nb], eng="act")
            P.copy(XE[:, 1, g0:g0 + 4, xoff:xoff + nb], ps_xi[:, :, 0:nb], eng="dve")
        order = range(nb) if not rev else range(nb - 1, -1, -1)
        for c in order:
            pc = c if not rev else c + 1
            cc_ = c + xoff
            P.tt(r1, MU1, XE[:, :, :, pc], ALU.mult)
            P.tt(r2[:, 0, :], MUa, XE[:, 1, :, pc], ALU.mult)
            P.tt(r2[:, 1, :], MUb, XE[:, 0, :, pc], ALU.mult)
            P.tt(r1, r1, r2, ALU.add)
            P.tt(XE[:, :, :, cc_], XE[:, :, :, cc_], r1, ALU.add)
        sh = 0 if not rev else 1
        P.copy(Ebf[:, :, :, 0:nb], XE[:, :, :, sh:sh + nb], eng="act")
        for g0 in range(0, G, 4):
            py = ps_y[(g0 // 4) % 2]
            for gi in range(4):
                g = g0 + gi
                P.mm(py[0:nb, gi, :], Ubf[:, g, 0:nb], Tm[:, g, :], start=True, stop=False)
                P.mm(py[0:nb, gi, :], Ebf[:, 0, g, 0:nb], Vre[:, g, :], start=False, stop=False)
                P.mm(py[0:nb, gi, :], Ebf[:, 1, g, 0:nb], Vim[:, g, :], start=False, stop=True)
            P.copy(ybuf[0:nb, :, g0 * 16:(g0 + 4) * 16].rearrange("c j (g p) -> c g j p", p=16),
                   py[0:nb, :, :].rearrange("c g (j p) -> c g j p", p=16), eng="act" if (g0 // 4) % 2 else "dve")
        if kind == "ctx":
            for rho in range(2):
                P.dma(Od[rho * TOKR + LATR:rho * TOKR + LATR + 128, :].rearrange("(c j) ch -> c j ch", j=8),
                      ybuf[rho * 16:(rho + 1) * 16, :, :], q="pool")
        else:
            P.dma(Od[r0:r0 + 1024, :].rearrange("(c j) ch -> c j ch", j=8), ybuf[:, :, :], q="pool")
        last_col = nb if not rev else 0
        nxt_nb = NBM
        nxt_ccol = 0 if not rev else nxt_nb
        if (kind, bi) != blocks[-1]:
            if last_col != nxt_ccol:
                P.copy(XE[:, :, :, nxt_ccol], XE[:, :, :, last_col])
    return P


def build_s5(G, GB, NB, NBLK, name="s5"):
    P = Prog(name)
    emit_s5(P, G, GB, NB, NBLK)
    return P


EPS = 1e-6


class TokCtx:
    def __init__(self, P, D, DFF, TMAX, nwst=3, nwbf=4, look=3):
        self.P, self.D, self.DFF, self.T = P, D, DFF, TMAX
        self.KC = D // 128
        self.JC = DFF // 128
        KC, JC, T = self.KC, self.JC, TMAX
        self.ones = P.sb([128, 128], F32, "ones")
        P.memset(self.ones, 1.0)
        self.XB = P.sb([128, KC, T], F32, "XB")
        self.h = P.sb([128, KC, T], BF16, "h")
        self.hid = P.sb([128, JC, T], BF16, "hid")
        self.sq = [P.sb([128, T], F32, "sq%d" % i) for i in range(2)]
        self.rstd = P.sb([128, T], F32, "rstd")
        self.tmp = [P.sb([128, T], F32, "tmp%d" % i) for i in range(2)]
        self.sg = [P.sb([128, T], F32, "sg%d" % i) for i in range(2)]
        self.xr = [P.sb([128, T], F32, "xr%d" % i) for i in range(2)]
        self.WSZ = KC * 128
        self.wst = [P.sb([128, self.WSZ], F32, "wst%d" % i) for i in range(nwst)]
        self.wbf = [P.sb([128, self.WSZ], BF16, "wbf%d" % i) for i in range(nwbf)]
        self.look = look
        self.wi = 0
        self.ps_g = [P.ps([128, T], F32, "psg%d" % i) for i in range(2)]
        self.ps_u = [P.ps([128, T], F32, "psu%d" % i) for i in range(2)]
        self.ps_y = [P.ps([128, T], F32, "psy%d" % i) for i in range(2)]
        self.ps_s = P.ps([128, T], F32, "pss")
        self.ps_m = P.ps([128, 512], F32, "psm")
        self.cast_i = 0
        self.eps_col = P.sb([128, 1], F32, "epsc")
        P.memset(self.eps_col, EPS)

    def stream(self, items, look=None):
        loaded = {}
        look = look or self.look

        def get(i):
            for k in range(i, min(i + look, len(items))):
                if k not in loaded:
                    loaded[k] = self.load_w(*items[k])
            return loaded[i]
        return get

    def load_w(self, src_ap, nrow_chunks, ncols):
        P = self.P
        i = self.wi % len(self.wst)
        ib = self.wi % len(self.wbf)
        self.wi += 1
        n = nrow_chunks * ncols
        assert n <= self.WSZ
        st = self.wst[i][:, 0:n].re("p (a b) -> p a b", b=ncols)
        bf = self.wbf[ib][:, 0:n].re("p (a b) -> p a b", b=ncols)
        P.dma(st, src_ap, q="sp")
        self.cast_i += 1
        if self.cast_i % 3 == 0:
            P.copy(bf, st, eng="act")
        else:
            P.copy(bf, st, eng="dve")
        return bf


def rms_rstd(C, chunks, T, Dtot, out_rstd, ps=None):
    P = C.P
    ps = ps or C.ps_s
    n = len(chunks)
    for i, src in enumerate(chunks):
        sq = C.sq[i % 2][:, 0:T]
        P.act(sq, src, AF.Square)
        P.mm(ps[:, 0:T], C.ones, sq, start=(i == 0), stop=(i == n - 1))
    P.act(out_rstd[:, 0:T], ps[:, 0:T], AF.Sqrt, scale=1.0 / Dtot, bias=C.eps_col)
    P.recip(out_rstd[:, 0:T], out_rstd[:, 0:T])


def sublayer_in(C, T, Ain, shift, col):
    P = C.P
    KC = C.KC
    rms_rstd(C, [C.XB[:, kc, 0:T] for kc in range(KC)], T, C.D, C.rstd)
    for kc in range(KC):
        t = C.tmp[kc % 2][:, 0:T]
        P.stt(t, C.XB[:, kc, 0:T], Ain[:, kc, col:col + 1], C.rstd[:, 0:T], ALU.mult, ALU.mult)
        P.act(C.h[:, kc, 0:T], t, AF.Identity, bias=shift[:, kc, col:col + 1])


def sublayer_out(C, T, Gout, col, resid):
    P = C.P
    KC = C.KC
    rms_rstd(C, [C.XB[:, kc, 0:T] for kc in range(KC)], T, C.D, C.rstd)
    for kc in range(KC):
        xr = C.xr[kc % 2][:, 0:T]
        P.dma(xr, resid(kc), q="sp")
        t = C.tmp[kc % 2][:, 0:T]
        P.stt(t, C.XB[:, kc, 0:T], Gout[:, kc, col:col + 1], C.rstd[:, 0:T], ALU.mult, ALU.mult)
        P.tt(C.XB[:, kc, 0:T], xr, t, ALU.add, eng="pool")


def split_parts(n, maxp):
    k = (n + maxp - 1) // maxp
    base, rem = divmod(n, k)
    out, s = [], 0
    for i in range(k):
        sz = base + (1 if i < rem else 0)
        out.append((s, s + sz))
        s += sz
    return out


def ffn_core(C, T, wg, wu, wd, tiled=False):
    P = C.P
    KC, JC = C.KC, C.JC
    parts = split_parts(JC, C.WSZ // 128)
    items = []
    if tiled:
        for j in range(JC):
            items.append((wg[j], KC, 128))
            items.append((wu[j], KC, 128))
        for m in range(KC):
            for (j0, j1) in parts:
                items.append((wd[m][:, j0:j1, :], j1 - j0, 128))
    else:
        wg_v = wg.rearrange("(kc p) n -> p kc n", p=128)
        wu_v = wu.rearrange("(kc p) n -> p kc n", p=128)
        wd_v = wd.rearrange("(jc p) n -> p jc n", p=128)
        for j in range(JC):
            items.append((wg_v[:, :, j * 128:(j + 1) * 128], KC, 128))
            items.append((wu_v[:, :, j * 128:(j + 1) * 128], KC, 128))
        for m in range(KC):
            for (j0, j1) in parts:
                items.append((wd_v[:, j0:j1, m * 128:(m + 1) * 128], j1 - j0, 128))
    get = C.stream(items)
    for j in range(JC):
        wgb = get(2 * j)
        wub = get(2 * j + 1)
        pg = C.ps_g[j % 2][:, 0:T]
        pu = C.ps_u[j % 2][:, 0:T]
        for kc in range(KC):
            P.mm(pg, wgb[:, kc, :], C.h[:, kc, 0:T], start=(kc == 0), stop=(kc == KC - 1))
        for kc in range(KC):
            P.mm(pu, wub[:, kc, :], C.h[:, kc, 0:T], start=(kc == 0), stop=(kc == KC - 1))
        sg = C.sg[j % 2][:, 0:T]
        P.act(sg, pg, AF.Silu)
        P.tt(C.hid[:, j, 0:T], sg, pu, ALU.mult)
    npart = len(parts)
    for m in range(KC):
        py = C.ps_y[m % 2][:, 0:T]
        for hi, (j0, j1) in enumerate(parts):
            wdb = get(2 * JC + npart * m + hi)
            for j in range(j0, j1):
                P.mm(py, wdb[:, j - j0, :], C.hid[:, j, 0:T], start=(j == 0), stop=(j == JC - 1))
        P.copy(C.XB[:, m, 0:T], py, eng="act")


def proj(C, T, w, nk, ncol, rhs, sink, bias_fn=None, tiled=False):
    P = C.P
    nch = (ncol + 127) // 128
    if tiled:
        items = [(w[j], nk, 128) for j in range(nch)]
    else:
        w_v = w.rearrange("(kc p) n -> p kc n", p=128)
        items = [(w_v[:, :, j * 128:min(ncol, (j + 1) * 128)], nk, min(128, ncol - j * 128)) for j in range(nch)]
    get = C.stream(items)
    for j in range(nch):
        cw = min(128, ncol - j * 128)
        wb = get(j)
        pg = C.ps_g[j % 2][0:cw, 0:T]
        for kc in range(nk):
            P.mm(pg, wb[:, kc, :], rhs(kc), start=(kc == 0), stop=(kc == nk - 1))
        sink(j, cw, pg)


def in_proj(C, T, w_in, ncol, sT_out, t0):
    P = C.P

    def sink(j, cw, pg):
        so = C.sg[j % 2][0:cw, 0:T]
        P.copy(so, pg, eng="act")
        P.dma(sT_out[j * 128:j * 128 + cw, t0:t0 + T], so, q="pool", is_output=True)
    proj(C, T, w_in, C.KC, ncol, lambda kc: C.h[:, kc, 0:T], sink)


def load_consts(C, norm_pre, norm_post):
    P = C.P
    C.npre = P.sb([128, 3, C.KC], F32, "npre")
    C.npost = P.sb([128, 3, C.KC], F32, "npost")
    with P.nc.allow_non_contiguous_dma(reason="small const loads"):
        P.dma(C.npre, norm_pre.rearrange("s (kc p) -> p s kc", p=128))
        P.dma(C.npost, norm_post.rearrange("s (kc p) -> p s kc", p=128))


def compute_mod(C, cT, w_mod, b_mod, mod_out=None, tiled=False):
    P = C.P
    KC = C.KC
    NJ = 9 * KC
    C.mod = P.sb([128, NJ, 2], F32, "mod")
    cs = P.sb([128, KC, 2], F32, "csilu")
    bm = P.sb([128, NJ], F32, "bmod")
    with P.nc.allow_non_contiguous_dma(reason="small const loads"):
        P.dma(cs, cT.rearrange("(kc p) c -> p kc c", p=128))
        P.dma(bm, b_mod.rearrange("(j p) -> p j", p=128))
    P.act(cs, cs, AF.Silu)
    w_v = None if tiled else w_mod.rearrange("(kc p) n -> p kc n", p=128)
    assert 2 * NJ <= 512
    psm = C.ps_m[:, 0:2 * NJ].re("p (j c) -> p j c", c=2)
    for j in range(NJ):
        i = C.wi % len(C.wst)
        C.wi += 1
        st = C.wst[i][:, 0:KC * 128].re("p (a b) -> p a b", b=128)
        P.dma(st, w_mod[j] if tiled else w_v[:, :, j * 128:(j + 1) * 128], q="sp")
        for kc in range(KC):
            P.mm(psm[:, j, :], st[:, kc, :], cs[:, kc, :], start=(kc == 0), stop=(kc == KC - 1))
    for c in range(2):
        P.tt(C.mod[:, :, c], psm[:, :, c], bm, ALU.add)
    if mod_out is not None:
        P.dma(mod_out, C.mod, q="pool", is_output=True)


def load_mod(C, modi):
    P = C.P
    C.mod = P.sb([128, 9 * C.KC, 2], F32, "mod")
    P.dma(C.mod, modi)


def derive_sub(C, s, weight):
    P = C.P
    KC = C.KC
    Ain = P.sb([128, KC, 2], F32, "Ain%d" % s)
    Gout = P.sb([128, KC, 2], F32, "Gout%d" % s)
    shift = C.mod[:, (3 * s) * KC:(3 * s + 1) * KC, :]
    scale = C.mod[:, (3 * s + 1) * KC:(3 * s + 2) * KC, :]
    gate = C.mod[:, (3 * s + 2) * KC:(3 * s + 3) * KC, :]
    for c in range(2):
        P.stt(Ain[:, :, c], scale[:, :, c], 1.0, C.npre[:, s, :], ALU.add, ALU.mult)
        P.stt(Gout[:, :, c], gate[:, :, c], float(weight), C.npost[:, s, :], ALU.mult, ALU.mult)
    return Ain, shift, Gout


def build_stageA(D, DFF, NCOL, tiles, Ttot, name="stageA"):
    P = Prog(name)
    xT = P.dram("xT", [D, Ttot])
    cT = P.dram("cT", [D, 2])
    w_mod = P.dram("w_mod", [D, 9 * D])
    b_mod = P.dram("b_mod", [9 * D])
    norm_pre = P.dram("norm_pre", [3, D])
    norm_post = P.dram("norm_post", [3, D])
    wg = P.dram("wg", [D, DFF])
    wu = P.dram("wu", [D, DFF])
    wd = P.dram("wd", [DFF, D])
    w_in = P.dram("w_in", [D, NCOL])
    x1T = P.dram("x1T", [D, Ttot], kind="ExternalOutput")
    sT = P.dram("sT", [NCOL, Ttot], kind="ExternalOutput")
    KC = D // 128
    modo = P.dram("modo", [128, 9 * KC, 2], kind="ExternalOutput")
    TMAX = max(t[2] for t in tiles)
    C = TokCtx(P, D, DFF, TMAX)
    load_consts(C, norm_pre, norm_post)
    compute_mod(C, cT, w_mod, b_mod, modo)
    A0, sh0, G0 = derive_sub(C, 0, 0.5)
    A1, sh1, G1 = derive_sub(C, 1, 1.0)
    xv = xT.rearrange("(kc p) t -> p kc t", p=128)
    x1v = x1T.rearrange("(kc p) t -> p kc t", p=128)
    for (col, t0, T) in tiles:
        P.dma(C.XB[:, :, 0:T], xv[:, :, t0:t0 + T], q="sp")
        sublayer_in(C, T, A0, sh0, col)
        ffn_core(C, T, wg, wu, wd)
        sublayer_out(C, T, G0, col, lambda kc: xv[:, kc, t0:t0 + T])
        P.dma(x1v[:, :, t0:t0 + T], C.XB[:, :, 0:T], q="pool", is_output=True)
        sublayer_in(C, T, A1, sh1, col)
        in_proj(C, T, w_in, NCOL, sT, t0)
    return P


def head_norm_chunks(C, T, osum, gvec, gcols, gact, center, dst_chunks, HD):
    P = C.P
    n = len(dst_chunks)
    if center:
        for i in range(n):
            P.mm(C.ps_m[:, 0:T], C.ones, osum[:, i, 0:T], start=(i == 0), stop=(i == n - 1))
        for i in range(n):
            sq = C.sq[i % 2][:, 0:T]
            P.act(sq, osum[:, i, 0:T], AF.Square)
            P.mm(C.ps_s[:, 0:T], C.ones, sq, start=(i == 0), stop=(i == n - 1))
        mean = C.xr[0][:, 0:T]
        var = C.xr[1][:, 0:T]
        P.ts(mean, C.ps_m[:, 0:T], 1.0 / HD, ALU.mult)
        P.tt(var, mean, mean, ALU.mult)
        P.stt(var, C.ps_s[:, 0:T], 1.0 / HD, var, ALU.mult, ALU.subtract)
        P.act(C.rstd[:, 0:T], var, AF.Sqrt, bias=C.eps_col)
        P.recip(C.rstd[:, 0:T], C.rstd[:, 0:T])
        for i in range(n):
            t = C.tmp[i % 2][:, 0:T]
            P.tt(t, osum[:, i, 0:T], mean, ALU.subtract)
            P.tt(t, t, C.rstd[:, 0:T], ALU.mult)
            P.stt(dst_chunks[i], t, gvec[:, gcols[i]:gcols[i] + 1], gact[i], ALU.mult, ALU.mult)
    else:
        rms_rstd(C, [osum[:, i, 0:T] for i in range(n)], T, HD, C.rstd)
        for i in range(n):
            t = C.tmp[i % 2][:, 0:T]
            P.tt(t, osum[:, i, 0:T], C.rstd[:, 0:T], ALU.mult)
            P.stt(dst_chunks[i], t, gvec[:, gcols[i]:gcols[i] + 1], gact[i], ALU.mult, ALU.mult)


def build_stageC(D, DFF, tiles, Ttot, parity, HD=256, name="stageC"):
    P = Prog(name)
    KC = D // 128
    H2 = D // 2
    KH = KC // 2
    x1T = P.dram("x1T", [D, Ttot])
    modi = P.dram("modi", [128, 9 * KC, 2])
    norm_pre = P.dram("norm_pre", [3, D])
    norm_post = P.dram("norm_post", [3, D])
    w_out = P.dram("w_out", [D, D])
    wg = P.dram("wg", [D, DFF])
    wu = P.dram("wu", [D, DFF])
    wd = P.dram("wd", [DFF, D])
    oAf = P.dram("oAf", [H2, Ttot])
    oAb = P.dram("oAb", [H2, Ttot])
    oBf = P.dram("oBf", [H2, Ttot])
    oBb = P.dram("oBb", [H2, Ttot])
    gB = P.dram("gB", [H2, Ttot])
    nB = P.dram("nB", [H2])
    if parity == 0:
        gA = P.dram("gA", [H2, Ttot])
        nA = P.dram("nA", [H2])
    else:
        uT = P.dram("uT", [H2, Ttot])
        s5d = P.dram("s5d", [H2])
        w_glu = P.dram("w_glu", [H2, H2])
        b_glu = P.dram("b_glu", [H2])
    x3T = P.dram("x3T", [D, Ttot], kind="ExternalOutput")
    x2s = V(P.dram("x2s", [D, Ttot], kind="Internal"), "x2s")
    TMAX = max(t[2] for t in tiles)
    C = TokCtx(P, D, DFF, TMAX)
    load_consts(C, norm_pre, norm_post)
    load_mod(C, modi)
    A1, sh1, G1 = derive_sub(C, 1, 1.0)
    A2, sh2, G2 = derive_sub(C, 2, 0.5)
    gv = P.sb([128, 4, KH], F32, "gv")
    with P.nc.allow_non_contiguous_dma(reason="small const loads"):
        P.dma(gv[:, 1, :], nB.rearrange("(kc p) -> p kc", p=128))
        if parity == 0:
            P.dma(gv[:, 0, :], nA.rearrange("(kc p) -> p kc", p=128))
        else:
            P.dma(gv[:, 0, :], s5d.rearrange("(kc p) -> p kc", p=128))
            P.dma(gv[:, 2, :], b_glu.rearrange("(kc p) -> p kc", p=128))
    NH = HD // 128
    ld = [[P.sb([128, TMAX], F32, "ld%d_%d" % (a, b)) for b in range(2)] for a in range(3)]
    osum = P.sb([128, NH, TMAX], F32, "osum")
    gact = [P.sb([128, TMAX], F32, "gact%d" % i) for i in range(NH)]
    x1v = x1T.rearrange("(kc p) t -> p kc t", p=128)
    x3v = x3T.rearrange("(kc p) t -> p kc t", p=128)
    x2v = x2s.re("(kc p) t -> p kc t", p=128)
    li = [0]

    def normed_half(T, t0, of_, ob_, g_, gvrow, func, center, kc0):
        for hh in range(H2 // HD):
            for i in range(NH):
                r0 = (hh * NH + i) * 128
                b = li[0] % 2
                li[0] += 1
                P.dma(ld[0][b][:, 0:T], of_[r0:r0 + 128, t0:t0 + T])
                P.dma(ld[1][b][:, 0:T], ob_[r0:r0 + 128, t0:t0 + T])
                P.dma(ld[2][b][:, 0:T], g_[r0:r0 + 128, t0:t0 + T])
                P.tt(osum[:, i, 0:T], ld[0][b][:, 0:T], ld[1][b][:, 0:T], ALU.add, eng="pool")
                P.act(gact[i][:, 0:T], ld[2][b][:, 0:T], func)
            cols = [hh * NH + i for i in range(NH)]
            head_norm_chunks(C, T, osum, gv[:, gvrow, :], cols, [g[:, 0:T] for g in gact], center,
                             [C.h[:, kc0 + hh * NH + i, 0:T] for i in range(NH)], HD)

    for (col, t0, T) in tiles:
        if parity == 0:
            normed_half(T, t0, oAf, oAb, gA, 0, AF.Silu, False, 0)
            normed_half(T, t0, oBf, oBb, gB, 1, AF.Sigmoid, True, KH)
        else:
            for kc in range(KH):
                b = li[0] % 2
                li[0] += 1
                r0 = kc * 128
                P.dma(ld[0][b][:, 0:T], oAf[r0:r0 + 128, t0:t0 + T])
                P.dma(ld[1][b][:, 0:T], oAb[r0:r0 + 128, t0:t0 + T])
                P.dma(ld[2][b][:, 0:T], uT[r0:r0 + 128, t0:t0 + T])
                yv = C.tmp[0][:, 0:T]
                t2 = C.tmp[1][:, 0:T]
                P.tt(yv, ld[0][b][:, 0:T], ld[1][b][:, 0:T], ALU.add, eng="pool")
                P.stt(yv, ld[2][b][:, 0:T], gv[:, 0, kc:kc + 1], yv, ALU.mult, ALU.add)
                P.tt(t2, yv, yv, ALU.mult)
                P.ts(t2, t2, 0.044715, ALU.mult, 1.0, ALU.add)
                P.tt(t2, t2, yv, ALU.mult)
                P.act(t2, t2, AF.Sigmoid, scale=1.5957691216057308)
                P.tt(C.XB[:, kc, 0:T], t2, yv, ALU.mult)
                P.copy(C.h[:, KH + kc, 0:T], C.XB[:, kc, 0:T], eng="act")

            def sink(j, cw, pg):
                sgm = C.sg[j % 2][:, 0:T]
                P.act(sgm, pg, AF.Sigmoid, bias=gv[:, 2, j:j + 1])
                P.tt(C.h[:, j, 0:T], C.XB[:, j, 0:T], sgm, ALU.mult)
            proj(C, T, w_glu, KH, H2, lambda kc: C.h[:, KH + kc, 0:T], sink)
            normed_half(T, t0, oBf, oBb, gB, 1, AF.Silu, True, KH)

        def sink_y(j, cw, pg):
            P.copy(C.XB[:, j, 0:T], pg, eng="act")
        proj(C, T, w_out, KC, D, lambda kc: C.h[:, kc, 0:T], sink_y)
        sublayer_out(C, T, G1, col, lambda kc: x1v[:, kc, t0:t0 + T])
        P.dma(x2v[:, :, t0:t0 + T], C.XB[:, :, 0:T], q="pool")
        sublayer_in(C, T, A2, sh2, col)
        ffn_core(C, T, wg, wu, wd)
        sublayer_out(C, T, G2, col, lambda kc: x2v[:, kc, t0:t0 + T])
        P.dma(x3v[:, :, t0:t0 + T], C.XB[:, :, 0:T], q="pool", is_output=True)
    return P


G2 = [[0, 1], [2, 3], [4, 5], [6, 7]]
TOKR = 2176
LATR = 2048
NSQ = 4352


def nat_block(c):
    if c < 2:
        return c * TOKR + LATR
    i = c - 2
    return (i // 16) * TOKR + (i % 16) * 128


def rev_chunk(c):
    return 1 - c if c < 2 else 2 + (33 - c)


class Relay:
    def __init__(self, P, maxc):
        self.P = P
        self.I = P.sb([128, 128], F32, "rI")
        self.J = P.sb([128, 128], F32, "rJ")
        P.memset(self.I, 1.0, eng="pool")
        P.memset(self.J, 1.0, eng="pool")
        P.aselect(self.I, self.I, [[-1, 128]], ALU.is_equal, 0.0, 0, 1)
        P.aselect(self.J, self.J, [[1, 128]], ALU.is_equal, 0.0, -127, 1)
        self.tl = [P.sb([128, maxc], F32, "rtl%d" % i) for i in range(2)]
        self.ob = [P.sb([128, 512], F32, "rob%d" % i) for i in range(4)]
        self.ps = [P.ps([128, 512], F32, "rps%d" % i) for i in range(4)]
        self.k = 0
        self.ti = 0

    def load(self, ST, c, c0, ncols, cm):
        P = self.P
        t = self.tl[self.ti % 2]
        self.ti += 1
        if c < 2 or not cm:
            r0 = nat_block(c)
            P.dma(t[:, 0:ncols], ST[r0:r0 + 128, c0:c0 + ncols])
        else:
            i = c - 2
            for cl in range(2):
                for rho in range(2):
                    src = ST[rho * TOKR:rho * TOKR + LATR, c0:c0 + ncols].rearrange("(r w) c -> r w c", w=64)[:, 2 * i + cl, :]
                    P.dma(t[cl * 64 + rho * 32:cl * 64 + rho * 32 + 32, 0:ncols], src)
        return t

    def fm(self, t, c0, w, flip, dst):
        P = self.P
        k = self.k % 4
        self.k += 1
        P.mm(self.ps[k][0:w, 0:128], t[:, c0:c0 + w], self.J if flip else self.I)
        if k % 2 == 0:
            P.copy(self.ob[k][0:w, 0:128], self.ps[k][0:w, 0:128], eng="act")
        else:
            P.copy(self.ob[k][0:w, 0:128], self.ps[k][0:w, 0:128], eng="dve")
        return self.ob[k]

    def tmflip(self, t, c0, w):
        P = self.P
        k = self.k % 4
        self.k += 1
        P.mm(self.ps[k][:, 0:w], self.J, t[:, c0:c0 + w])
        if k % 2 == 0:
            P.copy(self.ob[k][:, 0:w], self.ps[k][:, 0:w], eng="act")
        else:
            P.copy(self.ob[k][:, 0:w], self.ps[k][:, 0:w], eng="dve")
        return self.ob[k]


def relayout_gla(P, ST, c0, DK, cm, arr, lr=True):
    NDC = DK // 128
    ncols = 4 * DK + 512 + (32 if lr else 0)
    R = Relay(P, ncols)
    qo, ko, vo, lo = 0, 2 * DK, 4 * DK, 4 * DK + 512
    for c in range(34):
        t = R.load(ST, c, c0, ncols, cm)
        for d in range(2):
            cc = c if d == 0 else rev_chunk(c)
            ps_ = slice(cc * 128, cc * 128 + 128)
            fl = (d == 1)
            for hl in range(2):
                u = d * 2 + hl
                for dc in range(NDC):
                    o_ = R.fm(t, qo + hl * DK + dc * 128, 128, fl, None)
                    P.dma(arr["qT"][u, dc * 128:(dc + 1) * 128, ps_], o_[:, 0:128], q="sp")
                    o_ = R.fm(t, ko + hl * DK + dc * 128, 128, fl, None)
                    P.dma(arr["kT"][u, dc * 128:(dc + 1) * 128, ps_], o_[:, 0:128], q="sp")
            if lr:
                o_ = R.fm(t, lo + d * 16, 16, fl, None)
                for hl in range(2):
                    P.dma(arr["lrT"][d * 2 + hl, :, ps_], o_[0:16, 0:128], q="sp")
            if d == 0:
                for hl in range(2):
                    P.dma(arr["k"][hl, ps_, :], t[:, ko + hl * DK:ko + (hl + 1) * DK], q="sp")
                    P.dma(arr["v"][hl, ps_, :], t[:, vo + hl * 256:vo + (hl + 1) * 256], q="sp")
            else:
                for hl in range(2):
                    o_ = R.tmflip(t, ko + hl * DK, DK)
                    P.dma(arr["k"][2 + hl, ps_, :], o_[:, 0:DK], q="sp")
                o_ = R.tmflip(t, vo, 512)
                for hl in range(2):
                    P.dma(arr["v"][2 + hl, ps_, :], o_[:, hl * 256:(hl + 1) * 256], q="sp")


def relayout_mlstm(P, ST, c0, arr):
    ncols = 1544
    R = Relay(P, ncols)
    for c in range(34):
        t = R.load(ST, c, c0, ncols, True)
        for d in range(2):
            cc = c if d == 0 else rev_chunk(c)
            ps_ = slice(cc * 128, cc * 128 + 128)
            fl = (d == 1)
            for hl in range(2):
                u = d * 2 + hl
                for dc in range(2):
                    o_ = R.fm(t, hl * 256 + dc * 128, 128, fl, None)
                    P.dma(arr["qpT"][u, dc * 128:(dc + 1) * 128, ps_], o_[:, 0:128], q="sp")
                    o_ = R.fm(t, 512 + hl * 256 + dc * 128, 128, fl, None)
                    P.dma(arr["kpT"][u, dc * 128:(dc + 1) * 128, ps_], o_[:, 0:128], q="sp")
            o_ = R.fm(t, 1536 + d * 4, 4, fl, None)
            for hl in range(2):
                P.dma(arr["gf"][d * 2 + hl:d * 2 + hl + 1, ps_], o_[hl:hl + 1, 0:128], q="sp")
                P.dma(arr["gi"][d * 2 + hl:d * 2 + hl + 1, ps_], o_[2 + hl:3 + hl, 0:128], q="sp")
            if d == 0:
                for hl in range(2):
                    P.dma(arr["v"][hl, ps_, :], t[:, 1024 + hl * 256:1024 + (hl + 1) * 256], q="sp")
            else:
                o_ = R.tmflip(t, 1024, 512)
                for hl in range(2):
                    P.dma(arr["v"][2 + hl, ps_, :], o_[:, hl * 256:(hl + 1) * 256], q="sp")


def emit_seq_inproj(P, Hall, wtok, NT, ST, wfm, NFM, SG, KC=16):
    hT = P.sb([128, KC, TOKR], BF16, "sq_hT")
    wst = [P.sb([128, KC, 256], F32, "sq_wst%d" % i) for i in range(2)]
    wbf = [P.sb([128, KC, 256], BF16, "sq_wbf%d" % i) for i in range(2)]
    osb = [P.sb([128, 512], F32, "sq_o%d" % i) for i in range(3)]
    ps = [P.ps([128, 512], F32, "sq_ps%d" % i) for i in range(4)]
    wtv = wtok.rearrange("(kc p) n -> p kc n", p=128)
    wfv = wfm.rearrange("(kc p) n -> p kc n", p=128)
    wi = 0
    oi = 0
    for rho in range(2):
        for ti, (_c, t0_, T_) in enumerate(TILES_ALL):
            P.dma(hT[:, :, t0_:t0_ + T_], Hall[ti][rho * 2048:(rho + 1) * 2048, :].rearrange("(kc p) t -> p kc t", p=128))
        for cb0 in range(0, NT, 256):
            cw = min(256, NT - cb0)
            b = wi % 2
            wi += 1
            P.dma(wst[b][:, :, 0:cw], wtv[:, :, cb0:cb0 + cw])
            P.copy(wbf[b][:, :, 0:cw], wst[b][:, :, 0:cw], eng="dve" if wi % 2 else "act")
            for tt in range(TOKR // 128):
                pz = ps[oi % 4]
                for kc in range(KC):
                    P.mm(pz[:, 0:cw], hT[:, kc, tt * 128:(tt + 1) * 128], wbf[b][:, kc, 0:cw], start=(kc == 0), stop=(kc == KC - 1))
                ob = osb[oi % 3]
                P.copy(ob[:, 0:cw], pz[:, 0:cw], eng="act" if oi % 2 else "dve")
                oi += 1
                P.dma(ST[rho * TOKR + tt * 128:rho * TOKR + (tt + 1) * 128, cb0:cb0 + cw], ob[:, 0:cw], q="pool")
        for j in range(NFM // 128):
            b = wi % 2
            wi += 1
            P.dma(wst[b][:, :, 0:128], wfv[:, :, j * 128:(j + 1) * 128])
            P.copy(wbf[b][:, :, 0:128], wst[b][:, :, 0:128], eng="dve" if wi % 2 else "act")
            for (t0, T) in [(0, 512), (512, 512), (1024, 512), (1536, 512), (2048, 128)]:
                pz = ps[oi % 4]
                for kc in range(KC):
                    P.mm(pz[:, 0:T], wbf[b][:, kc, 0:128], hT[:, kc, t0:t0 + T], start=(kc == 0), stop=(kc == KC - 1))
                ob = osb[oi % 3]
                P.copy(ob[:, 0:T], pz[:, 0:T], eng="act" if oi % 2 else "dve")
                oi += 1
                P.dma(SG[j * 128:(j + 1) * 128, rho * TOKR + t0:rho * TOKR + t0 + T], ob[:, 0:T], q="pool")


class MergeIn:
    def __init__(self, P):
        self.P = P
        self.I = P.sb([128, 128], F32, "mI")
        self.J = P.sb([128, 128], F32, "mJ")
        P.memset(self.I, 1.0, eng="pool")
        P.memset(self.J, 1.0, eng="pool")
        P.aselect(self.I, self.I, [[-1, 128]], ALU.is_equal, 0.0, 0, 1)
        P.aselect(self.J, self.J, [[1, 128]], ALU.is_equal, 0.0, -127, 1)
        self.tl = [P.sb([128, 256], F32, "mtl%d" % i) for i in range(4)]
        self.ti = 0

    def load(self, O, u, d, cm, rho, t0):
        P = self.P
        t = self.tl[self.ti % 4]
        self.ti += 1
        if t0 >= LATR:
            c = rho if d == 0 else 1 - rho
            P.dma(t, O[u, c * 128:(c + 1) * 128, :])
            return t, (d == 1)
        if not cm:
            i = (rho * LATR + t0) // 128
            c = 2 + i if d == 0 else 2 + (31 - i)
            P.dma(t, O[u, c * 128:(c + 1) * 128, :])
            return t, (d == 1)
        r = (rho * LATR + t0) // 64
        lat = O[u, 256:256 + 4096, :].rearrange("(col row) c -> col row c", row=64)
        if d == 0:
            for rl in range(2):
                P.dma(t[rl * 64:(rl + 1) * 64, :], lat[:, r + rl, :])
            return t, False
        base = 62 - r
        for rl in range(2):
            P.dma(t[rl * 64:(rl + 1) * 64, :], lat[:, base + rl, :])
        return t, True


def emit_merge_layer0(P, C, M, O_gla, O_ml, SG, gvA, gvB, wout, Ypart, tiles_rho):
    T_ = C.T
    osum = P.sb([128, 2, T_], F32, "osum")
    gact = [P.sb([128, T_], F32, "gact%d" % i) for i in range(2)]
    gld = [P.sb([128, T_], F32, "gld%d" % i) for i in range(2)]
    gv = P.sb([128, 2, 4], F32, "gvm")
    with P.nc.allow_non_contiguous_dma(reason="small const loads"):
        P.dma(gv[:, 0, :], gvA.rearrange("(kc p) -> p kc", p=128))
        P.dma(gv[:, 1, :], gvB.rearrange("(kc p) -> p kc", p=128))
    pst = [C.ps_u[0], C.ps_u[1]]
    for rho in range(2):
        for ti, (t0, T) in enumerate(tiles_rho):
            for mix in range(2):
                O = O_gla if mix == 0 else O_ml
                for hl in range(2):
                    for i in range(2):
                        kcl = mix * 4 + hl * 2 + i
                        P.dma(gld[i][:, 0:T], SG[kcl * 128:(kcl + 1) * 128, rho * TOKR + t0:rho * TOKR + t0 + T])
                        P.act(gact[i][:, 0:T], gld[i][:, 0:T], AF.Silu if mix == 0 else AF.Sigmoid)
                    for sub in range(T // 128):
                        ssl = slice(sub * 128, (sub + 1) * 128)
                        tf, ff = M.load(O, hl, 0, mix == 1, rho, t0 + sub * 128)
                        tb, fb = M.load(O, 2 + hl, 1, mix == 1, rho, t0 + sub * 128)
                        for i in range(2):
                            P.mm(pst[0][:, 0:128], tf[:, i * 128:(i + 1) * 128], M.J if ff else M.I)
                            P.mm(pst[1][:, 0:128], tb[:, i * 128:(i + 1) * 128], M.J if fb else M.I)
                            P.copy(osum[:, i, ssl], pst[0][:, 0:128], eng="act")
                            P.tt(osum[:, i, ssl], osum[:, i, ssl], pst[1][:, 0:128], ALU.add)
                    cols = [hl * 2, hl * 2 + 1]
                    head_norm_chunks(C, T, osum, gv[:, mix, :], cols, [g[:, 0:T] for g in gact], mix == 1,
                                     [C.h[:, mix * 4 + hl * 2 + i, 0:T] for i in range(2)], 256)

            def sink(j, cw, pg, rho=rho, ti=ti, T=T):
                so = C.sg[j % 2][:, 0:T]
                P.copy(so, pg, eng="act")
                P.dma(Ypart[ti][j // 4][rho * 512 + (j % 4) * 128:rho * 512 + (j % 4 + 1) * 128, :], so, q="pool")
            proj(C, T, wout, 8, 2048, lambda kc: C.h[:, kc, 0:T], sink, tiled=True)


def emit_merge_layer1(P, C, M, O_s5, O_ret, SG, ins, Gf, Gown, Gall, wout, Ypart, tiles_rho):
    T_ = C.T
    osum = P.sb([128, 2, T_], F32, "osum")
    ysum = P.sb([128, 4, T_], F32, "ysum")
    gact = [P.sb([128, T_], F32, "gact%d" % i) for i in range(2)]
    gld = [P.sb([128, T_], F32, "gld%d" % i) for i in range(2)]
    otl = [P.sb([128, 512], F32, "otl%d" % i) for i in range(4)]
    gv = P.sb([128, 3, 4], F32, "gvm")
    with P.nc.allow_non_contiguous_dma(reason="small const loads"):
        P.dma(gv[:, 0, :], ins["s5d"].rearrange("(kc p) -> p kc", p=128))
        P.dma(gv[:, 1, :], ins["nB"].rearrange("(kc p) -> p kc", p=128))
        P.dma(gv[:, 2, :], ins["bglu"].rearrange("(kc p) -> p kc", p=128))
    pst = [C.ps_u[0], C.ps_u[1]]
    oi = 0
    for rho in range(2):
        for ti, (t0, T) in enumerate(tiles_rho):
            for sub in range(T // 128):
                ssl = slice(sub * 128, (sub + 1) * 128)
                r0 = rho * TOKR + t0 + sub * 128
                tf = otl[oi % 4]
                tb = otl[(oi + 1) % 4]
                oi += 2
                P.dma(tf, O_s5[0][r0:r0 + 128, :])
                P.dma(tb, O_s5[1][r0:r0 + 128, :])
                for kc in range(4):
                    P.mm(pst[0][:, 0:128], tf[:, kc * 128:(kc + 1) * 128], M.I)
                    P.mm(pst[1][:, 0:128], tb[:, kc * 128:(kc + 1) * 128], M.I)
                    P.copy(ysum[:, kc, ssl], pst[0][:, 0:128], eng="act")
                    P.tt(ysum[:, kc, ssl], ysum[:, kc, ssl], pst[1][:, 0:128], ALU.add)
            csl = slice(rho * TOKR + t0, rho * TOKR + t0 + T)
            for kc in range(4):
                u_ = gld[kc % 2][:, 0:T]
                P.dma(u_, SG[kc * 128:(kc + 1) * 128, csl])
                yv = C.tmp[0][:, 0:T]
                t2 = C.tmp[1][:, 0:T]
                P.stt(yv, u_, gv[:, 0, kc:kc + 1], ysum[:, kc, 0:T], ALU.mult, ALU.add)
                P.tt(t2, yv, yv, ALU.mult)
                P.ts(t2, t2, 0.044715, ALU.mult, 1.0, ALU.add)
                P.tt(t2, t2, yv, ALU.mult)
                P.act(t2, t2, AF.Sigmoid, scale=1.5957691216057308)
                P.tt(C.XB[:, kc, 0:T], t2, yv, ALU.mult)
                P.copy(C.h[:, kc, 0:T], C.XB[:, kc, 0:T], eng="act")
                P.dma(Gf[kc * 128:(kc + 1) * 128, csl], C.XB[:, kc, 0:T], q="pool")
                P.dma(Gown[rho][ti][kc * 128:(kc + 1) * 128, :], C.h[:, kc, 0:T], q="pool")
    for rho in range(2):
        for ti in range(len(tiles_rho)):
            P.collective("AllGather", Gall[rho][ti], Gown[rho][ti], G2)
    for rho in range(2):
        for ti, (t0, T) in enumerate(tiles_rho):
            csl = slice(rho * TOKR + t0, rho * TOKR + t0 + T)
            P.dma(C.hid[:, 0:8, 0:T], Gall[rho][ti].rearrange("(kc p) t -> p kc t", p=128))

            def sink_g(j, cw, pg, T=T, csl=csl):
                sgm = C.sg[j % 2][:, 0:T]
                P.act(sgm, pg, AF.Sigmoid, bias=gv[:, 2, j:j + 1])
                g_ = gld[j % 2][:, 0:T]
                P.dma(g_, Gf[j * 128:(j + 1) * 128, csl])
                P.tt(C.h[:, j, 0:T], g_, sgm, ALU.mult)
            proj(C, T, ins["wglu"], 8, 512, lambda kc: C.hid[:, kc, 0:T], sink_g, tiled=True)
            for hl in range(2):
                for i in range(2):
                    kcl = 4 + hl * 2 + i
                    P.dma(gld[i][:, 0:T], SG[kcl * 128:(kcl + 1) * 128, csl])
                    P.act(gact[i][:, 0:T], gld[i][:, 0:T], AF.Silu)
                for sub in range(T // 128):
                    ssl = slice(sub * 128, (sub + 1) * 128)
                    tf, ff = M.load(O_ret, hl, 0, True, rho, t0 + sub * 128)
                    tb, fb = M.load(O_ret, 2 + hl, 1, True, rho, t0 + sub * 128)
                    for i in range(2):
                        P.mm(pst[0][:, 0:128], tf[:, i * 128:(i + 1) * 128], M.J if ff else M.I)
                        P.mm(pst[1][:, 0:128], tb[:, i * 128:(i + 1) * 128], M.J if fb else M.I)
                        P.copy(osum[:, i, ssl], pst[0][:, 0:128], eng="act")
                        P.tt(osum[:, i, ssl], osum[:, i, ssl], pst[1][:, 0:128], ALU.add)
                cols = [hl * 2, hl * 2 + 1]
                head_norm_chunks(C, T, osum, gv[:, 1, :], cols, [g[:, 0:T] for g in gact], True,
                                 [C.h[:, 4 + hl * 2 + i, 0:T] for i in range(2)], 256)

            def sink(j, cw, pg, rho=rho, ti=ti, T=T):
                so = C.sg[j % 2][:, 0:T]
                P.copy(so, pg, eng="act")
                P.dma(Ypart[ti][j // 4][rho * 512 + (j % 4) * 128:rho * 512 + (j % 4 + 1) * 128, :], so, q="pool")
            proj(C, T, wout, 8, 2048, lambda kc: C.h[:, kc, 0:T], sink, tiled=True)


D_, DFF_M = 2048, 5504
TILES_ALL = [(0, i * 512, 512) for i in range(4)] + [(1, 2048, 128)]
TILES_LAT = [(0, i * 512, 512) for i in range(4)]


def emit_P1(P, l, xin, E, X1, Hown, modo):
    C = TokCtx(P, D_, DFF_M, 512, nwst=5, nwbf=7, look=5)
    load_consts(C, E["npre%d" % l], E["npost%d" % l])
    compute_mod(C, E["cT"], E["w_mod%d" % l], E["b_mod%d" % l], modo, tiled=True)
    A0, sh0, G0 = derive_sub(C, 0, 0.5)
    A1, sh1, G1 = derive_sub(C, 1, 1.0)
    xv = xin.rearrange("(kc p) t -> p kc t", p=128)
    x1v = X1.rearrange("(kc p) t -> p kc t", p=128)
    for ti, (col, t0, T) in enumerate(TILES_ALL):
        P.dma(C.XB[:, :, 0:T], xv[:, :, t0:t0 + T], q="sp")
        sublayer_in(C, T, A0, sh0, col)
        ffn_core(C, T, E["wg%da" % l], E["wu%da" % l], E["wd%da" % l], tiled=True)
        sublayer_out(C, T, G0, col, lambda kc, t0=t0, T=T: xv[:, kc, t0:t0 + T])
        P.dma(x1v[:, :, t0:t0 + T], C.XB[:, :, 0:T], q="pool")
        sublayer_in(C, T, A1, sh1, col)
        P.dma(Hown[ti].rearrange("(kc p) t -> p kc t", p=128), C.h[:, :, 0:T], q="pool")


def emit_P3(P, l, E, X1, X2, Yown, modo, xout, tiles):
    C = TokCtx(P, D_, DFF_M, 512, nwst=5, nwbf=7, look=5)
    load_consts(C, E["npre%d" % l], E["npost%d" % l])
    load_mod(C, modo)
    A1, sh1, G1 = derive_sub(C, 1, 1.0)
    A2, sh2, G2_ = derive_sub(C, 2, 0.5)
    x1v = X1.rearrange("(kc p) t -> p kc t", p=128)
    x2v = X2.rearrange("(kc p) t -> p kc t", p=128)
    xov = xout.rearrange("(kc p) t -> p kc t", p=128)
    for ti, (col, t0, T) in enumerate(tiles):
        for q_ in range(4):
            P.dma(C.XB[:, q_ * 4:(q_ + 1) * 4, 0:T], Yown[ti][q_].rearrange("(kc p) t -> p kc t", p=128), q="sp")
        sublayer_out(C, T, G1, col, lambda kc, t0=t0, T=T: x1v[:, kc, t0:t0 + T])
        P.dma(x2v[:, :, t0:t0 + T], C.XB[:, :, 0:T], q="pool")
        sublayer_in(C, T, A2, sh2, col)
        ffn_core(C, T, E["wg%db" % l], E["wu%db" % l], E["wd%db" % l], tiled=True)
        sublayer_out(C, T, G2_, col, lambda kc, t0=t0, T=T: x2v[:, kc, t0:t0 + T])
        P.dma(xov[:, :, t0:t0 + T], C.XB[:, :, 0:T], q="pool", is_output=True)


EXT_SHAPES = {
    "xT": [2048, TOKR], "cT": [2048, 2],
    "gwg": [4, 17, 128], "mcwq": [4, 256, 3], "mcwk": [4, 256, 3], "mcbq": [4, 256, 1], "mcbk": [4, 256, 1],
    "mbi": [4, 1], "mbf": [4, 1], "nA0": [512], "nB0": [512],
    "rdec": [4, 128, 1], "s5d": [512], "wglu": [4, 128, 8, 128], "bglu": [512], "nB1": [512],
    "wtok0": [2048, 2600], "wtok1": [2048, 2048],
}
for _l in range(2):
    EXT_SHAPES.update({"w_mod%d" % _l: [144, 128, 16, 128], "b_mod%d" % _l: [18432], "npre%d" % _l: [3, 2048], "npost%d" % _l: [3, 2048],
                       "wfm%d" % _l: [2048, 1024], "wout%d" % _l: [16, 128, 8, 128]})
    for _s in "ab":
        EXT_SHAPES.update({"wg%d%s" % (_l, _s): [43, 128, 16, 128], "wu%d%s" % (_l, _s): [43, 128, 16, 128], "wd%d%s" % (_l, _s): [16, 128, 43, 128]})
for _d in range(2):
    for _k, _sh in (("lamre", [64, 32]), ("lamim", [64, 32]), ("lstep", [64, 32]), ("Bre", [64, 32, 16]), ("Bim", [64, 32, 16]),
                    ("Cre", [64, 32, 16]), ("Cim", [64, 32, 16])):
        EXT_SHAPES["s5%d_%s" % (_d, _k)] = _sh


class _Stop(Exception):
    pass


def build_mega(stop=None, dump=()):
    P = Prog("mega")
    try:
        _build_mega(P, stop, dump)
    except _Stop:
        pass
    return P


def _build_mega(P, stop, dump):
    cnt = [0]

    def chk(tag, tensors=()):
        cnt[0] += 1
        if stop is not None and cnt[0] == stop:
            for nm, v in tensors:
                if nm in dump:
                    o = P.dram("dbg_" + nm, list(v.shape), v.ap.dtype, kind="ExternalOutput")
                    P.dma(o, v, q="sp")
            print("STOP at", cnt[0], tag)
            raise _Stop()

    class _LazyE(dict):
        def __missing__(self, k):
            v = P.dram(k, EXT_SHAPES[k])
            self[k] = v
            return v
    E = _LazyE()
    P.ext = E
    x3T = P.dram("x3T", [2048, LATR], kind="ExternalOutput")
    N = NSQ
    X1 = P.dram_i("X1", [2048, TOKR])
    X2 = P.dram_i("X2", [2048, TOKR])
    X3 = P.dram_i("X3", [2048, TOKR])
    Hown = [P.dram_i("Hown%d" % i, [2048, T], BF16) for i, (_, _t, T) in enumerate(TILES_ALL)]
    Hall = [P.dram_i("Hall%d" % i, [4096, T], BF16) for i, (_, _t, T) in enumerate(TILES_ALL)]
    ST = P.dram_i("ST", [N, 2600])
    SG = P.dram_i("SG", [1024, N])
    modo = P.dram_i("modo", [128, 144, 2])
    Ypart = [[P.dram_i("Yp%d_%d" % (i, q), [1024, T]) for q in range(4)] for i, (_, _t, T) in enumerate(TILES_ALL)]
    Yown = [[P.dram_i("Yo%d_%d" % (i, q), [512, T]) for q in range(4)] for i, (_, _t, T) in enumerate(TILES_ALL)]
    for l in range(2):
        xin = E["xT"] if l == 0 else X3
        P.push_scope()
        emit_P1(P, l, xin, E, X1, Hown, modo)
        P.pop_scope()
        chk("P1_%d" % l, [("X1", X1), ("Hown", Hown[0])])
        for ti in range(len(TILES_ALL)):
            P.collective("AllGather", Hall[ti], Hown[ti], G2)
        chk("AG_%d" % l, [("Hall", Hall[0])])
        P.push_scope()
        emit_seq_inproj(P, Hall, E["wtok%d" % l], 2600 if l == 0 else 2048, ST, E["wfm%d" % l], 1024, SG)
        P.pop_scope()
        chk("inproj_%d" % l, [("ST", ST), ("SG", SG)])
        if l == 0:
            ga = {"qT": P.dram_i("g_qT", [4, 128, N]), "kT": P.dram_i("g_kT", [4, 128, N]), "k": P.dram_i("g_k", [4, N, 128]),
                  "v": P.dram_i("g_v", [4, N, 256]), "lrT": P.dram_i("g_lrT", [4, 16, N]), "o": P.dram_i("g_o", [4, N, 256])}
            P.push_scope()
            relayout_gla(P, ST, 0, 128, False, ga, lr=True)
            P.pop_scope()
            chk("relay_gla", [("g_qT", ga["qT"]), ("g_k", ga["k"]), ("g_v", ga["v"]), ("g_lrT", ga["lrT"])])
            P.push_scope()
            P.bind = dict(ga)
            P.bind["wg"] = E["gwg"]
            emit_gla(P, 4, 34, 128, 256, "gla", 128 ** -0.5, 1.0)
            P.bind = None
            P.pop_scope()
            chk("gla", [("g_o", ga["o"])])
            ma = {"qpT": P.dram_i("m_qpT", [4, 256, N]), "kpT": P.dram_i("m_kpT", [4, 256, N]), "v": P.dram_i("m_v", [4, N, 256]),
                  "gi": P.dram_i("m_gi", [4, N]), "gf": P.dram_i("m_gf", [4, N]), "h": P.dram_i("m_h", [4, N, 256])}
            P.push_scope()
            relayout_mlstm(P, ST, 1056, ma)
            P.pop_scope()
            chk("relay_ml", [("m_qpT", ma["qpT"]), ("m_gi", ma["gi"]), ("m_v", ma["v"])])
            P.push_scope()
            P.bind = dict(ma)
            P.bind.update({"cwq": E["mcwq"], "cwk": E["mcwk"], "cbq": E["mcbq"], "cbk": E["mcbk"], "bi": E["mbi"], "bf": E["mbf"]})
            emit_mlstm(P, 4, [2, 32], 256, 256 ** -0.5)
            P.bind = None
            P.pop_scope()
            chk("mlstm", [("m_h", ma["h"])])
            P.push_scope()
            C = TokCtx(P, D_, DFF_M, 512)
            M = MergeIn(P)
            emit_merge_layer0(P, C, M, ga["o"], ma["h"], SG, E["nA0"], E["nB0"], E["wout0"], Ypart,
                              [(t0, T) for (_, t0, T) in TILES_ALL])
            P.pop_scope()
            chk("merge0", [("Ypart", Ypart[0][0])])
        else:
            O_s5 = [P.dram_i("s5_o%d" % d, [N, 512]) for d in range(2)]
            for d in range(2):
                P.push_scope()
                prm = {k: E["s5%d_%s" % (d, k)] for k in ("lamre", "lamim", "lstep", "Bre", "Bim", "Cre", "Cim")}
                emit_s5v2(P, 32, 8, d == 1, ST, 0, O_s5[d], prm)
                P.pop_scope()
                chk("s5_%d" % d, [("s5o", O_s5[d])])
            ra = {"qT": P.dram_i("r_qT", [4, 256, N]), "kT": P.dram_i("r_kT", [4, 256, N]), "k": P.dram_i("r_k", [4, N, 256]),
                  "v": P.dram_i("r_v", [4, N, 256]), "o": P.dram_i("r_o", [4, N, 256])}
            P.push_scope()
            relayout_gla(P, ST, 512, 256, True, ra, lr=False)
            P.pop_scope()
            chk("relay_ret", [("r_qT", ra["qT"])])
            P.push_scope()
            P.bind = dict(ra)
            P.bind["dec"] = E["rdec"]
            emit_gla(P, 4, 34, 256, 256, "ret", 1.0, 256 ** -0.5)
            P.bind = None
            P.pop_scope()
            chk("ret", [("r_o", ra["o"])])
            Gf = P.dram_i("Gf", [512, N])
            Gown = [[P.dram_i("Gown%d_%d" % (r_, i), [512, 512], BF16) for i in range(4)] for r_ in range(2)]
            Gall = [[P.dram_i("Gall%d_%d" % (r_, i), [1024, 512], BF16) for i in range(4)] for r_ in range(2)]
            P.push_scope()
            C = TokCtx(P, D_, DFF_M, 512)
            M = MergeIn(P)
            emit_merge_layer1(P, C, M, O_s5, ra["o"], SG, {"s5d": E["s5d"], "wglu": E["wglu"], "bglu": E["bglu"], "nB": E["nB1"]},
                              Gf, Gown, Gall, E["wout1"], Ypart, [(t0, T) for (_, t0, T) in TILES_LAT])
            P.pop_scope()
            chk("merge1", [("Ypart", Ypart[0][0])])
        for ti in range(5 if l == 0 else 4):
            for q_ in range(4):
                P.collective("ReduceScatter", Yown[ti][q_], Ypart[ti][q_], G2, op=ALU.add)
        chk("RS_%d" % l, [("Yown", Yown[0][0])])
        P.push_scope()
        if l == 0:
            emit_P3(P, l, E, X1, X2, Yown, modo, X3, TILES_ALL)
            P.pop_scope()
            chk("P3_0", [("X3", X3)])
            P.push_scope()
        else:
            emit_P3(P, l, E, X1, X2, Yown, modo, x3T, TILES_LAT)
        P.pop_scope()
    return P

_C = np.ascontiguousarray
_NCORE = 8


def _tile_w(w):
    K, N = w.shape
    return _C(w.reshape(K // 128, 128, N // 128, 128).transpose(2, 1, 0, 3))


def _core_inputs(inp, r):
    b, hf = r // 2, r % 2
    hs = [2 * hf, 2 * hf + 1]
    m = {}
    x = inp["x"][b]
    ctx = inp["ctx"][b]
    m["xT"] = _C(np.concatenate([x[hf * 2048:(hf + 1) * 2048], ctx[hf * 128:(hf + 1) * 128]], axis=0).T)
    m["cT"] = _C(np.stack([inp["c"][b], inp["c_ctx"]], axis=1))
    for l in range(2):
        m["w_mod%d" % l] = _tile_w(inp["w_mod"][l])
        m["b_mod%d" % l] = _C(inp["b_mod"][l])
        m["npre%d" % l] = _C(inp["norm_pre"][l])
        m["npost%d" % l] = _C(inp["norm_post"][l])
        for si, s in enumerate("ab"):
            m["wg%d%s" % (l, s)] = _tile_w(inp["ffn_w_gate"][l, si])
            m["wu%d%s" % (l, s)] = _tile_w(inp["ffn_w_up"][l, si])
            m["wd%d%s" % (l, s)] = _tile_w(inp["ffn_w_down"][l, si])
    ar = np.arange
    cols = []
    for off, w in ((0, 128), (512, 128), (1024, 256)):
        for h in hs:
            cols.append(off + h * w + ar(w))
    cols.append(3072 + ar(32))
    for off in (3104, 4128, 5152):
        for h in hs:
            cols.append(off + h * 256 + ar(256))
    for d in range(2):
        cols.append(np.array([7200 + d * 8 + 4 + hs[0], 7200 + d * 8 + 4 + hs[1], 7200 + d * 8 + hs[0], 7200 + d * 8 + hs[1]]))
    cols = np.concatenate(cols)
    w_in0 = inp["ev_w_in"][0]
    m["wtok0"] = _C(w_in0[:, cols])
    own512 = np.concatenate([h * 256 + ar(256) for h in hs])
    m["wfm0"] = _C(w_in0[:, np.concatenate([2048 + own512, 6176 + own512])])
    m["wout0"] = _tile_w(inp["ev_w_out"][0][np.concatenate([own512, 1024 + own512])])
    m["nA0"] = _C(inp["gla_norm"][0][own512])
    m["nB0"] = _C(inp["ml_norm"][0][own512])
    gwg, cwq, cwk, cbq, cbk, bi, bf = [], [], [], [], [], [], []
    cw = inp["ml_conv_w"][0]
    cb = inp["ml_conv_b"][0]
    bg = inp["ml_b_gates"][0]
    for d in range(2):
        for h in hs:
            gwg.append(np.concatenate([inp["gla_w_gate"][0, d][:, h * 128:(h + 1) * 128],
                                       inp["gla_b_gate"][0, d][None, h * 128:(h + 1) * 128]], axis=0))
            wq = cw[:, h * 256:(h + 1) * 256].T
            wk = cw[:, 1024 + h * 256:1024 + (h + 1) * 256].T
            if d == 1:
                wq, wk = wq[:, ::-1], wk[:, ::-1]
            cwq.append(wq)
            cwk.append(wk)
            cbq.append(cb[h * 256:(h + 1) * 256][:, None])
            cbk.append(cb[1024 + h * 256:1024 + (h + 1) * 256][:, None])
            bi.append([bg[d, 0, h]])
            bf.append([bg[d, 1, h]])
    m["gwg"] = _C(np.stack(gwg)).astype(np.float32)
    m["mcwq"] = _C(np.stack(cwq)).astype(np.float32)
    m["mcwk"] = _C(np.stack(cwk)).astype(np.float32)
    m["mcbq"] = _C(np.stack(cbq)).astype(np.float32)
    m["mcbk"] = _C(np.stack(cbk)).astype(np.float32)
    m["mbi"] = np.array(bi, np.float32)
    m["mbf"] = np.array(bf, np.float32)
    w_in1 = inp["od_w_in"][0]
    ch512 = hf * 512 + ar(512)
    cols1 = [ch512]
    for off in (1024, 2048, 3072):
        for h in hs:
            cols1.append(off + h * 256 + ar(256))
    m["wtok1"] = _C(w_in1[:, np.concatenate(cols1)])
    m["wfm1"] = _C(w_in1[:, np.concatenate([ch512, 4096 + own512])])
    m["wout1"] = _tile_w(inp["od_w_out"][0][np.concatenate([ch512, 1024 + own512])])
    m["nB1"] = _C(inp["ret_norm"][0][own512])
    m["s5d"] = _C(inp["s5_d"][0][ch512])
    m["wglu"] = _tile_w(inp["s5_w_glu"][0][:, ch512])
    m["bglu"] = _C(inp["s5_b_glu"][0][ch512])
    gs = slice(hf * 32, (hf + 1) * 32)
    for d in range(2):
        pre = "s5%d_" % d
        m[pre + "lamre"] = _C(inp["s5_lam_re"][0, d][gs].T)
        m[pre + "lamim"] = _C(inp["s5_lam_im"][0, d][gs].T)
        m[pre + "lstep"] = _C(np.broadcast_to(inp["s5_log_step"][0, d][gs][None, :], (64, 32)))
        m[pre + "Bre"] = _C(inp["s5_b_re"][0, d][gs].transpose(1, 0, 2))
        m[pre + "Bim"] = _C(inp["s5_b_im"][0, d][gs].transpose(1, 0, 2))
        m[pre + "Cre"] = _C(inp["s5_c_re"][0, d][gs].transpose(2, 0, 1))
        m[pre + "Cim"] = _C(inp["s5_c_im"][0, d][gs].transpose(2, 0, 1))
    m["rdec"] = _C(np.stack([np.full((128, 1), inp["ret_log_decay"][0, d, h], np.float32) for d in range(2) for h in hs]))
    return m


def kernel(**inp):
    inp = {k: np.asarray(v, dtype=np.float32) for k, v in inp.items()}
    P = build_mega()
    nc = P.finish()
    maps = [{k: v for k, v in _core_inputs(inp, r).items() if k in P.ext} for r in range(_NCORE)]
    res = run_bass_kernel_spmd(nc, maps, core_ids=list(range(_NCORE))).results
    out = np.empty((4, 4096, 2048), np.float32)
    for r in range(_NCORE):
        b, hf = r // 2, r % 2
        out[b, hf * 2048:(hf + 1) * 2048] = res[r]["x3T"].T
    return out
```

```python
import math

import numpy as np
import concourse.bass as bass
import concourse.mybir as mybir
from concourse.bass_utils import run_bass_kernel_spmd

F32 = mybir.dt.float32
BF16 = mybir.dt.bfloat16
I32 = mybir.dt.int32
AF = mybir.ActivationFunctionType
ALU = mybir.AluOpType

_uid = [0]
import os as _os
SES_DEFAULT = _os.environ.get("SES", "1") == "1"
WAW_SKIP = _os.environ.get("WAWSKIP", "1") == "1"


class V:
    __slots__ = ("ap", "keys")

    def __init__(self, ap, keys):
        self.ap = ap
        self.keys = keys if isinstance(keys, tuple) else (keys,)

    def __getitem__(self, idx):
        return V(self.ap[idx], self.keys)

    def k(self, *keys):
        return V(self.ap, tuple(keys))

    def re(self, s, **kw):
        return V(self.ap.rearrange(s, **kw), self.keys)

    def bc(self, shape):
        return V(self.ap.to_broadcast(shape), self.keys)

    def rearrange(self, s, **kw):
        return V(self.ap.rearrange(s, **kw), self.keys)

    @property
    def shape(self):
        return self.ap.shape


class Prog:
    ENG = ("pe", "act", "dve", "pool", "sp")

    def __init__(self, name="k", ring=6, same_engine_sync=SES_DEFAULT):
        self.nc = bass.Bass("TRN2", target_bir_lowering=False, name=name)
        nc = self.nc
        self.eng = {"pe": nc.tensor, "act": nc.scalar, "dve": nc.vector, "pool": nc.gpsimd, "sp": nc.sync}
        self.sem = {k: nc.alloc_semaphore("c_" + k) for k in self.ENG}
        self.cnt = {k: 0 for k in self.ENG}
        self.seen = {k: {} for k in self.ENG}
        self.ring = {q: [[nc.alloc_semaphore("d_%s%d" % (q, i)), 0] for i in range(ring)] for q in ("sp", "pool", "act")}
        self.rpos = {q: 0 for q in self.ring}
        self.lastw = {}
        self.reads = {}
        self.ses = same_engine_sync
        self.out_events = []
        self.n_inst = 0

    def dram(self, name, shape, dt=F32, kind="ExternalInput"):
        bind = getattr(self, "bind", None)
        if bind is not None and name in bind:
            return bind[name]
        return self.nc.dram_tensor(name, list(shape), dt, kind=kind).ap()

    def dram_i(self, name, shape, dt=F32):
        return V(self.nc.dram_tensor(name, list(shape), dt, kind="Internal").ap(), "dram_" + name)

    def push_scope(self):
        import contextlib
        if not hasattr(self, "scopes"):
            self.scopes = []
            self.scope_id = 0
        self.scope_id += 1
        self.scopes.append((contextlib.ExitStack(), self.scope_id))

    def pop_scope(self):
        self.barrier()
        st, _ = self.scopes.pop()
        st.close()

    def _uname(self, name):
        if getattr(self, "scopes", None):
            return "%s_s%d" % (name, self.scopes[-1][1])
        return name

    def sb(self, shape, dt=F32, name=None):
        _uid[0] += 1
        name = self._uname(name or ("t%d" % _uid[0]))
        if getattr(self, "scopes", None):
            t = self.scopes[-1][0].enter_context(self.nc.sbuf_tensor("sb_" + name, list(shape), dt))
        else:
            t = self.nc.alloc_sbuf_tensor("sb_" + name, list(shape), dt)
        return V(t.ap() if hasattr(t, "ap") else t[:], name)

    def ps(self, shape, dt=F32, name=None):
        _uid[0] += 1
        base = name or ("p%d" % _uid[0])
        name = self._uname(base)
        esz = 2 if dt == BF16 else 4
        if getattr(self, "scopes", None):
            t = self.scopes[-1][0].enter_context(self.nc.psum_tensor("ps_" + name, [128, 2048 // esz], dt))
        else:
            t = self.nc.alloc_psum_tensor("ps_" + name, [128, 2048 // esz], dt)
        full = V(t.ap() if hasattr(t, "ap") else t[:], name)
        if not hasattr(self, "banks"):
            self.banks = {}
        self.banks[base] = full
        return self.ps_alias(base, shape)

    def barrier(self):
        evs = []
        for x in self.ENG:
            if self.cnt[x] > 0:
                evs.append((self.sem[x], self.cnt[x], "c_" + x))
        for q in self.ring:
            for i, slot in enumerate(self.ring[q]):
                if slot[1] > 0:
                    evs.append((slot[0], slot[1], "d_%s%d" % (q, i)))
        if hasattr(self, "cc_sem") and self.cc_cnt > 0:
            evs.append((self.cc_sem, self.cc_cnt, "cc_sem"))
        for e in self.ENG:
            for ev in evs:
                if ev[2] == "c_" + e:
                    continue
                self._wait(e, ev)

    def ps_alias(self, name, shape):
        full = self.banks[name]
        n = 1
        for d in shape[1:]:
            n *= d
        v = full[0:shape[0], 0:n]
        if len(shape) > 2:
            letters = "abcdefg"[:len(shape) - 1]
            kw = {letters[i]: shape[i + 1] for i in range(1, len(shape) - 1)}
            v = v.re("p (%s) -> p %s" % (" ".join(letters), " ".join(letters)), **kw)
        return v

    def _wait(self, e, ev):
        sem, val, key = ev
        if self.seen[e].get(key, 0) >= val:
            return
        self.eng[e].wait_ge(sem, val)
        self.seen[e][key] = val

    def _deps(self, e, reads, writes):
        raw, waw = [], []
        for v in reads:
            for k in v.keys:
                lw = self.lastw.get(k)
                if lw is not None:
                    raw.append(lw)
        for v in writes:
            for k in v.keys:
                lw = self.lastw.get(k)
                if lw is not None:
                    waw.append(lw)
                waw.extend(self.reads.get(k, ()))
        for ev in raw:
            if ev[3] == e and (e == "pe" or not self.ses):
                continue
            self._wait(e, ev[:3])
        for ev in waw:
            if ev[3] == e and (e == "pe" or not self.ses or WAW_SKIP):
                continue
            self._wait(e, ev[:3])

    def _commit(self, ev, reads, writes):
        for v in writes:
            for k in v.keys:
                self.lastw[k] = ev
                self.reads[k] = []
        for v in reads:
            for k in v.keys:
                self.reads.setdefault(k, []).append(ev)
                if len(self.reads[k]) > 24:
                    best = {}
                    for r in self.reads[k]:
                        if r[2] not in best or best[r[2]][1] < r[1]:
                            best[r[2]] = r
                    self.reads[k] = list(best.values())

    def op(self, e, fn, reads=(), writes=()):
        self._deps(e, reads, writes)
        inst = fn(self.eng[e])
        self.cnt[e] += 1
        inst.then_inc(self.sem[e], 1)
        ev = (self.sem[e], self.cnt[e], "c_" + e, e)
        self._commit(ev, reads, writes)
        self.n_inst += 1
        return ev

    def dma(self, out, in_, q="sp", is_output=False, **kw):
        reads = [in_] if isinstance(in_, V) else []
        writes = [out] if isinstance(out, V) else []
        self._deps(q, reads, writes)
        slot = self.ring[q][self.rpos[q] % len(self.ring[q])]
        key = "d_%s%d" % (q, self.rpos[q] % len(self.ring[q]))
        self.rpos[q] += 1
        if slot[1] > 0:
            self._wait(q, (slot[0], slot[1], key))
        o = out.ap if isinstance(out, V) else out
        i = in_.ap if isinstance(in_, V) else in_
        inst = self.eng[q].dma_start(out=o, in_=i, **kw)
        slot[1] += 16
        inst.then_inc(slot[0], 16)
        ev = (slot[0], slot[1], key, "dma")
        self._commit(ev, reads, writes)
        if is_output:
            self.out_events.append(ev)
        self.n_inst += 1
        return ev

    def finish(self):
        for q in self.ring:
            for i, slot in enumerate(self.ring[q]):
                if slot[1] > 0:
                    self._wait("sp", (slot[0], slot[1], "d_%s%d" % (q, i)))
        return self.nc

    def mm(self, out, lhsT, rhs, start=True, stop=True):
        return self.op("pe", lambda e: e.matmul(out.ap, lhsT.ap, rhs.ap, start=start, stop=stop),
                       reads=[lhsT, rhs], writes=[out])

    def transpose(self, out, in_, ident):
        return self.op("pe", lambda e: e.transpose(out.ap, in_.ap, ident.ap), reads=[in_, ident], writes=[out])

    def act(self, out, in_, func, bias=None, scale=None, accum_out=None, eng="act"):
        reads = [in_]
        kw = {}
        if bias is not None:
            if isinstance(bias, V):
                reads.append(bias)
                kw["bias"] = bias.ap
            else:
                kw["bias"] = bias
        if scale is not None:
            if isinstance(scale, V):
                reads.append(scale)
                kw["scale"] = scale.ap
            else:
                kw["scale"] = scale
        writes = [out]
        if accum_out is not None:
            writes.append(accum_out)
            kw["accum_out"] = accum_out.ap
        return self.op("act", lambda e: e.activation(out.ap, in_.ap, func, **kw), reads=reads, writes=writes)

    def tt(self, out, a, b, op, eng="dve"):
        return self.op(eng, lambda e: e.tensor_tensor(out.ap, a.ap, b.ap, op), reads=[a, b], writes=[out])

    def ts(self, out, a, s1, op0, s2=None, op1=None, eng="dve", accum_out=None):
        reads = [a]
        x1 = s1.ap if isinstance(s1, V) else s1
        x2 = s2.ap if isinstance(s2, V) else s2
        if isinstance(s1, V):
            reads.append(s1)
        if isinstance(s2, V):
            reads.append(s2)
        kw = {}
        writes = [out]
        if accum_out is not None:
            kw["accum_out"] = accum_out.ap
            writes.append(accum_out)
        if op1 is None:
            return self.op(eng, lambda e: e.tensor_scalar(out.ap, a.ap, x1, None, op0, **kw), reads=reads, writes=writes)
        return self.op(eng, lambda e: e.tensor_scalar(out.ap, a.ap, x1, x2, op0, op1, **kw), reads=reads, writes=writes)

    def stt(self, out, a, s, b, op0, op1):
        reads = [a, b]
        x = s.ap if isinstance(s, V) else s
        if isinstance(s, V):
            reads.append(s)
        return self.op("dve", lambda e: e.scalar_tensor_tensor(out.ap, a.ap, x, b.ap, op0, op1), reads=reads, writes=[out])

    def copy(self, out, in_, eng="dve"):
        if eng == "act":
            return self.op("act", lambda e: e.copy(out.ap, in_.ap), reads=[in_], writes=[out])
        return self.op(eng, lambda e: e.tensor_copy(out.ap, in_.ap), reads=[in_], writes=[out])

    def memset(self, out, val, eng="dve"):
        return self.op(eng, lambda e: e.memset(out.ap, val), writes=[out])

    def recip(self, out, in_):
        return self.op("dve", lambda e: e.reciprocal(out.ap, in_.ap), reads=[in_], writes=[out])

    def scan(self, out, d0, d1, init, op0, op1):
        reads = [d0, d1]
        x = init.ap if isinstance(init, V) else init
        if isinstance(init, V):
            reads.append(init)
        return self.op("dve", lambda e: e.tensor_tensor_scan(out.ap, d0.ap, d1.ap, x, op0, op1), reads=reads, writes=[out])

    def aselect(self, out, in_, pattern, cmp, fill, base, cm):
        return self.op("pool", lambda e: e.affine_select(out.ap, in_.ap, pattern, cmp, fill, base=base, channel_multiplier=cm),
                       reads=[in_], writes=[out])

    def iota(self, out, pattern, base, cm):
        return self.op("pool", lambda e: e.iota(out.ap, pattern, base=base, channel_multiplier=cm,
                                                 allow_small_or_imprecise_dtypes=True), writes=[out])


def run(prog, in_maps, n=8, trace=False):
    nc = prog.finish()
    res = run_bass_kernel_spmd(nc, in_maps, core_ids=list(range(n)), trace=trace)
    return res


def _collective(self, kind, out, in_, groups, op=None):
    q = "pool"
    if not hasattr(self, "cc_sem"):
        self.cc_sem = self.nc.alloc_semaphore("cc_sem")
        self.cc_cnt = 0
    self._deps(q, [in_], [out])
    inst = self.eng[q].collective_compute(kind, op or ALU.bypass, replica_groups=groups, ins=[in_.ap], outs=[out.ap])
    self.cc_cnt += 1
    inst.then_inc(self.cc_sem)
    ev = (self.cc_sem, self.cc_cnt, "cc_sem", "dma")
    self._commit(ev, [in_], [out])
    self._wait(q, ev[:3])
    self.n_inst += 1
    return ev


Prog.collective = _collective


def make_masks(P):
    U = P.sb([128, 128], F32, "Uincl")
    L = P.sb([128, 128], F32, "Lstrict")
    P.memset(U, 1.0, eng="pool")
    P.memset(L, 1.0, eng="pool")
    P.aselect(U, U, [[1, 128]], ALU.is_ge, 0.0, 0, -1)
    P.aselect(L, L, [[-1, 128]], ALU.is_gt, 0.0, 0, 1)
    return U, L


def emit_gla(P, NU, NCH, DK, DV, mode, qscale, kscale):
    N = NCH * 128
    NDK = DK // 128
    GN = 16.0 if mode == "gla" else 1.0
    qT = P.dram("qT", [NU, DK, N])
    kT = P.dram("kT", [NU, DK, N])
    kk = P.dram("k", [NU, N, DK])
    vv = P.dram("v", [NU, N, DV])
    if mode == "gla":
        lrT = P.dram("lrT", [NU, 16, N])
        wg = P.dram("wg", [NU, 17, DK])
    else:
        dec = P.dram("dec", [NU, 128, 1])
    o = P.dram("o", [NU, N, DV], kind="ExternalOutput")
    U, L = make_masks(P)
    ps_b = [P.ps([128, 128], F32, "psb%d" % i) for i in range(2)]
    ps_d = P.ps([128, DK], F32, "psd")
    ps_z = P.ps([128, DK], F32, "psz")
    ps_sc = P.ps([128, 128], F32, "pssc")
    ps_o = P.ps([128, DV], F32, "pso")
    ps_S = [P.ps([128, DV], F32, "psS%d" % i) for i in range(2)]
    if mode != "gla":
        zer = P.sb([128, DK], F32, "zer")
        P.memset(zer, 0.0)

    class B_:
        pass

    def alloc(u):
        B = B_()
        n = lambda s_: "%s_u%d" % (s_, u)
        B.qT_sb = [P.sb([128, NDK, 128], F32, n("qTs%d" % i)) for i in range(2)]
        B.kT_sb = [P.sb([128, NDK, 128], F32, n("kTs%d" % i)) for i in range(2)]
        B.k_sb = [P.sb([128, DK], F32, n("ks%d" % i)) for i in range(2)]
        B.v_sb = [P.sb([128, DV], F32, n("vs%d" % i)) for i in range(2)]
        B.v_bf = [P.sb([128, DV], BF16, n("vb%d" % i)) for i in range(2)]
        B.sp_t = [P.sb([128, DK], F32, n("spt%d" % i)) for i in range(2)]
        B.ex = P.sb([128, DK], F32, n("ex"))
        B.E1 = P.sb([128, NDK, 128], F32, n("E1"))
        B.E2 = P.sb([128, NDK, 128], F32, n("E2"))
        B.Dk = P.sb([128, DK], F32, n("Dk"))
        B.QtT = P.sb([128, NDK, 128], BF16, n("QtT"))
        B.KtT = P.sb([128, NDK, 128], BF16, n("KtT"))
        B.QbT = P.sb([128, NDK, 128], BF16, n("QbT"))
        B.Ke = P.sb([128, DK], BF16, n("Ke"))
        B.scm = P.sb([128, 128], BF16, n("scm"))
        B.S = P.sb([128, NDK, DV], F32, n("S"))
        B.Sbf = P.sb([128, NDK, DV], BF16, n("Sbf"))
        B.cols = P.sb([128, NDK, 4], F32, n("cols"))
        B.o_sb = [P.sb([128, DV], F32, n("osb%d" % i)) for i in range(2)]
        if mode == "gla":
            B.lr_sb = P.sb([17, N], F32, n("lrsb"))
            B.wg_sb = P.sb([17, DK], F32, n("wgsb"))
        else:
            B.dec_sb = P.sb([128, 1], F32, n("decsb"))
        return B

    Bs = [alloc(u) for u in range(NU)]
    for u in range(NU):
        B = Bs[u]
        P.memset(B.S, 0.0)
        P.memset(B.Sbf, 0.0, eng="pool")
        if mode == "gla":
            P.memset(B.lr_sb, 1.0)
            P.dma(B.lr_sb[0:16, :], lrT[u])
            P.dma(B.wg_sb, wg[u])
        else:
            P.dma(B.dec_sb, dec[u])
            P.act(B.sp_t[0], zer, AF.Exp, bias=B.dec_sb)
            P.act(B.sp_t[1], zer, AF.Exp, bias=B.dec_sb)
    for c in range(NCH):
        for u in range(NU):
            B = Bs[u]
            b = c % 2
            tsl = slice(c * 128, (c + 1) * 128)
            P.dma(B.qT_sb[b], qT[u, :, tsl].rearrange("(dc p) t -> p dc t", p=128))
            P.dma(B.kT_sb[b], kT[u, :, tsl].rearrange("(dc p) t -> p dc t", p=128))
            P.dma(B.k_sb[b], kk[u, tsl, :])
            P.dma(B.v_sb[b], vv[u, tsl, :])
            P.copy(B.v_bf[b], B.v_sb[b], eng="pool")
            spt = B.sp_t[b]
            if mode == "gla":
                P.mm(ps_z, B.lr_sb[:, tsl], B.wg_sb)
                P.act(B.ex, ps_z, AF.Exp, scale=-1.0)
                P.act(spt, B.ex, AF.Ln, bias=1.0)
            P.mm(ps_d, L, spt)
            P.act(B.Dk, ps_d, AF.Exp, scale=-1.0 / GN)
            P.stt(B.Ke, B.k_sb[b], float(kscale), B.Dk, ALU.mult, ALU.mult)
            for dc in range(NDK):
                pb = ps_b[dc % 2]
                P.mm(pb, spt[:, dc * 128:(dc + 1) * 128], U)
                cm = B.cols[:, dc, :]
                P.ts(cm[:, 0:1], pb[:, 63:64], 1.0 / GN, ALU.mult)
                P.ts(cm[:, 1:2], pb[:, 63:64], -1.0 / GN, ALU.mult)
                P.act(B.E1[:, dc, :], pb, AF.Exp, scale=-1.0 / GN, bias=cm[:, 0:1])
                P.act(B.E2[:, dc, :], pb, AF.Exp, scale=1.0 / GN, bias=cm[:, 1:2])
                P.act(cm[:, 2:3], pb[:, 63:64], AF.Exp, scale=-1.0 / GN)
                P.act(cm[:, 3:4], pb[:, 127:128], AF.Exp, scale=-1.0 / GN)
                P.stt(B.QtT[:, dc, :], B.qT_sb[b][:, dc, :], float(qscale), B.E1[:, dc, :], ALU.mult, ALU.mult)
                P.stt(B.KtT[:, dc, :], B.kT_sb[b][:, dc, :], float(kscale), B.E2[:, dc, :], ALU.mult, ALU.mult)
                P.ts(B.QbT[:, dc, :], B.QtT[:, dc, :], cm[:, 2:3], ALU.mult)
            for dc in range(NDK):
                P.mm(ps_sc, B.KtT[:, dc, :], B.QtT[:, dc, :], start=(dc == 0), stop=(dc == NDK - 1))
            P.tt(B.scm, ps_sc, U, ALU.mult)
            P.mm(ps_o, B.scm, B.v_bf[b], start=True, stop=False)
            for dc in range(NDK):
                P.mm(ps_o, B.QbT[:, dc, :], B.Sbf[:, dc, :], start=False, stop=(dc == NDK - 1))
            ob = B.o_sb[b]
            P.copy(ob, ps_o, eng="act")
            P.dma(o[u, tsl, :], ob, q="pool", is_output=True)
            for dc in range(NDK):
                pS = ps_S[dc % 2]
                P.mm(pS, B.Ke[:, dc * 128:(dc + 1) * 128], B.v_bf[b])
                P.stt(B.S[:, dc, :], B.S[:, dc, :], B.cols[:, dc, 3:4], pS, ALU.mult, ALU.add)
                P.copy(B.Sbf[:, dc, :], B.S[:, dc, :], eng="act")
    return P


def build_gla(NU, NCH, DK, DV, mode, qscale, kscale, name="gla"):
    P = Prog(name)
    emit_gla(P, NU, NCH, DK, DV, mode, qscale, kscale)
    return P


def emit_mlstm(P, NU, segs, DH, kscale):
    NCH = sum(segs)
    N = NCH * 128
    NDC = DH // 128
    DA = DH + 1
    qpT = P.dram("qpT", [NU, DH, N])
    kpT = P.dram("kpT", [NU, DH, N])
    vv = P.dram("v", [NU, N, DH])
    cwq = P.dram("cwq", [NU, DH, 3])
    cwk = P.dram("cwk", [NU, DH, 3])
    cbq = P.dram("cbq", [NU, DH, 1])
    cbk = P.dram("cbk", [NU, DH, 1])
    gi = P.dram("gi", [NU, N])
    gf = P.dram("gf", [NU, N])
    bi = P.dram("bi", [NU, 1])
    bf_ = P.dram("bf", [NU, 1])
    ho = P.dram("h", [NU, N, DH], kind="ExternalOutput")
    U, L = make_masks(P)
    identf = P.sb([128, 128], F32, "identf")
    P.memset(identf, 1.0, eng="pool")
    P.aselect(identf, identf, [[-1, 128]], ALU.is_equal, 0.0, 0, 1)
    identb = P.sb([128, 128], BF16, "identb")
    P.copy(identb, identf)
    R = P.sb([NU, 4, N], F32, "R")
    bcol = P.sb([NU, 4], F32, "bcol")
    with P.nc.allow_non_contiguous_dma(reason="small"):
        P.dma(R[:, 0, :], gf)
        P.dma(R[:, 1, :], gi)
        P.dma(bcol[:, 0:1], bf_)
        P.dma(bcol[:, 1:2], bi)
    P.ts(bcol[:, 2:3], bcol[:, 0:1], -1.0, ALU.mult)
    P.act(R[:, 2, :], R[:, 0, :], AF.Exp, scale=-1.0, bias=bcol[:, 2:3])
    P.act(R[:, 2, :], R[:, 2, :], AF.Ln, bias=1.0)
    P.scan(R[:, 3, :], R[:, 2, :], R[:, 2, :], 0.0, ALU.add, ALU.max)
    P.stt(R[:, 1, :], R[:, 1, :], bcol[:, 1:2], R[:, 3, :], ALU.add, ALU.add)
    P.scan(R[:, 2, :], R[:, 1, :], R[:, 1, :], 0.0, ALU.max, ALU.max)
    P.tt(R[:, 0, :], R[:, 2, :], R[:, 3, :], ALU.subtract)
    idn = P.sb([NU, NU], F32, "idn")
    P.memset(idn, 1.0, eng="pool")
    P.aselect(idn, idn, [[-1, NU]], ALU.is_equal, 0.0, 0, 1)
    sel = []
    for u in range(NU):
        s_ = P.sb([NU, 128], F32, "sel%d" % u)
        P.memset(s_, 1.0, eng="pool")
        P.aselect(s_, s_, [[0, 128]], ALU.is_equal, 0.0, -u, 1)
        sel.append(s_)
    assert NCH * 3 * NU <= 512
    ps_c = P.ps([128, NCH, 3, NU], F32, "psc")
    for c in range(NCH):
        for qi, row in enumerate((1, 2, 0)):
            P.mm(ps_c[:, c, qi, :], R[:, row, c * 128:(c + 1) * 128], idn)
    colsb = P.sb([128, NCH, 3, NU], F32, "colsb")
    P.copy(colsb, ps_c)
    HW = 130
    ps_M = P.ps([128, 128], F32, "psM")
    ps_sc = P.ps([128, 128], F32, "pssc")
    ps_o = P.ps([128, DA], F32, "pso")
    ps_t = P.ps([128, DH], BF16, "pst")
    ps_C = [P.ps([128, DA], F32, "psC%d" % i) for i in range(2)]
    seg_first = set()
    seg_last = set()
    c0 = 0
    for n in segs:
        seg_first.add(c0)
        seg_last.add(c0 + n - 1)
        c0 += n

    class B_:
        pass

    def alloc(u):
        B = B_()
        n = lambda s_: "%s_u%d" % (s_, u)
        B.qp_sb = [P.sb([128, NDC, HW], F32, n("qps%d" % i)) for i in range(2)]
        B.kp_sb = [P.sb([128, NDC, HW], F32, n("kps%d" % i)) for i in range(2)]
        B.acc = [P.sb([128, 128], F32, n("acc%d" % i)) for i in range(2)]
        B.qT = P.sb([128, NDC, 128], BF16, n("qT"))
        B.kT = P.sb([128, NDC, 128], BF16, n("kT"))
        B.qw = P.sb([128, NDC, 128], BF16, n("qw"))
        B.ktok = P.sb([128, DH], BF16, n("ktok"))
        B.va = [P.sb([128, DA], F32, n("va%d" % i)) for i in range(2)]
        for i in range(2):
            P.memset(B.va[i], 1.0)
        B.va_bf = P.sb([128, DA], BF16, n("vabf"))
        B.vw = P.sb([128, DA], BF16, n("vw"))
        B.cw = P.sb([128, 2, NDC, 4], F32, n("cw"))
        B.arg = P.sb([128, 128], F32, n("arg"))
        B.Dm = P.sb([128, 128], F32, n("Dm"))
        B.Wbc = P.sb([128, 128], F32, n("Wbc"))
        B.scm = P.sb([128, 128], BF16, n("scm"))
        B.Cst = P.sb([128, NDC, DA], F32, n("Cst"))
        B.Cbf = P.sb([128, NDC, DA], BF16, n("Cbf"))
        B.sc = P.sb([128, 12], F32, n("sc"))
        B.h_sb = [P.sb([128, DH], F32, n("hsb%d" % i)) for i in range(2)]
        return B

    Bs = [alloc(u) for u in range(NU)]
    for u in range(NU):
        B = Bs[u]
        P.memset(B.Cst, 0.0)
        P.memset(B.Cbf, 0.0, eng="pool")
        P.memset(B.sc[:, 0:1], 0.0)
        with P.nc.allow_non_contiguous_dma(reason="small"):
            P.dma(B.cw[:, 0, :, 0:3], cwq[u].rearrange("(dc p) k -> p dc k", p=128))
            P.dma(B.cw[:, 1, :, 0:3], cwk[u].rearrange("(dc p) k -> p dc k", p=128))
            P.dma(B.cw[:, 0, :, 3:4], cbq[u].rearrange("(dc p) k -> p dc k", p=128))
            P.dma(B.cw[:, 1, :, 3:4], cbk[u].rearrange("(dc p) k -> p dc k", p=128))
    for c in range(NCH):
        for u in range(NU):
            B = Bs[u]
            sc = B.sc
            b = c % 2
            tsl = slice(c * 128, (c + 1) * 128)
            lo = c * 128 - 1
            hi = c * 128 + 129
            dlo, dhi = 0, HW
            if c in seg_first:
                lo += 1
                dlo = 1
            if c in seg_last:
                hi -= 1
                dhi = HW - 1
            for (src, dst) in ((qpT, B.qp_sb[b]), (kpT, B.kp_sb[b])):
                if c in seg_first:
                    P.memset(dst[:, :, 0:1], 0.0, eng="pool")
                if c in seg_last:
                    P.memset(dst[:, :, HW - 1:HW], 0.0, eng="pool")
                P.dma(dst[:, :, dlo:dhi], src[u, :, lo:hi].rearrange("(dc p) t -> p dc t", p=128))
            P.dma(B.va[b][:, 0:DH], vv[u, tsl, :])
            P.copy(B.va_bf, B.va[b], eng="pool")
            k_ = 0
            for qk, (src, dstT) in enumerate(((B.qp_sb[b], B.qT), (B.kp_sb[b], B.kT))):
                for dc in range(NDC):
                    a_ = B.acc[k_ % 2]
                    k_ += 1
                    w = B.cw[:, qk, dc, :]
                    P.ts(a_, src[:, dc, 1:129], w[:, 1:2], ALU.mult, w[:, 3:4], ALU.add)
                    P.stt(a_, src[:, dc, 0:128], w[:, 0:1], a_, ALU.mult, ALU.add)
                    P.stt(a_, src[:, dc, 2:130], w[:, 2:3], a_, ALU.mult, ALU.add)
                    P.act(dstT[:, dc, :], a_, AF.Silu)
            a_col = colsb[:, c, 0, u:u + 1]
            m_col = colsb[:, c, 2, u:u + 1]
            P.mm(ps_M, sel[u], R[:, 2, tsl])
            P.ts(B.arg, ps_M, a_col, ALU.subtract, 0.0, ALU.max)
            P.act(B.Dm, B.arg, AF.Exp, scale=-1.0)
            P.tt(B.Dm, B.Dm, U, ALU.mult)
            for dc in range(NDC):
                P.mm(ps_sc, B.kT[:, dc, :], B.qT[:, dc, :], start=(dc == 0), stop=(dc == NDC - 1))
            P.stt(B.scm, ps_sc, float(kscale), B.Dm, ALU.mult, ALU.mult)
            P.act(B.Wbc, ps_M, AF.Exp, scale=-1.0, bias=sc[:, 0:1])
            for dc in range(NDC):
                P.tt(B.qw[:, dc, :], B.qT[:, dc, :], B.Wbc, ALU.mult)
            P.mm(ps_o, B.scm, B.va_bf, start=True, stop=False)
            for dc in range(NDC):
                P.mm(ps_o, B.qw[:, dc, :], B.Cbf[:, dc, :], start=False, stop=(dc == NDC - 1))
            P.act(sc[:, 5:6], m_col, AF.Exp, scale=-1.0)
            P.copy(sc[:, 9:10], ps_o[:, DH:DA])
            P.stt(sc[:, 6:7], sc[:, 9:10], -1.0, sc[:, 9:10], ALU.mult, ALU.max)
            P.tt(sc[:, 7:8], sc[:, 6:7], sc[:, 5:6], ALU.max)
            P.recip(sc[:, 8:9], sc[:, 7:8])
            hb = B.h_sb[b]
            P.ts(hb, ps_o[:, 0:DH], sc[:, 8:9], ALU.mult)
            P.dma(ho[u, tsl, :], hb, q="pool", is_output=True)
            P.copy(sc[:, 1:2], ps_M[:, 127:128])
            P.ts(sc[:, 2:3], ps_M[:, 127:128], -1.0, ALU.mult)
            P.act(sc[:, 3:4], a_col, AF.Exp, bias=sc[:, 2:3])
            P.ts(sc[:, 3:4], sc[:, 3:4], float(kscale), ALU.mult)
            P.ts(B.vw, B.va[b], sc[:, 3:4], ALU.mult)
            for dc in range(NDC):
                P.transpose(ps_t[:, dc * 128:(dc + 1) * 128], B.kT[:, dc, :], identb)
            P.copy(B.ktok, ps_t, eng="act")
            P.act(sc[:, 4:5], sc[:, 0:1], AF.Exp, bias=sc[:, 2:3])
            for dc in range(NDC):
                pC = ps_C[dc % 2]
                P.mm(pC, B.ktok[:, dc * 128:(dc + 1) * 128], B.vw)
                P.stt(B.Cst[:, dc, :], B.Cst[:, dc, :], sc[:, 4:5], pC, ALU.mult, ALU.add)
                P.copy(B.Cbf[:, dc, :], B.Cst[:, dc, :], eng="act")
            P.copy(sc[:, 0:1], sc[:, 1:2])
    return P


def build_mlstm(NU, segs, DH, kscale, name="mlstm"):
    P = Prog(name)
    emit_mlstm(P, NU, segs, DH, kscale)
    return P


TWO_PI = 2.0 * math.pi
TOKR = 2176
LATR = 2048


def emit_s5(P, G, GB, NB, NBLK):
    NMC = NB * NBLK
    Uin = P.dram("Uin", [G, 128, NMC])
    lamre_d = P.dram("lamre", [64, G])
    lamim_d = P.dram("lamim", [64, G])
    lstep_d = P.dram("lstep", [64, G])
    Bre_d = P.dram("Bre", [64, G, 16])
    Bim_d = P.dram("Bim", [64, G, 16])
    Cre_d = P.dram("Cre", [64, G, 16])
    Cim_d = P.dram("Cim", [64, G, 16])
    Y = P.dram("Y", [G, 128, NMC], kind="ExternalOutput")
    NK = 24
    kk = [7, 6, 5, 4, 3, 2, 1, 0] + [1, 2, 3, 4, 5, 6, 7, 8] + [-1, -2, -3, -4, -5, -6, -7, -8]
    I64 = P.sb([64, 64], F32, "I64")
    P.memset(I64, 1.0, eng="pool")
    P.aselect(I64, I64, [[-1, 64]], ALU.is_equal, 0.0, 0, 1)
    BM = P.sb([128, 8, 16], F32, "BM")
    P.memset(BM, 1.0, eng="pool")
    P.aselect(BM, BM, [[16, 8], [0, 16]], ALU.is_ge, 0.0, 15, -1)
    Tm = P.sb([128, G, 128], BF16, "Tm")
    W2 = P.sb([128, G, 128], BF16, "W2")
    Vre = P.sb([64, G, 128], BF16, "Vre")
    Vim = P.sb([64, G, 128], BF16, "Vim")
    MU1 = P.sb([64, 2, G], F32, "MU1")
    MUa = P.sb([64, G], F32, "MUa")
    MUb = P.sb([64, G], F32, "MUb")
    lre = P.sb([64, GB], F32, "lre")
    lim = P.sb([64, GB], F32, "lim")
    lst = P.sb([64, GB], F32, "lst")
    Bre = P.sb([64, GB, 16], F32, "sBre")
    Bim = P.sb([64, GB, 16], F32, "sBim")
    Cre = P.sb([64, GB, 16], F32, "sCre")
    Cim = P.sb([64, GB, 16], F32, "sCim")
    step = P.sb([64, GB], F32, "step")
    rho = P.sb([64, GB], F32, "rho")
    th = P.sb([64, GB], F32, "th")
    PH = P.sb([64, GB, NK], F32, "PH")
    RH = P.sb([64, GB, NK], F32, "RH")
    mag = P.sb([64, GB, NK], F32, "mag")
    TT = P.sb([64, GB, NK, 2], F32, "TT")
    TI = P.sb([64, GB, NK, 2], I32, "TI")
    TF = P.sb([64, GB, NK, 2], F32, "TF")
    LPre = P.sb([64, GB, NK], F32, "LPre")
    LPim = P.sb([64, GB, NK], F32, "LPim")
    w = [P.sb([64, GB], F32, "w%d" % i) for i in range(6)]
    BTre = P.sb([64, GB, 16], F32, "BTre")
    BTim = P.sb([64, GB, 16], F32, "BTim")
    t16a = P.sb([64, GB, 16], F32, "t16a")
    t16b = P.sb([64, GB, 16], F32, "t16b")
    Are = P.sb([64, GB, 8, 16], F32, "Are")
    Aim = P.sb([64, GB, 8, 16], F32, "Aim")
    Apre = P.sb([64, GB, 8, 16], F32, "Apre")
    Apim = P.sb([64, GB, 8, 16], F32, "Apim")
    Dre = P.sb([64, GB, 8, 16], F32, "Dre")
    nDim = P.sb([64, GB, 8, 16], F32, "nDim")
    t1 = P.sb([64, GB, 8, 16], F32, "t1")
    t2 = P.sb([64, GB, 8, 16], F32, "t2")
    ps_T = [P.ps([128, 128], F32, "psT%d" % i) for i in range(2)]
    ps_W = [P.ps([128, 128], F32, "psW%d" % i) for i in range(2)]

    def bc16(v):
        return V(v.ap.unsqueeze(2).to_broadcast([64, GB, 16]), v.keys)

    def cmul(out_re, out_im, lp0, xr, xi, neg_im=False):
        a_re = V(LPre[:, :, lp0:lp0 + 8].ap.unsqueeze(3).to_broadcast([64, GB, 8, 16]), LPre.keys)
        a_im = V(LPim[:, :, lp0:lp0 + 8].ap.unsqueeze(3).to_broadcast([64, GB, 8, 16]), LPim.keys)
        b_re = V(xr.ap.unsqueeze(2).to_broadcast([64, GB, 8, 16]), xr.keys)
        b_im = V(xi.ap.unsqueeze(2).to_broadcast([64, GB, 8, 16]), xi.keys)
        P.tt(t1, a_re, b_re, ALU.mult)
        P.tt(t2, a_im, b_im, ALU.mult)
        P.tt(out_re, t1, t2, ALU.subtract)
        P.tt(t1, a_re, b_im, ALU.mult)
        P.tt(t2, a_im, b_re, ALU.mult)
        P.tt(out_im, t1, t2, ALU.add)
        if neg_im:
            P.ts(out_im, out_im, -1.0, ALU.mult)

    fl = lambda v, gi: v[:, gi, :, :].re("p a b -> p (a b)")
    for g0 in range(0, G, GB):
        gsl = slice(g0, g0 + GB)
        for d_, s_ in ((lre, lamre_d), (lim, lamim_d), (lst, lstep_d)):
            P.dma(d_, s_[:, gsl])
        for d_, s_ in ((Bre, Bre_d), (Bim, Bim_d), (Cre, Cre_d), (Cim, Cim_d)):
            P.dma(d_, s_[:, gsl, :])
        P.act(step, lst, AF.Exp)
        P.tt(rho, lre, step, ALU.mult)
        P.tt(th, lim, step, ALU.mult)
        for i, kv in enumerate(kk):
            P.ts(PH[:, :, i], th, float(kv), ALU.mult)
            P.ts(RH[:, :, i], rho, float(kv), ALU.mult)
        P.act(mag, RH, AF.Exp)
        P.ts(TT[:, :, :, 0], PH, 1.0 / TWO_PI, ALU.mult, 0.5, ALU.add)
        P.ts(TT[:, :, :, 1], PH, 1.0 / TWO_PI, ALU.mult, 0.75, ALU.add)
        P.copy(TI, TT)
        P.copy(TF, TI)
        P.tt(TT, TT, TF, ALU.subtract)
        P.ts(TF, TT, 0.0, ALU.is_lt)
        P.tt(TT, TT, TF, ALU.add)
        P.ts(TT, TT, TWO_PI, ALU.mult, -math.pi, ALU.add)
        P.ts(TT, TT, 3.1415925, ALU.min, -3.1415925, ALU.max)
        P.act(TT, TT, AF.Sin)
        P.tt(LPre, mag, TT[:, :, :, 1], ALU.mult)
        P.tt(LPim, mag, TT[:, :, :, 0], ALU.mult)
        abre = LPre[:, :, 8]
        abim = LPim[:, :, 8]
        P.tt(w[0], lre, lre, ALU.mult)
        P.tt(w[1], lim, lim, ALU.mult)
        P.tt(w[0], w[0], w[1], ALU.add)
        P.recip(w[0], w[0])
        P.ts(w[1], abre, -1.0, ALU.add)
        P.tt(w[2], w[1], lre, ALU.mult)
        P.tt(w[3], abim, lim, ALU.mult)
        P.tt(w[2], w[2], w[3], ALU.add)
        P.tt(w[2], w[2], w[0], ALU.mult)
        P.tt(w[4], abim, lre, ALU.mult)
        P.tt(w[5], w[1], lim, ALU.mult)
        P.tt(w[4], w[4], w[5], ALU.subtract)
        P.tt(w[4], w[4], w[0], ALU.mult)
        cre, cim = w[2], w[4]
        P.tt(t16a, Bre, bc16(cre), ALU.mult)
        P.tt(t16b, Bim, bc16(cim), ALU.mult)
        P.tt(BTre, t16a, t16b, ALU.subtract)
        P.tt(t16a, Bim, bc16(cre), ALU.mult)
        P.tt(t16b, Bre, bc16(cim), ALU.mult)
        P.tt(BTim, t16a, t16b, ALU.add)
        P.copy(MU1[:, 0, gsl], LPre[:, :, 15])
        P.copy(MU1[:, 1, gsl], LPre[:, :, 15])
        P.copy(MUb[:, gsl], LPim[:, :, 15])
        P.ts(MUa[:, gsl], LPim[:, :, 15], -1.0, ALU.mult)
        cmul(Are, Aim, 0, BTre, BTim)
        cmul(Apre, Apim, 16, BTre, BTim)
        cmul(Dre, nDim, 8, Cre, Cim, neg_im=True)
        for gi in range(GB):
            g = g0 + gi
            pT = ps_T[g % 2]
            P.mm(pT, fl(Apre, gi), fl(Dre, gi), start=True, stop=False)
            P.mm(pT, fl(Apim, gi), fl(nDim, gi), start=False, stop=True)
            P.tt(Tm[:, g, :], pT, BM.re("p a b -> p (a b)"), ALU.mult)
            pW = ps_W[g % 2]
            P.mm(pW[:, 0:64], fl(Are, gi), I64)
            P.mm(pW[:, 64:128], fl(Aim, gi), I64)
            P.copy(W2[:, g, :], pW, eng="act")
        P.copy(Vre[:, gsl, :], Dre.re("p g a b -> p g (a b)"), eng="act")
        P.copy(Vim[:, gsl, :], nDim.re("p g a b -> p g (a b)"), eng="act")
    GQ = 4
    XE = P.sb([64, 2, G, NB + 1], F32, "XE")
    P.memset(XE[:, :, :, 0], 0.0)
    Ebf = P.sb([64, 2, G, NB], BF16, "Ebf")
    Ubf = P.sb([128, G, NB], BF16, "Ubf")
    ust = [P.sb([128, GQ, NB], F32, "ust%d" % i) for i in range(2)]
    yst = [P.sb([128, GQ, NB], F32, "yst%d" % i) for i in range(2)]
    r1 = P.sb([64, 2, G], F32, "r1")
    r2 = P.sb([64, 2, G], F32, "r2")
    ps_xr = [P.ps_alias("psT%d" % i, [64, GQ, NB]) for i in range(2)]
    ps_xi = [P.ps_alias("psW%d" % i, [64, GQ, NB]) for i in range(2)]
    ps_y = [P.ps([128, GQ, NB], F32, "psy%d" % i) for i in range(2)]
    qi = 0
    for blk in range(NBLK):
        csl = slice(blk * NB, (blk + 1) * NB)
        for g0 in range(0, G, GQ):
            b = qi % 2
            qi += 1
            P.dma(ust[b], Uin[g0:g0 + GQ, :, csl].rearrange("g p c -> p g c"))
            P.copy(Ubf[:, g0:g0 + GQ, :], ust[b], eng="pool")
            for gi in range(GQ):
                g = g0 + gi
                P.mm(ps_xr[b][:, gi, :], W2[:, g, 0:64], Ubf[:, g, :])
                P.mm(ps_xi[b][:, gi, :], W2[:, g, 64:128], Ubf[:, g, :])
            P.copy(XE[:, 0, g0:g0 + GQ, 1:NB + 1], ps_xr[b], eng="act")
            P.copy(XE[:, 1, g0:g0 + GQ, 1:NB + 1], ps_xi[b], eng="dve")
        for c in range(NB):
            prev = XE[:, :, :, c]
            cur = XE[:, :, :, c + 1]
            P.tt(r1, MU1, prev, ALU.mult)
            P.tt(r2[:, 0, :], MUa, XE[:, 1, :, c], ALU.mult)
            P.tt(r2[:, 1, :], MUb, XE[:, 0, :, c], ALU.mult)
            P.tt(r1, r1, r2, ALU.add)
            P.tt(cur, cur, r1, ALU.add)
        P.copy(Ebf, XE[:, :, :, 0:NB], eng="act")
        for g0 in range(0, G, GQ):
            b = qi % 2
            qi += 1
            for gi in range(GQ):
                g = g0 + gi
                py = ps_y[b][:, gi, :]
                P.mm(py, Tm[:, g, :], Ubf[:, g, :], start=True, stop=False)
                P.mm(py, Vre[:, g, :], Ebf[:, 0, g, :], start=False, stop=False)
                P.mm(py, Vim[:, g, :], Ebf[:, 1, g, :], start=False, stop=True)
            P.copy(yst[b], ps_y[b], eng="act")
            P.dma(Y[g0:g0 + GQ, :, csl].rearrange("g p c -> p g c"), yst[b], q="pool", is_output=True)
        if blk + 1 < NBLK:
            P.copy(XE[:, :, :, 0], XE[:, :, :, NB])
    return P


def emit_s5v2(P, G, GB, rev, ST, c0, Od, prm):
    lamre_d, lamim_d, lstep_d = prm["lamre"], prm["lamim"], prm["lstep"]
    Bre_d, Bim_d, Cre_d, Cim_d = prm["Bre"], prm["Bim"], prm["Cre"], prm["Cim"]
    NK = 24
    if not rev:
        kk = [7, 6, 5, 4, 3, 2, 1, 0] + [1, 2, 3, 4, 5, 6, 7, 8] + [-1, -2, -3, -4, -5, -6, -7, -8]
    else:
        kk = [0, 1, 2, 3, 4, 5, 6, 7] + [8, 7, 6, 5, 4, 3, 2, 1] + [-8, -7, -6, -5, -4, -3, -2, -1]
    i1 = 8 + kk[8:16].index(1)
    i8 = 8 + kk[8:16].index(8)
    I64 = P.sb([64, 64], F32, "I64")
    P.memset(I64, 1.0, eng="pool")
    P.aselect(I64, I64, [[-1, 64]], ALU.is_equal, 0.0, 0, 1)
    BM = P.sb([128, 8, 16], F32, "BM")
    P.memset(BM, 1.0, eng="pool")
    if not rev:
        P.aselect(BM, BM, [[16, 8], [0, 16]], ALU.is_ge, 0.0, 15, -1)
    else:
        P.aselect(BM, BM, [[-16, 8], [0, 16]], ALU.is_ge, 0.0, 0, 1)
    Tm = P.sb([128, G, 128], BF16, "Tm")
    W2 = P.sb([128, G, 128], BF16, "W2")
    Vre = P.sb([64, G, 128], BF16, "Vre")
    Vim = P.sb([64, G, 128], BF16, "Vim")
    MU1 = P.sb([64, 2, G], F32, "MU1")
    MUa = P.sb([64, G], F32, "MUa")
    MUb = P.sb([64, G], F32, "MUb")
    lre = P.sb([64, GB], F32, "lre")
    lim = P.sb([64, GB], F32, "lim")
    lst = P.sb([64, GB], F32, "lst")
    Bre = P.sb([64, GB, 16], F32, "sBre")
    Bim = P.sb([64, GB, 16], F32, "sBim")
    Cre = P.sb([64, GB, 16], F32, "sCre")
    Cim = P.sb([64, GB, 16], F32, "sCim")
    step = P.sb([64, GB], F32, "step")
    rho = P.sb([64, GB], F32, "rho")
    th = P.sb([64, GB], F32, "th")
    PH = P.sb([64, GB, NK], F32, "PH")
    RH = P.sb([64, GB, NK], F32, "RH")
    mag = P.sb([64, GB, NK], F32, "mag")
    TT = P.sb([64, GB, NK, 2], F32, "TT")
    TI = P.sb([64, GB, NK, 2], I32, "TI")
    TF = P.sb([64, GB, NK, 2], F32, "TF")
    LPre = P.sb([64, GB, NK], F32, "LPre")
    LPim = P.sb([64, GB, NK], F32, "LPim")
    w = [P.sb([64, GB], F32, "w%d" % i) for i in range(6)]
    BTre = P.sb([64, GB, 16], F32, "BTre")
    BTim = P.sb([64, GB, 16], F32, "BTim")
    t16a = P.sb([64, GB, 16], F32, "t16a")
    t16b = P.sb([64, GB, 16], F32, "t16b")
    Are = P.sb([64, GB, 8, 16], F32, "Are")
    Aim = P.sb([64, GB, 8, 16], F32, "Aim")
    Apre = P.sb([64, GB, 8, 16], F32, "Apre")
    Apim = P.sb([64, GB, 8, 16], F32, "Apim")
    Dre = P.sb([64, GB, 8, 16], F32, "Dre")
    nDim = P.sb([64, GB, 8, 16], F32, "nDim")
    t1 = P.sb([64, GB, 8, 16], F32, "t1")
    t2 = P.sb([64, GB, 8, 16], F32, "t2")
    ps_T = [P.ps([128, 128], F32, "psT%d" % i) for i in range(2)]
    ps_W = [P.ps([128, 128], F32, "psW%d" % i) for i in range(2)]

    def bc16(v):
        return V(v.ap.unsqueeze(2).to_broadcast([64, GB, 16]), v.keys)

    def cmul(out_re, out_im, lp0, xr, xi, neg_im=False):
        a_re = V(LPre[:, :, lp0:lp0 + 8].ap.unsqueeze(3).to_broadcast([64, GB, 8, 16]), LPre.keys)
        a_im = V(LPim[:, :, lp0:lp0 + 8].ap.unsqueeze(3).to_broadcast([64, GB, 8, 16]), LPim.keys)
        b_re = V(xr.ap.unsqueeze(2).to_broadcast([64, GB, 8, 16]), xr.keys)
        b_im = V(xi.ap.unsqueeze(2).to_broadcast([64, GB, 8, 16]), xi.keys)
        P.tt(t1, a_re, b_re, ALU.mult)
        P.tt(t2, a_im, b_im, ALU.mult)
        P.tt(out_re, t1, t2, ALU.subtract)
        P.tt(t1, a_re, b_im, ALU.mult)
        P.tt(t2, a_im, b_re, ALU.mult)
        P.tt(out_im, t1, t2, ALU.add)
        if neg_im:
            P.ts(out_im, out_im, -1.0, ALU.mult)

    fl = lambda v, gi: v[:, gi, :, :].re("p a b -> p (a b)")
    for g0 in range(0, G, GB):
        gsl = slice(g0, g0 + GB)
        for d_, s_ in ((lre, lamre_d), (lim, lamim_d), (lst, lstep_d)):
            P.dma(d_, s_[:, gsl])
        for d_, s_ in ((Bre, Bre_d), (Bim, Bim_d), (Cre, Cre_d), (Cim, Cim_d)):
            P.dma(d_, s_[:, gsl, :])
        P.act(step, lst, AF.Exp)
        P.tt(rho, lre, step, ALU.mult)
        P.tt(th, lim, step, ALU.mult)
        for i, kv in enumerate(kk):
            P.ts(PH[:, :, i], th, float(kv), ALU.mult)
            P.ts(RH[:, :, i], rho, float(kv), ALU.mult)
        P.act(mag, RH, AF.Exp)
        P.ts(TT[:, :, :, 0], PH, 1.0 / TWO_PI, ALU.mult, 0.5, ALU.add)
        P.ts(TT[:, :, :, 1], PH, 1.0 / TWO_PI, ALU.mult, 0.75, ALU.add)
        P.copy(TI, TT)
        P.copy(TF, TI)
        P.tt(TT, TT, TF, ALU.subtract)
        P.ts(TF, TT, 0.0, ALU.is_lt)
        P.tt(TT, TT, TF, ALU.add)
        P.ts(TT, TT, TWO_PI, ALU.mult, -math.pi, ALU.add)
        P.ts(TT, TT, 3.1415925, ALU.min, -3.1415925, ALU.max)
        P.act(TT, TT, AF.Sin)
        P.tt(LPre, mag, TT[:, :, :, 1], ALU.mult)
        P.tt(LPim, mag, TT[:, :, :, 0], ALU.mult)
        abre = LPre[:, :, i1]
        abim = LPim[:, :, i1]
        P.tt(w[0], lre, lre, ALU.mult)
        P.tt(w[1], lim, lim, ALU.mult)
        P.tt(w[0], w[0], w[1], ALU.add)
        P.recip(w[0], w[0])
        P.ts(w[1], abre, -1.0, ALU.add)
        P.tt(w[2], w[1], lre, ALU.mult)
        P.tt(w[3], abim, lim, ALU.mult)
        P.tt(w[2], w[2], w[3], ALU.add)
        P.tt(w[2], w[2], w[0], ALU.mult)
        P.tt(w[4], abim, lre, ALU.mult)
        P.tt(w[5], w[1], lim, ALU.mult)
        P.tt(w[4], w[4], w[5], ALU.subtract)
        P.tt(w[4], w[4], w[0], ALU.mult)
        cre, cim = w[2], w[4]
        P.tt(t16a, Bre, bc16(cre), ALU.mult)
        P.tt(t16b, Bim, bc16(cim), ALU.mult)
        P.tt(BTre, t16a, t16b, ALU.subtract)
        P.tt(t16a, Bim, bc16(cre), ALU.mult)
        P.tt(t16b, Bre, bc16(cim), ALU.mult)
        P.tt(BTim, t16a, t16b, ALU.add)
        P.copy(MU1[:, 0, gsl], LPre[:, :, i8])
        P.copy(MU1[:, 1, gsl], LPre[:, :, i8])
        P.copy(MUb[:, gsl], LPim[:, :, i8])
        P.ts(MUa[:, gsl], LPim[:, :, i8], -1.0, ALU.mult)
        cmul(Are, Aim, 0, BTre, BTim)
        cmul(Apre, Apim, 16, BTre, BTim)
        cmul(Dre, nDim, 8, Cre, Cim, neg_im=True)
        for gi in range(GB):
            g = g0 + gi
            pT = ps_T[g % 2]
            P.mm(pT, fl(Apre, gi), fl(Dre, gi), start=True, stop=False)
            P.mm(pT, fl(Apim, gi), fl(nDim, gi), start=False, stop=True)
            P.tt(Tm[:, g, :], pT, BM.re("p a b -> p (a b)"), ALU.mult)
            pW = ps_W[g % 2]
            P.mm(pW[:, 0:64], fl(Are, gi), I64)
            P.mm(pW[:, 64:128], fl(Aim, gi), I64)
            P.copy(W2[:, g, :], pW, eng="act")
        P.copy(Vre[:, gsl, :], Dre.re("p g a b -> p g (a b)"), eng="act")
        P.copy(Vim[:, gsl, :], nDim.re("p g a b -> p g (a b)"), eng="act")
    NBM = 128
    Iid = P.sb([128, 128], F32, "s5I")
    P.memset(Iid, 1.0, eng="pool")
    P.aselect(Iid, Iid, [[-1, 128]], ALU.is_equal, 0.0, 0, 1)
    XE = P.sb([64, 2, G, NBM + 1], F32, "XE")
    Ebf = P.sb([64, 2, G, NBM], BF16, "Ebf")
    Ubf = P.sb([128, G, NBM], BF16, "Ubf")
    utile = P.sb([128, 8, G * 16], F32, "utile")
    ybuf = P.sb([128, 8, G * 16], F32, "ybuf")
    ugs = [P.sb([128, 8, 16], F32, "ugs%d" % i) for i in range(2)]
    r1 = P.sb([64, 2, G], F32, "r1")
    r2 = P.sb([64, 2, G], F32, "r2")
    ps_u = [P.ps_alias("psT%d" % i, [128, NBM]) for i in range(2)]
    ps_xr = P.ps([64, 4, NBM], F32, "psxr")
    ps_xi = P.ps([64, 4, NBM], F32, "psxi")
    ps_y = [P.ps([128, 4, 128], F32, "psy%d" % i) for i in range(2)]
    GC = G * 16
    lat_blocks = [0, 1, 2, 3] if not rev else [3, 2, 1, 0]
    blocks = [("ctx", 0)] + [("lat", b) for b in lat_blocks]
    carry_col = 0 if not rev else NBM
    first = True
    for kind, bi in blocks:
        nb = 32 if kind == "ctx" else NBM
        if kind == "ctx":
            for rho in range(2):
                P.dma(utile[rho * 16:(rho + 1) * 16, :, :],
                      ST[rho * TOKR + LATR:rho * TOKR + LATR + 128, c0:c0 + GC].rearrange("(c j) ch -> c j ch", j=8))
        else:
            rho, hb = bi // 2, bi % 2
            r0 = rho * TOKR + hb * 1024
            P.dma(utile[:, :, :], ST[r0:r0 + 1024, c0:c0 + GC].rearrange("(c j) ch -> c j ch", j=8))
        xoff = 1 if not rev else 0
        ccol = 0 if not rev else nb
        if first:
            P.memset(XE[:, :, :, ccol], 0.0)
            first = False
        for g0 in range(0, G, 4):
            for gi in range(4):
                g = g0 + gi
                pu = ps_u[g % 2]
                ug = ugs[g % 2]
                P.copy(ug[0:nb, :, :], utile[0:nb, :, g * 16:(g + 1) * 16], eng="pool")
                P.mm(pu[:, 0:nb], ug[0:nb, :, :].rearrange("c j p -> c (j p)"), Iid[0:nb, 0:nb])
                P.copy(Ubf[:, g, 0:nb], pu[:, 0:nb], eng="act" if g % 2 else "dve")
                P.mm(ps_xr[:, gi, 0:nb], W2[:, g, 0:64], Ubf[:, g, 0:nb])
                P.mm(ps_xi[:, gi, 0:nb], W2[:, g, 64:128], Ubf[:, g, 0:nb])
            P.copy(XE[:, 0, g0:g0 + 4, xoff:xoff + nb], ps_xr[:, :, 0:nb], eng="act")
            P.copy(XE[:, 1, g0:g0 + 4, xoff:xoff + nb], ps_xi[:, :, 0:nb], eng="dve")
        order = range(nb) if not rev else range(nb - 1, -1, -1)
        for c in order:
            pc = c if not rev else c + 1
            cc_ = c + xoff
            P.tt(r1, MU1, XE[:, :, :, pc], ALU.mult)
            P.tt(r2[:, 0, :], MUa, XE[:, 1, :, pc], ALU.mult)
            P.tt(r2[:, 1, :], MUb, XE[:, 0, :, pc], ALU.mult)
            P.tt(r1, r1, r2, ALU.add)
            P.tt(XE[:, :, :, cc_], XE[:, :, :, cc_], r1, ALU.add)
        sh = 0 if not rev else 1
        P.copy(Ebf[:, :, :, 0:nb], XE[:, :, :, sh:sh + nb], eng="act")
        for g0 in range(0, G, 4):
            py = ps_y[(g0 // 4) % 2]
            for gi in range(4):
                g = g0 + gi
                P.mm(py[0:nb, gi, :], Ubf[:, g, 0:nb], Tm[:, g, :], start=True, stop=False)
                P.mm(py[0:nb, gi, :], Ebf[:, 0, g, 0:nb], Vre[:, g, :], start=False, stop=False)
                P.mm(py[0:nb, gi, :], Ebf[:, 1, g, 0:nb], Vim[:, g, :], start=False, stop=True)
            P.copy(ybuf[0:nb, :, g0 * 16:(g0 + 4) * 16].rearrange("c j (g p) -> c g j p", p=16),
                   py[0:nb, :, :].rearrange("c g (j p) -> c g j p", p=16), eng="act" if (g0 // 4) % 2 else "dve")
        if kind == "ctx":
            for rho in range(2):
                P.dma(Od[rho * TOKR + LATR:rho * TOKR + LATR + 128, :].rearrange("(c j) ch -> c j ch", j=8),
                      ybuf[rho * 16:(rho + 1) * 16, :, :], q="pool")
        else:
            P.dma(Od[r0:r0 + 1024, :].rearrange("(c j) ch -> c j ch", j=8), ybuf[:, :, :], q="pool")
        last_col = nb if not rev else 0
        nxt_nb = NBM
        nxt_ccol = 0 if not rev else nxt_nb
        if (kind, bi) != blocks[-1]:
            if last_col != nxt_ccol:
                P.copy(XE[:, :, :, nxt_ccol], XE[:, :, :, last_col])
    return P


def build_s5(G, GB, NB, NBLK, name="s5"):
    P = Prog(name)
    emit_s5(P, G, GB, NB, NBLK)
    return P


EPS = 1e-6


class TokCtx:
    def __init__(self, P, D, DFF, TMAX, nwst=3, nwbf=4, look=3):
        self.P, self.D, self.DFF, self.T = P, D, DFF, TMAX
        self.KC = D // 128
        self.JC = DFF // 128
        KC, JC, T = self.KC, self.JC, TMAX
        self.ones = P.sb([128, 128], F32, "ones")
        P.memset(self.ones, 1.0)
        self.ones_bf = P.sb([128, 128], BF16, "ones_bf")
        P.memset(self.ones_bf, 1.0)
        self.XB = P.sb([128, KC, T], F32, "XB")
        self.h = P.sb([128, KC, T], BF16, "h")
        self.hid = P.sb([128, JC, T], BF16, "hid")
        self.sq = [P.sb([128, T], F32, "sq%d" % i) for i in range(2)]
        self.rstd = P.sb([128, T], F32, "rstd")
        self.tmp = [P.sb([128, T], F32, "tmp%d" % i) for i in range(4)]
        self.sg = [P.sb([128, T], F32, "sg%d" % i) for i in range(2)]
        self.xr = [P.sb([128, T], F32, "xr%d" % i) for i in range(4)]
        self.WSZ = KC * 128
        self.wst = [P.sb([128, self.WSZ], F32, "wst%d" % i) for i in range(nwst)]
        self.wbf = [P.sb([128, self.WSZ], BF16, "wbf%d" % i) for i in range(nwbf)]
        self.look = look
        self.wi = 0
        self.ps_g = [P.ps([128, T], F32, "psg%d" % i) for i in range(2)]
        self.ps_u = [P.ps([128, T], F32, "psu%d" % i) for i in range(2)]
        self.ps_y = [P.ps([128, T], F32, "psy%d" % i) for i in range(2)]
        self.ps_s = P.ps([128, T], F32, "pss")
        self.ps_m = P.ps([128, 512], F32, "psm")
        self.cast_i = 0
        self.eps_col = P.sb([128, 1], F32, "epsc")
        P.memset(self.eps_col, EPS)

    def stream(self, items, look=None):
        loaded = {}
        look = look or self.look

        def get(i):
            for k in range(i, min(i + look, len(items))):
                if k not in loaded:
                    loaded[k] = self.load_w(*items[k])
            return loaded[i]
        return get

    def load_w(self, src_ap, nrow_chunks, ncols):
        P = self.P
        i = self.wi % len(self.wst)
        ib = self.wi % len(self.wbf)
        self.wi += 1
        n = nrow_chunks * ncols
        assert n <= self.WSZ
        st = self.wst[i][:, 0:n].re("p (a b) -> p a b", b=ncols)
        bf = self.wbf[ib][:, 0:n].re("p (a b) -> p a b", b=ncols)
        P.dma(st, src_ap, q="sp")
        self.cast_i += 1
        if self.cast_i % 3 == 0:
            P.copy(bf, st, eng="act")
        else:
            P.copy(bf, st, eng="dve")
        return bf


def rms_rstd(C, chunks, T, Dtot, out_rstd, ps=None, big=None):
    P = C.P
    ps = ps or C.ps_s
    n = len(chunks)
    if big is not None and n >= 4 and C.JC >= n:
        sqb = C.hid[:, 0:n, 0:T]
        h1 = n // 2
        P.act(sqb[:, 0:h1, :], big[:, 0:h1, :], AF.Square)
        P.tt(sqb[:, h1:n, :], big[:, h1:n, :], big[:, h1:n, :], ALU.mult)
        for i in range(n):
            P.mm(ps[:, 0:T], C.ones_bf, sqb[:, i, :], start=(i == 0), stop=(i == n - 1))
    else:
        for i, src in enumerate(chunks):
            sq = C.sq[i % 2][:, 0:T]
            P.act(sq, src, AF.Square)
            P.mm(ps[:, 0:T], C.ones, sq, start=(i == 0), stop=(i == n - 1))
    P.act(out_rstd[:, 0:T], ps[:, 0:T], AF.Sqrt, scale=1.0 / Dtot, bias=C.eps_col)
    P.recip(out_rstd[:, 0:T], out_rstd[:, 0:T])


def sublayer_in(C, T, Ain, shift, col):
    P = C.P
    KC = C.KC
    rms_rstd(C, [C.XB[:, kc, 0:T] for kc in range(KC)], T, C.D, C.rstd, big=C.XB[:, :, 0:T])
    for kc in range(KC):
        t = C.tmp[kc % 4][:, 0:T]
        P.stt(t, C.XB[:, kc, 0:T], Ain[:, kc, col:col + 1], C.rstd[:, 0:T], ALU.mult, ALU.mult)
        P.act(C.h[:, kc, 0:T], t, AF.Identity, bias=shift[:, kc, col:col + 1])


def sublayer_out(C, T, Gout, col, resid):
    P = C.P
    KC = C.KC
    rms_rstd(C, [C.XB[:, kc, 0:T] for kc in range(KC)], T, C.D, C.rstd, big=C.XB[:, :, 0:T])
    for kc in range(KC):
        xr = C.xr[kc % 4][:, 0:T]
        P.dma(xr, resid(kc), q="sp")
        t = C.tmp[kc % 4][:, 0:T]
        P.stt(t, C.XB[:, kc, 0:T], Gout[:, kc, col:col + 1], C.rstd[:, 0:T], ALU.mult, ALU.mult)
        P.tt(C.XB[:, kc, 0:T], xr, t, ALU.add, eng="pool")


def split_parts(n, maxp):
    k = (n + maxp - 1) // maxp
    base, rem = divmod(n, k)
    out, s = [], 0
    for i in range(k):
        sz = base + (1 if i < rem else 0)
        out.append((s, s + sz))
        s += sz
    return out


def ffn_core(C, T, wg, wu, wd, tiled=False):
    P = C.P
    KC, JC = C.KC, C.JC
    parts = split_parts(JC, C.WSZ // 128)
    items = []
    if tiled:
        for j in range(JC):
            items.append((wg[j], KC, 128))
            items.append((wu[j], KC, 128))
        for m in range(KC):
            for (j0, j1) in parts:
                items.append((wd[m][:, j0:j1, :], j1 - j0, 128))
    else:
        wg_v = wg.rearrange("(kc p) n -> p kc n", p=128)
        wu_v = wu.rearrange("(kc p) n -> p kc n", p=128)
        wd_v = wd.rearrange("(jc p) n -> p jc n", p=128)
        for j in range(JC):
            items.append((wg_v[:, :, j * 128:(j + 1) * 128], KC, 128))
            items.append((wu_v[:, :, j * 128:(j + 1) * 128], KC, 128))
        for m in range(KC):
            for (j0, j1) in parts:
                items.append((wd_v[:, j0:j1, m * 128:(m + 1) * 128], j1 - j0, 128))
    get = C.stream(items)
    for j in range(JC):
        wgb = get(2 * j)
        wub = get(2 * j + 1)
        pg = C.ps_g[j % 2][:, 0:T]
        pu = C.ps_u[j % 2][:, 0:T]
        for kc in range(KC):
            P.mm(pg, wgb[:, kc, :], C.h[:, kc, 0:T], start=(kc == 0), stop=(kc == KC - 1))
        for kc in range(KC):
            P.mm(pu, wub[:, kc, :], C.h[:, kc, 0:T], start=(kc == 0), stop=(kc == KC - 1))
        sg = C.sg[j % 2][:, 0:T]
        P.act(sg, pg, AF.Silu)
        P.tt(C.hid[:, j, 0:T], sg, pu, ALU.mult)
    npart = len(parts)
    for m in range(KC):
        py = C.ps_y[m % 2][:, 0:T]
        for hi, (j0, j1) in enumerate(parts):
            wdb = get(2 * JC + npart * m + hi)
            for j in range(j0, j1):
                P.mm(py, wdb[:, j - j0, :], C.hid[:, j, 0:T], start=(j == 0), stop=(j == JC - 1))
        P.copy(C.XB[:, m, 0:T], py, eng="act")


def proj(C, T, w, nk, ncol, rhs, sink, bias_fn=None, tiled=False):
    P = C.P
    nch = (ncol + 127) // 128
    if tiled:
        items = [(w[j], nk, 128) for j in range(nch)]
    else:
        w_v = w.rearrange("(kc p) n -> p kc n", p=128)
        items = [(w_v[:, :, j * 128:min(ncol, (j + 1) * 128)], nk, min(128, ncol - j * 128)) for j in range(nch)]
    get = C.stream(items)
    for j in range(nch):
        cw = min(128, ncol - j * 128)
        wb = get(j)
        pg = C.ps_g[j % 2][0:cw, 0:T]
        for kc in range(nk):
            P.mm(pg, wb[:, kc, :], rhs(kc), start=(kc == 0), stop=(kc == nk - 1))
        sink(j, cw, pg)


def in_proj(C, T, w_in, ncol, sT_out, t0):
    P = C.P

    def sink(j, cw, pg):
        so = C.sg[j % 2][0:cw, 0:T]
        P.copy(so, pg, eng="act")
        P.dma(sT_out[j * 128:j * 128 + cw, t0:t0 + T], so, q="pool", is_output=True)
    proj(C, T, w_in, C.KC, ncol, lambda kc: C.h[:, kc, 0:T], sink)


def load_consts(C, norm_pre, norm_post):
    P = C.P
    C.npre = P.sb([128, 3, C.KC], F32, "npre")
    C.npost = P.sb([128, 3, C.KC], F32, "npost")
    with P.nc.allow_non_contiguous_dma(reason="small const loads"):
        P.dma(C.npre, norm_pre.rearrange("s (kc p) -> p s kc", p=128))
        P.dma(C.npost, norm_post.rearrange("s (kc p) -> p s kc", p=128))


def compute_mod(C, cT, w_mod, b_mod, mod_out=None, tiled=False):
    P = C.P
    KC = C.KC
    NJ = 9 * KC
    C.mod = P.sb([128, NJ, 2], F32, "mod")
    cs = P.sb([128, KC, 2], F32, "csilu")
    bm = P.sb([128, NJ], F32, "bmod")
    with P.nc.allow_non_contiguous_dma(reason="small const loads"):
        P.dma(cs, cT.rearrange("(kc p) c -> p kc c", p=128))
        P.dma(bm, b_mod.rearrange("(j p) -> p j", p=128))
    P.act(cs, cs, AF.Silu)
    w_v = None if tiled else w_mod.rearrange("(kc p) n -> p kc n", p=128)
    assert 2 * NJ <= 512
    psm = C.ps_m[:, 0:2 * NJ].re("p (j c) -> p j c", c=2)
    for j in range(NJ):
        i = C.wi % len(C.wst)
        C.wi += 1
        st = C.wst[i][:, 0:KC * 128].re("p (a b) -> p a b", b=128)
        P.dma(st, w_mod[j] if tiled else w_v[:, :, j * 128:(j + 1) * 128], q="sp")
        for kc in range(KC):
            P.mm(psm[:, j, :], st[:, kc, :], cs[:, kc, :], start=(kc == 0), stop=(kc == KC - 1))
    for c in range(2):
        P.tt(C.mod[:, :, c], psm[:, :, c], bm, ALU.add)
    if mod_out is not None:
        P.dma(mod_out, C.mod, q="pool", is_output=True)


def load_mod(C, modi):
    P = C.P
    C.mod = P.sb([128, 9 * C.KC, 2], F32, "mod")
    P.dma(C.mod, modi)


def derive_sub(C, s, weight):
    P = C.P
    KC = C.KC
    Ain = P.sb([128, KC, 2], F32, "Ain%d" % s)
    Gout = P.sb([128, KC, 2], F32, "Gout%d" % s)
    shift = C.mod[:, (3 * s) * KC:(3 * s + 1) * KC, :]
    scale = C.mod[:, (3 * s + 1) * KC:(3 * s + 2) * KC, :]
    gate = C.mod[:, (3 * s + 2) * KC:(3 * s + 3) * KC, :]
    for c in range(2):
        P.stt(Ain[:, :, c], scale[:, :, c], 1.0, C.npre[:, s, :], ALU.add, ALU.mult)
        P.stt(Gout[:, :, c], gate[:, :, c], float(weight), C.npost[:, s, :], ALU.mult, ALU.mult)
    return Ain, shift, Gout


def build_stageA(D, DFF, NCOL, tiles, Ttot, name="stageA"):
    P = Prog(name)
    xT = P.dram("xT", [D, Ttot])
    cT = P.dram("cT", [D, 2])
    w_mod = P.dram("w_mod", [D, 9 * D])
    b_mod = P.dram("b_mod", [9 * D])
    norm_pre = P.dram("norm_pre", [3, D])
    norm_post = P.dram("norm_post", [3, D])
    wg = P.dram("wg", [D, DFF])
    wu = P.dram("wu", [D, DFF])
    wd = P.dram("wd", [DFF, D])
    w_in = P.dram("w_in", [D, NCOL])
    x1T = P.dram("x1T", [D, Ttot], kind="ExternalOutput")
    sT = P.dram("sT", [NCOL, Ttot], kind="ExternalOutput")
    KC = D // 128
    modo = P.dram("modo", [128, 9 * KC, 2], kind="ExternalOutput")
    TMAX = max(t[2] for t in tiles)
    C = TokCtx(P, D, DFF, TMAX)
    load_consts(C, norm_pre, norm_post)
    compute_mod(C, cT, w_mod, b_mod, modo)
    A0, sh0, G0 = derive_sub(C, 0, 0.5)
    A1, sh1, G1 = derive_sub(C, 1, 1.0)
    xv = xT.rearrange("(kc p) t -> p kc t", p=128)
    x1v = x1T.rearrange("(kc p) t -> p kc t", p=128)
    for (col, t0, T) in tiles:
        P.dma(C.XB[:, :, 0:T], xv[:, :, t0:t0 + T], q="sp")
        sublayer_in(C, T, A0, sh0, col)
        ffn_core(C, T, wg, wu, wd)
        sublayer_out(C, T, G0, col, lambda kc: xv[:, kc, t0:t0 + T])
        P.dma(x1v[:, :, t0:t0 + T], C.XB[:, :, 0:T], q="pool", is_output=True)
        sublayer_in(C, T, A1, sh1, col)
        in_proj(C, T, w_in, NCOL, sT, t0)
    return P


def head_norm_chunks(C, T, osum, gvec, gcols, gact, center, dst_chunks, HD):
    P = C.P
    n = len(dst_chunks)
    if center:
        for i in range(n):
            P.mm(C.ps_m[:, 0:T], C.ones, osum[:, i, 0:T], start=(i == 0), stop=(i == n - 1))
        for i in range(n):
            sq = C.sq[i % 2][:, 0:T]
            P.act(sq, osum[:, i, 0:T], AF.Square)
            P.mm(C.ps_s[:, 0:T], C.ones, sq, start=(i == 0), stop=(i == n - 1))
        mean = C.xr[0][:, 0:T]
        var = C.xr[1][:, 0:T]
        P.ts(mean, C.ps_m[:, 0:T], 1.0 / HD, ALU.mult)
        P.tt(var, mean, mean, ALU.mult)
        P.stt(var, C.ps_s[:, 0:T], 1.0 / HD, var, ALU.mult, ALU.subtract)
        P.act(C.rstd[:, 0:T], var, AF.Sqrt, bias=C.eps_col)
        P.recip(C.rstd[:, 0:T], C.rstd[:, 0:T])
        for i in range(n):
            t = C.tmp[i % 2][:, 0:T]
            P.tt(t, osum[:, i, 0:T], mean, ALU.subtract)
            P.tt(t, t, C.rstd[:, 0:T], ALU.mult)
            P.stt(dst_chunks[i], t, gvec[:, gcols[i]:gcols[i] + 1], gact[i], ALU.mult, ALU.mult)
    else:
        rms_rstd(C, [osum[:, i, 0:T] for i in range(n)], T, HD, C.rstd)
        for i in range(n):
            t = C.tmp[i % 2][:, 0:T]
            P.tt(t, osum[:, i, 0:T], C.rstd[:, 0:T], ALU.mult)
            P.stt(dst_chunks[i], t, gvec[:, gcols[i]:gcols[i] + 1], gact[i], ALU.mult, ALU.mult)


def build_stageC(D, DFF, tiles, Ttot, parity, HD=256, name="stageC"):
    P = Prog(name)
    KC = D // 128
    H2 = D // 2
    KH = KC // 2
    x1T = P.dram("x1T", [D, Ttot])
    modi = P.dram("modi", [128, 9 * KC, 2])
    norm_pre = P.dram("norm_pre", [3, D])
    norm_post = P.dram("norm_post", [3, D])
    w_out = P.dram("w_out", [D, D])
    wg = P.dram("wg", [D, DFF])
    wu = P.dram("wu", [D, DFF])
    wd = P.dram("wd", [DFF, D])
    oAf = P.dram("oAf", [H2, Ttot])
    oAb = P.dram("oAb", [H2, Ttot])
    oBf = P.dram("oBf", [H2, Ttot])
    oBb = P.dram("oBb", [H2, Ttot])
    gB = P.dram("gB", [H2, Ttot])
    nB = P.dram("nB", [H2])
    if parity == 0:
        gA = P.dram("gA", [H2, Ttot])
        nA = P.dram("nA", [H2])
    else:
        uT = P.dram("uT", [H2, Ttot])
        s5d = P.dram("s5d", [H2])
        w_glu = P.dram("w_glu", [H2, H2])
        b_glu = P.dram("b_glu", [H2])
    x3T = P.dram("x3T", [D, Ttot], kind="ExternalOutput")
    x2s = V(P.dram("x2s", [D, Ttot], kind="Internal"), "x2s")
    TMAX = max(t[2] for t in tiles)
    C = TokCtx(P, D, DFF, TMAX)
    load_consts(C, norm_pre, norm_post)
    load_mod(C, modi)
    A1, sh1, G1 = derive_sub(C, 1, 1.0)
    A2, sh2, G2 = derive_sub(C, 2, 0.5)
    gv = P.sb([128, 4, KH], F32, "gv")
    with P.nc.allow_non_contiguous_dma(reason="small const loads"):
        P.dma(gv[:, 1, :], nB.rearrange("(kc p) -> p kc", p=128))
        if parity == 0:
            P.dma(gv[:, 0, :], nA.rearrange("(kc p) -> p kc", p=128))
        else:
            P.dma(gv[:, 0, :], s5d.rearrange("(kc p) -> p kc", p=128))
            P.dma(gv[:, 2, :], b_glu.rearrange("(kc p) -> p kc", p=128))
    NH = HD // 128
    ld = [[P.sb([128, TMAX], F32, "ld%d_%d" % (a, b)) for b in range(2)] for a in range(3)]
    osum = P.sb([128, NH, TMAX], F32, "osum")
    gact = [P.sb([128, TMAX], F32, "gact%d" % i) for i in range(NH)]
    x1v = x1T.rearrange("(kc p) t -> p kc t", p=128)
    x3v = x3T.rearrange("(kc p) t -> p kc t", p=128)
    x2v = x2s.re("(kc p) t -> p kc t", p=128)
    li = [0]

    def normed_half(T, t0, of_, ob_, g_, gvrow, func, center, kc0):
        for hh in range(H2 // HD):
            for i in range(NH):
                r0 = (hh * NH + i) * 128
                b = li[0] % 2
                li[0] += 1
                P.dma(ld[0][b][:, 0:T], of_[r0:r0 + 128, t0:t0 + T])
                P.dma(ld[1][b][:, 0:T], ob_[r0:r0 + 128, t0:t0 + T])
                P.dma(ld[2][b][:, 0:T], g_[r0:r0 + 128, t0:t0 + T])
                P.tt(osum[:, i, 0:T], ld[0][b][:, 0:T], ld[1][b][:, 0:T], ALU.add, eng="pool")
                P.act(gact[i][:, 0:T], ld[2][b][:, 0:T], func)
            cols = [hh * NH + i for i in range(NH)]
            head_norm_chunks(C, T, osum, gv[:, gvrow, :], cols, [g[:, 0:T] for g in gact], center,
                             [C.h[:, kc0 + hh * NH + i, 0:T] for i in range(NH)], HD)

    for (col, t0, T) in tiles:
        if parity == 0:
            normed_half(T, t0, oAf, oAb, gA, 0, AF.Silu, False, 0)
            normed_half(T, t0, oBf, oBb, gB, 1, AF.Sigmoid, True, KH)
        else:
            for kc in range(KH):
                b = li[0] % 2
                li[0] += 1
                r0 = kc * 128
                P.dma(ld[0][b][:, 0:T], oAf[r0:r0 + 128, t0:t0 + T])
                P.dma(ld[1][b][:, 0:T], oAb[r0:r0 + 128, t0:t0 + T])
                P.dma(ld[2][b][:, 0:T], uT[r0:r0 + 128, t0:t0 + T])
                yv = C.tmp[0][:, 0:T]
                t2 = C.tmp[1][:, 0:T]
                P.tt(yv, ld[0][b][:, 0:T], ld[1][b][:, 0:T], ALU.add, eng="pool")
                P.stt(yv, ld[2][b][:, 0:T], gv[:, 0, kc:kc + 1], yv, ALU.mult, ALU.add)
                P.tt(t2, yv, yv, ALU.mult)
                P.ts(t2, t2, 0.044715, ALU.mult, 1.0, ALU.add)
                P.tt(t2, t2, yv, ALU.mult)
                P.act(t2, t2, AF.Sigmoid, scale=1.5957691216057308)
                P.tt(C.XB[:, kc, 0:T], t2, yv, ALU.mult)
                P.copy(C.h[:, KH + kc, 0:T], C.XB[:, kc, 0:T], eng="act")

            def sink(j, cw, pg):
                sgm = C.sg[j % 2][:, 0:T]
                P.act(sgm, pg, AF.Sigmoid, bias=gv[:, 2, j:j + 1])
                P.tt(C.h[:, j, 0:T], C.XB[:, j, 0:T], sgm, ALU.mult)
            proj(C, T, w_glu, KH, H2, lambda kc: C.h[:, KH + kc, 0:T], sink)
            normed_half(T, t0, oBf, oBb, gB, 1, AF.Silu, True, KH)

        def sink_y(j, cw, pg):
            P.copy(C.XB[:, j, 0:T], pg, eng="act")
        proj(C, T, w_out, KC, D, lambda kc: C.h[:, kc, 0:T], sink_y)
        sublayer_out(C, T, G1, col, lambda kc: x1v[:, kc, t0:t0 + T])
        P.dma(x2v[:, :, t0:t0 + T], C.XB[:, :, 0:T], q="pool")
        sublayer_in(C, T, A2, sh2, col)
        ffn_core(C, T, wg, wu, wd)
        sublayer_out(C, T, G2, col, lambda kc: x2v[:, kc, t0:t0 + T])
        P.dma(x3v[:, :, t0:t0 + T], C.XB[:, :, 0:T], q="pool", is_output=True)
    return P


G2 = [[0, 1], [2, 3], [4, 5], [6, 7]]
TOKR = 2176
LATR = 2048
NSQ = 4352


def nat_block(c):
    if c < 2:
        return c * TOKR + LATR
    i = c - 2
    return (i // 16) * TOKR + (i % 16) * 128


def rev_chunk(c):
    return 1 - c if c < 2 else 2 + (33 - c)


class Relay:
    def __init__(self, P, maxc):
        self.P = P
        self.I = P.sb([128, 128], F32, "rI")
        self.J = P.sb([128, 128], F32, "rJ")
        P.memset(self.I, 1.0, eng="pool")
        P.memset(self.J, 1.0, eng="pool")
        P.aselect(self.I, self.I, [[-1, 128]], ALU.is_equal, 0.0, 0, 1)
        P.aselect(self.J, self.J, [[1, 128]], ALU.is_equal, 0.0, -127, 1)
        self.tl = [P.sb([128, maxc], F32, "rtl%d" % i) for i in range(2)]
        self.ob = [P.sb([128, 512], F32, "rob%d" % i) for i in range(4)]
        self.ps = [P.ps([128, 512], F32, "rps%d" % i) for i in range(4)]
        self.k = 0
        self.ti = 0

    def load(self, ST, c, c0, ncols, cm):
        P = self.P
        t = self.tl[self.ti % 2]
        self.ti += 1
        if c < 2 or not cm:
            r0 = nat_block(c)
            P.dma(t[:, 0:ncols], ST[r0:r0 + 128, c0:c0 + ncols])
        else:
            i = c - 2
            for cl in range(2):
                for rho in range(2):
                    src = ST[rho * TOKR:rho * TOKR + LATR, c0:c0 + ncols].rearrange("(r w) c -> r w c", w=64)[:, 2 * i + cl, :]
                    P.dma(t[cl * 64 + rho * 32:cl * 64 + rho * 32 + 32, 0:ncols], src)
        return t

    def fm(self, t, c0, w, flip, dst):
        P = self.P
        k = self.k % 4
        self.k += 1
        P.mm(self.ps[k][0:w, 0:128], t[:, c0:c0 + w], self.J if flip else self.I)
        if k % 2 == 0:
            P.copy(self.ob[k][0:w, 0:128], self.ps[k][0:w, 0:128], eng="act")
        else:
            P.copy(self.ob[k][0:w, 0:128], self.ps[k][0:w, 0:128], eng="dve")
        return self.ob[k]

    def tmflip(self, t, c0, w):
        P = self.P
        k = self.k % 4
        self.k += 1
        P.mm(self.ps[k][:, 0:w], self.J, t[:, c0:c0 + w])
        if k % 2 == 0:
            P.copy(self.ob[k][:, 0:w], self.ps[k][:, 0:w], eng="act")
        else:
            P.copy(self.ob[k][:, 0:w], self.ps[k][:, 0:w], eng="dve")
        return self.ob[k]


def relayout_gla(P, ST, c0, DK, cm, arr, lr=True):
    NDC = DK // 128
    ncols = 4 * DK + 512 + (32 if lr else 0)
    R = Relay(P, ncols)
    qo, ko, vo, lo = 0, 2 * DK, 4 * DK, 4 * DK + 512
    for c in range(34):
        t = R.load(ST, c, c0, ncols, cm)
        for d in range(2):
            cc = c if d == 0 else rev_chunk(c)
            ps_ = slice(cc * 128, cc * 128 + 128)
            fl = (d == 1)
            for hl in range(2):
                u = d * 2 + hl
                for dc in range(NDC):
                    o_ = R.fm(t, qo + hl * DK + dc * 128, 128, fl, None)
                    P.dma(arr["qT"][u, dc * 128:(dc + 1) * 128, ps_], o_[:, 0:128], q="sp")
                    o_ = R.fm(t, ko + hl * DK + dc * 128, 128, fl, None)
                    P.dma(arr["kT"][u, dc * 128:(dc + 1) * 128, ps_], o_[:, 0:128], q="sp")
            if lr:
                o_ = R.fm(t, lo + d * 16, 16, fl, None)
                for hl in range(2):
                    P.dma(arr["lrT"][d * 2 + hl, :, ps_], o_[0:16, 0:128], q="sp")
            if d == 0:
                for hl in range(2):
                    P.dma(arr["k"][hl, ps_, :], t[:, ko + hl * DK:ko + (hl + 1) * DK], q="sp")
                    P.dma(arr["v"][hl, ps_, :], t[:, vo + hl * 256:vo + (hl + 1) * 256], q="sp")
            else:
                for hl in range(2):
                    o_ = R.tmflip(t, ko + hl * DK, DK)
                    P.dma(arr["k"][2 + hl, ps_, :], o_[:, 0:DK], q="sp")
                o_ = R.tmflip(t, vo, 512)
                for hl in range(2):
                    P.dma(arr["v"][2 + hl, ps_, :], o_[:, hl * 256:(hl + 1) * 256], q="sp")


def relayout_mlstm(P, ST, c0, arr):
    ncols = 1544
    R = Relay(P, ncols)
    for c in range(34):
        t = R.load(ST, c, c0, ncols, True)
        for d in range(2):
            cc = c if d == 0 else rev_chunk(c)
            ps_ = slice(cc * 128, cc * 128 + 128)
            fl = (d == 1)
            for hl in range(2):
                u = d * 2 + hl
                for dc in range(2):
                    o_ = R.fm(t, hl * 256 + dc * 128, 128, fl, None)
                    P.dma(arr["qpT"][u, dc * 128:(dc + 1) * 128, ps_], o_[:, 0:128], q="sp")
                    o_ = R.fm(t, 512 + hl * 256 + dc * 128, 128, fl, None)
                    P.dma(arr["kpT"][u, dc * 128:(dc + 1) * 128, ps_], o_[:, 0:128], q="sp")
            o_ = R.fm(t, 1536 + d * 4, 4, fl, None)
            for hl in range(2):
                P.dma(arr["gf"][d * 2 + hl:d * 2 + hl + 1, ps_], o_[hl:hl + 1, 0:128], q="sp")
                P.dma(arr["gi"][d * 2 + hl:d * 2 + hl + 1, ps_], o_[2 + hl:3 + hl, 0:128], q="sp")
            if d == 0:
                for hl in range(2):
                    P.dma(arr["v"][hl, ps_, :], t[:, 1024 + hl * 256:1024 + (hl + 1) * 256], q="sp")
            else:
                o_ = R.tmflip(t, 1024, 512)
                for hl in range(2):
                    P.dma(arr["v"][2 + hl, ps_, :], o_[:, hl * 256:(hl + 1) * 256], q="sp")


def emit_seq_inproj(P, Hall, wtok, NT, ST, wfm, NFM, SG, KC=16):
    hT = P.sb([128, KC, TOKR], BF16, "sq_hT")
    wst = [P.sb([128, KC, 256], F32, "sq_wst%d" % i) for i in range(2)]
    wbf = [P.sb([128, KC, 256], BF16, "sq_wbf%d" % i) for i in range(2)]
    osb = [P.sb([128, 512], F32, "sq_o%d" % i) for i in range(3)]
    ps = [P.ps([128, 512], F32, "sq_ps%d" % i) for i in range(4)]
    wtv = wtok.rearrange("(kc p) n -> p kc n", p=128)
    wfv = wfm.rearrange("(kc p) n -> p kc n", p=128)
    wi = 0
    oi = 0
    for rho in range(2):
        for ti, (_c, t0_, T_) in enumerate(TILES_ALL):
            P.dma(hT[:, :, t0_:t0_ + T_], Hall[ti][rho * 2048:(rho + 1) * 2048, :].rearrange("(kc p) t -> p kc t", p=128))
        for cb0 in range(0, NT, 256):
            cw = min(256, NT - cb0)
            b = wi % 2
            wi += 1
            P.dma(wst[b][:, :, 0:cw], wtv[:, :, cb0:cb0 + cw])
            P.copy(wbf[b][:, :, 0:cw], wst[b][:, :, 0:cw], eng="dve" if wi % 2 else "act")
            for tt in range(TOKR // 128):
                pz = ps[oi % 4]
                for kc in range(KC):
                    P.mm(pz[:, 0:cw], hT[:, kc, tt * 128:(tt + 1) * 128], wbf[b][:, kc, 0:cw], start=(kc == 0), stop=(kc == KC - 1))
                ob = osb[oi % 3]
                P.copy(ob[:, 0:cw], pz[:, 0:cw], eng="act" if oi % 2 else "dve")
                oi += 1
                P.dma(ST[rho * TOKR + tt * 128:rho * TOKR + (tt + 1) * 128, cb0:cb0 + cw], ob[:, 0:cw], q="pool")
        for j in range(NFM // 128):
            b = wi % 2
            wi += 1
            P.dma(wst[b][:, :, 0:128], wfv[:, :, j * 128:(j + 1) * 128])
            P.copy(wbf[b][:, :, 0:128], wst[b][:, :, 0:128], eng="dve" if wi % 2 else "act")
            for (t0, T) in [(0, 512), (512, 512), (1024, 512), (1536, 512), (2048, 128)]:
                pz = ps[oi % 4]
                for kc in range(KC):
                    P.mm(pz[:, 0:T], wbf[b][:, kc, 0:128], hT[:, kc, t0:t0 + T], start=(kc == 0), stop=(kc == KC - 1))
                ob = osb[oi % 3]
                P.copy(ob[:, 0:T], pz[:, 0:T], eng="act" if oi % 2 else "dve")
                oi += 1
                P.dma(SG[j * 128:(j + 1) * 128, rho * TOKR + t0:rho * TOKR + t0 + T], ob[:, 0:T], q="pool")


class MergeIn:
    def __init__(self, P):
        self.P = P
        self.I = P.sb([128, 128], F32, "mI")
        self.J = P.sb([128, 128], F32, "mJ")
        P.memset(self.I, 1.0, eng="pool")
        P.memset(self.J, 1.0, eng="pool")
        P.aselect(self.I, self.I, [[-1, 128]], ALU.is_equal, 0.0, 0, 1)
        P.aselect(self.J, self.J, [[1, 128]], ALU.is_equal, 0.0, -127, 1)
        self.tl = [P.sb([128, 256], F32, "mtl%d" % i) for i in range(4)]
        self.ti = 0

    def load(self, O, u, d, cm, rho, t0):
        P = self.P
        t = self.tl[self.ti % 4]
        self.ti += 1
        if t0 >= LATR:
            c = rho if d == 0 else 1 - rho
            P.dma(t, O[u, c * 128:(c + 1) * 128, :])
            return t, (d == 1)
        if not cm:
            i = (rho * LATR + t0) // 128
            c = 2 + i if d == 0 else 2 + (31 - i)
            P.dma(t, O[u, c * 128:(c + 1) * 128, :])
            return t, (d == 1)
        r = (rho * LATR + t0) // 64
        lat = O[u, 256:256 + 4096, :].rearrange("(col row) c -> col row c", row=64)
        if d == 0:
            for rl in range(2):
                P.dma(t[rl * 64:(rl + 1) * 64, :], lat[:, r + rl, :])
            return t, False
        base = 62 - r
        for rl in range(2):
            P.dma(t[rl * 64:(rl + 1) * 64, :], lat[:, base + rl, :])
        return t, True


def emit_merge_layer0(P, C, M, O_gla, O_ml, SG, gvA, gvB, wout, Ypart, tiles_rho):
    T_ = C.T
    osum = P.sb([128, 2, T_], F32, "osum")
    gact = [P.sb([128, T_], F32, "gact%d" % i) for i in range(2)]
    gld = [P.sb([128, T_], F32, "gld%d" % i) for i in range(2)]
    gv = P.sb([128, 2, 4], F32, "gvm")
    with P.nc.allow_non_contiguous_dma(reason="small const loads"):
        P.dma(gv[:, 0, :], gvA.rearrange("(kc p) -> p kc", p=128))
        P.dma(gv[:, 1, :], gvB.rearrange("(kc p) -> p kc", p=128))
    pst = [C.ps_u[0], C.ps_u[1]]
    for rho in range(2):
        for ti, (t0, T) in enumerate(tiles_rho):
            for mix in range(2):
                O = O_gla if mix == 0 else O_ml
                for hl in range(2):
                    for i in range(2):
                        kcl = mix * 4 + hl * 2 + i
                        P.dma(gld[i][:, 0:T], SG[kcl * 128:(kcl + 1) * 128, rho * TOKR + t0:rho * TOKR + t0 + T])
                        P.act(gact[i][:, 0:T], gld[i][:, 0:T], AF.Silu if mix == 0 else AF.Sigmoid)
                    for sub in range(T // 128):
                        ssl = slice(sub * 128, (sub + 1) * 128)
                        tf, ff = M.load(O, hl, 0, mix == 1, rho, t0 + sub * 128)
                        tb, fb = M.load(O, 2 + hl, 1, mix == 1, rho, t0 + sub * 128)
                        for i in range(2):
                            P.mm(pst[0][:, 0:128], tf[:, i * 128:(i + 1) * 128], M.J if ff else M.I)
                            P.mm(pst[1][:, 0:128], tb[:, i * 128:(i + 1) * 128], M.J if fb else M.I)
                            P.copy(osum[:, i, ssl], pst[0][:, 0:128], eng="act")
                            P.tt(osum[:, i, ssl], osum[:, i, ssl], pst[1][:, 0:128], ALU.add)
                    cols = [hl * 2, hl * 2 + 1]
                    head_norm_chunks(C, T, osum, gv[:, mix, :], cols, [g[:, 0:T] for g in gact], mix == 1,
                                     [C.h[:, mix * 4 + hl * 2 + i, 0:T] for i in range(2)], 256)

            def sink(j, cw, pg, rho=rho, ti=ti, T=T):
                so = C.sg[j % 2][:, 0:T]
                P.copy(so, pg, eng="act")
                P.dma(Ypart[ti][j // 4][rho * 512 + (j % 4) * 128:rho * 512 + (j % 4 + 1) * 128, :], so, q="pool")
            proj(C, T, wout, 8, 2048, lambda kc: C.h[:, kc, 0:T], sink, tiled=True)


def emit_merge_layer1(P, C, M, O_s5, O_ret, SG, ins, Gf, Gown, Gall, wout, Ypart, tiles_rho):
    T_ = C.T
    osum = P.sb([128, 2, T_], F32, "osum")
    ysum = P.sb([128, 4, T_], F32, "ysum")
    gact = [P.sb([128, T_], F32, "gact%d" % i) for i in range(2)]
    gld = [P.sb([128, T_], F32, "gld%d" % i) for i in range(2)]
    otl = [P.sb([128, 512], F32, "otl%d" % i) for i in range(4)]
    gv = P.sb([128, 3, 4], F32, "gvm")
    with P.nc.allow_non_contiguous_dma(reason="small const loads"):
        P.dma(gv[:, 0, :], ins["s5d"].rearrange("(kc p) -> p kc", p=128))
        P.dma(gv[:, 1, :], ins["nB"].rearrange("(kc p) -> p kc", p=128))
        P.dma(gv[:, 2, :], ins["bglu"].rearrange("(kc p) -> p kc", p=128))
    pst = [C.ps_u[0], C.ps_u[1]]
    oi = 0
    for rho in range(2):
        for ti, (t0, T) in enumerate(tiles_rho):
            for sub in range(T // 128):
                ssl = slice(sub * 128, (sub + 1) * 128)
                r0 = rho * TOKR + t0 + sub * 128
                tf = otl[oi % 4]
                tb = otl[(oi + 1) % 4]
                oi += 2
                P.dma(tf, O_s5[0][r0:r0 + 128, :])
                P.dma(tb, O_s5[1][r0:r0 + 128, :])
                for kc in range(4):
                    P.mm(pst[0][:, 0:128], tf[:, kc * 128:(kc + 1) * 128], M.I)
                    P.mm(pst[1][:, 0:128], tb[:, kc * 128:(kc + 1) * 128], M.I)
                    P.copy(ysum[:, kc, ssl], pst[0][:, 0:128], eng="act")
                    P.tt(ysum[:, kc, ssl], ysum[:, kc, ssl], pst[1][:, 0:128], ALU.add)
            csl = slice(rho * TOKR + t0, rho * TOKR + t0 + T)
            for kc in range(4):
                u_ = gld[kc % 2][:, 0:T]
                P.dma(u_, SG[kc * 128:(kc + 1) * 128, csl])
                yv = C.tmp[0][:, 0:T]
                t2 = C.tmp[1][:, 0:T]
                P.stt(yv, u_, gv[:, 0, kc:kc + 1], ysum[:, kc, 0:T], ALU.mult, ALU.add)
                P.tt(t2, yv, yv, ALU.mult)
                P.ts(t2, t2, 0.044715, ALU.mult, 1.0, ALU.add)
                P.tt(t2, t2, yv, ALU.mult)
                P.act(t2, t2, AF.Sigmoid, scale=1.5957691216057308)
                P.tt(C.XB[:, kc, 0:T], t2, yv, ALU.mult)
                P.copy(C.h[:, kc, 0:T], C.XB[:, kc, 0:T], eng="act")
                P.dma(Gf[kc * 128:(kc + 1) * 128, csl], C.XB[:, kc, 0:T], q="pool")
                P.dma(Gown[rho][ti][kc * 128:(kc + 1) * 128, :], C.h[:, kc, 0:T], q="pool")
    for rho in range(2):
        for ti in range(len(tiles_rho)):
            P.collective("AllGather", Gall[rho][ti], Gown[rho][ti], G2)
    for rho in range(2):
        for ti, (t0, T) in enumerate(tiles_rho):
            csl = slice(rho * TOKR + t0, rho * TOKR + t0 + T)
            P.dma(C.hid[:, 0:8, 0:T], Gall[rho][ti].rearrange("(kc p) t -> p kc t", p=128))

            def sink_g(j, cw, pg, T=T, csl=csl):
                sgm = C.sg[j % 2][:, 0:T]
                P.act(sgm, pg, AF.Sigmoid, bias=gv[:, 2, j:j + 1])
                g_ = gld[j % 2][:, 0:T]
                P.dma(g_, Gf[j * 128:(j + 1) * 128, csl])
                P.tt(C.h[:, j, 0:T], g_, sgm, ALU.mult)
            proj(C, T, ins["wglu"], 8, 512, lambda kc: C.hid[:, kc, 0:T], sink_g, tiled=True)
            for hl in range(2):
                for i in range(2):
                    kcl = 4 + hl * 2 + i
                    P.dma(gld[i][:, 0:T], SG[kcl * 128:(kcl + 1) * 128, csl])
                    P.act(gact[i][:, 0:T], gld[i][:, 0:T], AF.Silu)
                for sub in range(T // 128):
                    ssl = slice(sub * 128, (sub + 1) * 128)
                    tf, ff = M.load(O_ret, hl, 0, True, rho, t0 + sub * 128)
                    tb, fb = M.load(O_ret, 2 + hl, 1, True, rho, t0 + sub * 128)
                    for i in range(2):
                        P.mm(pst[0][:, 0:128], tf[:, i * 128:(i + 1) * 128], M.J if ff else M.I)
                        P.mm(pst[1][:, 0:128], tb[:, i * 128:(i + 1) * 128], M.J if fb else M.I)
                        P.copy(osum[:, i, ssl], pst[0][:, 0:128], eng="act")
                        P.tt(osum[:, i, ssl], osum[:, i, ssl], pst[1][:, 0:128], ALU.add)
                cols = [hl * 2, hl * 2 + 1]
                head_norm_chunks(C, T, osum, gv[:, 1, :], cols, [g[:, 0:T] for g in gact], True,
                                 [C.h[:, 4 + hl * 2 + i, 0:T] for i in range(2)], 256)

            def sink(j, cw, pg, rho=rho, ti=ti, T=T):
                so = C.sg[j % 2][:, 0:T]
                P.copy(so, pg, eng="act")
                P.dma(Ypart[ti][j // 4][rho * 512 + (j % 4) * 128:rho * 512 + (j % 4 + 1) * 128, :], so, q="pool")
            proj(C, T, wout, 8, 2048, lambda kc: C.h[:, kc, 0:T], sink, tiled=True)


D_, DFF_M = 2048, 5504
TILES_ALL = [(0, i * 512, 512) for i in range(4)] + [(1, 2048, 128)]
TILES_LAT = [(0, i * 512, 512) for i in range(4)]


def emit_P1(P, l, xin, E, X1, Hown, modo):
    C = TokCtx(P, D_, DFF_M, 512, nwst=5, nwbf=7, look=5)
    load_consts(C, E["npre%d" % l], E["npost%d" % l])
    compute_mod(C, E["cT"], E["w_mod%d" % l], E["b_mod%d" % l], modo, tiled=True)
    A0, sh0, G0 = derive_sub(C, 0, 0.5)
    A1, sh1, G1 = derive_sub(C, 1, 1.0)
    xv = xin.rearrange("(kc p) t -> p kc t", p=128)
    x1v = X1.rearrange("(kc p) t -> p kc t", p=128)
    for ti, (col, t0, T) in enumerate(TILES_ALL):
        P.dma(C.XB[:, :, 0:T], xv[:, :, t0:t0 + T], q="sp")
        sublayer_in(C, T, A0, sh0, col)
        ffn_core(C, T, E["wg%da" % l], E["wu%da" % l], E["wd%da" % l], tiled=True)
        sublayer_out(C, T, G0, col, lambda kc, t0=t0, T=T: xv[:, kc, t0:t0 + T])
        P.dma(x1v[:, :, t0:t0 + T], C.XB[:, :, 0:T], q="pool")
        sublayer_in(C, T, A1, sh1, col)
        P.dma(Hown[ti].rearrange("(kc p) t -> p kc t", p=128), C.h[:, :, 0:T], q="pool")


def emit_P3(P, l, E, X1, X2, Yown, modo, xout, tiles):
    C = TokCtx(P, D_, DFF_M, 512, nwst=5, nwbf=7, look=5)
    load_consts(C, E["npre%d" % l], E["npost%d" % l])
    load_mod(C, modo)
    A1, sh1, G1 = derive_sub(C, 1, 1.0)
    A2, sh2, G2_ = derive_sub(C, 2, 0.5)
    x1v = X1.rearrange("(kc p) t -> p kc t", p=128)
    x2v = X2.rearrange("(kc p) t -> p kc t", p=128)
    xov = xout.rearrange("(kc p) t -> p kc t", p=128)
    for ti, (col, t0, T) in enumerate(tiles):
        for q_ in range(4):
            P.dma(C.XB[:, q_ * 4:(q_ + 1) * 4, 0:T], Yown[ti][q_].rearrange("(kc p) t -> p kc t", p=128), q="sp")
        sublayer_out(C, T, G1, col, lambda kc, t0=t0, T=T: x1v[:, kc, t0:t0 + T])
        P.dma(x2v[:, :, t0:t0 + T], C.XB[:, :, 0:T], q="pool")
        sublayer_in(C, T, A2, sh2, col)
        ffn_core(C, T, E["wg%db" % l], E["wu%db" % l], E["wd%db" % l], tiled=True)
        sublayer_out(C, T, G2_, col, lambda kc, t0=t0, T=T: x2v[:, kc, t0:t0 + T])
        P.dma(xov[:, :, t0:t0 + T], C.XB[:, :, 0:T], q="pool", is_output=True)


EXT_SHAPES = {
    "xT": [2048, TOKR], "cT": [2048, 2],
    "gwg": [4, 17, 128], "mcwq": [4, 256, 3], "mcwk": [4, 256, 3], "mcbq": [4, 256, 1], "mcbk": [4, 256, 1],
    "mbi": [4, 1], "mbf": [4, 1], "nA0": [512], "nB0": [512],
    "rdec": [4, 128, 1], "s5d": [512], "wglu": [4, 128, 8, 128], "bglu": [512], "nB1": [512],
    "wtok0": [2048, 2600], "wtok1": [2048, 2048],
}
for _l in range(2):
    EXT_SHAPES.update({"w_mod%d" % _l: [144, 128, 16, 128], "b_mod%d" % _l: [18432], "npre%d" % _l: [3, 2048], "npost%d" % _l: [3, 2048],
                       "wfm%d" % _l: [2048, 1024], "wout%d" % _l: [16, 128, 8, 128]})
    for _s in "ab":
        EXT_SHAPES.update({"wg%d%s" % (_l, _s): [43, 128, 16, 128], "wu%d%s" % (_l, _s): [43, 128, 16, 128], "wd%d%s" % (_l, _s): [16, 128, 43, 128]})
for _d in range(2):
    for _k, _sh in (("lamre", [64, 32]), ("lamim", [64, 32]), ("lstep", [64, 32]), ("Bre", [64, 32, 16]), ("Bim", [64, 32, 16]),
                    ("Cre", [64, 32, 16]), ("Cim", [64, 32, 16])):
        EXT_SHAPES["s5%d_%s" % (_d, _k)] = _sh


class _Stop(Exception):
    pass


def build_mega(stop=None, dump=()):
    P = Prog("mega")
    try:
        _build_mega(P, stop, dump)
    except _Stop:
        pass
    return P


def _build_mega(P, stop, dump):
    cnt = [0]

    def chk(tag, tensors=()):
        cnt[0] += 1
        if stop is not None and cnt[0] == stop:
            for nm, v in tensors:
                if nm in dump:
                    o = P.dram("dbg_" + nm, list(v.shape), v.ap.dtype, kind="ExternalOutput")
                    P.dma(o, v, q="sp")
            print("STOP at", cnt[0], tag)
            raise _Stop()

    class _LazyE(dict):
        def __missing__(self, k):
            v = P.dram(k, EXT_SHAPES[k])
            self[k] = v
            return v
    E = _LazyE()
    P.ext = E
    x3T = P.dram("x3T", [2048, LATR], kind="ExternalOutput")
    N = NSQ
    X1 = P.dram_i("X1", [2048, TOKR])
    X2 = P.dram_i("X2", [2048, TOKR])
    X3 = P.dram_i("X3", [2048, TOKR])
    Hown = [P.dram_i("Hown%d" % i, [2048, T], BF16) for i, (_, _t, T) in enumerate(TILES_ALL)]
    Hall = [P.dram_i("Hall%d" % i, [4096, T], BF16) for i, (_, _t, T) in enumerate(TILES_ALL)]
    ST = P.dram_i("ST", [N, 2600])
    SG = P.dram_i("SG", [1024, N])
    modo = P.dram_i("modo", [128, 144, 2])
    Ypart = [[P.dram_i("Yp%d_%d" % (i, q), [1024, T]) for q in range(4)] for i, (_, _t, T) in enumerate(TILES_ALL)]
    Yown = [[P.dram_i("Yo%d_%d" % (i, q), [512, T]) for q in range(4)] for i, (_, _t, T) in enumerate(TILES_ALL)]
    for l in range(2):
        xin = E["xT"] if l == 0 else X3
        P.push_scope()
        emit_P1(P, l, xin, E, X1, Hown, modo)
        P.pop_scope()
        chk("P1_%d" % l, [("X1", X1), ("Hown", Hown[0])])
        for ti in range(len(TILES_ALL)):
            P.collective("AllGather", Hall[ti], Hown[ti], G2)
        chk("AG_%d" % l, [("Hall", Hall[0])])
        P.push_scope()
        emit_seq_inproj(P, Hall, E["wtok%d" % l], 2600 if l == 0 else 2048, ST, E["wfm%d" % l], 1024, SG)
        P.pop_scope()
        chk("inproj_%d" % l, [("ST", ST), ("SG", SG)])
        if l == 0:
            ga = {"qT": P.dram_i("g_qT", [4, 128, N]), "kT": P.dram_i("g_kT", [4, 128, N]), "k": P.dram_i("g_k", [4, N, 128]),
                  "v": P.dram_i("g_v", [4, N, 256]), "lrT": P.dram_i("g_lrT", [4, 16, N]), "o": P.dram_i("g_o", [4, N, 256])}
            P.push_scope()
            relayout_gla(P, ST, 0, 128, False, ga, lr=True)
            P.pop_scope()
            chk("relay_gla", [("g_qT", ga["qT"]), ("g_k", ga["k"]), ("g_v", ga["v"]), ("g_lrT", ga["lrT"])])
            P.push_scope()
            P.bind = dict(ga)
            P.bind["wg"] = E["gwg"]
            emit_gla(P, 4, 34, 128, 256, "gla", 128 ** -0.5, 1.0)
            P.bind = None
            P.pop_scope()
            chk("gla", [("g_o", ga["o"])])
            ma = {"qpT": P.dram_i("m_qpT", [4, 256, N]), "kpT": P.dram_i("m_kpT", [4, 256, N]), "v": P.dram_i("m_v", [4, N, 256]),
                  "gi": P.dram_i("m_gi", [4, N]), "gf": P.dram_i("m_gf", [4, N]), "h": P.dram_i("m_h", [4, N, 256])}
            P.push_scope()
            relayout_mlstm(P, ST, 1056, ma)
            P.pop_scope()
            chk("relay_ml", [("m_qpT", ma["qpT"]), ("m_gi", ma["gi"]), ("m_v", ma["v"])])
            P.push_scope()
            P.bind = dict(ma)
            P.bind.update({"cwq": E["mcwq"], "cwk": E["mcwk"], "cbq": E["mcbq"], "cbk": E["mcbk"], "bi": E["mbi"], "bf": E["mbf"]})
            emit_mlstm(P, 4, [2, 32], 256, 256 ** -0.5)
            P.bind = None
            P.pop_scope()
            chk("mlstm", [("m_h", ma["h"])])
            P.push_scope()
            C = TokCtx(P, D_, DFF_M, 512)
            M = MergeIn(P)
            emit_merge_layer0(P, C, M, ga["o"], ma["h"], SG, E["nA0"], E["nB0"], E["wout0"], Ypart,
                              [(t0, T) for (_, t0, T) in TILES_ALL])
            P.pop_scope()
            chk("merge0", [("Ypart", Ypart[0][0])])
        else:
            O_s5 = [P.dram_i("s5_o%d" % d, [N, 512]) for d in range(2)]
            for d in range(2):
                P.push_scope()
                prm = {k: E["s5%d_%s" % (d, k)] for k in ("lamre", "lamim", "lstep", "Bre", "Bim", "Cre", "Cim")}
                emit_s5v2(P, 32, 8, d == 1, ST, 0, O_s5[d], prm)
                P.pop_scope()
                chk("s5_%d" % d, [("s5o", O_s5[d])])
            ra = {"qT": P.dram_i("r_qT", [4, 256, N]), "kT": P.dram_i("r_kT", [4, 256, N]), "k": P.dram_i("r_k", [4, N, 256]),
                  "v": P.dram_i("r_v", [4, N, 256]), "o": P.dram_i("r_o", [4, N, 256])}
            P.push_scope()
            relayout_gla(P, ST, 512, 256, True, ra, lr=False)
            P.pop_scope()
            chk("relay_ret", [("r_qT", ra["qT"])])
            P.push_scope()
            P.bind = dict(ra)
            P.bind["dec"] = E["rdec"]
            emit_gla(P, 4, 34, 256, 256, "ret", 1.0, 256 ** -0.5)
            P.bind = None
            P.pop_scope()
            chk("ret", [("r_o", ra["o"])])
            Gf = P.dram_i("Gf", [512, N])
            Gown = [[P.dram_i("Gown%d_%d" % (r_, i), [512, 512], BF16) for i in range(4)] for r_ in range(2)]
            Gall = [[P.dram_i("Gall%d_%d" % (r_, i), [1024, 512], BF16) for i in range(4)] for r_ in range(2)]
            P.push_scope()
            C = TokCtx(P, D_, DFF_M, 512)
            M = MergeIn(P)
            emit_merge_layer1(P, C, M, O_s5, ra["o"], SG, {"s5d": E["s5d"], "wglu": E["wglu"], "bglu": E["bglu"], "nB": E["nB1"]},
                              Gf, Gown, Gall, E["wout1"], Ypart, [(t0, T) for (_, t0, T) in TILES_LAT])
            P.pop_scope()
            chk("merge1", [("Ypart", Ypart[0][0])])
        for ti in range(5 if l == 0 else 4):
            for q_ in range(4):
                P.collective("ReduceScatter", Yown[ti][q_], Ypart[ti][q_], G2, op=ALU.add)
        chk("RS_%d" % l, [("Yown", Yown[0][0])])
        P.push_scope()
        if l == 0:
            emit_P3(P, l, E, X1, X2, Yown, modo, X3, TILES_ALL)
            P.pop_scope()
            chk("P3_0", [("X3", X3)])
            P.push_scope()
        else:
            emit_P3(P, l, E, X1, X2, Yown, modo, x3T, TILES_LAT)
        P.pop_scope()
    return P

_C = np.ascontiguousarray
_NCORE = 8


def _tile_w(w):
    K, N = w.shape
    return _C(w.reshape(K // 128, 128, N // 128, 128).transpose(2, 1, 0, 3))


def _core_inputs(inp, r):
    b, hf = r // 2, r % 2
    hs = [2 * hf, 2 * hf + 1]
    m = {}
    x = inp["x"][b]
    ctx = inp["ctx"][b]
    m["xT"] = _C(np.concatenate([x[hf * 2048:(hf + 1) * 2048], ctx[hf * 128:(hf + 1) * 128]], axis=0).T)
    m["cT"] = _C(np.stack([inp["c"][b], inp["c_ctx"]], axis=1))
    for l in range(2):
        m["w_mod%d" % l] = _tile_w(inp["w_mod"][l])
        m["b_mod%d" % l] = _C(inp["b_mod"][l])
        m["npre%d" % l] = _C(inp["norm_pre"][l])
        m["npost%d" % l] = _C(inp["norm_post"][l])
        for si, s in enumerate("ab"):
            m["wg%d%s" % (l, s)] = _tile_w(inp["ffn_w_gate"][l, si])
            m["wu%d%s" % (l, s)] = _tile_w(inp["ffn_w_up"][l, si])
            m["wd%d%s" % (l, s)] = _tile_w(inp["ffn_w_down"][l, si])
    ar = np.arange
    cols = []
    for off, w in ((0, 128), (512, 128), (1024, 256)):
        for h in hs:
            cols.append(off + h * w + ar(w))
    cols.append(3072 + ar(32))
    for off in (3104, 4128, 5152):
        for h in hs:
            cols.append(off + h * 256 + ar(256))
    for d in range(2):
        cols.append(np.array([7200 + d * 8 + 4 + hs[0], 7200 + d * 8 + 4 + hs[1], 7200 + d * 8 + hs[0], 7200 + d * 8 + hs[1]]))
    cols = np.concatenate(cols)
    w_in0 = inp["ev_w_in"][0]
    m["wtok0"] = _C(w_in0[:, cols])
    own512 = np.concatenate([h * 256 + ar(256) for h in hs])
    m["wfm0"] = _C(w_in0[:, np.concatenate([2048 + own512, 6176 + own512])])
    m["wout0"] = _tile_w(inp["ev_w_out"][0][np.concatenate([own512, 1024 + own512])])
    m["nA0"] = _C(inp["gla_norm"][0][own512])
    m["nB0"] = _C(inp["ml_norm"][0][own512])
    gwg, cwq, cwk, cbq, cbk, bi, bf = [], [], [], [], [], [], []
    cw = inp["ml_conv_w"][0]
    cb = inp["ml_conv_b"][0]
    bg = inp["ml_b_gates"][0]
    for d in range(2):
        for h in hs:
            gwg.append(np.concatenate([inp["gla_w_gate"][0, d][:, h * 128:(h + 1) * 128],
                                       inp["gla_b_gate"][0, d][None, h * 128:(h + 1) * 128]], axis=0))
            wq = cw[:, h * 256:(h + 1) * 256].T
            wk = cw[:, 1024 + h * 256:1024 + (h + 1) * 256].T
            if d == 1:
                wq, wk = wq[:, ::-1], wk[:, ::-1]
            cwq.append(wq)
            cwk.append(wk)
            cbq.append(cb[h * 256:(h + 1) * 256][:, None])
            cbk.append(cb[1024 + h * 256:1024 + (h + 1) * 256][:, None])
            bi.append([bg[d, 0, h]])
            bf.append([bg[d, 1, h]])
    m["gwg"] = _C(np.stack(gwg)).astype(np.float32)
    m["mcwq"] = _C(np.stack(cwq)).astype(np.float32)
    m["mcwk"] = _C(np.stack(cwk)).astype(np.float32)
    m["mcbq"] = _C(np.stack(cbq)).astype(np.float32)
    m["mcbk"] = _C(np.stack(cbk)).astype(np.float32)
    m["mbi"] = np.array(bi, np.float32)
    m["mbf"] = np.array(bf, np.float32)
    w_in1 = inp["od_w_in"][0]
    ch512 = hf * 512 + ar(512)
    cols1 = [ch512]
    for off in (1024, 2048, 3072):
        for h in hs:
            cols1.append(off + h * 256 + ar(256))
    m["wtok1"] = _C(w_in1[:, np.concatenate(cols1)])
    m["wfm1"] = _C(w_in1[:, np.concatenate([ch512, 4096 + own512])])
    m["wout1"] = _tile_w(inp["od_w_out"][0][np.concatenate([ch512, 1024 + own512])])
    m["nB1"] = _C(inp["ret_norm"][0][own512])
    m["s5d"] = _C(inp["s5_d"][0][ch512])
    m["wglu"] = _tile_w(inp["s5_w_glu"][0][:, ch512])
    m["bglu"] = _C(inp["s5_b_glu"][0][ch512])
    gs = slice(hf * 32, (hf + 1) * 32)
    for d in range(2):
        pre = "s5%d_" % d
        m[pre + "lamre"] = _C(inp["s5_lam_re"][0, d][gs].T)
        m[pre + "lamim"] = _C(inp["s5_lam_im"][0, d][gs].T)
        m[pre + "lstep"] = _C(np.broadcast_to(inp["s5_log_step"][0, d][gs][None, :], (64, 32)))
        m[pre + "Bre"] = _C(inp["s5_b_re"][0, d][gs].transpose(1, 0, 2))
        m[pre + "Bim"] = _C(inp["s5_b_im"][0, d][gs].transpose(1, 0, 2))
        m[pre + "Cre"] = _C(inp["s5_c_re"][0, d][gs].transpose(2, 0, 1))
        m[pre + "Cim"] = _C(inp["s5_c_im"][0, d][gs].transpose(2, 0, 1))
    m["rdec"] = _C(np.stack([np.full((128, 1), inp["ret_log_decay"][0, d, h], np.float32) for d in range(2) for h in hs]))
    return m


def kernel(**inp):
    inp = {k: np.asarray(v, dtype=np.float32) for k, v in inp.items()}
    P = build_mega()
    nc = P.finish()
    maps = [{k: v for k, v in _core_inputs(inp, r).items() if k in P.ext} for r in range(_NCORE)]
    res = run_bass_kernel_spmd(nc, maps, core_ids=list(range(_NCORE))).results
    out = np.empty((4, 4096, 2048), np.float32)
    for r in range(_NCORE):
        b, hf = r // 2, r % 2
        out[b, hf * 2048:(hf + 1) * 2048] = res[r]["x3T"].T
    return out
```

```python
import math

import numpy as np
import concourse.bass as bass
import concourse.mybir as mybir
from concourse.bass_utils import run_bass_kernel_spmd

F32 = mybir.dt.float32
BF16 = mybir.dt.bfloat16
I32 = mybir.dt.int32
AF = mybir.ActivationFunctionType
ALU = mybir.AluOpType

_uid = [0]
import os as _os
SES_DEFAULT = _os.environ.get("SES", "1") == "1"
WAW_SKIP = _os.environ.get("WAWSKIP", "1") == "1"


class V:
    __slots__ = ("ap", "keys")

    def __init__(self, ap, keys):
        self.ap = ap
        self.keys = keys if isinstance(keys, tuple) else (keys,)

    def __getitem__(self, idx):
        return V(self.ap[idx], self.keys)

    def k(self, *keys):
        return V(self.ap, tuple(keys))

    def re(self, s, **kw):
        return V(self.ap.rearrange(s, **kw), self.keys)

    def bc(self, shape):
        return V(self.ap.to_broadcast(shape), self.keys)

    def rearrange(self, s, **kw):
        return V(self.ap.rearrange(s, **kw), self.keys)

    @property
    def shape(self):
        return self.ap.shape


class Prog:
    ENG = ("pe", "act", "dve", "pool", "sp")

    def __init__(self, name="k", ring=6, same_engine_sync=SES_DEFAULT):
        self.nc = bass.Bass("TRN2", target_bir_lowering=False, name=name)
        nc = self.nc
        self.eng = {"pe": nc.tensor, "act": nc.scalar, "dve": nc.vector, "pool": nc.gpsimd, "sp": nc.sync}
        self.sem = {k: nc.alloc_semaphore("c_" + k) for k in self.ENG}
        self.cnt = {k: 0 for k in self.ENG}
        self.seen = {k: {} for k in self.ENG}
        self.ring = {q: [[nc.alloc_semaphore("d_%s%d" % (q, i)), 0] for i in range(ring)] for q in ("sp", "pool", "act")}
        self.rpos = {q: 0 for q in self.ring}
        self.lastw = {}
        self.reads = {}
        self.ses = same_engine_sync
        self.out_events = []
        self.n_inst = 0

    def dram(self, name, shape, dt=F32, kind="ExternalInput"):
        bind = getattr(self, "bind", None)
        if bind is not None and name in bind:
            return bind[name]
        return self.nc.dram_tensor(name, list(shape), dt, kind=kind).ap()

    def dram_i(self, name, shape, dt=F32):
        return V(self.nc.dram_tensor(name, list(shape), dt, kind="Internal").ap(), "dram_" + name)

    def push_scope(self):
        import contextlib
        if not hasattr(self, "scopes"):
            self.scopes = []
            self.scope_id = 0
        self.scope_id += 1
        self.scopes.append((contextlib.ExitStack(), self.scope_id))

    def pop_scope(self):
        self.barrier()
        st, _ = self.scopes.pop()
        st.close()

    def _uname(self, name):
        if getattr(self, "scopes", None):
            return "%s_s%d" % (name, self.scopes[-1][1])
        return name

    def sb(self, shape, dt=F32, name=None):
        _uid[0] += 1
        name = self._uname(name or ("t%d" % _uid[0]))
        if getattr(self, "scopes", None):
            t = self.scopes[-1][0].enter_context(self.nc.sbuf_tensor("sb_" + name, list(shape), dt))
        else:
            t = self.nc.alloc_sbuf_tensor("sb_" + name, list(shape), dt)
        return V(t.ap() if hasattr(t, "ap") else t[:], name)

    def ps(self, shape, dt=F32, name=None):
        _uid[0] += 1
        base = name or ("p%d" % _uid[0])
        name = self._uname(base)
        esz = 2 if dt == BF16 else 4
        if getattr(self, "scopes", None):
            t = self.scopes[-1][0].enter_context(self.nc.psum_tensor("ps_" + name, [128, 2048 // esz], dt))
        else:
            t = self.nc.alloc_psum_tensor("ps_" + name, [128, 2048 // esz], dt)
        full = V(t.ap() if hasattr(t, "ap") else t[:], name)
        if not hasattr(self, "banks"):
            self.banks = {}
        self.banks[base] = full
        return self.ps_alias(base, shape)

    def barrier(self):
        evs = []
        for x in self.ENG:
            if self.cnt[x] > 0:
                evs.append((self.sem[x], self.cnt[x], "c_" + x))
        for q in self.ring:
            for i, slot in enumerate(self.ring[q]):
                if slot[1] > 0:
                    evs.append((slot[0], slot[1], "d_%s%d" % (q, i)))
        if hasattr(self, "cc_sem") and self.cc_cnt > 0:
            evs.append((self.cc_sem, self.cc_cnt, "cc_sem"))
        for e in self.ENG:
            for ev in evs:
                if ev[2] == "c_" + e:
                    continue
                self._wait(e, ev)

    def ps_alias(self, name, shape):
        full = self.banks[name]
        n = 1
        for d in shape[1:]:
            n *= d
        v = full[0:shape[0], 0:n]
        if len(shape) > 2:
            letters = "abcdefg"[:len(shape) - 1]
            kw = {letters[i]: shape[i + 1] for i in range(1, len(shape) - 1)}
            v = v.re("p (%s) -> p %s" % (" ".join(letters), " ".join(letters)), **kw)
        return v

    def _wait(self, e, ev):
        sem, val, key = ev
        if self.seen[e].get(key, 0) >= val:
            return
        self.eng[e].wait_ge(sem, val)
        self.seen[e][key] = val

    def _deps(self, e, reads, writes):
        raw, waw = [], []
        for v in reads:
            for k in v.keys:
                lw = self.lastw.get(k)
                if lw is not None:
                    raw.append(lw)
        for v in writes:
            for k in v.keys:
                lw = self.lastw.get(k)
                if lw is not None:
                    waw.append(lw)
                waw.extend(self.reads.get(k, ()))
        for ev in raw:
            if ev[3] == e and (e == "pe" or not self.ses):
                continue
            self._wait(e, ev[:3])
        for ev in waw:
            if ev[3] == e and (e == "pe" or not self.ses or WAW_SKIP):
                continue
            self._wait(e, ev[:3])

    def _commit(self, ev, reads, writes):
        for v in writes:
            for k in v.keys:
                self.lastw[k] = ev
                self.reads[k] = []
        for v in reads:
            for k in v.keys:
                self.reads.setdefault(k, []).append(ev)
                if len(self.reads[k]) > 24:
                    best = {}
                    for r in self.reads[k]:
                        if r[2] not in best or best[r[2]][1] < r[1]:
                            best[r[2]] = r
                    self.reads[k] = list(best.values())

    def op(self, e, fn, reads=(), writes=()):
        self._deps(e, reads, writes)
        inst = fn(self.eng[e])
        self.cnt[e] += 1
        inst.then_inc(self.sem[e], 1)
        ev = (self.sem[e], self.cnt[e], "c_" + e, e)
        self._commit(ev, reads, writes)
        self.n_inst += 1
        return ev

    def dma(self, out, in_, q="sp", is_output=False, **kw):
        reads = [in_] if isinstance(in_, V) else []
        writes = [out] if isinstance(out, V) else []
        self._deps(q, reads, writes)
        slot = self.ring[q][self.rpos[q] % len(self.ring[q])]
        key = "d_%s%d" % (q, self.rpos[q] % len(self.ring[q]))
        self.rpos[q] += 1
        if slot[1] > 0:
            self._wait(q, (slot[0], slot[1], key))
        o = out.ap if isinstance(out, V) else out
        i = in_.ap if isinstance(in_, V) else in_
        inst = self.eng[q].dma_start(out=o, in_=i, **kw)
        slot[1] += 16
        inst.then_inc(slot[0], 16)
        ev = (slot[0], slot[1], key, "dma")
        self._commit(ev, reads, writes)
        if is_output:
            self.out_events.append(ev)
        self.n_inst += 1
        return ev

    def finish(self):
        for q in self.ring:
            for i, slot in enumerate(self.ring[q]):
                if slot[1] > 0:
                    self._wait("sp", (slot[0], slot[1], "d_%s%d" % (q, i)))
        return self.nc

    def mm(self, out, lhsT, rhs, start=True, stop=True):
        return self.op("pe", lambda e: e.matmul(out.ap, lhsT.ap, rhs.ap, start=start, stop=stop),
                       reads=[lhsT, rhs], writes=[out])

    def transpose(self, out, in_, ident):
        return self.op("pe", lambda e: e.transpose(out.ap, in_.ap, ident.ap), reads=[in_, ident], writes=[out])

    def act(self, out, in_, func, bias=None, scale=None, accum_out=None, eng="act"):
        reads = [in_]
        kw = {}
        if bias is not None:
            if isinstance(bias, V):
                reads.append(bias)
                kw["bias"] = bias.ap
            else:
                kw["bias"] = bias
        if scale is not None:
            if isinstance(scale, V):
                reads.append(scale)
                kw["scale"] = scale.ap
            else:
                kw["scale"] = scale
        writes = [out]
        if accum_out is not None:
            writes.append(accum_out)
            kw["accum_out"] = accum_out.ap
        return self.op("act", lambda e: e.activation(out.ap, in_.ap, func, **kw), reads=reads, writes=writes)

    def tt(self, out, a, b, op, eng="dve"):
        return self.op(eng, lambda e: e.tensor_tensor(out.ap, a.ap, b.ap, op), reads=[a, b], writes=[out])

    def ts(self, out, a, s1, op0, s2=None, op1=None, eng="dve", accum_out=None):
        reads = [a]
        x1 = s1.ap if isinstance(s1, V) else s1
        x2 = s2.ap if isinstance(s2, V) else s2
        if isinstance(s1, V):
            reads.append(s1)
        if isinstance(s2, V):
            reads.append(s2)
        kw = {}
        writes = [out]
        if accum_out is not None:
            kw["accum_out"] = accum_out.ap
            writes.append(accum_out)
        if op1 is None:
            return self.op(eng, lambda e: e.tensor_scalar(out.ap, a.ap, x1, None, op0, **kw), reads=reads, writes=writes)
        return self.op(eng, lambda e: e.tensor_scalar(out.ap, a.ap, x1, x2, op0, op1, **kw), reads=reads, writes=writes)

    def stt(self, out, a, s, b, op0, op1):
        reads = [a, b]
        x = s.ap if isinstance(s, V) else s
        if isinstance(s, V):
            reads.append(s)
        return self.op("dve", lambda e: e.scalar_tensor_tensor(out.ap, a.ap, x, b.ap, op0, op1), reads=reads, writes=[out])

    def copy(self, out, in_, eng="dve"):
        if eng == "act":
            return self.op("act", lambda e: e.copy(out.ap, in_.ap), reads=[in_], writes=[out])
        return self.op(eng, lambda e: e.tensor_copy(out.ap, in_.ap), reads=[in_], writes=[out])

    def memset(self, out, val, eng="dve"):
        return self.op(eng, lambda e: e.memset(out.ap, val), writes=[out])

    def recip(self, out, in_):
        return self.op("dve", lambda e: e.reciprocal(out.ap, in_.ap), reads=[in_], writes=[out])

    def scan(self, out, d0, d1, init, op0, op1):
        reads = [d0, d1]
        x = init.ap if isinstance(init, V) else init
        if isinstance(init, V):
            reads.append(init)
        return self.op("dve", lambda e: e.tensor_tensor_scan(out.ap, d0.ap, d1.ap, x, op0, op1), reads=reads, writes=[out])

    def aselect(self, out, in_, pattern, cmp, fill, base, cm):
        return self.op("pool", lambda e: e.affine_select(out.ap, in_.ap, pattern, cmp, fill, base=base, channel_multiplier=cm),
                       reads=[in_], writes=[out])

    def iota(self, out, pattern, base, cm):
        return self.op("pool", lambda e: e.iota(out.ap, pattern, base=base, channel_multiplier=cm,
                                                 allow_small_or_imprecise_dtypes=True), writes=[out])


def run(prog, in_maps, n=8, trace=False):
    nc = prog.finish()
    res = run_bass_kernel_spmd(nc, in_maps, core_ids=list(range(n)), trace=trace)
    return res


def _collective(self, kind, out, in_, groups, op=None):
    q = "pool"
    if not hasattr(self, "cc_sem"):
        self.cc_sem = self.nc.alloc_semaphore("cc_sem")
        self.cc_cnt = 0
    self._deps(q, [in_], [out])
    inst = self.eng[q].collective_compute(kind, op or ALU.bypass, replica_groups=groups, ins=[in_.ap], outs=[out.ap])
    self.cc_cnt += 1
    inst.then_inc(self.cc_sem)
    ev = (self.cc_sem, self.cc_cnt, "cc_sem", "dma")
    self._commit(ev, [in_], [out])
    self._wait(q, ev[:3])
    self.n_inst += 1
    return ev


Prog.collective = _collective


def make_masks(P):
    U = P.sb([128, 128], F32, "Uincl")
    L = P.sb([128, 128], F32, "Lstrict")
    P.memset(U, 1.0, eng="pool")
    P.memset(L, 1.0, eng="pool")
    P.aselect(U, U, [[1, 128]], ALU.is_ge, 0.0, 0, -1)
    P.aselect(L, L, [[-1, 128]], ALU.is_gt, 0.0, 0, 1)
    return U, L


def emit_gla(P, NU, NCH, DK, DV, mode, qscale, kscale):
    N = NCH * 128
    NDK = DK // 128
    GN = 16.0 if mode == "gla" else 1.0
    qT = P.dram("qT", [NU, DK, N])
    kT = P.dram("kT", [NU, DK, N])
    kk = P.dram("k", [NU, N, DK])
    vv = P.dram("v", [NU, N, DV])
    if mode == "gla":
        lrT = P.dram("lrT", [NU, 16, N])
        wg = P.dram("wg", [NU, 17, DK])
    else:
        dec = P.dram("dec", [NU, 128, 1])
    o = P.dram("o", [NU, N, DV], kind="ExternalOutput")
    U, L = make_masks(P)
    ps_b = [P.ps([128, 128], F32, "psb%d" % i) for i in range(2)]
    ps_d = P.ps([128, DK], F32, "psd")
    ps_z = P.ps([128, DK], F32, "psz")
    ps_sc = P.ps([128, 128], F32, "pssc")
    ps_o = P.ps([128, DV], F32, "pso")
    ps_S = [P.ps([128, DV], F32, "psS%d" % i) for i in range(2)]
    if mode != "gla":
        zer = P.sb([128, DK], F32, "zer")
        P.memset(zer, 0.0)

    class B_:
        pass

    def alloc(u):
        B = B_()
        n = lambda s_: "%s_u%d" % (s_, u)
        B.qT_sb = [P.sb([128, NDK, 128], F32, n("qTs%d" % i)) for i in range(2)]
        B.kT_sb = [P.sb([128, NDK, 128], F32, n("kTs%d" % i)) for i in range(2)]
        B.k_sb = [P.sb([128, DK], F32, n("ks%d" % i)) for i in range(2)]
        B.v_sb = [P.sb([128, DV], F32, n("vs%d" % i)) for i in range(2)]
        B.v_bf = [P.sb([128, DV], BF16, n("vb%d" % i)) for i in range(2)]
        B.sp_t = [P.sb([128, DK], F32, n("spt%d" % i)) for i in range(2)]
        B.ex = P.sb([128, DK], F32, n("ex"))
        B.E1 = P.sb([128, NDK, 128], F32, n("E1"))
        B.E2 = P.sb([128, NDK, 128], F32, n("E2"))
        B.Dk = P.sb([128, DK], F32, n("Dk"))
        B.QtT = P.sb([128, NDK, 128], BF16, n("QtT"))
        B.KtT = P.sb([128, NDK, 128], BF16, n("KtT"))
        B.QbT = P.sb([128, NDK, 128], BF16, n("QbT"))
        B.Ke = P.sb([128, DK], BF16, n("Ke"))
        B.scm = P.sb([128, 128], BF16, n("scm"))
        B.S = P.sb([128, NDK, DV], F32, n("S"))
        B.Sbf = P.sb([128, NDK, DV], BF16, n("Sbf"))
        B.cols = P.sb([128, NDK, 4], F32, n("cols"))
        B.o_sb = [P.sb([128, DV], F32, n("osb%d" % i)) for i in range(2)]
        if mode == "gla":
            B.lr_sb = P.sb([17, N], F32, n("lrsb"))
            B.wg_sb = P.sb([17, DK], F32, n("wgsb"))
        else:
            B.dec_sb = P.sb([128, 1], F32, n("decsb"))
        return B

    Bs = [alloc(u) for u in range(NU)]
    for u in range(NU):
        B = Bs[u]
        P.memset(B.S, 0.0)
        P.memset(B.Sbf, 0.0, eng="pool")
        if mode == "gla":
            P.memset(B.lr_sb, 1.0)
            P.dma(B.lr_sb[0:16, :], lrT[u])
            P.dma(B.wg_sb, wg[u])
        else:
            P.dma(B.dec_sb, dec[u])
            P.act(B.sp_t[0], zer, AF.Exp, bias=B.dec_sb)
            P.act(B.sp_t[1], zer, AF.Exp, bias=B.dec_sb)
    for c in range(NCH):
        for u in range(NU):
            B = Bs[u]
            b = c % 2
            tsl = slice(c * 128, (c + 1) * 128)
            P.dma(B.qT_sb[b], qT[u, :, tsl].rearrange("(dc p) t -> p dc t", p=128))
            P.dma(B.kT_sb[b], kT[u, :, tsl].rearrange("(dc p) t -> p dc t", p=128))
            P.dma(B.k_sb[b], kk[u, tsl, :])
            P.dma(B.v_sb[b], vv[u, tsl, :])
            P.copy(B.v_bf[b], B.v_sb[b], eng="pool")
            spt = B.sp_t[b]
            if mode == "gla":
                P.mm(ps_z, B.lr_sb[:, tsl], B.wg_sb)
                P.act(B.ex, ps_z, AF.Exp, scale=-1.0)
                P.act(spt, B.ex, AF.Ln, bias=1.0)
            P.mm(ps_d, L, spt)
            P.act(B.Dk, ps_d, AF.Exp, scale=-1.0 / GN)
            P.stt(B.Ke, B.k_sb[b], float(kscale), B.Dk, ALU.mult, ALU.mult)
            for dc in range(NDK):
                pb = ps_b[dc % 2]
                P.mm(pb, spt[:, dc * 128:(dc + 1) * 128], U)
                cm = B.cols[:, dc, :]
                P.ts(cm[:, 0:1], pb[:, 63:64], 1.0 / GN, ALU.mult)
                P.ts(cm[:, 1:2], pb[:, 63:64], -1.0 / GN, ALU.mult)
                P.act(B.E1[:, dc, :], pb, AF.Exp, scale=-1.0 / GN, bias=cm[:, 0:1])
                P.act(B.E2[:, dc, :], pb, AF.Exp, scale=1.0 / GN, bias=cm[:, 1:2])
                P.act(cm[:, 2:3], pb[:, 63:64], AF.Exp, scale=-1.0 / GN)
                P.act(cm[:, 3:4], pb[:, 127:128], AF.Exp, scale=-1.0 / GN)
                P.stt(B.QtT[:, dc, :], B.qT_sb[b][:, dc, :], float(qscale), B.E1[:, dc, :], ALU.mult, ALU.mult)
                P.stt(B.KtT[:, dc, :], B.kT_sb[b][:, dc, :], float(kscale), B.E2[:, dc, :], ALU.mult, ALU.mult)
                P.ts(B.QbT[:, dc, :], B.QtT[:, dc, :], cm[:, 2:3], ALU.mult)
            for dc in range(NDK):
                P.mm(ps_sc, B.KtT[:, dc, :], B.QtT[:, dc, :], start=(dc == 0), stop=(dc == NDK - 1))
            P.tt(B.scm, ps_sc, U, ALU.mult)
            P.mm(ps_o, B.scm, B.v_bf[b], start=True, stop=False)
            for dc in range(NDK):
                P.mm(ps_o, B.QbT[:, dc, :], B.Sbf[:, dc, :], start=False, stop=(dc == NDK - 1))
            ob = B.o_sb[b]
            P.copy(ob, ps_o, eng="act")
            P.dma(o[u, tsl, :], ob, q="pool", is_output=True)
            for dc in range(NDK):
                pS = ps_S[dc % 2]
                P.mm(pS, B.Ke[:, dc * 128:(dc + 1) * 128], B.v_bf[b])
                P.stt(B.S[:, dc, :], B.S[:, dc, :], B.cols[:, dc, 3:4], pS, ALU.mult, ALU.add)
                P.copy(B.Sbf[:, dc, :], B.S[:, dc, :], eng="act")
    return P


def build_gla(NU, NCH, DK, DV, mode, qscale, kscale, name="gla"):
    P = Prog(name)
    emit_gla(P, NU, NCH, DK, DV, mode, qscale, kscale)
    return P


def emit_mlstm(P, NU, segs, DH, kscale):
    NCH = sum(segs)
    N = NCH * 128
    NDC = DH // 128
    DA = DH + 1
    qpT = P.dram("qpT", [NU, DH, N])
    kpT = P.dram("kpT", [NU, DH, N])
    vv = P.dram("v", [NU, N, DH])
    cwq = P.dram("cwq", [NU, DH, 3])
    cwk = P.dram("cwk", [NU, DH, 3])
    cbq = P.dram("cbq", [NU, DH, 1])
    cbk = P.dram("cbk", [NU, DH, 1])
    gi = P.dram("gi", [NU, N])
    gf = P.dram("gf", [NU, N])
    bi = P.dram("bi", [NU, 1])
    bf_ = P.dram("bf", [NU, 1])
    ho = P.dram("h", [NU, N, DH], kind="ExternalOutput")
    U, L = make_masks(P)
    identf = P.sb([128, 128], F32, "identf")
    P.memset(identf, 1.0, eng="pool")
    P.aselect(identf, identf, [[-1, 128]], ALU.is_equal, 0.0, 0, 1)
    identb = P.sb([128, 128], BF16, "identb")
    P.copy(identb, identf)
    R = P.sb([NU, 4, N], F32, "R")
    bcol = P.sb([NU, 4], F32, "bcol")
    with P.nc.allow_non_contiguous_dma(reason="small"):
        P.dma(R[:, 0, :], gf)
        P.dma(R[:, 1, :], gi)
        P.dma(bcol[:, 0:1], bf_)
        P.dma(bcol[:, 1:2], bi)
    P.ts(bcol[:, 2:3], bcol[:, 0:1], -1.0, ALU.mult)
    P.act(R[:, 2, :], R[:, 0, :], AF.Exp, scale=-1.0, bias=bcol[:, 2:3])
    P.act(R[:, 2, :], R[:, 2, :], AF.Ln, bias=1.0)
    P.scan(R[:, 3, :], R[:, 2, :], R[:, 2, :], 0.0, ALU.add, ALU.max)
    P.stt(R[:, 1, :], R[:, 1, :], bcol[:, 1:2], R[:, 3, :], ALU.add, ALU.add)
    P.scan(R[:, 2, :], R[:, 1, :], R[:, 1, :], 0.0, ALU.max, ALU.max)
    P.tt(R[:, 0, :], R[:, 2, :], R[:, 3, :], ALU.subtract)
    idn = P.sb([NU, NU], F32, "idn")
    P.memset(idn, 1.0, eng="pool")
    P.aselect(idn, idn, [[-1, NU]], ALU.is_equal, 0.0, 0, 1)
    sel = []
    for u in range(NU):
        s_ = P.sb([NU, 128], F32, "sel%d" % u)
        P.memset(s_, 1.0, eng="pool")
        P.aselect(s_, s_, [[0, 128]], ALU.is_equal, 0.0, -u, 1)
        sel.append(s_)
    assert NCH * 3 * NU <= 512
    ps_c = P.ps([128, NCH, 3, NU], F32, "psc")
    for c in range(NCH):
        for qi, row in enumerate((1, 2, 0)):
            P.mm(ps_c[:, c, qi, :], R[:, row, c * 128:(c + 1) * 128], idn)
    colsb = P.sb([128, NCH, 3, NU], F32, "colsb")
    P.copy(colsb, ps_c)
    HW = 130
    ps_M = P.ps([128, 128], F32, "psM")
    ps_sc = P.ps([128, 128], F32, "pssc")
    ps_o = P.ps([128, DA], F32, "pso")
    ps_t = P.ps([128, DH], BF16, "pst")
    ps_C = [P.ps([128, DA], F32, "psC%d" % i) for i in range(2)]
    seg_first = set()
    seg_last = set()
    c0 = 0
    for n in segs:
        seg_first.add(c0)
        seg_last.add(c0 + n - 1)
        c0 += n

    class B_:
        pass

    def alloc(u):
        B = B_()
        n = lambda s_: "%s_u%d" % (s_, u)
        B.qp_sb = [P.sb([128, NDC, HW], F32, n("qps%d" % i)) for i in range(2)]
        B.kp_sb = [P.sb([128, NDC, HW], F32, n("kps%d" % i)) for i in range(2)]
        B.acc = [P.sb([128, 128], F32, n("acc%d" % i)) for i in range(2)]
        B.qT = P.sb([128, NDC, 128], BF16, n("qT"))
        B.kT = P.sb([128, NDC, 128], BF16, n("kT"))
        B.qw = P.sb([128, NDC, 128], BF16, n("qw"))
        B.ktok = P.sb([128, DH], BF16, n("ktok"))
        B.va = [P.sb([128, DA], F32, n("va%d" % i)) for i in range(2)]
        for i in range(2):
            P.memset(B.va[i], 1.0)
        B.va_bf = P.sb([128, DA], BF16, n("vabf"))
        B.vw = P.sb([128, DA], BF16, n("vw"))
        B.cw = P.sb([128, 2, NDC, 4], F32, n("cw"))
        B.arg = P.sb([128, 128], F32, n("arg"))
        B.Dm = P.sb([128, 128], F32, n("Dm"))
        B.Wbc = P.sb([128, 128], F32, n("Wbc"))
        B.scm = P.sb([128, 128], BF16, n("scm"))
        B.Cst = P.sb([128, NDC, DA], F32, n("Cst"))
        B.Cbf = P.sb([128, NDC, DA], BF16, n("Cbf"))
        B.sc = P.sb([128, 12], F32, n("sc"))
        B.h_sb = [P.sb([128, DH], F32, n("hsb%d" % i)) for i in range(2)]
        return B

    Bs = [alloc(u) for u in range(NU)]
    for u in range(NU):
        B = Bs[u]
        P.memset(B.Cst, 0.0)
        P.memset(B.Cbf, 0.0, eng="pool")
        P.memset(B.sc[:, 0:1], 0.0)
        with P.nc.allow_non_contiguous_dma(reason="small"):
            P.dma(B.cw[:, 0, :, 0:3], cwq[u].rearrange("(dc p) k -> p dc k", p=128))
            P.dma(B.cw[:, 1, :, 0:3], cwk[u].rearrange("(dc p) k -> p dc k", p=128))
            P.dma(B.cw[:, 0, :, 3:4], cbq[u].rearrange("(dc p) k -> p dc k", p=128))
            P.dma(B.cw[:, 1, :, 3:4], cbk[u].rearrange("(dc p) k -> p dc k", p=128))
    for c in range(NCH):
        for u in range(NU):
            B = Bs[u]
            sc = B.sc
            b = c % 2
            tsl = slice(c * 128, (c + 1) * 128)
            lo = c * 128 - 1
            hi = c * 128 + 129
            dlo, dhi = 0, HW
            if c in seg_first:
                lo += 1
                dlo = 1
            if c in seg_last:
                hi -= 1
                dhi = HW - 1
            for (src, dst) in ((qpT, B.qp_sb[b]), (kpT, B.kp_sb[b])):
                if c in seg_first:
                    P.memset(dst[:, :, 0:1], 0.0, eng="pool")
                if c in seg_last:
                    P.memset(dst[:, :, HW - 1:HW], 0.0, eng="pool")
                P.dma(dst[:, :, dlo:dhi], src[u, :, lo:hi].rearrange("(dc p) t -> p dc t", p=128))
            P.dma(B.va[b][:, 0:DH], vv[u, tsl, :])
            P.copy(B.va_bf, B.va[b], eng="pool")
            k_ = 0
            for qk, (src, dstT) in enumerate(((B.qp_sb[b], B.qT), (B.kp_sb[b], B.kT))):
                for dc in range(NDC):
                    a_ = B.acc[k_ % 2]
                    k_ += 1
                    w = B.cw[:, qk, dc, :]
                    P.ts(a_, src[:, dc, 1:129], w[:, 1:2], ALU.mult, w[:, 3:4], ALU.add)
                    P.stt(a_, src[:, dc, 0:128], w[:, 0:1], a_, ALU.mult, ALU.add)
                    P.stt(a_, src[:, dc, 2:130], w[:, 2:3], a_, ALU.mult, ALU.add)
                    P.act(dstT[:, dc, :], a_, AF.Silu)
            a_col = colsb[:, c, 0, u:u + 1]
            m_col = colsb[:, c, 2, u:u + 1]
            P.mm(ps_M, sel[u], R[:, 2, tsl])
            P.ts(B.arg, ps_M, a_col, ALU.subtract, 0.0, ALU.max)
            P.act(B.Dm, B.arg, AF.Exp, scale=-1.0)
            P.tt(B.Dm, B.Dm, U, ALU.mult)
            for dc in range(NDC):
                P.mm(ps_sc, B.kT[:, dc, :], B.qT[:, dc, :], start=(dc == 0), stop=(dc == NDC - 1))
            P.stt(B.scm, ps_sc, float(kscale), B.Dm, ALU.mult, ALU.mult)
            P.act(B.Wbc, ps_M, AF.Exp, scale=-1.0, bias=sc[:, 0:1])
            for dc in range(NDC):
                P.tt(B.qw[:, dc, :], B.qT[:, dc, :], B.Wbc, ALU.mult)
            P.mm(ps_o, B.scm, B.va_bf, start=True, stop=False)
            for dc in range(NDC):
                P.mm(ps_o, B.qw[:, dc, :], B.Cbf[:, dc, :], start=False, stop=(dc == NDC - 1))
            P.act(sc[:, 5:6], m_col, AF.Exp, scale=-1.0)
            P.copy(sc[:, 9:10], ps_o[:, DH:DA])
            P.stt(sc[:, 6:7], sc[:, 9:10], -1.0, sc[:, 9:10], ALU.mult, ALU.max)
            P.tt(sc[:, 7:8], sc[:, 6:7], sc[:, 5:6], ALU.max)
            P.recip(sc[:, 8:9], sc[:, 7:8])
            hb = B.h_sb[b]
            P.ts(hb, ps_o[:, 0:DH], sc[:, 8:9], ALU.mult)
            P.dma(ho[u, tsl, :], hb, q="pool", is_output=True)
            P.copy(sc[:, 1:2], ps_M[:, 127:128])
            P.ts(sc[:, 2:3], ps_M[:, 127:128], -1.0, ALU.mult)
            P.act(sc[:, 3:4], a_col, AF.Exp, bias=sc[:, 2:3])
            P.ts(sc[:, 3:4], sc[:, 3:4], float(kscale), ALU.mult)
            P.ts(B.vw, B.va[b], sc[:, 3:4], ALU.mult)
            for dc in range(NDC):
                P.transpose(ps_t[:, dc * 128:(dc + 1) * 128], B.kT[:, dc, :], identb)
            P.copy(B.ktok, ps_t, eng="act")
            P.act(sc[:, 4:5], sc[:, 0:1], AF.Exp, bias=sc[:, 2:3])
            for dc in range(NDC):
                pC = ps_C[dc % 2]
                P.mm(pC, B.ktok[:, dc * 128:(dc + 1) * 128], B.vw)
                P.stt(B.Cst[:, dc, :], B.Cst[:, dc, :], sc[:, 4:5], pC, ALU.mult, ALU.add)
                P.copy(B.Cbf[:, dc, :], B.Cst[:, dc, :], eng="act")
            P.copy(sc[:, 0:1], sc[:, 1:2])
    return P


def build_mlstm(NU, segs, DH, kscale, name="mlstm"):
    P = Prog(name)
    emit_mlstm(P, NU, segs, DH, kscale)
    return P


TWO_PI = 2.0 * math.pi
TOKR = 2176
LATR = 2048


def emit_s5(P, G, GB, NB, NBLK):
    NMC = NB * NBLK
    Uin = P.dram("Uin", [G, 128, NMC])
    lamre_d = P.dram("lamre", [64, G])
    lamim_d = P.dram("lamim", [64, G])
    lstep_d = P.dram("lstep", [64, G])
    Bre_d = P.dram("Bre", [64, G, 16])
    Bim_d = P.dram("Bim", [64, G, 16])
    Cre_d = P.dram("Cre", [64, G, 16])
    Cim_d = P.dram("Cim", [64, G, 16])
    Y = P.dram("Y", [G, 128, NMC], kind="ExternalOutput")
    NK = 24
    kk = [7, 6, 5, 4, 3, 2, 1, 0] + [1, 2, 3, 4, 5, 6, 7, 8] + [-1, -2, -3, -4, -5, -6, -7, -8]
    I64 = P.sb([64, 64], F32, "I64")
    P.memset(I64, 1.0, eng="pool")
    P.aselect(I64, I64, [[-1, 64]], ALU.is_equal, 0.0, 0, 1)
    BM = P.sb([128, 8, 16], F32, "BM")
    P.memset(BM, 1.0, eng="pool")
    P.aselect(BM, BM, [[16, 8], [0, 16]], ALU.is_ge, 0.0, 15, -1)
    Tm = P.sb([128, G, 128], BF16, "Tm")
    W2 = P.sb([128, G, 128], BF16, "W2")
    Vre = P.sb([64, G, 128], BF16, "Vre")
    Vim = P.sb([64, G, 128], BF16, "Vim")
    MU1 = P.sb([64, 2, G], F32, "MU1")
    MUa = P.sb([64, G], F32, "MUa")
    MUb = P.sb([64, G], F32, "MUb")
    lre = P.sb([64, GB], F32, "lre")
    lim = P.sb([64, GB], F32, "lim")
    lst = P.sb([64, GB], F32, "lst")
    Bre = P.sb([64, GB, 16], F32, "sBre")
    Bim = P.sb([64, GB, 16], F32, "sBim")
    Cre = P.sb([64, GB, 16], F32, "sCre")
    Cim = P.sb([64, GB, 16], F32, "sCim")
    step = P.sb([64, GB], F32, "step")
    rho = P.sb([64, GB], F32, "rho")
    th = P.sb([64, GB], F32, "th")
    PH = P.sb([64, GB, NK], F32, "PH")
    RH = P.sb([64, GB, NK], F32, "RH")
    mag = P.sb([64, GB, NK], F32, "mag")
    TT = P.sb([64, GB, NK, 2], F32, "TT")
    TI = P.sb([64, GB, NK, 2], I32, "TI")
    TF = P.sb([64, GB, NK, 2], F32, "TF")
    LPre = P.sb([64, GB, NK], F32, "LPre")
    LPim = P.sb([64, GB, NK], F32, "LPim")
    w = [P.sb([64, GB], F32, "w%d" % i) for i in range(6)]
    BTre = P.sb([64, GB, 16], F32, "BTre")
    BTim = P.sb([64, GB, 16], F32, "BTim")
    t16a = P.sb([64, GB, 16], F32, "t16a")
    t16b = P.sb([64, GB, 16], F32, "t16b")
    Are = P.sb([64, GB, 8, 16], F32, "Are")
    Aim = P.sb([64, GB, 8, 16], F32, "Aim")
    Apre = P.sb([64, GB, 8, 16], F32, "Apre")
    Apim = P.sb([64, GB, 8, 16], F32, "Apim")
    Dre = P.sb([64, GB, 8, 16], F32, "Dre")
    nDim = P.sb([64, GB, 8, 16], F32, "nDim")
    t1 = P.sb([64, GB, 8, 16], F32, "t1")
    t2 = P.sb([64, GB, 8, 16], F32, "t2")
    ps_T = [P.ps([128, 128], F32, "psT%d" % i) for i in range(2)]
    ps_W = [P.ps([128, 128], F32, "psW%d" % i) for i in range(2)]

    def bc16(v):
        return V(v.ap.unsqueeze(2).to_broadcast([64, GB, 16]), v.keys)

    def cmul(out_re, out_im, lp0, xr, xi, neg_im=False):
        a_re = V(LPre[:, :, lp0:lp0 + 8].ap.unsqueeze(3).to_broadcast([64, GB, 8, 16]), LPre.keys)
        a_im = V(LPim[:, :, lp0:lp0 + 8].ap.unsqueeze(3).to_broadcast([64, GB, 8, 16]), LPim.keys)
        b_re = V(xr.ap.unsqueeze(2).to_broadcast([64, GB, 8, 16]), xr.keys)
        b_im = V(xi.ap.unsqueeze(2).to_broadcast([64, GB, 8, 16]), xi.keys)
        P.tt(t1, a_re, b_re, ALU.mult)
        P.tt(t2, a_im, b_im, ALU.mult)
        P.tt(out_re, t1, t2, ALU.subtract)
        P.tt(t1, a_re, b_im, ALU.mult)
        P.tt(t2, a_im, b_re, ALU.mult)
        P.tt(out_im, t1, t2, ALU.add)
        if neg_im:
            P.ts(out_im, out_im, -1.0, ALU.mult)

    fl = lambda v, gi: v[:, gi, :, :].re("p a b -> p (a b)")
    for g0 in range(0, G, GB):
        gsl = slice(g0, g0 + GB)
        for d_, s_ in ((lre, lamre_d), (lim, lamim_d), (lst, lstep_d)):
            P.dma(d_, s_[:, gsl])
        for d_, s_ in ((Bre, Bre_d), (Bim, Bim_d), (Cre, Cre_d), (Cim, Cim_d)):
            P.dma(d_, s_[:, gsl, :])
        P.act(step, lst, AF.Exp)
        P.tt(rho, lre, step, ALU.mult)
        P.tt(th, lim, step, ALU.mult)
        for i, kv in enumerate(kk):
            P.ts(PH[:, :, i], th, float(kv), ALU.mult)
            P.ts(RH[:, :, i], rho, float(kv), ALU.mult)
        P.act(mag, RH, AF.Exp)
        P.ts(TT[:, :, :, 0], PH, 1.0 / TWO_PI, ALU.mult, 0.5, ALU.add)
        P.ts(TT[:, :, :, 1], PH, 1.0 / TWO_PI, ALU.mult, 0.75, ALU.add)
        P.copy(TI, TT)
        P.copy(TF, TI)
        P.tt(TT, TT, TF, ALU.subtract)
        P.ts(TF, TT, 0.0, ALU.is_lt)
        P.tt(TT, TT, TF, ALU.add)
        P.ts(TT, TT, TWO_PI, ALU.mult, -math.pi, ALU.add)
        P.ts(TT, TT, 3.1415925, ALU.min, -3.1415925, ALU.max)
        P.act(TT, TT, AF.Sin)
        P.tt(LPre, mag, TT[:, :, :, 1], ALU.mult)
        P.tt(LPim, mag, TT[:, :, :, 0], ALU.mult)
        abre = LPre[:, :, 8]
        abim = LPim[:, :, 8]
        P.tt(w[0], lre, lre, ALU.mult)
        P.tt(w[1], lim, lim, ALU.mult)
        P.tt(w[0], w[0], w[1], ALU.add)
        P.recip(w[0], w[0])
        P.ts(w[1], abre, -1.0, ALU.add)
        P.tt(w[2], w[1], lre, ALU.mult)
        P.tt(w[3], abim, lim, ALU.mult)
        P.tt(w[2], w[2], w[3], ALU.add)
        P.tt(w[2], w[2], w[0], ALU.mult)
        P.tt(w[4], abim, lre, ALU.mult)
        P.tt(w[5], w[1], lim, ALU.mult)
        P.tt(w[4], w[4], w[5], ALU.subtract)
        P.tt(w[4], w[4], w[0], ALU.mult)
        cre, cim = w[2], w[4]
        P.tt(t16a, Bre, bc16(cre), ALU.mult)
        P.tt(t16b, Bim, bc16(cim), ALU.mult)
        P.tt(BTre, t16a, t16b, ALU.subtract)
        P.tt(t16a, Bim, bc16(cre), ALU.mult)
        P.tt(t16b, Bre, bc16(cim), ALU.mult)
        P.tt(BTim, t16a, t16b, ALU.add)
        P.copy(MU1[:, 0, gsl], LPre[:, :, 15])
        P.copy(MU1[:, 1, gsl], LPre[:, :, 15])
        P.copy(MUb[:, gsl], LPim[:, :, 15])
        P.ts(MUa[:, gsl], LPim[:, :, 15], -1.0, ALU.mult)
        cmul(Are, Aim, 0, BTre, BTim)
        cmul(Apre, Apim, 16, BTre, BTim)
        cmul(Dre, nDim, 8, Cre, Cim, neg_im=True)
        for gi in range(GB):
            g = g0 + gi
            pT = ps_T[g % 2]
            P.mm(pT, fl(Apre, gi), fl(Dre, gi), start=True, stop=False)
            P.mm(pT, fl(Apim, gi), fl(nDim, gi), start=False, stop=True)
            P.tt(Tm[:, g, :], pT, BM.re("p a b -> p (a b)"), ALU.mult)
            pW = ps_W[g % 2]
            P.mm(pW[:, 0:64], fl(Are, gi), I64)
            P.mm(pW[:, 64:128], fl(Aim, gi), I64)
            P.copy(W2[:, g, :], pW, eng="act")
        P.copy(Vre[:, gsl, :], Dre.re("p g a b -> p g (a b)"), eng="act")
        P.copy(Vim[:, gsl, :], nDim.re("p g a b -> p g (a b)"), eng="act")
    GQ = 4
    XE = P.sb([64, 2, G, NB + 1], F32, "XE")
    P.memset(XE[:, :, :, 0], 0.0)
    Ebf = P.sb([64, 2, G, NB], BF16, "Ebf")
    Ubf = P.sb([128, G, NB], BF16, "Ubf")
    ust = [P.sb([128, GQ, NB], F32, "ust%d" % i) for i in range(2)]
    yst = [P.sb([128, GQ, NB], F32, "yst%d" % i) for i in range(2)]
    r1 = P.sb([64, 2, G], F32, "r1")
    r2 = P.sb([64, 2, G], F32, "r2")
    ps_xr = [P.ps_alias("psT%d" % i, [64, GQ, NB]) for i in range(2)]
    ps_xi = [P.ps_alias("psW%d" % i, [64, GQ, NB]) for i in range(2)]
    ps_y = [P.ps([128, GQ, NB], F32, "psy%d" % i) for i in range(2)]
    qi = 0
    for blk in range(NBLK):
        csl = slice(blk * NB, (blk + 1) * NB)
        for g0 in range(0, G, GQ):
            b = qi % 2
            qi += 1
            P.dma(ust[b], Uin[g0:g0 + GQ, :, csl].rearrange("g p c -> p g c"))
            P.copy(Ubf[:, g0:g0 + GQ, :], ust[b], eng="pool")
            for gi in range(GQ):
                g = g0 + gi
                P.mm(ps_xr[b][:, gi, :], W2[:, g, 0:64], Ubf[:, g, :])
                P.mm(ps_xi[b][:, gi, :], W2[:, g, 64:128], Ubf[:, g, :])
            P.copy(XE[:, 0, g0:g0 + GQ, 1:NB + 1], ps_xr[b], eng="act")
            P.copy(XE[:, 1, g0:g0 + GQ, 1:NB + 1], ps_xi[b], eng="dve")
        for c in range(NB):
            prev = XE[:, :, :, c]
            cur = XE[:, :, :, c + 1]
            P.tt(r1, MU1, prev, ALU.mult)
            P.tt(r2[:, 0, :], MUa, XE[:, 1, :, c], ALU.mult)
            P.tt(r2[:, 1, :], MUb, XE[:, 0, :, c], ALU.mult)
            P.tt(r1, r1, r2, ALU.add)
            P.tt(cur, cur, r1, ALU.add)
        P.copy(Ebf, XE[:, :, :, 0:NB], eng="act")
        for g0 in range(0, G, GQ):
            b = qi % 2
            qi += 1
            for gi in range(GQ):
                g = g0 + gi
                py = ps_y[b][:, gi, :]
                P.mm(py, Tm[:, g, :], Ubf[:, g, :], start=True, stop=False)
                P.mm(py, Vre[:, g, :], Ebf[:, 0, g, :], start=False, stop=False)
                P.mm(py, Vim[:, g, :], Ebf[:, 1, g, :], start=False, stop=True)
            P.copy(yst[b], ps_y[b], eng="act")
            P.dma(Y[g0:g0 + GQ, :, csl].rearrange("g p c -> p g c"), yst[b], q="pool", is_output=True)
        if blk + 1 < NBLK:
            P.copy(XE[:, :, :, 0], XE[:, :, :, NB])
    return P


def emit_s5v2(P, G, GB, rev, ST, c0, Od, prm):
    lamre_d, lamim_d, lstep_d = prm["lamre"], prm["lamim"], prm["lstep"]
    Bre_d, Bim_d, Cre_d, Cim_d = prm["Bre"], prm["Bim"], prm["Cre"], prm["Cim"]
    NK = 24
    if not rev:
        kk = [7, 6, 5, 4, 3, 2, 1, 0] + [1, 2, 3, 4, 5, 6, 7, 8] + [-1, -2, -3, -4, -5, -6, -7, -8]
    else:
        kk = [0, 1, 2, 3, 4, 5, 6, 7] + [8, 7, 6, 5, 4, 3, 2, 1] + [-8, -7, -6, -5, -4, -3, -2, -1]
    i1 = 8 + kk[8:16].index(1)
    i8 = 8 + kk[8:16].index(8)
    I64 = P.sb([64, 64], F32, "I64")
    P.memset(I64, 1.0, eng="pool")
    P.aselect(I64, I64, [[-1, 64]], ALU.is_equal, 0.0, 0, 1)
    BM = P.sb([128, 8, 16], F32, "BM")
    P.memset(BM, 1.0, eng="pool")
    if not rev:
        P.aselect(BM, BM, [[16, 8], [0, 16]], ALU.is_ge, 0.0, 15, -1)
    else:
        P.aselect(BM, BM, [[-16, 8], [0, 16]], ALU.is_ge, 0.0, 0, 1)
    Tm = P.sb([128, G, 128], BF16, "Tm")
    W2 = P.sb([128, G, 128], BF16, "W2")
    Vre = P.sb([64, G, 128], BF16, "Vre")
    Vim = P.sb([64, G, 128], BF16, "Vim")
    MU1 = P.sb([64, 2, G], F32, "MU1")
    MUa = P.sb([64, G], F32, "MUa")
    MUb = P.sb([64, G], F32, "MUb")
    lre = P.sb([64, GB], F32, "lre")
    lim = P.sb([64, GB], F32, "lim")
    lst = P.sb([64, GB], F32, "lst")
    Bre = P.sb([64, GB, 16], F32, "sBre")
    Bim = P.sb([64, GB, 16], F32, "sBim")
    Cre = P.sb([64, GB, 16], F32, "sCre")
    Cim = P.sb([64, GB, 16], F32, "sCim")
    step = P.sb([64, GB], F32, "step")
    rho = P.sb([64, GB], F32, "rho")
    th = P.sb([64, GB], F32, "th")
    PH = P.sb([64, GB, NK], F32, "PH")
    RH = P.sb([64, GB, NK], F32, "RH")
    mag = P.sb([64, GB, NK], F32, "mag")
    TT = P.sb([64, GB, NK, 2], F32, "TT")
    TI = P.sb([64, GB, NK, 2], I32, "TI")
    TF = P.sb([64, GB, NK, 2], F32, "TF")
    LPre = P.sb([64, GB, NK], F32, "LPre")
    LPim = P.sb([64, GB, NK], F32, "LPim")
    w = [P.sb([64, GB], F32, "w%d" % i) for i in range(6)]
    BTre = P.sb([64, GB, 16], F32, "BTre")
    BTim = P.sb([64, GB, 16], F32, "BTim")
    t16a = P.sb([64, GB, 16], F32, "t16a")
    t16b = P.sb([64, GB, 16], F32, "t16b")
    Are = P.sb([64, GB, 8, 16], F32, "Are")
    Aim = P.sb([64, GB, 8, 16], F32, "Aim")
    Apre = P.sb([64, GB, 8, 16], F32, "Apre")
    Apim = P.sb([64, GB, 8, 16], F32, "Apim")
    Dre = P.sb([64, GB, 8, 16], F32, "Dre")
    nDim = P.sb([64, GB, 8, 16], F32, "nDim")
    t1 = P.sb([64, GB, 8, 16], F32, "t1")
    t2 = P.sb([64, GB, 8, 16], F32, "t2")
    ps_T = [P.ps([128, 128], F32, "psT%d" % i) for i in range(2)]
    ps_W = [P.ps([128, 128], F32, "psW%d" % i) for i in range(2)]

    def bc16(v):
        return V(v.ap.unsqueeze(2).to_broadcast([64, GB, 16]), v.keys)

    def cmul(out_re, out_im, lp0, xr, xi, neg_im=False):
        a_re = V(LPre[:, :, lp0:lp0 + 8].ap.unsqueeze(3).to_broadcast([64, GB, 8, 16]), LPre.keys)
        a_im = V(LPim[:, :, lp0:lp0 + 8].ap.unsqueeze(3).to_broadcast([64, GB, 8, 16]), LPim.keys)
        b_re = V(xr.ap.unsqueeze(2).to_broadcast([64, GB, 8, 16]), xr.keys)
        b_im = V(xi.ap.unsqueeze(2).to_broadcast([64, GB, 8, 16]), xi.keys)
        P.tt(t1, a_re, b_re, ALU.mult)
        P.tt(t2, a_im, b_im, ALU.mult)
        P.tt(out_re, t1, t2, ALU.subtract)
        P.tt(t1, a_re, b_im, ALU.mult)
        P.tt(t2, a_im, b_re, ALU.mult)
        P.tt(out_im, t1, t2, ALU.add)
        if neg_im:
            P.ts(out_im, out_im, -1.0, ALU.mult)

    fl = lambda v, gi: v[:, gi, :, :].re("p a b -> p (a b)")
    for g0 in range(0, G, GB):
        gsl = slice(g0, g0 + GB)
        for d_, s_ in ((lre, lamre_d), (lim, lamim_d), (lst, lstep_d)):
            P.dma(d_, s_[:, gsl])
        for d_, s_ in ((Bre, Bre_d), (Bim, Bim_d), (Cre, Cre_d), (Cim, Cim_d)):
            P.dma(d_, s_[:, gsl, :])
        P.act(step, lst, AF.Exp)
        P.tt(rho, lre, step, ALU.mult)
        P.tt(th, lim, step, ALU.mult)
        for i, kv in enumerate(kk):
            P.ts(PH[:, :, i], th, float(kv), ALU.mult)
            P.ts(RH[:, :, i], rho, float(kv), ALU.mult)
        P.act(mag, RH, AF.Exp)
        P.ts(TT[:, :, :, 0], PH, 1.0 / TWO_PI, ALU.mult, 0.5, ALU.add)
        P.ts(TT[:, :, :, 1], PH, 1.0 / TWO_PI, ALU.mult, 0.75, ALU.add)
        P.copy(TI, TT)
        P.copy(TF, TI)
        P.tt(TT, TT, TF, ALU.subtract)
        P.ts(TF, TT, 0.0, ALU.is_lt)
        P.tt(TT, TT, TF, ALU.add)
        P.ts(TT, TT, TWO_PI, ALU.mult, -math.pi, ALU.add)
        P.ts(TT, TT, 3.1415925, ALU.min, -3.1415925, ALU.max)
        P.act(TT, TT, AF.Sin)
        P.tt(LPre, mag, TT[:, :, :, 1], ALU.mult)
        P.tt(LPim, mag, TT[:, :, :, 0], ALU.mult)
        abre = LPre[:, :, i1]
        abim = LPim[:, :, i1]
        P.tt(w[0], lre, lre, ALU.mult)
        P.tt(w[1], lim, lim, ALU.mult)
        P.tt(w[0], w[0], w[1], ALU.add)
        P.recip(w[0], w[0])
        P.ts(w[1], abre, -1.0, ALU.add)
        P.tt(w[2], w[1], lre, ALU.mult)
        P.tt(w[3], abim, lim, ALU.mult)
        P.tt(w[2], w[2], w[3], ALU.add)
        P.tt(w[2], w[2], w[0], ALU.mult)
        P.tt(w[4], abim, lre, ALU.mult)
        P.tt(w[5], w[1], lim, ALU.mult)
        P.tt(w[4], w[4], w[5], ALU.subtract)
        P.tt(w[4], w[4], w[0], ALU.mult)
        cre, cim = w[2], w[4]
        P.tt(t16a, Bre, bc16(cre), ALU.mult)
        P.tt(t16b, Bim, bc16(cim), ALU.mult)
        P.tt(BTre, t16a, t16b, ALU.subtract)
        P.tt(t16a, Bim, bc16(cre), ALU.mult)
        P.tt(t16b, Bre, bc16(cim), ALU.mult)
        P.tt(BTim, t16a, t16b, ALU.add)
        P.copy(MU1[:, 0, gsl], LPre[:, :, i8])
        P.copy(MU1[:, 1, gsl], LPre[:, :, i8])
        P.copy(MUb[:, gsl], LPim[:, :, i8])
        P.ts(MUa[:, gsl], LPim[:, :, i8], -1.0, ALU.mult)
        cmul(Are, Aim, 0, BTre, BTim)
        cmul(Apre, Apim, 16, BTre, BTim)
        cmul(Dre, nDim, 8, Cre, Cim, neg_im=True)
        for gi in range(GB):
            g = g0 + gi
            pT = ps_T[g % 2]
            P.mm(pT, fl(Apre, gi), fl(Dre, gi), start=True, stop=False)
            P.mm(pT, fl(Apim, gi), fl(nDim, gi), start=False, stop=True)
            P.tt(Tm[:, g, :], pT, BM.re("p a b -> p (a b)"), ALU.mult)
            pW = ps_W[g % 2]
            P.mm(pW[:, 0:64], fl(Are, gi), I64)
            P.mm(pW[:, 64:128], fl(Aim, gi), I64)
            P.copy(W2[:, g, :], pW, eng="act")
        P.copy(Vre[:, gsl, :], Dre.re("p g a b -> p g (a b)"), eng="act")
        P.copy(Vim[:, gsl, :], nDim.re("p g a b -> p g (a b)"), eng="act")
    NBM = 128
    Iid = P.sb([128, 128], F32, "s5I")
    P.memset(Iid, 1.0, eng="pool")
    P.aselect(Iid, Iid, [[-1, 128]], ALU.is_equal, 0.0, 0, 1)
    XE = P.sb([64, 2, G, NBM + 1], F32, "XE")
    Ebf = P.sb([64, 2, G, NBM], BF16, "Ebf")
    Ubf = P.sb([128, G, NBM], BF16, "Ubf")
    utile = P.sb([128, 8, G * 16], F32, "utile")
    ybuf = P.sb([128, 8, G * 16], F32, "ybuf")
    ugs = [P.sb([128, 8, 16], F32, "ugs%d" % i) for i in range(2)]
    r1 = P.sb([64, 2, G], F32, "r1")
    r2 = P.sb([64, 2, G], F32, "r2")
    ps_u = [P.ps_alias("psT%d" % i, [128, NBM]) for i in range(2)]
    ps_xr = P.ps([64, 4, NBM], F32, "psxr")
    ps_xi = P.ps([64, 4, NBM], F32, "psxi")
    ps_y = [P.ps([128, 4, 128], F32, "psy%d" % i) for i in range(2)]
    GC = G * 16
    lat_blocks = [0, 1, 2, 3] if not rev else [3, 2, 1, 0]
    blocks = [("ctx", 0)] + [("lat", b) for b in lat_blocks]
    carry_col = 0 if not rev else NBM
    first = True
    for kind, bi in blocks:
        nb = 32 if kind == "ctx" else NBM
        if kind == "ctx":
            for rho in range(2):
                P.dma(utile[rho * 16:(rho + 1) * 16, :, :],
                      ST[rho * TOKR + LATR:rho * TOKR + LATR + 128, c0:c0 + GC].rearrange("(c j) ch -> c j ch", j=8))
        else:
            rho, hb = bi // 2, bi % 2
            r0 = rho * TOKR + hb * 1024
            P.dma(utile[:, :, :], ST[r0:r0 + 1024, c0:c0 + GC].rearrange("(c j) ch -> c j ch", j=8))
        xoff = 1 if not rev else 0
        ccol = 0 if not rev else nb
        if first:
            P.memset(XE[:, :, :, ccol], 0.0)
            first = False
        for g0 in range(0, G, 4):
            for gi in range(4):
                g = g0 + gi
                pu = ps_u[g % 2]
                ug = ugs[g % 2]
                P.copy(ug[0:nb, :, :], utile[0:nb, :, g * 16:(g + 1) * 16], eng="pool")
                P.mm(pu[:, 0:nb], ug[0:nb, :, :].rearrange("c j p -> c (j p)"), Iid[0:nb, 0:nb])
                P.copy(Ubf[:, g, 0:nb], pu[:, 0:nb], eng="act" if g % 2 else "dve")
                P.mm(ps_xr[:, gi, 0:nb], W2[:, g, 0:64], Ubf[:, g, 0:nb])
                P.mm(ps_xi[:, gi, 0:nb], W2[:, g, 64:128], Ubf[:, g, 0:nb])
            P.copy(XE[:, 0, g0:g0 + 4, xoff:xoff + nb], ps_xr[:, :, 0:nb], eng="act")
            P.copy(XE[:, 1, g0:g0 + 4, xoff:xoff + nb], ps_xi[:, :, 0:nb], eng="dve")
        order = range(nb) if not rev else range(nb - 1, -1, -1)
        for c in order:
            pc = c if not rev else c + 1
            cc_ = c + xoff
            P.tt(r1, MU1, XE[:, :, :, pc], ALU.mult)
            P.tt(r2[:, 0, :], MUa, XE[:, 1, :, pc], ALU.mult)
            P.tt(r2[:, 1, :], MUb, XE[:, 0, :, pc], ALU.mult)
            P.tt(r1, r1, r2, ALU.add)
            P.tt(XE[:, :, :, cc_], XE[:, :, :, cc_], r1, ALU.add)
        sh = 0 if not rev else 1
        P.copy(Ebf[:, :, :, 0:nb], XE[:, :, :, sh:sh + nb], eng="act")
        for g0 in range(0, G, 4):
            py = ps_y[(g0 // 4) % 2]
            for gi in range(4):
                g = g0 + gi
                P.mm(py[0:nb, gi, :], Ubf[:, g, 0:nb], Tm[:, g, :], start=True, stop=False)
                P.mm(py[0:nb, gi, :], Ebf[:, 0, g, 0:nb], Vre[:, g, :], start=False, stop=False)
                P.mm(py[0:nb, gi, :], Ebf[:, 1, g, 0:nb], Vim[:, g, :], start=False, stop=True)
            P.copy(ybuf[0:nb, :, g0 * 16:(g0 + 4) * 16].rearrange("c j (g p) -> c g j p", p=16),
                   py[0:nb, :, :].rearrange("c g (j p) -> c g j p", p=16), eng="act" if (g0 // 4) % 2 else "dve")
        if kind == "ctx":
            for rho in range(2):
                P.dma(Od[rho * TOKR + LATR:rho * TOKR + LATR + 128, :].rearrange("(c j) ch -> c j ch", j=8),
                      ybuf[rho * 16:(rho + 1) * 16, :, :], q="pool")
        else:
            P.dma(Od[r0:r0 + 1024, :].rearrange("(c j) ch -> c j ch", j=8), ybuf[:, :, :], q="pool")
        last_col = nb if not rev else 0
        nxt_nb = NBM
        nxt_ccol = 0 if not rev else nxt_nb
        if (kind, bi) != blocks[-1]:
            if last_col != nxt_ccol:
                P.copy(XE[:, :, :, nxt_ccol], XE[:, :, :, last_col])
    return P


def build_s5(G, GB, NB, NBLK, name="s5"):
    P = Prog(name)
    emit_s5(P, G, GB, NB, NBLK)
    return P


EPS = 1e-6


class TokCtx:
    def __init__(self, P, D, DFF, TMAX, nwst=3, nwbf=4, look=3):
        self.P, self.D, self.DFF, self.T = P, D, DFF, TMAX
        self.KC = D // 128
        self.JC = DFF // 128
        KC, JC, T = self.KC, self.JC, TMAX
        self.ones = P.sb([128, 128], F32, "ones")
        P.memset(self.ones, 1.0)
        self.ones_bf = P.sb([128, 128], BF16, "ones_bf")
        P.memset(self.ones_bf, 1.0)
        self.XB = P.sb([128, KC, T], F32, "XB")
        self.h = P.sb([128, KC, T], BF16, "h")
        self.hid = P.sb([128, JC, T], BF16, "hid")
        self.sq = [P.sb([128, T], F32, "sq%d" % i) for i in range(2)]
        self.rstd = P.sb([128, T], F32, "rstd")
        self.tmp = [P.sb([128, T], F32, "tmp%d" % i) for i in range(4)]
        self.sg = [P.sb([128, T], F32, "sg%d" % i) for i in range(2)]
        self.xr = [P.sb([128, T], F32, "xr%d" % i) for i in range(4)]
        self.WSZ = KC * 128
        self.wst = [P.sb([128, self.WSZ], F32, "wst%d" % i) for i in range(nwst)]
        self.wbf = [P.sb([128, self.WSZ], BF16, "wbf%d" % i) for i in range(nwbf)]
        self.look = look
        self.wi = 0
        self.ps_g = [P.ps([128, T], F32, "psg%d" % i) for i in range(2)]
        self.ps_u = [P.ps([128, T], F32, "psu%d" % i) for i in range(2)]
        self.ps_y = [P.ps([128, T], F32, "psy%d" % i) for i in range(2)]
        self.ps_s = P.ps([128, T], F32, "pss")
        self.ps_m = P.ps([128, 512], F32, "psm")
        self.cast_i = 0
        self.eps_col = P.sb([128, 1], F32, "epsc")
        P.memset(self.eps_col, EPS)

    def stream(self, items, look=None):
        loaded = {}
        look = look or self.look

        def get(i):
            for k in range(i, min(i + look, len(items))):
                if k not in loaded:
                    loaded[k] = self.load_w(*items[k])
            return loaded[i]
        return get

    def load_w(self, src_ap, nrow_chunks, ncols):
        P = self.P
        i = self.wi % len(self.wst)
        ib = self.wi % len(self.wbf)
        self.wi += 1
        n = nrow_chunks * ncols
        assert n <= self.WSZ
        st = self.wst[i][:, 0:n].re("p (a b) -> p a b", b=ncols)
        bf = self.wbf[ib][:, 0:n].re("p (a b) -> p a b", b=ncols)
        P.dma(st, src_ap, q="sp")
        self.cast_i += 1
        if self.cast_i % 3 == 0:
            P.copy(bf, st, eng="act")
        else:
            P.copy(bf, st, eng="dve")
        return bf


def rms_rstd(C, chunks, T, Dtot, out_rstd, ps=None, big=None):
    P = C.P
    ps = ps or C.ps_s
    n = len(chunks)
    if big is not None and n >= 4 and C.JC >= n:
        sqb = C.hid[:, 0:n, 0:T]
        h1 = n // 2
        P.act(sqb[:, 0:h1, :], big[:, 0:h1, :], AF.Square)
        P.tt(sqb[:, h1:n, :], big[:, h1:n, :], big[:, h1:n, :], ALU.mult)
        for i in range(n):
            P.mm(ps[:, 0:T], C.ones_bf, sqb[:, i, :], start=(i == 0), stop=(i == n - 1))
    else:
        for i, src in enumerate(chunks):
            sq = C.sq[i % 2][:, 0:T]
            P.act(sq, src, AF.Square)
            P.mm(ps[:, 0:T], C.ones, sq, start=(i == 0), stop=(i == n - 1))
    P.act(out_rstd[:, 0:T], ps[:, 0:T], AF.Sqrt, scale=1.0 / Dtot, bias=C.eps_col)
    P.recip(out_rstd[:, 0:T], out_rstd[:, 0:T])


def sublayer_in(C, T, Ain, shift, col):
    P = C.P
    KC = C.KC
    rms_rstd(C, [C.XB[:, kc, 0:T] for kc in range(KC)], T, C.D, C.rstd, big=C.XB[:, :, 0:T])
    for kc in range(KC):
        t = C.tmp[kc % 4][:, 0:T]
        P.stt(t, C.XB[:, kc, 0:T], Ain[:, kc, col:col + 1], C.rstd[:, 0:T], ALU.mult, ALU.mult)
        P.act(C.h[:, kc, 0:T], t, AF.Identity, bias=shift[:, kc, col:col + 1])


def sublayer_out(C, T, Gout, col, resid):
    P = C.P
    KC = C.KC
    rms_rstd(C, [C.XB[:, kc, 0:T] for kc in range(KC)], T, C.D, C.rstd, big=C.XB[:, :, 0:T])
    for kc in range(KC):
        xr = C.xr[kc % 4][:, 0:T]
        P.dma(xr, resid(kc), q="sp")
        t = C.tmp[kc % 4][:, 0:T]
        P.stt(t, C.XB[:, kc, 0:T], Gout[:, kc, col:col + 1], C.rstd[:, 0:T], ALU.mult, ALU.mult)
        P.tt(C.XB[:, kc, 0:T], xr, t, ALU.add, eng="pool")


def split_parts(n, maxp):
    k = (n + maxp - 1) // maxp
    base, rem = divmod(n, k)
    out, s = [], 0
    for i in range(k):
        sz = base + (1 if i < rem else 0)
        out.append((s, s + sz))
        s += sz
    return out


def ffn_core(C, T, wg, wu, wd, tiled=False):
    P = C.P
    KC, JC = C.KC, C.JC
    parts = split_parts(JC, C.WSZ // 128)
    items = []
    if tiled:
        for j in range(JC):
            items.append((wg[j], KC, 128))
            items.append((wu[j], KC, 128))
        for m in range(KC):
            for (j0, j1) in parts:
                items.append((wd[m][:, j0:j1, :], j1 - j0, 128))
    else:
        wg_v = wg.rearrange("(kc p) n -> p kc n", p=128)
        wu_v = wu.rearrange("(kc p) n -> p kc n", p=128)
        wd_v = wd.rearrange("(jc p) n -> p jc n", p=128)
        for j in range(JC):
            items.append((wg_v[:, :, j * 128:(j + 1) * 128], KC, 128))
            items.append((wu_v[:, :, j * 128:(j + 1) * 128], KC, 128))
        for m in range(KC):
            for (j0, j1) in parts:
                items.append((wd_v[:, j0:j1, m * 128:(m + 1) * 128], j1 - j0, 128))
    get = C.stream(items)
    for j in range(JC):
        wgb = get(2 * j)
        wub = get(2 * j + 1)
        pg = C.ps_g[j % 2][:, 0:T]
        pu = C.ps_u[j % 2][:, 0:T]
        for kc in range(KC):
            P.mm(pg, wgb[:, kc, :], C.h[:, kc, 0:T], start=(kc == 0), stop=(kc == KC - 1))
        for kc in range(KC):
            P.mm(pu, wub[:, kc, :], C.h[:, kc, 0:T], start=(kc == 0), stop=(kc == KC - 1))
        sg = C.sg[j % 2][:, 0:T]
        P.act(sg, pg, AF.Silu)
        P.tt(C.hid[:, j, 0:T], sg, pu, ALU.mult)
    npart = len(parts)
    for m in range(KC):
        py = C.ps_y[m % 2][:, 0:T]
        for hi, (j0, j1) in enumerate(parts):
            wdb = get(2 * JC + npart * m + hi)
            for j in range(j0, j1):
                P.mm(py, wdb[:, j - j0, :], C.hid[:, j, 0:T], start=(j == 0), stop=(j == JC - 1))
        P.copy(C.XB[:, m, 0:T], py, eng="act")


def proj(C, T, w, nk, ncol, rhs, sink, bias_fn=None, tiled=False):
    P = C.P
    nch = (ncol + 127) // 128
    if tiled:
        items = [(w[j], nk, 128) for j in range(nch)]
    else:
        w_v = w.rearrange("(kc p) n -> p kc n", p=128)
        items = [(w_v[:, :, j * 128:min(ncol, (j + 1) * 128)], nk, min(128, ncol - j * 128)) for j in range(nch)]
    get = C.stream(items)
    for j in range(nch):
        cw = min(128, ncol - j * 128)
        wb = get(j)
        pg = C.ps_g[j % 2][0:cw, 0:T]
        for kc in range(nk):
            P.mm(pg, wb[:, kc, :], rhs(kc), start=(kc == 0), stop=(kc == nk - 1))
        sink(j, cw, pg)


def in_proj(C, T, w_in, ncol, sT_out, t0):
    P = C.P

    def sink(j, cw, pg):
        so = C.sg[j % 2][0:cw, 0:T]
        P.copy(so, pg, eng="act")
        P.dma(sT_out[j * 128:j * 128 + cw, t0:t0 + T], so, q="pool", is_output=True)
    proj(C, T, w_in, C.KC, ncol, lambda kc: C.h[:, kc, 0:T], sink)


def load_consts(C, norm_pre, norm_post):
    P = C.P
    C.npre = P.sb([128, 3, C.KC], F32, "npre")
    C.npost = P.sb([128, 3, C.KC], F32, "npost")
    with P.nc.allow_non_contiguous_dma(reason="small const loads"):
        P.dma(C.npre, norm_pre.rearrange("s (kc p) -> p s kc", p=128))
        P.dma(C.npost, norm_post.rearrange("s (kc p) -> p s kc", p=128))


def compute_mod(C, cT, w_mod, b_mod, mod_out=None, tiled=False):
    P = C.P
    KC = C.KC
    NJ = 9 * KC
    C.mod = P.sb([128, NJ, 2], F32, "mod")
    cs = P.sb([128, KC, 2], F32, "csilu")
    bm = P.sb([128, NJ], F32, "bmod")
    with P.nc.allow_non_contiguous_dma(reason="small const loads"):
        P.dma(cs, cT.rearrange("(kc p) c -> p kc c", p=128))
        P.dma(bm, b_mod.rearrange("(j p) -> p j", p=128))
    P.act(cs, cs, AF.Silu)
    w_v = None if tiled else w_mod.rearrange("(kc p) n -> p kc n", p=128)
    assert 2 * NJ <= 512
    psm = C.ps_m[:, 0:2 * NJ].re("p (j c) -> p j c", c=2)
    for j in range(NJ):
        i = C.wi % len(C.wst)
        C.wi += 1
        st = C.wst[i][:, 0:KC * 128].re("p (a b) -> p a b", b=128)
        P.dma(st, w_mod[j] if tiled else w_v[:, :, j * 128:(j + 1) * 128], q="sp")
        for kc in range(KC):
            P.mm(psm[:, j, :], st[:, kc, :], cs[:, kc, :], start=(kc == 0), stop=(kc == KC - 1))
    for c in range(2):
        P.tt(C.mod[:, :, c], psm[:, :, c], bm, ALU.add)
    if mod_out is not None:
        P.dma(mod_out, C.mod, q="pool", is_output=True)


def load_mod(C, modi):
    P = C.P
    C.mod = P.sb([128, 9 * C.KC, 2], F32, "mod")
    P.dma(C.mod, modi)


def derive_sub(C, s, weight):
    P = C.P
    KC = C.KC
    Ain = P.sb([128, KC, 2], F32, "Ain%d" % s)
    Gout = P.sb([128, KC, 2], F32, "Gout%d" % s)
    shift = C.mod[:, (3 * s) * KC:(3 * s + 1) * KC, :]
    scale = C.mod[:, (3 * s + 1) * KC:(3 * s + 2) * KC, :]
    gate = C.mod[:, (3 * s + 2) * KC:(3 * s + 3) * KC, :]
    for c in range(2):
        P.stt(Ain[:, :, c], scale[:, :, c], 1.0, C.npre[:, s, :], ALU.add, ALU.mult)
        P.stt(Gout[:, :, c], gate[:, :, c], float(weight), C.npost[:, s, :], ALU.mult, ALU.mult)
    return Ain, shift, Gout


def build_stageA(D, DFF, NCOL, tiles, Ttot, name="stageA"):
    P = Prog(name)
    xT = P.dram("xT", [D, Ttot])
    cT = P.dram("cT", [D, 2])
    w_mod = P.dram("w_mod", [D, 9 * D])
    b_mod = P.dram("b_mod", [9 * D])
    norm_pre = P.dram("norm_pre", [3, D])
    norm_post = P.dram("norm_post", [3, D])
    wg = P.dram("wg", [D, DFF])
    wu = P.dram("wu", [D, DFF])
    wd = P.dram("wd", [DFF, D])
    w_in = P.dram("w_in", [D, NCOL])
    x1T = P.dram("x1T", [D, Ttot], kind="ExternalOutput")
    sT = P.dram("sT", [NCOL, Ttot], kind="ExternalOutput")
    KC = D // 128
    modo = P.dram("modo", [128, 9 * KC, 2], kind="ExternalOutput")
    TMAX = max(t[2] for t in tiles)
    C = TokCtx(P, D, DFF, TMAX)
    load_consts(C, norm_pre, norm_post)
    compute_mod(C, cT, w_mod, b_mod, modo)
    A0, sh0, G0 = derive_sub(C, 0, 0.5)
    A1, sh1, G1 = derive_sub(C, 1, 1.0)
    xv = xT.rearrange("(kc p) t -> p kc t", p=128)
    x1v = x1T.rearrange("(kc p) t -> p kc t", p=128)
    for (col, t0, T) in tiles:
        P.dma(C.XB[:, :, 0:T], xv[:, :, t0:t0 + T], q="sp")
        sublayer_in(C, T, A0, sh0, col)
        ffn_core(C, T, wg, wu, wd)
        sublayer_out(C, T, G0, col, lambda kc: xv[:, kc, t0:t0 + T])
        P.dma(x1v[:, :, t0:t0 + T], C.XB[:, :, 0:T], q="pool", is_output=True)
        sublayer_in(C, T, A1, sh1, col)
        in_proj(C, T, w_in, NCOL, sT, t0)
    return P


def head_norm_chunks(C, T, osum, gvec, gcols, gact, center, dst_chunks, HD):
    P = C.P
    n = len(dst_chunks)
    if center:
        for i in range(n):
            P.mm(C.ps_m[:, 0:T], C.ones, osum[:, i, 0:T], start=(i == 0), stop=(i == n - 1))
        for i in range(n):
            sq = C.sq[i % 2][:, 0:T]
            P.act(sq, osum[:, i, 0:T], AF.Square)
            P.mm(C.ps_s[:, 0:T], C.ones, sq, start=(i == 0), stop=(i == n - 1))
        mean = C.xr[0][:, 0:T]
        var = C.xr[1][:, 0:T]
        P.ts(mean, C.ps_m[:, 0:T], 1.0 / HD, ALU.mult)
        P.tt(var, mean, mean, ALU.mult)
        P.stt(var, C.ps_s[:, 0:T], 1.0 / HD, var, ALU.mult, ALU.subtract)
        P.act(C.rstd[:, 0:T], var, AF.Sqrt, bias=C.eps_col)
        P.recip(C.rstd[:, 0:T], C.rstd[:, 0:T])
        for i in range(n):
            t = C.tmp[i % 2][:, 0:T]
            P.tt(t, osum[:, i, 0:T], mean, ALU.subtract)
            P.tt(t, t, C.rstd[:, 0:T], ALU.mult)
            P.stt(dst_chunks[i], t, gvec[:, gcols[i]:gcols[i] + 1], gact[i], ALU.mult, ALU.mult)
    else:
        rms_rstd(C, [osum[:, i, 0:T] for i in range(n)], T, HD, C.rstd)
        for i in range(n):
            t = C.tmp[i % 2][:, 0:T]
            P.tt(t, osum[:, i, 0:T], C.rstd[:, 0:T], ALU.mult)
            P.stt(dst_chunks[i], t, gvec[:, gcols[i]:gcols[i] + 1], gact[i], ALU.mult, ALU.mult)


def build_stageC(D, DFF, tiles, Ttot, parity, HD=256, name="stageC"):
    P = Prog(name)
    KC = D // 128
    H2 = D // 2
    KH = KC // 2
    x1T = P.dram("x1T", [D, Ttot])
    modi = P.dram("modi", [128, 9 * KC, 2])
    norm_pre = P.dram("norm_pre", [3, D])
    norm_post = P.dram("norm_post", [3, D])
    w_out = P.dram("w_out", [D, D])
    wg = P.dram("wg", [D, DFF])
    wu = P.dram("wu", [D, DFF])
    wd = P.dram("wd", [DFF, D])
    oAf = P.dram("oAf", [H2, Ttot])
    oAb = P.dram("oAb", [H2, Ttot])
    oBf = P.dram("oBf", [H2, Ttot])
    oBb = P.dram("oBb", [H2, Ttot])
    gB = P.dram("gB", [H2, Ttot])
    nB = P.dram("nB", [H2])
    if parity == 0:
        gA = P.dram("gA", [H2, Ttot])
        nA = P.dram("nA", [H2])
    else:
        uT = P.dram("uT", [H2, Ttot])
        s5d = P.dram("s5d", [H2])
        w_glu = P.dram("w_glu", [H2, H2])
        b_glu = P.dram("b_glu", [H2])
    x3T = P.dram("x3T", [D, Ttot], kind="ExternalOutput")
    x2s = V(P.dram("x2s", [D, Ttot], kind="Internal"), "x2s")
    TMAX = max(t[2] for t in tiles)
    C = TokCtx(P, D, DFF, TMAX)
    load_consts(C, norm_pre, norm_post)
    load_mod(C, modi)
    A1, sh1, G1 = derive_sub(C, 1, 1.0)
    A2, sh2, G2 = derive_sub(C, 2, 0.5)
    gv = P.sb([128, 4, KH], F32, "gv")
    with P.nc.allow_non_contiguous_dma(reason="small const loads"):
        P.dma(gv[:, 1, :], nB.rearrange("(kc p) -> p kc", p=128))
        if parity == 0:
            P.dma(gv[:, 0, :], nA.rearrange("(kc p) -> p kc", p=128))
        else:
            P.dma(gv[:, 0, :], s5d.rearrange("(kc p) -> p kc", p=128))
            P.dma(gv[:, 2, :], b_glu.rearrange("(kc p) -> p kc", p=128))
    NH = HD // 128
    ld = [[P.sb([128, TMAX], F32, "ld%d_%d" % (a, b)) for b in range(2)] for a in range(3)]
    osum = P.sb([128, NH, TMAX], F32, "osum")
    gact = [P.sb([128, TMAX], F32, "gact%d" % i) for i in range(NH)]
    x1v = x1T.rearrange("(kc p) t -> p kc t", p=128)
    x3v = x3T.rearrange("(kc p) t -> p kc t", p=128)
    x2v = x2s.re("(kc p) t -> p kc t", p=128)
    li = [0]

    def normed_half(T, t0, of_, ob_, g_, gvrow, func, center, kc0):
        for hh in range(H2 // HD):
            for i in range(NH):
                r0 = (hh * NH + i) * 128
                b = li[0] % 2
                li[0] += 1
                P.dma(ld[0][b][:, 0:T], of_[r0:r0 + 128, t0:t0 + T])
                P.dma(ld[1][b][:, 0:T], ob_[r0:r0 + 128, t0:t0 + T])
                P.dma(ld[2][b][:, 0:T], g_[r0:r0 + 128, t0:t0 + T])
                P.tt(osum[:, i, 0:T], ld[0][b][:, 0:T], ld[1][b][:, 0:T], ALU.add, eng="pool")
                P.act(gact[i][:, 0:T], ld[2][b][:, 0:T], func)
            cols = [hh * NH + i for i in range(NH)]
            head_norm_chunks(C, T, osum, gv[:, gvrow, :], cols, [g[:, 0:T] for g in gact], center,
                             [C.h[:, kc0 + hh * NH + i, 0:T] for i in range(NH)], HD)

    for (col, t0, T) in tiles:
        if parity == 0:
            normed_half(T, t0, oAf, oAb, gA, 0, AF.Silu, False, 0)
            normed_half(T, t0, oBf, oBb, gB, 1, AF.Sigmoid, True, KH)
        else:
            for kc in range(KH):
                b = li[0] % 2
                li[0] += 1
                r0 = kc * 128
                P.dma(ld[0][b][:, 0:T], oAf[r0:r0 + 128, t0:t0 + T])
                P.dma(ld[1][b][:, 0:T], oAb[r0:r0 + 128, t0:t0 + T])
                P.dma(ld[2][b][:, 0:T], uT[r0:r0 + 128, t0:t0 + T])
                yv = C.tmp[0][:, 0:T]
                t2 = C.tmp[1][:, 0:T]
                P.tt(yv, ld[0][b][:, 0:T], ld[1][b][:, 0:T], ALU.add, eng="pool")
                P.stt(yv, ld[2][b][:, 0:T], gv[:, 0, kc:kc + 1], yv, ALU.mult, ALU.add)
                P.tt(t2, yv, yv, ALU.mult)
                P.ts(t2, t2, 0.044715, ALU.mult, 1.0, ALU.add)
                P.tt(t2, t2, yv, ALU.mult)
                P.act(t2, t2, AF.Sigmoid, scale=1.5957691216057308)
                P.tt(C.XB[:, kc, 0:T], t2, yv, ALU.mult)
                P.copy(C.h[:, KH + kc, 0:T], C.XB[:, kc, 0:T], eng="act")

            def sink(j, cw, pg):
                sgm = C.sg[j % 2][:, 0:T]
                P.act(sgm, pg, AF.Sigmoid, bias=gv[:, 2, j:j + 1])
                P.tt(C.h[:, j, 0:T], C.XB[:, j, 0:T], sgm, ALU.mult)
            proj(C, T, w_glu, KH, H2, lambda kc: C.h[:, KH + kc, 0:T], sink)
            normed_half(T, t0, oBf, oBb, gB, 1, AF.Silu, True, KH)

        def sink_y(j, cw, pg):
            P.copy(C.XB[:, j, 0:T], pg, eng="act")
        proj(C, T, w_out, KC, D, lambda kc: C.h[:, kc, 0:T], sink_y)
        sublayer_out(C, T, G1, col, lambda kc: x1v[:, kc, t0:t0 + T])
        P.dma(x2v[:, :, t0:t0 + T], C.XB[:, :, 0:T], q="pool")
        sublayer_in(C, T, A2, sh2, col)
        ffn_core(C, T, wg, wu, wd)
        sublayer_out(C, T, G2, col, lambda kc: x2v[:, kc, t0:t0 + T])
        P.dma(x3v[:, :, t0:t0 + T], C.XB[:, :, 0:T], q="pool", is_output=True)
    return P


def compute_mod_half(C, cT, w_half, b_half, Mown, Mall, groups):
    P = C.P
    KC = C.KC
    NJ = 9 * KC
    NH = NJ // 2
    mh = P.sb([128, NH, 2], F32, "modh")
    cs = P.sb([128, KC, 2], F32, "csilu")
    bm = P.sb([128, NH], F32, "bmod")
    with P.nc.allow_non_contiguous_dma(reason="small const loads"):
        P.dma(cs, cT.rearrange("(kc p) c -> p kc c", p=128))
        P.dma(bm, b_half.rearrange("(j p) -> p j", p=128))
    P.act(cs, cs, AF.Silu)
    psm = C.ps_m[:, 0:2 * NH].re("p (j c) -> p j c", c=2)
    for j in range(NH):
        i = C.wi % len(C.wst)
        C.wi += 1
        st = C.wst[i][:, 0:KC * 128].re("p (a b) -> p a b", b=128)
        P.dma(st, w_half[j], q="sp")
        for kc in range(KC):
            P.mm(psm[:, j, :], st[:, kc, :], cs[:, kc, :], start=(kc == 0), stop=(kc == KC - 1))
    for c in range(2):
        P.tt(mh[:, :, c], psm[:, :, c], bm, ALU.add)
    P.dma(Mown, mh.re("p j c -> p (j c)"), q="pool")
    P.collective("AllGather", Mall, Mown, groups)
    load_mod_pair(C, Mall)


def load_mod_pair(C, Mall):
    P = C.P
    NJ = 9 * C.KC
    NH = NJ // 2
    C.mod = P.sb([128, NJ, 2], F32, "mod")
    P.dma(C.mod[:, 0:NH, :], Mall[0:128, :].rearrange("p (j c) -> p j c", c=2))
    P.dma(C.mod[:, NH:NJ, :], Mall[128:256, :].rearrange("p (j c) -> p j c", c=2))


G2 = [[0, 1], [2, 3], [4, 5], [6, 7]]
TOKR = 2176
LATR = 2048
NSQ = 4352


def nat_block(c):
    if c < 2:
        return c * TOKR + LATR
    i = c - 2
    return (i // 16) * TOKR + (i % 16) * 128


def rev_chunk(c):
    return 1 - c if c < 2 else 2 + (33 - c)


class Relay:
    def __init__(self, P, maxc):
        self.P = P
        self.I = P.sb([128, 128], F32, "rI")
        self.J = P.sb([128, 128], F32, "rJ")
        P.memset(self.I, 1.0, eng="pool")
        P.memset(self.J, 1.0, eng="pool")
        P.aselect(self.I, self.I, [[-1, 128]], ALU.is_equal, 0.0, 0, 1)
        P.aselect(self.J, self.J, [[1, 128]], ALU.is_equal, 0.0, -127, 1)
        self.tl = [P.sb([128, maxc], F32, "rtl%d" % i) for i in range(2)]
        self.ob = [P.sb([128, 512], F32, "rob%d" % i) for i in range(4)]
        self.ps = [P.ps([128, 512], F32, "rps%d" % i) for i in range(4)]
        self.k = 0
        self.ti = 0

    def load(self, ST, c, c0, ncols, cm):
        P = self.P
        t = self.tl[self.ti % 2]
        self.ti += 1
        if c < 2 or not cm:
            r0 = nat_block(c)
            P.dma(t[:, 0:ncols], ST[r0:r0 + 128, c0:c0 + ncols])
        else:
            i = c - 2
            for cl in range(2):
                for rho in range(2):
                    src = ST[rho * TOKR:rho * TOKR + LATR, c0:c0 + ncols].rearrange("(r w) c -> r w c", w=64)[:, 2 * i + cl, :]
                    P.dma(t[cl * 64 + rho * 32:cl * 64 + rho * 32 + 32, 0:ncols], src)
        return t

    def fm(self, t, c0, w, flip, dst):
        P = self.P
        k = self.k % 4
        self.k += 1
        P.mm(self.ps[k][0:w, 0:128], t[:, c0:c0 + w], self.J if flip else self.I)
        if k % 2 == 0:
            P.copy(self.ob[k][0:w, 0:128], self.ps[k][0:w, 0:128], eng="act")
        else:
            P.copy(self.ob[k][0:w, 0:128], self.ps[k][0:w, 0:128], eng="dve")
        return self.ob[k]

    def tmflip(self, t, c0, w):
        P = self.P
        k = self.k % 4
        self.k += 1
        P.mm(self.ps[k][:, 0:w], self.J, t[:, c0:c0 + w])
        if k % 2 == 0:
            P.copy(self.ob[k][:, 0:w], self.ps[k][:, 0:w], eng="act")
        else:
            P.copy(self.ob[k][:, 0:w], self.ps[k][:, 0:w], eng="dve")
        return self.ob[k]


def relayout_gla(P, ST, c0, DK, cm, arr, lr=True):
    NDC = DK // 128
    ncols = 4 * DK + 512 + (32 if lr else 0)
    R = Relay(P, ncols)
    qo, ko, vo, lo = 0, 2 * DK, 4 * DK, 4 * DK + 512
    for c in range(34):
        t = R.load(ST, c, c0, ncols, cm)
        for d in range(2):
            cc = c if d == 0 else rev_chunk(c)
            ps_ = slice(cc * 128, cc * 128 + 128)
            fl = (d == 1)
            for hl in range(2):
                u = d * 2 + hl
                for dc in range(NDC):
                    o_ = R.fm(t, qo + hl * DK + dc * 128, 128, fl, None)
                    P.dma(arr["qT"][u, dc * 128:(dc + 1) * 128, ps_], o_[:, 0:128], q="sp")
                    o_ = R.fm(t, ko + hl * DK + dc * 128, 128, fl, None)
                    P.dma(arr["kT"][u, dc * 128:(dc + 1) * 128, ps_], o_[:, 0:128], q="sp")
            if lr:
                o_ = R.fm(t, lo + d * 16, 16, fl, None)
                for hl in range(2):
                    P.dma(arr["lrT"][d * 2 + hl, :, ps_], o_[0:16, 0:128], q="sp")
            if d == 0:
                for hl in range(2):
                    P.dma(arr["k"][hl, ps_, :], t[:, ko + hl * DK:ko + (hl + 1) * DK], q="sp")
                    P.dma(arr["v"][hl, ps_, :], t[:, vo + hl * 256:vo + (hl + 1) * 256], q="sp")
            else:
                for hl in range(2):
                    o_ = R.tmflip(t, ko + hl * DK, DK)
                    P.dma(arr["k"][2 + hl, ps_, :], o_[:, 0:DK], q="sp")
                o_ = R.tmflip(t, vo, 512)
                for hl in range(2):
                    P.dma(arr["v"][2 + hl, ps_, :], o_[:, hl * 256:(hl + 1) * 256], q="sp")


def relayout_mlstm(P, ST, c0, arr):
    ncols = 1544
    R = Relay(P, ncols)
    for c in range(34):
        t = R.load(ST, c, c0, ncols, True)
        for d in range(2):
            cc = c if d == 0 else rev_chunk(c)
            ps_ = slice(cc * 128, cc * 128 + 128)
            fl = (d == 1)
            for hl in range(2):
                u = d * 2 + hl
                for dc in range(2):
                    o_ = R.fm(t, hl * 256 + dc * 128, 128, fl, None)
                    P.dma(arr["qpT"][u, dc * 128:(dc + 1) * 128, ps_], o_[:, 0:128], q="sp")
                    o_ = R.fm(t, 512 + hl * 256 + dc * 128, 128, fl, None)
                    P.dma(arr["kpT"][u, dc * 128:(dc + 1) * 128, ps_], o_[:, 0:128], q="sp")
            o_ = R.fm(t, 1536 + d * 4, 4, fl, None)
            for hl in range(2):
                P.dma(arr["gf"][d * 2 + hl:d * 2 + hl + 1, ps_], o_[hl:hl + 1, 0:128], q="sp")
                P.dma(arr["gi"][d * 2 + hl:d * 2 + hl + 1, ps_], o_[2 + hl:3 + hl, 0:128], q="sp")
            if d == 0:
                for hl in range(2):
                    P.dma(arr["v"][hl, ps_, :], t[:, 1024 + hl * 256:1024 + (hl + 1) * 256], q="sp")
            else:
                o_ = R.tmflip(t, 1024, 512)
                for hl in range(2):
                    P.dma(arr["v"][2 + hl, ps_, :], o_[:, hl * 256:(hl + 1) * 256], q="sp")


def emit_seq_inproj(P, Hall, wtok, NT, ST, wfm, NFM, SG, KC=16):
    hT = P.sb([128, KC, TOKR], BF16, "sq_hT")
    wst = [P.sb([128, KC, 512], F32, "sq_wst%d" % i) for i in range(2)]
    wbf = [P.sb([128, KC, 512], BF16, "sq_wbf%d" % i) for i in range(2)]
    osb = [P.sb([128, 512], F32, "sq_o%d" % i) for i in range(3)]
    ps = [P.ps([128, 512], F32, "sq_ps%d" % i) for i in range(4)]
    wtv = wtok.rearrange("(kc p) n -> p kc n", p=128)
    wfv = wfm.rearrange("(kc p) n -> p kc n", p=128)
    wi = 0
    oi = 0
    for rho in range(2):
        for ti, (_c, t0_, T_) in enumerate(TILES_ALL):
            P.dma(hT[:, :, t0_:t0_ + T_], Hall[ti][rho * 2048:(rho + 1) * 2048, :].rearrange("(kc p) t -> p kc t", p=128))
        for cb0 in range(0, NT, 512):
            cw = min(512, NT - cb0)
            b = wi % 2
            wi += 1
            P.dma(wst[b][:, :, 0:cw], wtv[:, :, cb0:cb0 + cw])
            P.copy(wbf[b][:, :, 0:cw], wst[b][:, :, 0:cw], eng="dve" if wi % 2 else "act")
            for tt in range(TOKR // 128):
                pz = ps[oi % 4]
                for kc in range(KC):
                    P.mm(pz[:, 0:cw], hT[:, kc, tt * 128:(tt + 1) * 128], wbf[b][:, kc, 0:cw], start=(kc == 0), stop=(kc == KC - 1))
                ob = osb[oi % 3]
                P.copy(ob[:, 0:cw], pz[:, 0:cw], eng="act" if oi % 2 else "dve")
                oi += 1
                P.dma(ST[rho * TOKR + tt * 128:rho * TOKR + (tt + 1) * 128, cb0:cb0 + cw], ob[:, 0:cw], q="pool")
        for j in range(NFM // 128):
            b = wi % 2
            wi += 1
            P.dma(wst[b][:, :, 0:128], wfv[:, :, j * 128:(j + 1) * 128])
            P.copy(wbf[b][:, :, 0:128], wst[b][:, :, 0:128], eng="dve" if wi % 2 else "act")
            for (t0, T) in [(0, 512), (512, 512), (1024, 512), (1536, 512), (2048, 128)]:
                pz = ps[oi % 4]
                for kc in range(KC):
                    P.mm(pz[:, 0:T], wbf[b][:, kc, 0:128], hT[:, kc, t0:t0 + T], start=(kc == 0), stop=(kc == KC - 1))
                ob = osb[oi % 3]
                P.copy(ob[:, 0:T], pz[:, 0:T], eng="act" if oi % 2 else "dve")
                oi += 1
                P.dma(SG[j * 128:(j + 1) * 128, rho * TOKR + t0:rho * TOKR + t0 + T], ob[:, 0:T], q="pool")


class MergeIn:
    def __init__(self, P):
        self.P = P
        self.I = P.sb([128, 128], F32, "mI")
        self.J = P.sb([128, 128], F32, "mJ")
        P.memset(self.I, 1.0, eng="pool")
        P.memset(self.J, 1.0, eng="pool")
        P.aselect(self.I, self.I, [[-1, 128]], ALU.is_equal, 0.0, 0, 1)
        P.aselect(self.J, self.J, [[1, 128]], ALU.is_equal, 0.0, -127, 1)
        self.tl = [P.sb([128, 256], F32, "mtl%d" % i) for i in range(4)]
        self.ti = 0

    def load(self, O, u, d, cm, rho, t0):
        P = self.P
        t = self.tl[self.ti % 4]
        self.ti += 1
        if t0 >= LATR:
            c = rho if d == 0 else 1 - rho
            P.dma(t, O[u, c * 128:(c + 1) * 128, :])
            return t, (d == 1)
        if not cm:
            i = (rho * LATR + t0) // 128
            c = 2 + i if d == 0 else 2 + (31 - i)
            P.dma(t, O[u, c * 128:(c + 1) * 128, :])
            return t, (d == 1)
        r = (rho * LATR + t0) // 64
        lat = O[u, 256:256 + 4096, :].rearrange("(col row) c -> col row c", row=64)
        if d == 0:
            for rl in range(2):
                P.dma(t[rl * 64:(rl + 1) * 64, :], lat[:, r + rl, :])
            return t, False
        base = 62 - r
        for rl in range(2):
            P.dma(t[rl * 64:(rl + 1) * 64, :], lat[:, base + rl, :])
        return t, True


def emit_merge_layer0(P, C, M, O_gla, O_ml, SG, gvA, gvB, wout, Ypart, tiles_rho):
    T_ = C.T
    osum = P.sb([128, 2, T_], F32, "osum")
    gact = [P.sb([128, T_], F32, "gact%d" % i) for i in range(2)]
    gld = [P.sb([128, T_], F32, "gld%d" % i) for i in range(2)]
    gv = P.sb([128, 2, 4], F32, "gvm")
    with P.nc.allow_non_contiguous_dma(reason="small const loads"):
        P.dma(gv[:, 0, :], gvA.rearrange("(kc p) -> p kc", p=128))
        P.dma(gv[:, 1, :], gvB.rearrange("(kc p) -> p kc", p=128))
    pst = [C.ps_u[0], C.ps_u[1]]
    for rho in range(2):
        for ti, (t0, T) in enumerate(tiles_rho):
            for mix in range(2):
                O = O_gla if mix == 0 else O_ml
                for hl in range(2):
                    for i in range(2):
                        kcl = mix * 4 + hl * 2 + i
                        P.dma(gld[i][:, 0:T], SG[kcl * 128:(kcl + 1) * 128, rho * TOKR + t0:rho * TOKR + t0 + T])
                        P.act(gact[i][:, 0:T], gld[i][:, 0:T], AF.Silu if mix == 0 else AF.Sigmoid)
                    for sub in range(T // 128):
                        ssl = slice(sub * 128, (sub + 1) * 128)
                        tf, ff = M.load(O, hl, 0, mix == 1, rho, t0 + sub * 128)
                        tb, fb = M.load(O, 2 + hl, 1, mix == 1, rho, t0 + sub * 128)
                        for i in range(2):
                            P.mm(pst[0][:, 0:128], tf[:, i * 128:(i + 1) * 128], M.J if ff else M.I)
                            P.mm(pst[1][:, 0:128], tb[:, i * 128:(i + 1) * 128], M.J if fb else M.I)
                            P.copy(osum[:, i, ssl], pst[0][:, 0:128], eng="act")
                            P.tt(osum[:, i, ssl], osum[:, i, ssl], pst[1][:, 0:128], ALU.add)
                    cols = [hl * 2, hl * 2 + 1]
                    head_norm_chunks(C, T, osum, gv[:, mix, :], cols, [g[:, 0:T] for g in gact], mix == 1,
                                     [C.h[:, mix * 4 + hl * 2 + i, 0:T] for i in range(2)], 256)

            def sink(j, cw, pg, rho=rho, ti=ti, T=T):
                so = C.sg[j % 2][:, 0:T]
                P.copy(so, pg, eng="act")
                P.dma(Ypart[ti][j // 4][rho * 512 + (j % 4) * 128:rho * 512 + (j % 4 + 1) * 128, :], so, q="pool")
            proj(C, T, wout, 8, 2048, lambda kc: C.h[:, kc, 0:T], sink, tiled=True)


def emit_merge_layer1(P, C, M, O_s5, O_ret, SG, ins, Gf, Gown, Gall, wout, Ypart, tiles_rho):
    T_ = C.T
    osum = P.sb([128, 2, T_], F32, "osum")
    ysum = P.sb([128, 4, T_], F32, "ysum")
    gact = [P.sb([128, T_], F32, "gact%d" % i) for i in range(2)]
    gld = [P.sb([128, T_], F32, "gld%d" % i) for i in range(2)]
    otl = [P.sb([128, 512], F32, "otl%d" % i) for i in range(4)]
    gv = P.sb([128, 3, 4], F32, "gvm")
    with P.nc.allow_non_contiguous_dma(reason="small const loads"):
        P.dma(gv[:, 0, :], ins["s5d"].rearrange("(kc p) -> p kc", p=128))
        P.dma(gv[:, 1, :], ins["nB"].rearrange("(kc p) -> p kc", p=128))
        P.dma(gv[:, 2, :], ins["bglu"].rearrange("(kc p) -> p kc", p=128))
    pst = [C.ps_u[0], C.ps_u[1]]
    oi = 0
    for rho in range(2):
        for ti, (t0, T) in enumerate(tiles_rho):
            for sub in range(T // 128):
                ssl = slice(sub * 128, (sub + 1) * 128)
                r0 = rho * TOKR + t0 + sub * 128
                tf = otl[oi % 4]
                tb = otl[(oi + 1) % 4]
                oi += 2
                P.dma(tf, O_s5[0][r0:r0 + 128, :])
                P.dma(tb, O_s5[1][r0:r0 + 128, :])
                for kc in range(4):
                    P.mm(pst[0][:, 0:128], tf[:, kc * 128:(kc + 1) * 128], M.I)
                    P.mm(pst[1][:, 0:128], tb[:, kc * 128:(kc + 1) * 128], M.I)
                    P.copy(ysum[:, kc, ssl], pst[0][:, 0:128], eng="act")
                    P.tt(ysum[:, kc, ssl], ysum[:, kc, ssl], pst[1][:, 0:128], ALU.add)
            csl = slice(rho * TOKR + t0, rho * TOKR + t0 + T)
            for kc in range(4):
                u_ = gld[kc % 2][:, 0:T]
                P.dma(u_, SG[kc * 128:(kc + 1) * 128, csl])
                yv = C.tmp[0][:, 0:T]
                t2 = C.tmp[1][:, 0:T]
                P.stt(yv, u_, gv[:, 0, kc:kc + 1], ysum[:, kc, 0:T], ALU.mult, ALU.add)
                P.tt(t2, yv, yv, ALU.mult)
                P.ts(t2, t2, 0.044715, ALU.mult, 1.0, ALU.add)
                P.tt(t2, t2, yv, ALU.mult)
                P.act(t2, t2, AF.Sigmoid, scale=1.5957691216057308)
                P.tt(C.XB[:, kc, 0:T], t2, yv, ALU.mult)
                P.copy(C.h[:, kc, 0:T], C.XB[:, kc, 0:T], eng="act")
                P.dma(Gf[kc * 128:(kc + 1) * 128, csl], C.XB[:, kc, 0:T], q="pool")
                P.dma(Gown[rho][ti][kc * 128:(kc + 1) * 128, :], C.h[:, kc, 0:T], q="pool")
    for rho in range(2):
        for ti in range(len(tiles_rho)):
            P.collective("AllGather", Gall[rho][ti], Gown[rho][ti], G2)
    for rho in range(2):
        for ti, (t0, T) in enumerate(tiles_rho):
            csl = slice(rho * TOKR + t0, rho * TOKR + t0 + T)
            P.dma(C.hid[:, 0:8, 0:T], Gall[rho][ti].rearrange("(kc p) t -> p kc t", p=128))

            def sink_g(j, cw, pg, T=T, csl=csl):
                sgm = C.sg[j % 2][:, 0:T]
                P.act(sgm, pg, AF.Sigmoid, bias=gv[:, 2, j:j + 1])
                g_ = gld[j % 2][:, 0:T]
                P.dma(g_, Gf[j * 128:(j + 1) * 128, csl])
                P.tt(C.h[:, j, 0:T], g_, sgm, ALU.mult)
            proj(C, T, ins["wglu"], 8, 512, lambda kc: C.hid[:, kc, 0:T], sink_g, tiled=True)
            for hl in range(2):
                for i in range(2):
                    kcl = 4 + hl * 2 + i
                    P.dma(gld[i][:, 0:T], SG[kcl * 128:(kcl + 1) * 128, csl])
                    P.act(gact[i][:, 0:T], gld[i][:, 0:T], AF.Silu)
                for sub in range(T // 128):
                    ssl = slice(sub * 128, (sub + 1) * 128)
                    tf, ff = M.load(O_ret, hl, 0, True, rho, t0 + sub * 128)
                    tb, fb = M.load(O_ret, 2 + hl, 1, True, rho, t0 + sub * 128)
                    for i in range(2):
                        P.mm(pst[0][:, 0:128], tf[:, i * 128:(i + 1) * 128], M.J if ff else M.I)
                        P.mm(pst[1][:, 0:128], tb[:, i * 128:(i + 1) * 128], M.J if fb else M.I)
                        P.copy(osum[:, i, ssl], pst[0][:, 0:128], eng="act")
                        P.tt(osum[:, i, ssl], osum[:, i, ssl], pst[1][:, 0:128], ALU.add)
                cols = [hl * 2, hl * 2 + 1]
                head_norm_chunks(C, T, osum, gv[:, 1, :], cols, [g[:, 0:T] for g in gact], True,
                                 [C.h[:, 4 + hl * 2 + i, 0:T] for i in range(2)], 256)

            def sink(j, cw, pg, rho=rho, ti=ti, T=T):
                so = C.sg[j % 2][:, 0:T]
                P.copy(so, pg, eng="act")
                P.dma(Ypart[ti][j // 4][rho * 512 + (j % 4) * 128:rho * 512 + (j % 4 + 1) * 128, :], so, q="pool")
            proj(C, T, wout, 8, 2048, lambda kc: C.h[:, kc, 0:T], sink, tiled=True)


D_, DFF_M = 2048, 5504
TILES_ALL = [(0, i * 512, 512) for i in range(4)] + [(1, 2048, 128)]
TILES_LAT = [(0, i * 512, 512) for i in range(4)]


def emit_P1(P, l, xin, E, X1, Hown, modo):
    C = TokCtx(P, D_, DFF_M, 512, nwst=5, nwbf=7, look=5)
    load_consts(C, E["npre%d" % l], E["npost%d" % l])
    compute_mod_half(C, E["cT"], E["w_mod%d" % l], E["b_mod%d" % l], modo[0], modo[1], G2)
    A0, sh0, G0 = derive_sub(C, 0, 0.5)
    A1, sh1, G1 = derive_sub(C, 1, 1.0)
    xv = xin.rearrange("(kc p) t -> p kc t", p=128)
    x1v = X1.rearrange("(kc p) t -> p kc t", p=128)
    for ti, (col, t0, T) in enumerate(TILES_ALL):
        P.dma(C.XB[:, :, 0:T], xv[:, :, t0:t0 + T], q="sp")
        sublayer_in(C, T, A0, sh0, col)
        ffn_core(C, T, E["wg%da" % l], E["wu%da" % l], E["wd%da" % l], tiled=True)
        sublayer_out(C, T, G0, col, lambda kc, t0=t0, T=T: xv[:, kc, t0:t0 + T])
        P.dma(x1v[:, :, t0:t0 + T], C.XB[:, :, 0:T], q="pool")
        sublayer_in(C, T, A1, sh1, col)
        P.dma(Hown[ti].rearrange("(kc p) t -> p kc t", p=128), C.h[:, :, 0:T], q="pool")


def emit_P3(P, l, E, X1, X2, Yown, modo, xout, tiles):
    C = TokCtx(P, D_, DFF_M, 512, nwst=5, nwbf=7, look=5)
    load_consts(C, E["npre%d" % l], E["npost%d" % l])
    load_mod_pair(C, modo[1])
    A1, sh1, G1 = derive_sub(C, 1, 1.0)
    A2, sh2, G2_ = derive_sub(C, 2, 0.5)
    x1v = X1.rearrange("(kc p) t -> p kc t", p=128)
    x2v = X2.rearrange("(kc p) t -> p kc t", p=128)
    xov = xout.rearrange("(kc p) t -> p kc t", p=128)
    for ti, (col, t0, T) in enumerate(tiles):
        for q_ in range(4):
            P.dma(C.XB[:, q_ * 4:(q_ + 1) * 4, 0:T], Yown[ti][q_].rearrange("(kc p) t -> p kc t", p=128), q="sp")
        sublayer_out(C, T, G1, col, lambda kc, t0=t0, T=T: x1v[:, kc, t0:t0 + T])
        P.dma(x2v[:, :, t0:t0 + T], C.XB[:, :, 0:T], q="pool")
        sublayer_in(C, T, A2, sh2, col)
        ffn_core(C, T, E["wg%db" % l], E["wu%db" % l], E["wd%db" % l], tiled=True)
        sublayer_out(C, T, G2_, col, lambda kc, t0=t0, T=T: x2v[:, kc, t0:t0 + T])
        P.dma(xov[:, :, t0:t0 + T], C.XB[:, :, 0:T], q="pool", is_output=True)


EXT_SHAPES = {
    "xT": [2048, TOKR], "cT": [2048, 2],
    "gwg": [4, 17, 128], "mcwq": [4, 256, 3], "mcwk": [4, 256, 3], "mcbq": [4, 256, 1], "mcbk": [4, 256, 1],
    "mbi": [4, 1], "mbf": [4, 1], "nA0": [512], "nB0": [512],
    "rdec": [4, 128, 1], "s5d": [512], "wglu": [4, 128, 8, 128], "bglu": [512], "nB1": [512],
    "wtok0": [2048, 2600], "wtok1": [2048, 2048],
}
for _l in range(2):
    EXT_SHAPES.update({"w_mod%d" % _l: [72, 128, 16, 128], "b_mod%d" % _l: [9216], "npre%d" % _l: [3, 2048], "npost%d" % _l: [3, 2048],
                       "wfm%d" % _l: [2048, 1024], "wout%d" % _l: [16, 128, 8, 128]})
    for _s in "ab":
        EXT_SHAPES.update({"wg%d%s" % (_l, _s): [43, 128, 16, 128], "wu%d%s" % (_l, _s): [43, 128, 16, 128], "wd%d%s" % (_l, _s): [16, 128, 43, 128]})
for _d in range(2):
    for _k, _sh in (("lamre", [64, 32]), ("lamim", [64, 32]), ("lstep", [64, 32]), ("Bre", [64, 32, 16]), ("Bim", [64, 32, 16]),
                    ("Cre", [64, 32, 16]), ("Cim", [64, 32, 16])):
        EXT_SHAPES["s5%d_%s" % (_d, _k)] = _sh


class _Stop(Exception):
    pass


def build_mega(stop=None, dump=()):
    P = Prog("mega")
    try:
        _build_mega(P, stop, dump)
    except _Stop:
        pass
    return P


def _build_mega(P, stop, dump):
    cnt = [0]

    def chk(tag, tensors=()):
        cnt[0] += 1
        if stop is not None and cnt[0] == stop:
            for nm, v in tensors:
                if nm in dump:
                    o = P.dram("dbg_" + nm, list(v.shape), v.ap.dtype, kind="ExternalOutput")
                    P.dma(o, v, q="sp")
            print("STOP at", cnt[0], tag)
            raise _Stop()

    class _LazyE(dict):
        def __missing__(self, k):
            v = P.dram(k, EXT_SHAPES[k])
            self[k] = v
            return v
    E = _LazyE()
    P.ext = E
    x3T = P.dram("x3T", [2048, LATR], kind="ExternalOutput")
    N = NSQ
    X1 = P.dram_i("X1", [2048, TOKR])
    X2 = P.dram_i("X2", [2048, TOKR])
    X3 = P.dram_i("X3", [2048, TOKR])
    Hown = [P.dram_i("Hown%d" % i, [2048, T], BF16) for i, (_, _t, T) in enumerate(TILES_ALL)]
    Hall = [P.dram_i("Hall%d" % i, [4096, T], BF16) for i, (_, _t, T) in enumerate(TILES_ALL)]
    ST = P.dram_i("ST", [N, 2600])
    SG = P.dram_i("SG", [1024, N])
    modo = (P.dram_i("Mown", [128, 144]), P.dram_i("Mall", [256, 144]))
    Ypart = [[P.dram_i("Yp%d_%d" % (i, q), [1024, T]) for q in range(4)] for i, (_, _t, T) in enumerate(TILES_ALL)]
    Yown = [[P.dram_i("Yo%d_%d" % (i, q), [512, T]) for q in range(4)] for i, (_, _t, T) in enumerate(TILES_ALL)]
    for l in range(2):
        xin = E["xT"] if l == 0 else X3
        P.push_scope()
        emit_P1(P, l, xin, E, X1, Hown, modo)
        P.pop_scope()
        chk("P1_%d" % l, [("X1", X1), ("Hown", Hown[0])])
        for ti in range(len(TILES_ALL)):
            P.collective("AllGather", Hall[ti], Hown[ti], G2)
        chk("AG_%d" % l, [("Hall", Hall[0])])
        P.push_scope()
        emit_seq_inproj(P, Hall, E["wtok%d" % l], 2600 if l == 0 else 2048, ST, E["wfm%d" % l], 1024, SG)
        P.pop_scope()
        chk("inproj_%d" % l, [("ST", ST), ("SG", SG)])
        if l == 0:
            ga = {"qT": P.dram_i("g_qT", [4, 128, N]), "kT": P.dram_i("g_kT", [4, 128, N]), "k": P.dram_i("g_k", [4, N, 128]),
                  "v": P.dram_i("g_v", [4, N, 256]), "lrT": P.dram_i("g_lrT", [4, 16, N]), "o": P.dram_i("g_o", [4, N, 256])}
            P.push_scope()
            relayout_gla(P, ST, 0, 128, False, ga, lr=True)
            P.pop_scope()
            chk("relay_gla", [("g_qT", ga["qT"]), ("g_k", ga["k"]), ("g_v", ga["v"]), ("g_lrT", ga["lrT"])])
            P.push_scope()
            P.bind = dict(ga)
            P.bind["wg"] = E["gwg"]
            emit_gla(P, 4, 34, 128, 256, "gla", 128 ** -0.5, 1.0)
            P.bind = None
            P.pop_scope()
            chk("gla", [("g_o", ga["o"])])
            ma = {"qpT": P.dram_i("m_qpT", [4, 256, N]), "kpT": P.dram_i("m_kpT", [4, 256, N]), "v": P.dram_i("m_v", [4, N, 256]),
                  "gi": P.dram_i("m_gi", [4, N]), "gf": P.dram_i("m_gf", [4, N]), "h": P.dram_i("m_h", [4, N, 256])}
            P.push_scope()
            relayout_mlstm(P, ST, 1056, ma)
            P.pop_scope()
            chk("relay_ml", [("m_qpT", ma["qpT"]), ("m_gi", ma["gi"]), ("m_v", ma["v"])])
            P.push_scope()
            P.bind = dict(ma)
            P.bind.update({"cwq": E["mcwq"], "cwk": E["mcwk"], "cbq": E["mcbq"], "cbk": E["mcbk"], "bi": E["mbi"], "bf": E["mbf"]})
            emit_mlstm(P, 4, [2, 32], 256, 256 ** -0.5)
            P.bind = None
            P.pop_scope()
            chk("mlstm", [("m_h", ma["h"])])
            P.push_scope()
            C = TokCtx(P, D_, DFF_M, 512)
            M = MergeIn(P)
            emit_merge_layer0(P, C, M, ga["o"], ma["h"], SG, E["nA0"], E["nB0"], E["wout0"], Ypart,
                              [(t0, T) for (_, t0, T) in TILES_ALL])
            P.pop_scope()
            chk("merge0", [("Ypart", Ypart[0][0])])
        else:
            O_s5 = [P.dram_i("s5_o%d" % d, [N, 512]) for d in range(2)]
            for d in range(2):
                P.push_scope()
                prm = {k: E["s5%d_%s" % (d, k)] for k in ("lamre", "lamim", "lstep", "Bre", "Bim", "Cre", "Cim")}
                emit_s5v2(P, 32, 8, d == 1, ST, 0, O_s5[d], prm)
                P.pop_scope()
                chk("s5_%d" % d, [("s5o", O_s5[d])])
            ra = {"qT": P.dram_i("r_qT", [4, 256, N]), "kT": P.dram_i("r_kT", [4, 256, N]), "k": P.dram_i("r_k", [4, N, 256]),
                  "v": P.dram_i("r_v", [4, N, 256]), "o": P.dram_i("r_o", [4, N, 256])}
            P.push_scope()
            relayout_gla(P, ST, 512, 256, True, ra, lr=False)
            P.pop_scope()
            chk("relay_ret", [("r_qT", ra["qT"])])
            P.push_scope()
            P.bind = dict(ra)
            P.bind["dec"] = E["rdec"]
            emit_gla(P, 4, 34, 256, 256, "ret", 1.0, 256 ** -0.5)
            P.bind = None
            P.pop_scope()
            chk("ret", [("r_o", ra["o"])])
            Gf = P.dram_i("Gf", [512, N])
            Gown = [[P.dram_i("Gown%d_%d" % (r_, i), [512, 512], BF16) for i in range(4)] for r_ in range(2)]
            Gall = [[P.dram_i("Gall%d_%d" % (r_, i), [1024, 512], BF16) for i in range(4)] for r_ in range(2)]
            P.push_scope()
            C = TokCtx(P, D_, DFF_M, 512)
            M = MergeIn(P)
            emit_merge_layer1(P, C, M, O_s5, ra["o"], SG, {"s5d": E["s5d"], "wglu": E["wglu"], "bglu": E["bglu"], "nB": E["nB1"]},
                              Gf, Gown, Gall, E["wout1"], Ypart, [(t0, T) for (_, t0, T) in TILES_LAT])
            P.pop_scope()
            chk("merge1", [("Ypart", Ypart[0][0])])
        for ti in range(5 if l == 0 else 4):
            for q_ in range(4):
                P.collective("ReduceScatter", Yown[ti][q_], Ypart[ti][q_], G2, op=ALU.add)
        chk("RS_%d" % l, [("Yown", Yown[0][0])])
        P.push_scope()
        if l == 0:
            emit_P3(P, l, E, X1, X2, Yown, modo, X3, TILES_ALL)
            P.pop_scope()
            chk("P3_0", [("X3", X3)])
            P.push_scope()
        else:
            emit_P3(P, l, E, X1, X2, Yown, modo, x3T, TILES_LAT)
        P.pop_scope()
    return P

_C = np.ascontiguousarray
_NCORE = 8


def _tile_w(w):
    K, N = w.shape
    return _C(w.reshape(K // 128, 128, N // 128, 128).transpose(2, 1, 0, 3))


def _core_inputs(inp, r):
    b, hf = r // 2, r % 2
    hs = [2 * hf, 2 * hf + 1]
    m = {}
    x = inp["x"][b]
    ctx = inp["ctx"][b]
    m["xT"] = _C(np.concatenate([x[hf * 2048:(hf + 1) * 2048], ctx[hf * 128:(hf + 1) * 128]], axis=0).T)
    m["cT"] = _C(np.stack([inp["c"][b], inp["c_ctx"]], axis=1))
    for l in range(2):
        m["w_mod%d" % l] = _C(_tile_w(inp["w_mod"][l])[hf * 72:(hf + 1) * 72])
        m["b_mod%d" % l] = _C(inp["b_mod"][l][hf * 9216:(hf + 1) * 9216])
        m["npre%d" % l] = _C(inp["norm_pre"][l])
        m["npost%d" % l] = _C(inp["norm_post"][l])
        for si, s in enumerate("ab"):
            m["wg%d%s" % (l, s)] = _tile_w(inp["ffn_w_gate"][l, si])
            m["wu%d%s" % (l, s)] = _tile_w(inp["ffn_w_up"][l, si])
            m["wd%d%s" % (l, s)] = _tile_w(inp["ffn_w_down"][l, si])
    ar = np.arange
    cols = []
    for off, w in ((0, 128), (512, 128), (1024, 256)):
        for h in hs:
            cols.append(off + h * w + ar(w))
    cols.append(3072 + ar(32))
    for off in (3104, 4128, 5152):
        for h in hs:
            cols.append(off + h * 256 + ar(256))
    for d in range(2):
        cols.append(np.array([7200 + d * 8 + 4 + hs[0], 7200 + d * 8 + 4 + hs[1], 7200 + d * 8 + hs[0], 7200 + d * 8 + hs[1]]))
    cols = np.concatenate(cols)
    w_in0 = inp["ev_w_in"][0]
    m["wtok0"] = _C(w_in0[:, cols])
    own512 = np.concatenate([h * 256 + ar(256) for h in hs])
    m["wfm0"] = _C(w_in0[:, np.concatenate([2048 + own512, 6176 + own512])])
    m["wout0"] = _tile_w(inp["ev_w_out"][0][np.concatenate([own512, 1024 + own512])])
    m["nA0"] = _C(inp["gla_norm"][0][own512])
    m["nB0"] = _C(inp["ml_norm"][0][own512])
    gwg, cwq, cwk, cbq, cbk, bi, bf = [], [], [], [], [], [], []
    cw = inp["ml_conv_w"][0]
    cb = inp["ml_conv_b"][0]
    bg = inp["ml_b_gates"][0]
    for d in range(2):
        for h in hs:
            gwg.append(np.concatenate([inp["gla_w_gate"][0, d][:, h * 128:(h + 1) * 128],
                                       inp["gla_b_gate"][0, d][None, h * 128:(h + 1) * 128]], axis=0))
            wq = cw[:, h * 256:(h + 1) * 256].T
            wk = cw[:, 1024 + h * 256:1024 + (h + 1) * 256].T
            if d == 1:
                wq, wk = wq[:, ::-1], wk[:, ::-1]
            cwq.append(wq)
            cwk.append(wk)
            cbq.append(cb[h * 256:(h + 1) * 256][:, None])
            cbk.append(cb[1024 + h * 256:1024 + (h + 1) * 256][:, None])
            bi.append([bg[d, 0, h]])
            bf.append([bg[d, 1, h]])
    m["gwg"] = _C(np.stack(gwg)).astype(np.float32)
    m["mcwq"] = _C(np.stack(cwq)).astype(np.float32)
    m["mcwk"] = _C(np.stack(cwk)).astype(np.float32)
    m["mcbq"] = _C(np.stack(cbq)).astype(np.float32)
    m["mcbk"] = _C(np.stack(cbk)).astype(np.float32)
    m["mbi"] = np.array(bi, np.float32)
    m["mbf"] = np.array(bf, np.float32)
    w_in1 = inp["od_w_in"][0]
    ch512 = hf * 512 + ar(512)
    cols1 = [ch512]
    for off in (1024, 2048, 3072):
        for h in hs:
            cols1.append(off + h * 256 + ar(256))
    m["wtok1"] = _C(w_in1[:, np.concatenate(cols1)])
    m["wfm1"] = _C(w_in1[:, np.concatenate([ch512, 4096 + own512])])
    m["wout1"] = _tile_w(inp["od_w_out"][0][np.concatenate([ch512, 1024 + own512])])
    m["nB1"] = _C(inp["ret_norm"][0][own512])
    m["s5d"] = _C(inp["s5_d"][0][ch512])
    m["wglu"] = _tile_w(inp["s5_w_glu"][0][:, ch512])
    m["bglu"] = _C(inp["s5_b_glu"][0][ch512])
    gs = slice(hf * 32, (hf + 1) * 32)
    for d in range(2):
        pre = "s5%d_" % d
        m[pre + "lamre"] = _C(inp["s5_lam_re"][0, d][gs].T)
        m[pre + "lamim"] = _C(inp["s5_lam_im"][0, d][gs].T)
        m[pre + "lstep"] = _C(np.broadcast_to(inp["s5_log_step"][0, d][gs][None, :], (64, 32)))
        m[pre + "Bre"] = _C(inp["s5_b_re"][0, d][gs].transpose(1, 0, 2))
        m[pre + "Bim"] = _C(inp["s5_b_im"][0, d][gs].transpose(1, 0, 2))
        m[pre + "Cre"] = _C(inp["s5_c_re"][0, d][gs].transpose(2, 0, 1))
        m[pre + "Cim"] = _C(inp["s5_c_im"][0, d][gs].transpose(2, 0, 1))
    m["rdec"] = _C(np.stack([np.full((128, 1), inp["ret_log_decay"][0, d, h], np.float32) for d in range(2) for h in hs]))
    return m


def kernel(**inp):
    inp = {k: np.asarray(v, dtype=np.float32) for k, v in inp.items()}
    P = build_mega()
    nc = P.finish()
    maps = [{k: v for k, v in _core_inputs(inp, r).items() if k in P.ext} for r in range(_NCORE)]
    res = run_bass_kernel_spmd(nc, maps, core_ids=list(range(_NCORE))).results
    out = np.empty((4, 4096, 2048), np.float32)
    for r in range(_NCORE):
        b, hf = r // 2, r % 2
        out[b, hf * 2048:(hf + 1) * 2048] = res[r]["x3T"].T
    return out
```

```python
import math

import numpy as np
import concourse.bass as bass
import concourse.mybir as mybir
from concourse.bass_utils import run_bass_kernel_spmd

F32 = mybir.dt.float32
BF16 = mybir.dt.bfloat16
I32 = mybir.dt.int32
AF = mybir.ActivationFunctionType
ALU = mybir.AluOpType

_uid = [0]
import os as _os
SES_DEFAULT = _os.environ.get("SES", "1") == "1"
WAW_SKIP = _os.environ.get("WAWSKIP", "1") == "1"


class V:
    __slots__ = ("ap", "keys")

    def __init__(self, ap, keys):
        self.ap = ap
        self.keys = keys if isinstance(keys, tuple) else (keys,)

    def __getitem__(self, idx):
        return V(self.ap[idx], self.keys)

    def k(self, *keys):
        return V(self.ap, tuple(keys))

    def re(self, s, **kw):
        return V(self.ap.rearrange(s, **kw), self.keys)

    def bc(self, shape):
        return V(self.ap.to_broadcast(shape), self.keys)

    def rearrange(self, s, **kw):
        return V(self.ap.rearrange(s, **kw), self.keys)

    @property
    def shape(self):
        return self.ap.shape


class Prog:
    ENG = ("pe", "act", "dve", "pool", "sp")

    def __init__(self, name="k", ring=6, same_engine_sync=SES_DEFAULT):
        self.nc = bass.Bass("TRN2", target_bir_lowering=False, name=name)
        nc = self.nc
        self.eng = {"pe": nc.tensor, "act": nc.scalar, "dve": nc.vector, "pool": nc.gpsimd, "sp": nc.sync}
        self.sem = {k: nc.alloc_semaphore("c_" + k) for k in self.ENG}
        self.cnt = {k: 0 for k in self.ENG}
        self.seen = {k: {} for k in self.ENG}
        self.ring = {q: [[nc.alloc_semaphore("d_%s%d" % (q, i)), 0] for i in range(ring)] for q in ("sp", "pool", "act")}
        self.rpos = {q: 0 for q in self.ring}
        self.lastw = {}
        self.reads = {}
        self.ses = same_engine_sync
        self.out_events = []
        self.n_inst = 0

    def dram(self, name, shape, dt=F32, kind="ExternalInput"):
        bind = getattr(self, "bind", None)
        if bind is not None and name in bind:
            return bind[name]
        return self.nc.dram_tensor(name, list(shape), dt, kind=kind).ap()

    def dram_i(self, name, shape, dt=F32):
        return V(self.nc.dram_tensor(name, list(shape), dt, kind="Internal").ap(), "dram_" + name)

    def push_scope(self):
        import contextlib
        if not hasattr(self, "scopes"):
            self.scopes = []
            self.scope_id = 0
        self.scope_id += 1
        self.scopes.append((contextlib.ExitStack(), self.scope_id))

    def pop_scope(self):
        self.barrier()
        st, _ = self.scopes.pop()
        st.close()

    def _uname(self, name):
        if getattr(self, "scopes", None):
            return "%s_s%d" % (name, self.scopes[-1][1])
        return name

    def sb(self, shape, dt=F32, name=None):
        _uid[0] += 1
        name = self._uname(name or ("t%d" % _uid[0]))
        if getattr(self, "scopes", None):
            t = self.scopes[-1][0].enter_context(self.nc.sbuf_tensor("sb_" + name, list(shape), dt))
        else:
            t = self.nc.alloc_sbuf_tensor("sb_" + name, list(shape), dt)
        return V(t.ap() if hasattr(t, "ap") else t[:], name)

    def ps(self, shape, dt=F32, name=None):
        _uid[0] += 1
        base = name or ("p%d" % _uid[0])
        name = self._uname(base)
        esz = 2 if dt == BF16 else 4
        if getattr(self, "scopes", None):
            t = self.scopes[-1][0].enter_context(self.nc.psum_tensor("ps_" + name, [128, 2048 // esz], dt))
        else:
            t = self.nc.alloc_psum_tensor("ps_" + name, [128, 2048 // esz], dt)
        full = V(t.ap() if hasattr(t, "ap") else t[:], name)
        if not hasattr(self, "banks"):
            self.banks = {}
        self.banks[base] = full
        return self.ps_alias(base, shape)

    def barrier(self):
        evs = []
        for x in self.ENG:
            if self.cnt[x] > 0:
                evs.append((self.sem[x], self.cnt[x], "c_" + x))
        for q in self.ring:
            for i, slot in enumerate(self.ring[q]):
                if slot[1] > 0:
                    evs.append((slot[0], slot[1], "d_%s%d" % (q, i)))
        if hasattr(self, "cc_sem") and self.cc_cnt > 0:
            evs.append((self.cc_sem, self.cc_cnt, "cc_sem"))
        for e in self.ENG:
            for ev in evs:
                if ev[2] == "c_" + e:
                    continue
                self._wait(e, ev)

    def ps_alias(self, name, shape):
        full = self.banks[name]
        n = 1
        for d in shape[1:]:
            n *= d
        v = full[0:shape[0], 0:n]
        if len(shape) > 2:
            letters = "abcdefg"[:len(shape) - 1]
            kw = {letters[i]: shape[i + 1] for i in range(1, len(shape) - 1)}
            v = v.re("p (%s) -> p %s" % (" ".join(letters), " ".join(letters)), **kw)
        return v

    def _wait(self, e, ev):
        sem, val, key = ev
        if self.seen[e].get(key, 0) >= val:
            return
        self.eng[e].wait_ge(sem, val)
        self.seen[e][key] = val

    def _deps(self, e, reads, writes):
        raw, waw = [], []
        for v in reads:
            for k in v.keys:
                lw = self.lastw.get(k)
                if lw is not None:
                    raw.append(lw)
        for v in writes:
            for k in v.keys:
                lw = self.lastw.get(k)
                if lw is not None:
                    waw.append(lw)
                waw.extend(self.reads.get(k, ()))
        for ev in raw:
            if ev[3] == e and (e == "pe" or not self.ses):
                continue
            self._wait(e, ev[:3])
        for ev in waw:
            if ev[3] == e and (e == "pe" or not self.ses or WAW_SKIP):
                continue
            self._wait(e, ev[:3])

    def _commit(self, ev, reads, writes):
        for v in writes:
            for k in v.keys:
                self.lastw[k] = ev
                self.reads[k] = []
        for v in reads:
            for k in v.keys:
                self.reads.setdefault(k, []).append(ev)
                if len(self.reads[k]) > 24:
                    best = {}
                    for r in self.reads[k]:
                        if r[2] not in best or best[r[2]][1] < r[1]:
                            best[r[2]] = r
                    self.reads[k] = list(best.values())

    def op(self, e, fn, reads=(), writes=()):
        self._deps(e, reads, writes)
        inst = fn(self.eng[e])
        self.cnt[e] += 1
        inst.then_inc(self.sem[e], 1)
        ev = (self.sem[e], self.cnt[e], "c_" + e, e)
        self._commit(ev, reads, writes)
        self.n_inst += 1
        return ev

    def dma(self, out, in_, q="sp", is_output=False, **kw):
        reads = [in_] if isinstance(in_, V) else []
        writes = [out] if isinstance(out, V) else []
        self._deps(q, reads, writes)
        slot = self.ring[q][self.rpos[q] % len(self.ring[q])]
        key = "d_%s%d" % (q, self.rpos[q] % len(self.ring[q]))
        self.rpos[q] += 1
        if slot[1] > 0:
            self._wait(q, (slot[0], slot[1], key))
        o = out.ap if isinstance(out, V) else out
        i = in_.ap if isinstance(in_, V) else in_
        inst = self.eng[q].dma_start(out=o, in_=i, **kw)
        slot[1] += 16
        inst.then_inc(slot[0], 16)
        ev = (slot[0], slot[1], key, "dma")
        self._commit(ev, reads, writes)
        if is_output:
            self.out_events.append(ev)
        self.n_inst += 1
        return ev

    def finish(self):
        for q in self.ring:
            for i, slot in enumerate(self.ring[q]):
                if slot[1] > 0:
                    self._wait("sp", (slot[0], slot[1], "d_%s%d" % (q, i)))
        return self.nc

    def mm(self, out, lhsT, rhs, start=True, stop=True):
        return self.op("pe", lambda e: e.matmul(out.ap, lhsT.ap, rhs.ap, start=start, stop=stop),
                       reads=[lhsT, rhs], writes=[out])

    def transpose(self, out, in_, ident):
        return self.op("pe", lambda e: e.transpose(out.ap, in_.ap, ident.ap), reads=[in_, ident], writes=[out])

    def act(self, out, in_, func, bias=None, scale=None, accum_out=None, eng="act"):
        reads = [in_]
        kw = {}
        if bias is not None:
            if isinstance(bias, V):
                reads.append(bias)
                kw["bias"] = bias.ap
            else:
                kw["bias"] = bias
        if scale is not None:
            if isinstance(scale, V):
                reads.append(scale)
                kw["scale"] = scale.ap
            else:
                kw["scale"] = scale
        writes = [out]
        if accum_out is not None:
            writes.append(accum_out)
            kw["accum_out"] = accum_out.ap
        return self.op("act", lambda e: e.activation(out.ap, in_.ap, func, **kw), reads=reads, writes=writes)

    def tt(self, out, a, b, op, eng="dve"):
        return self.op(eng, lambda e: e.tensor_tensor(out.ap, a.ap, b.ap, op), reads=[a, b], writes=[out])

    def ts(self, out, a, s1, op0, s2=None, op1=None, eng="dve", accum_out=None):
        reads = [a]
        x1 = s1.ap if isinstance(s1, V) else s1
        x2 = s2.ap if isinstance(s2, V) else s2
        if isinstance(s1, V):
            reads.append(s1)
        if isinstance(s2, V):
            reads.append(s2)
        kw = {}
        writes = [out]
        if accum_out is not None:
            kw["accum_out"] = accum_out.ap
            writes.append(accum_out)
        if op1 is None:
            return self.op(eng, lambda e: e.tensor_scalar(out.ap, a.ap, x1, None, op0, **kw), reads=reads, writes=writes)
        return self.op(eng, lambda e: e.tensor_scalar(out.ap, a.ap, x1, x2, op0, op1, **kw), reads=reads, writes=writes)

    def stt(self, out, a, s, b, op0, op1):
        reads = [a, b]
        x = s.ap if isinstance(s, V) else s
        if isinstance(s, V):
            reads.append(s)
        return self.op("dve", lambda e: e.scalar_tensor_tensor(out.ap, a.ap, x, b.ap, op0, op1), reads=reads, writes=[out])

    def copy(self, out, in_, eng="dve"):
        if eng == "act":
            return self.op("act", lambda e: e.copy(out.ap, in_.ap), reads=[in_], writes=[out])
        return self.op(eng, lambda e: e.tensor_copy(out.ap, in_.ap), reads=[in_], writes=[out])

    def memset(self, out, val, eng="dve"):
        return self.op(eng, lambda e: e.memset(out.ap, val), writes=[out])

    def recip(self, out, in_):
        return self.op("dve", lambda e: e.reciprocal(out.ap, in_.ap), reads=[in_], writes=[out])

    def scan(self, out, d0, d1, init, op0, op1):
        reads = [d0, d1]
        x = init.ap if isinstance(init, V) else init
        if isinstance(init, V):
            reads.append(init)
        return self.op("dve", lambda e: e.tensor_tensor_scan(out.ap, d0.ap, d1.ap, x, op0, op1), reads=reads, writes=[out])

    def aselect(self, out, in_, pattern, cmp, fill, base, cm):
        return self.op("pool", lambda e: e.affine_select(out.ap, in_.ap, pattern, cmp, fill, base=base, channel_multiplier=cm),
                       reads=[in_], writes=[out])

    def iota(self, out, pattern, base, cm):
        return self.op("pool", lambda e: e.iota(out.ap, pattern, base=base, channel_multiplier=cm,
                                                 allow_small_or_imprecise_dtypes=True), writes=[out])


def run(prog, in_maps, n=8, trace=False):
    nc = prog.finish()
    res = run_bass_kernel_spmd(nc, in_maps, core_ids=list(range(n)), trace=trace)
    return res


def _collective(self, kind, out, in_, groups, op=None):
    q = "pool"
    if not hasattr(self, "cc_sem"):
        self.cc_sem = self.nc.alloc_semaphore("cc_sem")
        self.cc_cnt = 0
    self._deps(q, [in_], [out])
    inst = self.eng[q].collective_compute(kind, op or ALU.bypass, replica_groups=groups, ins=[in_.ap], outs=[out.ap])
    self.cc_cnt += 1
    inst.then_inc(self.cc_sem)
    ev = (self.cc_sem, self.cc_cnt, "cc_sem", "dma")
    self._commit(ev, [in_], [out])
    self._wait(q, ev[:3])
    self.n_inst += 1
    return ev


Prog.collective = _collective


def make_masks(P):
    U = P.sb([128, 128], F32, "Uincl")
    L = P.sb([128, 128], F32, "Lstrict")
    P.memset(U, 1.0, eng="pool")
    P.memset(L, 1.0, eng="pool")
    P.aselect(U, U, [[1, 128]], ALU.is_ge, 0.0, 0, -1)
    P.aselect(L, L, [[-1, 128]], ALU.is_gt, 0.0, 0, 1)
    return U, L


def emit_gla(P, NU, NCH, DK, DV, mode, qscale, kscale):
    N = NCH * 128
    NDK = DK // 128
    GN = 16.0 if mode == "gla" else 1.0
    qT = P.dram("qT", [NU, DK, N])
    kT = P.dram("kT", [NU, DK, N])
    kk = P.dram("k", [NU, N, DK])
    vv = P.dram("v", [NU, N, DV])
    if mode == "gla":
        lrT = P.dram("lrT", [NU, 16, N])
        wg = P.dram("wg", [NU, 17, DK])
    else:
        dec = P.dram("dec", [NU, 128, 1])
    o = P.dram("o", [NU, N, DV], kind="ExternalOutput")
    U, L = make_masks(P)
    ps_b = [P.ps([128, 128], F32, "psb%d" % i) for i in range(2)]
    ps_d = P.ps([128, DK], F32, "psd")
    ps_z = P.ps([128, DK], F32, "psz")
    ps_sc = P.ps([128, 128], F32, "pssc")
    ps_o = P.ps([128, DV], F32, "pso")
    ps_S = [P.ps([128, DV], F32, "psS%d" % i) for i in range(2)]
    if mode != "gla":
        zer = P.sb([128, DK], F32, "zer")
        P.memset(zer, 0.0)

    class B_:
        pass

    def alloc(u):
        B = B_()
        n = lambda s_: "%s_u%d" % (s_, u)
        B.qT_sb = [P.sb([128, NDK, 128], F32, n("qTs%d" % i)) for i in range(2)]
        B.kT_sb = [P.sb([128, NDK, 128], F32, n("kTs%d" % i)) for i in range(2)]
        B.k_sb = [P.sb([128, DK], F32, n("ks%d" % i)) for i in range(2)]
        B.v_sb = [P.sb([128, DV], F32, n("vs%d" % i)) for i in range(2)]
        B.v_bf = [P.sb([128, DV], BF16, n("vb%d" % i)) for i in range(2)]
        B.sp_t = [P.sb([128, DK], F32, n("spt%d" % i)) for i in range(2)]
        B.ex = P.sb([128, DK], F32, n("ex"))
        B.E1 = P.sb([128, NDK, 128], F32, n("E1"))
        B.E2 = P.sb([128, NDK, 128], F32, n("E2"))
        B.Dk = P.sb([128, DK], F32, n("Dk"))
        B.QtT = P.sb([128, NDK, 128], BF16, n("QtT"))
        B.KtT = P.sb([128, NDK, 128], BF16, n("KtT"))
        B.QbT = P.sb([128, NDK, 128], BF16, n("QbT"))
        B.Ke = P.sb([128, DK], BF16, n("Ke"))
        B.scm = P.sb([128, 128], BF16, n("scm"))
        B.S = P.sb([128, NDK, DV], F32, n("S"))
        B.Sbf = P.sb([128, NDK, DV], BF16, n("Sbf"))
        B.cols = P.sb([128, NDK, 4], F32, n("cols"))
        B.o_sb = [P.sb([128, DV], F32, n("osb%d" % i)) for i in range(2)]
        if mode == "gla":
            B.lr_sb = P.sb([17, N], F32, n("lrsb"))
            B.wg_sb = P.sb([17, DK], F32, n("wgsb"))
        else:
            B.dec_sb = P.sb([128, 1], F32, n("decsb"))
        return B

    Bs = [alloc(u) for u in range(NU)]
    for u in range(NU):
        B = Bs[u]
        P.memset(B.S, 0.0)
        P.memset(B.Sbf, 0.0, eng="pool")
        if mode == "gla":
            P.memset(B.lr_sb, 1.0)
            P.dma(B.lr_sb[0:16, :], lrT[u])
            P.dma(B.wg_sb, wg[u])
        else:
            P.dma(B.dec_sb, dec[u])
            P.act(B.sp_t[0], zer, AF.Exp, bias=B.dec_sb)
            P.act(B.sp_t[1], zer, AF.Exp, bias=B.dec_sb)
    for c in range(NCH):
        for u in range(NU):
            B = Bs[u]
            b = c % 2
            tsl = slice(c * 128, (c + 1) * 128)
            P.dma(B.qT_sb[b], qT[u, :, tsl].rearrange("(dc p) t -> p dc t", p=128))
            P.dma(B.kT_sb[b], kT[u, :, tsl].rearrange("(dc p) t -> p dc t", p=128))
            P.dma(B.k_sb[b], kk[u, tsl, :])
            P.dma(B.v_sb[b], vv[u, tsl, :])
            P.copy(B.v_bf[b], B.v_sb[b], eng="pool")
            spt = B.sp_t[b]
            if mode == "gla":
                P.mm(ps_z, B.lr_sb[:, tsl], B.wg_sb)
                P.act(B.ex, ps_z, AF.Exp, scale=-1.0)
                P.act(spt, B.ex, AF.Ln, bias=1.0)
            P.mm(ps_d, L, spt)
            P.act(B.Dk, ps_d, AF.Exp, scale=-1.0 / GN)
            P.stt(B.Ke, B.k_sb[b], float(kscale), B.Dk, ALU.mult, ALU.mult)
            for dc in range(NDK):
                pb = ps_b[dc % 2]
                P.mm(pb, spt[:, dc * 128:(dc + 1) * 128], U)
                cm = B.cols[:, dc, :]
                P.ts(cm[:, 0:1], pb[:, 63:64], 1.0 / GN, ALU.mult)
                P.ts(cm[:, 1:2], pb[:, 63:64], -1.0 / GN, ALU.mult)
                P.act(B.E1[:, dc, :], pb, AF.Exp, scale=-1.0 / GN, bias=cm[:, 0:1])
                P.act(B.E2[:, dc, :], pb, AF.Exp, scale=1.0 / GN, bias=cm[:, 1:2])
                P.act(cm[:, 2:3], pb[:, 63:64], AF.Exp, scale=-1.0 / GN)
                P.act(cm[:, 3:4], pb[:, 127:128], AF.Exp, scale=-1.0 / GN)
                P.stt(B.QtT[:, dc, :], B.qT_sb[b][:, dc, :], float(qscale), B.E1[:, dc, :], ALU.mult, ALU.mult)
                P.stt(B.KtT[:, dc, :], B.kT_sb[b][:, dc, :], float(kscale), B.E2[:, dc, :], ALU.mult, ALU.mult)
                P.ts(B.QbT[:, dc, :], B.QtT[:, dc, :], cm[:, 2:3], ALU.mult)
            for dc in range(NDK):
                P.mm(ps_sc, B.KtT[:, dc, :], B.QtT[:, dc, :], start=(dc == 0), stop=(dc == NDK - 1))
            P.tt(B.scm, ps_sc, U, ALU.mult)
            P.mm(ps_o, B.scm, B.v_bf[b], start=True, stop=False)
            for dc in range(NDK):
                P.mm(ps_o, B.QbT[:, dc, :], B.Sbf[:, dc, :], start=False, stop=(dc == NDK - 1))
            ob = B.o_sb[b]
            P.copy(ob, ps_o, eng="act")
            P.dma(o[u, tsl, :], ob, q="pool", is_output=True)
            for dc in range(NDK):
                pS = ps_S[dc % 2]
                P.mm(pS, B.Ke[:, dc * 128:(dc + 1) * 128], B.v_bf[b])
                P.stt(B.S[:, dc, :], B.S[:, dc, :], B.cols[:, dc, 3:4], pS, ALU.mult, ALU.add)
                P.copy(B.Sbf[:, dc, :], B.S[:, dc, :], eng="act")
    return P


def build_gla(NU, NCH, DK, DV, mode, qscale, kscale, name="gla"):
    P = Prog(name)
    emit_gla(P, NU, NCH, DK, DV, mode, qscale, kscale)
    return P


def emit_mlstm(P, NU, segs, DH, kscale):
    NCH = sum(segs)
    N = NCH * 128
    NDC = DH // 128
    DA = DH + 1
    qpT = P.dram("qpT", [NU, DH, N])
    kpT = P.dram("kpT", [NU, DH, N])
    vv = P.dram("v", [NU, N, DH])
    cwq = P.dram("cwq", [NU, DH, 3])
    cwk = P.dram("cwk", [NU, DH, 3])
    cbq = P.dram("cbq", [NU, DH, 1])
    cbk = P.dram("cbk", [NU, DH, 1])
    gi = P.dram("gi", [NU, N])
    gf = P.dram("gf", [NU, N])
    bi = P.dram("bi", [NU, 1])
    bf_ = P.dram("bf", [NU, 1])
    ho = P.dram("h", [NU, N, DH], kind="ExternalOutput")
    U, L = make_masks(P)
    identf = P.sb([128, 128], F32, "identf")
    P.memset(identf, 1.0, eng="pool")
    P.aselect(identf, identf, [[-1, 128]], ALU.is_equal, 0.0, 0, 1)
    identb = P.sb([128, 128], BF16, "identb")
    P.copy(identb, identf)
    R = P.sb([NU, 4, N], F32, "R")
    bcol = P.sb([NU, 4], F32, "bcol")
    with P.nc.allow_non_contiguous_dma(reason="small"):
        P.dma(R[:, 0, :], gf)
        P.dma(R[:, 1, :], gi)
        P.dma(bcol[:, 0:1], bf_)
        P.dma(bcol[:, 1:2], bi)
    P.ts(bcol[:, 2:3], bcol[:, 0:1], -1.0, ALU.mult)
    P.act(R[:, 2, :], R[:, 0, :], AF.Exp, scale=-1.0, bias=bcol[:, 2:3])
    P.act(R[:, 2, :], R[:, 2, :], AF.Ln, bias=1.0)
    P.scan(R[:, 3, :], R[:, 2, :], R[:, 2, :], 0.0, ALU.add, ALU.max)
    P.stt(R[:, 1, :], R[:, 1, :], bcol[:, 1:2], R[:, 3, :], ALU.add, ALU.add)
    P.scan(R[:, 2, :], R[:, 1, :], R[:, 1, :], 0.0, ALU.max, ALU.max)
    P.tt(R[:, 0, :], R[:, 2, :], R[:, 3, :], ALU.subtract)
    idn = P.sb([NU, NU], F32, "idn")
    P.memset(idn, 1.0, eng="pool")
    P.aselect(idn, idn, [[-1, NU]], ALU.is_equal, 0.0, 0, 1)
    sel = []
    for u in range(NU):
        s_ = P.sb([NU, 128], F32, "sel%d" % u)
        P.memset(s_, 1.0, eng="pool")
        P.aselect(s_, s_, [[0, 128]], ALU.is_equal, 0.0, -u, 1)
        sel.append(s_)
    assert NCH * 3 * NU <= 512
    ps_c = P.ps([128, NCH, 3, NU], F32, "psc")
    for c in range(NCH):
        for qi, row in enumerate((1, 2, 0)):
            P.mm(ps_c[:, c, qi, :], R[:, row, c * 128:(c + 1) * 128], idn)
    colsb = P.sb([128, NCH, 3, NU], F32, "colsb")
    P.copy(colsb, ps_c)
    HW = 130
    ps_M = P.ps([128, 128], F32, "psM")
    ps_sc = P.ps([128, 128], F32, "pssc")
    ps_o = P.ps([128, DA], F32, "pso")
    ps_t = P.ps([128, DH], BF16, "pst")
    ps_C = [P.ps([128, DA], F32, "psC%d" % i) for i in range(2)]
    seg_first = set()
    seg_last = set()
    c0 = 0
    for n in segs:
        seg_first.add(c0)
        seg_last.add(c0 + n - 1)
        c0 += n

    class B_:
        pass

    def alloc(u):
        B = B_()
        n = lambda s_: "%s_u%d" % (s_, u)
        B.qp_sb = [P.sb([128, NDC, HW], F32, n("qps%d" % i)) for i in range(2)]
        B.kp_sb = [P.sb([128, NDC, HW], F32, n("kps%d" % i)) for i in range(2)]
        B.acc = [P.sb([128, 128], F32, n("acc%d" % i)) for i in range(2)]
        B.qT = P.sb([128, NDC, 128], BF16, n("qT"))
        B.kT = P.sb([128, NDC, 128], BF16, n("kT"))
        B.qw = P.sb([128, NDC, 128], BF16, n("qw"))
        B.ktok = P.sb([128, DH], BF16, n("ktok"))
        B.va = [P.sb([128, DA], F32, n("va%d" % i)) for i in range(2)]
        for i in range(2):
            P.memset(B.va[i], 1.0)
        B.va_bf = P.sb([128, DA], BF16, n("vabf"))
        B.vw = P.sb([128, DA], BF16, n("vw"))
        B.cw = P.sb([128, 2, NDC, 4], F32, n("cw"))
        B.arg = P.sb([128, 128], F32, n("arg"))
        B.Dm = P.sb([128, 128], F32, n("Dm"))
        B.Wbc = P.sb([128, 128], F32, n("Wbc"))
        B.scm = P.sb([128, 128], BF16, n("scm"))
        B.Cst = P.sb([128, NDC, DA], F32, n("Cst"))
        B.Cbf = P.sb([128, NDC, DA], BF16, n("Cbf"))
        B.sc = P.sb([128, 12], F32, n("sc"))
        B.h_sb = [P.sb([128, DH], F32, n("hsb%d" % i)) for i in range(2)]
        return B

    Bs = [alloc(u) for u in range(NU)]
    for u in range(NU):
        B = Bs[u]
        P.memset(B.Cst, 0.0)
        P.memset(B.Cbf, 0.0, eng="pool")
        P.memset(B.sc[:, 0:1], 0.0)
        with P.nc.allow_non_contiguous_dma(reason="small"):
            P.dma(B.cw[:, 0, :, 0:3], cwq[u].rearrange("(dc p) k -> p dc k", p=128))
            P.dma(B.cw[:, 1, :, 0:3], cwk[u].rearrange("(dc p) k -> p dc k", p=128))
            P.dma(B.cw[:, 0, :, 3:4], cbq[u].rearrange("(dc p) k -> p dc k", p=128))
            P.dma(B.cw[:, 1, :, 3:4], cbk[u].rearrange("(dc p) k -> p dc k", p=128))
    for c in range(NCH):
        for u in range(NU):
            B = Bs[u]
            sc = B.sc
            b = c % 2
            tsl = slice(c * 128, (c + 1) * 128)
            lo = c * 128 - 1
            hi = c * 128 + 129
            dlo, dhi = 0, HW
            if c in seg_first:
                lo += 1
                dlo = 1
            if c in seg_last:
                hi -= 1
                dhi = HW - 1
            for (src, dst) in ((qpT, B.qp_sb[b]), (kpT, B.kp_sb[b])):
                if c in seg_first:
                    P.memset(dst[:, :, 0:1], 0.0, eng="pool")
                if c in seg_last:
                    P.memset(dst[:, :, HW - 1:HW], 0.0, eng="pool")
                P.dma(dst[:, :, dlo:dhi], src[u, :, lo:hi].rearrange("(dc p) t -> p dc t", p=128))
            P.dma(B.va[b][:, 0:DH], vv[u, tsl, :])
            P.copy(B.va_bf, B.va[b], eng="pool")
            k_ = 0
            for qk, (src, dstT) in enumerate(((B.qp_sb[b], B.qT), (B.kp_sb[b], B.kT))):
                for dc in range(NDC):
                    a_ = B.acc[k_ % 2]
                    k_ += 1
                    w = B.cw[:, qk, dc, :]
                    P.ts(a_, src[:, dc, 1:129], w[:, 1:2], ALU.mult, w[:, 3:4], ALU.add)
                    P.stt(a_, src[:, dc, 0:128], w[:, 0:1], a_, ALU.mult, ALU.add)
                    P.stt(a_, src[:, dc, 2:130], w[:, 2:3], a_, ALU.mult, ALU.add)
                    P.act(dstT[:, dc, :], a_, AF.Silu)
            a_col = colsb[:, c, 0, u:u + 1]
            m_col = colsb[:, c, 2, u:u + 1]
            P.mm(ps_M, sel[u], R[:, 2, tsl])
            P.ts(B.arg, ps_M, a_col, ALU.subtract, 0.0, ALU.max)
            P.act(B.Dm, B.arg, AF.Exp, scale=-1.0)
            P.tt(B.Dm, B.Dm, U, ALU.mult)
            for dc in range(NDC):
                P.mm(ps_sc, B.kT[:, dc, :], B.qT[:, dc, :], start=(dc == 0), stop=(dc == NDC - 1))
            P.stt(B.scm, ps_sc, float(kscale), B.Dm, ALU.mult, ALU.mult)
            P.act(B.Wbc, ps_M, AF.Exp, scale=-1.0, bias=sc[:, 0:1])
            for dc in range(NDC):
                P.tt(B.qw[:, dc, :], B.qT[:, dc, :], B.Wbc, ALU.mult)
            P.mm(ps_o, B.scm, B.va_bf, start=True, stop=False)
            for dc in range(NDC):
                P.mm(ps_o, B.qw[:, dc, :], B.Cbf[:, dc, :], start=False, stop=(dc == NDC - 1))
            P.act(sc[:, 5:6], m_col, AF.Exp, scale=-1.0)
            P.copy(sc[:, 9:10], ps_o[:, DH:DA])
            P.stt(sc[:, 6:7], sc[:, 9:10], -1.0, sc[:, 9:10], ALU.mult, ALU.max)
            P.tt(sc[:, 7:8], sc[:, 6:7], sc[:, 5:6], ALU.max)
            P.recip(sc[:, 8:9], sc[:, 7:8])
            hb = B.h_sb[b]
            P.ts(hb, ps_o[:, 0:DH], sc[:, 8:9], ALU.mult)
            P.dma(ho[u, tsl, :], hb, q="pool", is_output=True)
            P.copy(sc[:, 1:2], ps_M[:, 127:128])
            P.ts(sc[:, 2:3], ps_M[:, 127:128], -1.0, ALU.mult)
            P.act(sc[:, 3:4], a_col, AF.Exp, bias=sc[:, 2:3])
            P.ts(sc[:, 3:4], sc[:, 3:4], float(kscale), ALU.mult)
            P.ts(B.vw, B.va[b], sc[:, 3:4], ALU.mult)
            for dc in range(NDC):
                P.transpose(ps_t[:, dc * 128:(dc + 1) * 128], B.kT[:, dc, :], identb)
            P.copy(B.ktok, ps_t, eng="act")
            P.act(sc[:, 4:5], sc[:, 0:1], AF.Exp, bias=sc[:, 2:3])
            for dc in range(NDC):
                pC = ps_C[dc % 2]
                P.mm(pC, B.ktok[:, dc * 128:(dc + 1) * 128], B.vw)
                P.stt(B.Cst[:, dc, :], B.Cst[:, dc, :], sc[:, 4:5], pC, ALU.mult, ALU.add)
                P.copy(B.Cbf[:, dc, :], B.Cst[:, dc, :], eng="act")
            P.copy(sc[:, 0:1], sc[:, 1:2])
    return P


def build_mlstm(NU, segs, DH, kscale, name="mlstm"):
    P = Prog(name)
    emit_mlstm(P, NU, segs, DH, kscale)
    return P


TWO_PI = 2.0 * math.pi
TOKR = 2176
LATR = 2048


def emit_s5(P, G, GB, NB, NBLK):
    NMC = NB * NBLK
    Uin = P.dram("Uin", [G, 128, NMC])
    lamre_d = P.dram("lamre", [64, G])
    lamim_d = P.dram("lamim", [64, G])
    lstep_d = P.dram("lstep", [64, G])
    Bre_d = P.dram("Bre", [64, G, 16])
    Bim_d = P.dram("Bim", [64, G, 16])
    Cre_d = P.dram("Cre", [64, G, 16])
    Cim_d = P.dram("Cim", [64, G, 16])
    Y = P.dram("Y", [G, 128, NMC], kind="ExternalOutput")
    NK = 24
    kk = [7, 6, 5, 4, 3, 2, 1, 0] + [1, 2, 3, 4, 5, 6, 7, 8] + [-1, -2, -3, -4, -5, -6, -7, -8]
    I64 = P.sb([64, 64], F32, "I64")
    P.memset(I64, 1.0, eng="pool")
    P.aselect(I64, I64, [[-1, 64]], ALU.is_equal, 0.0, 0, 1)
    BM = P.sb([128, 8, 16], F32, "BM")
    P.memset(BM, 1.0, eng="pool")
    P.aselect(BM, BM, [[16, 8], [0, 16]], ALU.is_ge, 0.0, 15, -1)
    Tm = P.sb([128, G, 128], BF16, "Tm")
    W2 = P.sb([128, G, 128], BF16, "W2")
    Vre = P.sb([64, G, 128], BF16, "Vre")
    Vim = P.sb([64, G, 128], BF16, "Vim")
    MU1 = P.sb([64, 2, G], F32, "MU1")
    MUa = P.sb([64, G], F32, "MUa")
    MUb = P.sb([64, G], F32, "MUb")
    lre = P.sb([64, GB], F32, "lre")
    lim = P.sb([64, GB], F32, "lim")
    lst = P.sb([64, GB], F32, "lst")
    Bre = P.sb([64, GB, 16], F32, "sBre")
    Bim = P.sb([64, GB, 16], F32, "sBim")
    Cre = P.sb([64, GB, 16], F32, "sCre")
    Cim = P.sb([64, GB, 16], F32, "sCim")
    step = P.sb([64, GB], F32, "step")
    rho = P.sb([64, GB], F32, "rho")
    th = P.sb([64, GB], F32, "th")
    PH = P.sb([64, GB, NK], F32, "PH")
    RH = P.sb([64, GB, NK], F32, "RH")
    mag = P.sb([64, GB, NK], F32, "mag")
    TT = P.sb([64, GB, NK, 2], F32, "TT")
    TI = P.sb([64, GB, NK, 2], I32, "TI")
    TF = P.sb([64, GB, NK, 2], F32, "TF")
    LPre = P.sb([64, GB, NK], F32, "LPre")
    LPim = P.sb([64, GB, NK], F32, "LPim")
    w = [P.sb([64, GB], F32, "w%d" % i) for i in range(6)]
    BTre = P.sb([64, GB, 16], F32, "BTre")
    BTim = P.sb([64, GB, 16], F32, "BTim")
    t16a = P.sb([64, GB, 16], F32, "t16a")
    t16b = P.sb([64, GB, 16], F32, "t16b")
    Are = P.sb([64, GB, 8, 16], F32, "Are")
    Aim = P.sb([64, GB, 8, 16], F32, "Aim")
    Apre = P.sb([64, GB, 8, 16], F32, "Apre")
    Apim = P.sb([64, GB, 8, 16], F32, "Apim")
    Dre = P.sb([64, GB, 8, 16], F32, "Dre")
    nDim = P.sb([64, GB, 8, 16], F32, "nDim")
    t1 = P.sb([64, GB, 8, 16], F32, "t1")
    t2 = P.sb([64, GB, 8, 16], F32, "t2")
    ps_T = [P.ps([128, 128], F32, "psT%d" % i) for i in range(2)]
    ps_W = [P.ps([128, 128], F32, "psW%d" % i) for i in range(2)]

    def bc16(v):
        return V(v.ap.unsqueeze(2).to_broadcast([64, GB, 16]), v.keys)

    def cmul(out_re, out_im, lp0, xr, xi, neg_im=False):
        a_re = V(LPre[:, :, lp0:lp0 + 8].ap.unsqueeze(3).to_broadcast([64, GB, 8, 16]), LPre.keys)
        a_im = V(LPim[:, :, lp0:lp0 + 8].ap.unsqueeze(3).to_broadcast([64, GB, 8, 16]), LPim.keys)
        b_re = V(xr.ap.unsqueeze(2).to_broadcast([64, GB, 8, 16]), xr.keys)
        b_im = V(xi.ap.unsqueeze(2).to_broadcast([64, GB, 8, 16]), xi.keys)
        P.tt(t1, a_re, b_re, ALU.mult)
        P.tt(t2, a_im, b_im, ALU.mult)
        P.tt(out_re, t1, t2, ALU.subtract)
        P.tt(t1, a_re, b_im, ALU.mult)
        P.tt(t2, a_im, b_re, ALU.mult)
        P.tt(out_im, t1, t2, ALU.add)
        if neg_im:
            P.ts(out_im, out_im, -1.0, ALU.mult)

    fl = lambda v, gi: v[:, gi, :, :].re("p a b -> p (a b)")
    for g0 in range(0, G, GB):
        gsl = slice(g0, g0 + GB)
        for d_, s_ in ((lre, lamre_d), (lim, lamim_d), (lst, lstep_d)):
            P.dma(d_, s_[:, gsl])
        for d_, s_ in ((Bre, Bre_d), (Bim, Bim_d), (Cre, Cre_d), (Cim, Cim_d)):
            P.dma(d_, s_[:, gsl, :])
        P.act(step, lst, AF.Exp)
        P.tt(rho, lre, step, ALU.mult)
        P.tt(th, lim, step, ALU.mult)
        for i, kv in enumerate(kk):
            P.ts(PH[:, :, i], th, float(kv), ALU.mult)
            P.ts(RH[:, :, i], rho, float(kv), ALU.mult)
        P.act(mag, RH, AF.Exp)
        P.ts(TT[:, :, :, 0], PH, 1.0 / TWO_PI, ALU.mult, 0.5, ALU.add)
        P.ts(TT[:, :, :, 1], PH, 1.0 / TWO_PI, ALU.mult, 0.75, ALU.add)
        P.copy(TI, TT)
        P.copy(TF, TI)
        P.tt(TT, TT, TF, ALU.subtract)
        P.ts(TF, TT, 0.0, ALU.is_lt)
        P.tt(TT, TT, TF, ALU.add)
        P.ts(TT, TT, TWO_PI, ALU.mult, -math.pi, ALU.add)
        P.ts(TT, TT, 3.1415925, ALU.min, -3.1415925, ALU.max)
        P.act(TT, TT, AF.Sin)
        P.tt(LPre, mag, TT[:, :, :, 1], ALU.mult)
        P.tt(LPim, mag, TT[:, :, :, 0], ALU.mult)
        abre = LPre[:, :, 8]
        abim = LPim[:, :, 8]
        P.tt(w[0], lre, lre, ALU.mult)
        P.tt(w[1], lim, lim, ALU.mult)
        P.tt(w[0], w[0], w[1], ALU.add)
        P.recip(w[0], w[0])
        P.ts(w[1], abre, -1.0, ALU.add)
        P.tt(w[2], w[1], lre, ALU.mult)
        P.tt(w[3], abim, lim, ALU.mult)
        P.tt(w[2], w[2], w[3], ALU.add)
        P.tt(w[2], w[2], w[0], ALU.mult)
        P.tt(w[4], abim, lre, ALU.mult)
        P.tt(w[5], w[1], lim, ALU.mult)
        P.tt(w[4], w[4], w[5], ALU.subtract)
        P.tt(w[4], w[4], w[0], ALU.mult)
        cre, cim = w[2], w[4]
        P.tt(t16a, Bre, bc16(cre), ALU.mult)
        P.tt(t16b, Bim, bc16(cim), ALU.mult)
        P.tt(BTre, t16a, t16b, ALU.subtract)
        P.tt(t16a, Bim, bc16(cre), ALU.mult)
        P.tt(t16b, Bre, bc16(cim), ALU.mult)
        P.tt(BTim, t16a, t16b, ALU.add)
        P.copy(MU1[:, 0, gsl], LPre[:, :, 15])
        P.copy(MU1[:, 1, gsl], LPre[:, :, 15])
        P.copy(MUb[:, gsl], LPim[:, :, 15])
        P.ts(MUa[:, gsl], LPim[:, :, 15], -1.0, ALU.mult)
        cmul(Are, Aim, 0, BTre, BTim)
        cmul(Apre, Apim, 16, BTre, BTim)
        cmul(Dre, nDim, 8, Cre, Cim, neg_im=True)
        for gi in range(GB):
            g = g0 + gi
            pT = ps_T[g % 2]
            P.mm(pT, fl(Apre, gi), fl(Dre, gi), start=True, stop=False)
            P.mm(pT, fl(Apim, gi), fl(nDim, gi), start=False, stop=True)
            P.tt(Tm[:, g, :], pT, BM.re("p a b -> p (a b)"), ALU.mult)
            pW = ps_W[g % 2]
            P.mm(pW[:, 0:64], fl(Are, gi), I64)
            P.mm(pW[:, 64:128], fl(Aim, gi), I64)
            P.copy(W2[:, g, :], pW, eng="act")
        P.copy(Vre[:, gsl, :], Dre.re("p g a b -> p g (a b)"), eng="act")
        P.copy(Vim[:, gsl, :], nDim.re("p g a b -> p g (a b)"), eng="act")
    GQ = 4
    XE = P.sb([64, 2, G, NB + 1], F32, "XE")
    P.memset(XE[:, :, :, 0], 0.0)
    Ebf = P.sb([64, 2, G, NB], BF16, "Ebf")
    Ubf = P.sb([128, G, NB], BF16, "Ubf")
    ust = [P.sb([128, GQ, NB], F32, "ust%d" % i) for i in range(2)]
    yst = [P.sb([128, GQ, NB], F32, "yst%d" % i) for i in range(2)]
    r1 = P.sb([64, 2, G], F32, "r1")
    r2 = P.sb([64, 2, G], F32, "r2")
    ps_xr = [P.ps_alias("psT%d" % i, [64, GQ, NB]) for i in range(2)]
    ps_xi = [P.ps_alias("psW%d" % i, [64, GQ, NB]) for i in range(2)]
    ps_y = [P.ps([128, GQ, NB], F32, "psy%d" % i) for i in range(2)]
    qi = 0
    for blk in range(NBLK):
        csl = slice(blk * NB, (blk + 1) * NB)
        for g0 in range(0, G, GQ):
            b = qi % 2
            qi += 1
            P.dma(ust[b], Uin[g0:g0 + GQ, :, csl].rearrange("g p c -> p g c"))
            P.copy(Ubf[:, g0:g0 + GQ, :], ust[b], eng="pool")
            for gi in range(GQ):
                g = g0 + gi
                P.mm(ps_xr[b][:, gi, :], W2[:, g, 0:64], Ubf[:, g, :])
                P.mm(ps_xi[b][:, gi, :], W2[:, g, 64:128], Ubf[:, g, :])
            P.copy(XE[:, 0, g0:g0 + GQ, 1:NB + 1], ps_xr[b], eng="act")
            P.copy(XE[:, 1, g0:g0 + GQ, 1:NB + 1], ps_xi[b], eng="dve")
        for c in range(NB):
            prev = XE[:, :, :, c]
            cur = XE[:, :, :, c + 1]
            P.tt(r1, MU1, prev, ALU.mult)
            P.tt(r2[:, 0, :], MUa, XE[:, 1, :, c], ALU.mult)
            P.tt(r2[:, 1, :], MUb, XE[:, 0, :, c], ALU.mult)
            P.tt(r1, r1, r2, ALU.add)
            P.tt(cur, cur, r1, ALU.add)
        P.copy(Ebf, XE[:, :, :, 0:NB], eng="act")
        for g0 in range(0, G, GQ):
            b = qi % 2
            qi += 1
            for gi in range(GQ):
                g = g0 + gi
                py = ps_y[b][:, gi, :]
                P.mm(py, Tm[:, g, :], Ubf[:, g, :], start=True, stop=False)
                P.mm(py, Vre[:, g, :], Ebf[:, 0, g, :], start=False, stop=False)
                P.mm(py, Vim[:, g, :], Ebf[:, 1, g, :], start=False, stop=True)
            P.copy(yst[b], ps_y[b], eng="act")
            P.dma(Y[g0:g0 + GQ, :, csl].rearrange("g p c -> p g c"), yst[b], q="pool", is_output=True)
        if blk + 1 < NBLK:
            P.copy(XE[:, :, :, 0], XE[:, :, :, NB])
    return P


def emit_s5v2(P, G, GB, rev, ST, c0, Od, prm):
    lamre_d, lamim_d, lstep_d = prm["lamre"], prm["lamim"], prm["lstep"]
    Bre_d, Bim_d, Cre_d, Cim_d = prm["Bre"], prm["Bim"], prm["Cre"], prm["Cim"]
    NK = 24
    if not rev:
        kk = [7, 6, 5, 4, 3, 2, 1, 0] + [1, 2, 3, 4, 5, 6, 7, 8] + [-1, -2, -3, -4, -5, -6, -7, -8]
    else:
        kk = [0, 1, 2, 3, 4, 5, 6, 7] + [8, 7, 6, 5, 4, 3, 2, 1] + [-8, -7, -6, -5, -4, -3, -2, -1]
    i1 = 8 + kk[8:16].index(1)
    i8 = 8 + kk[8:16].index(8)
    I64 = P.sb([64, 64], F32, "I64")
    P.memset(I64, 1.0, eng="pool")
    P.aselect(I64, I64, [[-1, 64]], ALU.is_equal, 0.0, 0, 1)
    BM = P.sb([128, 8, 16], F32, "BM")
    P.memset(BM, 1.0, eng="pool")
    if not rev:
        P.aselect(BM, BM, [[16, 8], [0, 16]], ALU.is_ge, 0.0, 15, -1)
    else:
        P.aselect(BM, BM, [[-16, 8], [0, 16]], ALU.is_ge, 0.0, 0, 1)
    Tm = P.sb([128, G, 128], BF16, "Tm")
    W2 = P.sb([128, G, 128], BF16, "W2")
    Vre = P.sb([64, G, 128], BF16, "Vre")
    Vim = P.sb([64, G, 128], BF16, "Vim")
    MU1 = P.sb([64, 2, G], F32, "MU1")
    MUa = P.sb([64, G], F32, "MUa")
    MUb = P.sb([64, G], F32, "MUb")
    lre = P.sb([64, GB], F32, "lre")
    lim = P.sb([64, GB], F32, "lim")
    lst = P.sb([64, GB], F32, "lst")
    Bre = P.sb([64, GB, 16], F32, "sBre")
    Bim = P.sb([64, GB, 16], F32, "sBim")
    Cre = P.sb([64, GB, 16], F32, "sCre")
    Cim = P.sb([64, GB, 16], F32, "sCim")
    step = P.sb([64, GB], F32, "step")
    rho = P.sb([64, GB], F32, "rho")
    th = P.sb([64, GB], F32, "th")
    PH = P.sb([64, GB, NK], F32, "PH")
    RH = P.sb([64, GB, NK], F32, "RH")
    mag = P.sb([64, GB, NK], F32, "mag")
    TT = P.sb([64, GB, NK, 2], F32, "TT")
    TI = P.sb([64, GB, NK, 2], I32, "TI")
    TF = P.sb([64, GB, NK, 2], F32, "TF")
    LPre = P.sb([64, GB, NK], F32, "LPre")
    LPim = P.sb([64, GB, NK], F32, "LPim")
    w = [P.sb([64, GB], F32, "w%d" % i) for i in range(6)]
    BTre = P.sb([64, GB, 16], F32, "BTre")
    BTim = P.sb([64, GB, 16], F32, "BTim")
    t16a = P.sb([64, GB, 16], F32, "t16a")
    t16b = P.sb([64, GB, 16], F32, "t16b")
    Are = P.sb([64, GB, 8, 16], F32, "Are")
    Aim = P.sb([64, GB, 8, 16], F32, "Aim")
    Apre = P.sb([64, GB, 8, 16], F32, "Apre")
    Apim = P.sb([64, GB, 8, 16], F32, "Apim")
    Dre = P.sb([64, GB, 8, 16], F32, "Dre")
    nDim = P.sb([64, GB, 8, 16], F32, "nDim")
    t1 = P.sb([64, GB, 8, 16], F32, "t1")
    t2 = P.sb([64, GB, 8, 16], F32, "t2")
    ps_T = [P.ps([128, 128], F32, "psT%d" % i) for i in range(2)]
    ps_W = [P.ps([128, 128], F32, "psW%d" % i) for i in range(2)]

    def bc16(v):
        return V(v.ap.unsqueeze(2).to_broadcast([64, GB, 16]), v.keys)

    def cmul(out_re, out_im, lp0, xr, xi, neg_im=False):
        a_re = V(LPre[:, :, lp0:lp0 + 8].ap.unsqueeze(3).to_broadcast([64, GB, 8, 16]), LPre.keys)
        a_im = V(LPim[:, :, lp0:lp0 + 8].ap.unsqueeze(3).to_broadcast([64, GB, 8, 16]), LPim.keys)
        b_re = V(xr.ap.unsqueeze(2).to_broadcast([64, GB, 8, 16]), xr.keys)
        b_im = V(xi.ap.unsqueeze(2).to_broadcast([64, GB, 8, 16]), xi.keys)
        P.tt(t1, a_re, b_re, ALU.mult)
        P.tt(t2, a_im, b_im, ALU.mult)
        P.tt(out_re, t1, t2, ALU.subtract)
        P.tt(t1, a_re, b_im, ALU.mult)
        P.tt(t2, a_im, b_re, ALU.mult)
        P.tt(out_im, t1, t2, ALU.add)
        if neg_im:
            P.ts(out_im, out_im, -1.0, ALU.mult)

    fl = lambda v, gi: v[:, gi, :, :].re("p a b -> p (a b)")
    for g0 in range(0, G, GB):
        gsl = slice(g0, g0 + GB)
        for d_, s_ in ((lre, lamre_d), (lim, lamim_d), (lst, lstep_d)):
            P.dma(d_, s_[:, gsl])
        for d_, s_ in ((Bre, Bre_d), (Bim, Bim_d), (Cre, Cre_d), (Cim, Cim_d)):
            P.dma(d_, s_[:, gsl, :])
        P.act(step, lst, AF.Exp)
        P.tt(rho, lre, step, ALU.mult)
        P.tt(th, lim, step, ALU.mult)
        for i, kv in enumerate(kk):
            P.ts(PH[:, :, i], th, float(kv), ALU.mult)
            P.ts(RH[:, :, i], rho, float(kv), ALU.mult)
        P.act(mag, RH, AF.Exp)
        P.ts(TT[:, :, :, 0], PH, 1.0 / TWO_PI, ALU.mult, 0.5, ALU.add)
        P.ts(TT[:, :, :, 1], PH, 1.0 / TWO_PI, ALU.mult, 0.75, ALU.add)
        P.copy(TI, TT)
        P.copy(TF, TI)
        P.tt(TT, TT, TF, ALU.subtract)
        P.ts(TF, TT, 0.0, ALU.is_lt)
        P.tt(TT, TT, TF, ALU.add)
        P.ts(TT, TT, TWO_PI, ALU.mult, -math.pi, ALU.add)
        P.ts(TT, TT, 3.1415925, ALU.min, -3.1415925, ALU.max)
        P.act(TT, TT, AF.Sin)
        P.tt(LPre, mag, TT[:, :, :, 1], ALU.mult)
        P.tt(LPim, mag, TT[:, :, :, 0], ALU.mult)
        abre = LPre[:, :, i1]
        abim = LPim[:, :, i1]
        P.tt(w[0], lre, lre, ALU.mult)
        P.tt(w[1], lim, lim, ALU.mult)
        P.tt(w[0], w[0], w[1], ALU.add)
        P.recip(w[0], w[0])
        P.ts(w[1], abre, -1.0, ALU.add)
        P.tt(w[2], w[1], lre, ALU.mult)
        P.tt(w[3], abim, lim, ALU.mult)
        P.tt(w[2], w[2], w[3], ALU.add)
        P.tt(w[2], w[2], w[0], ALU.mult)
        P.tt(w[4], abim, lre, ALU.mult)
        P.tt(w[5], w[1], lim, ALU.mult)
        P.tt(w[4], w[4], w[5], ALU.subtract)
        P.tt(w[4], w[4], w[0], ALU.mult)
        cre, cim = w[2], w[4]
        P.tt(t16a, Bre, bc16(cre), ALU.mult)
        P.tt(t16b, Bim, bc16(cim), ALU.mult)
        P.tt(BTre, t16a, t16b, ALU.subtract)
        P.tt(t16a, Bim, bc16(cre), ALU.mult)
        P.tt(t16b, Bre, bc16(cim), ALU.mult)
        P.tt(BTim, t16a, t16b, ALU.add)
        P.copy(MU1[:, 0, gsl], LPre[:, :, i8])
        P.copy(MU1[:, 1, gsl], LPre[:, :, i8])
        P.copy(MUb[:, gsl], LPim[:, :, i8])
        P.ts(MUa[:, gsl], LPim[:, :, i8], -1.0, ALU.mult)
        cmul(Are, Aim, 0, BTre, BTim)
        cmul(Apre, Apim, 16, BTre, BTim)
        cmul(Dre, nDim, 8, Cre, Cim, neg_im=True)
        for gi in range(GB):
            g = g0 + gi
            pT = ps_T[g % 2]
            P.mm(pT, fl(Apre, gi), fl(Dre, gi), start=True, stop=False)
            P.mm(pT, fl(Apim, gi), fl(nDim, gi), start=False, stop=True)
            P.tt(Tm[:, g, :], pT, BM.re("p a b -> p (a b)"), ALU.mult)
            pW = ps_W[g % 2]
            P.mm(pW[:, 0:64], fl(Are, gi), I64)
            P.mm(pW[:, 64:128], fl(Aim, gi), I64)
            P.copy(W2[:, g, :], pW, eng="act")
        P.copy(Vre[:, gsl, :], Dre.re("p g a b -> p g (a b)"), eng="act")
        P.copy(Vim[:, gsl, :], nDim.re("p g a b -> p g (a b)"), eng="act")
    NBM = 128
    Iid = P.sb([128, 128], F32, "s5I")
    P.memset(Iid, 1.0, eng="pool")
    P.aselect(Iid, Iid, [[-1, 128]], ALU.is_equal, 0.0, 0, 1)
    XE = P.sb([64, 2, G, NBM + 1], F32, "XE")
    Ebf = P.sb([64, 2, G, NBM], BF16, "Ebf")
    Ubf = P.sb([128, G, NBM], BF16, "Ubf")
    utile = P.sb([128, 8, G * 16], F32, "utile")
    ybuf = P.sb([128, 8, G * 16], F32, "ybuf")
    ugs = [P.sb([128, 8, 16], F32, "ugs%d" % i) for i in range(2)]
    r1 = P.sb([64, 2, G], F32, "r1")
    r2 = P.sb([64, 2, G], F32, "r2")
    ps_u = [P.ps_alias("psT%d" % i, [128, NBM]) for i in range(2)]
    ps_xr = P.ps([64, 4, NBM], F32, "psxr")
    ps_xi = P.ps([64, 4, NBM], F32, "psxi")
    ps_y = [P.ps([128, 4, 128], F32, "psy%d" % i) for i in range(2)]
    GC = G * 16
    lat_blocks = [0, 1, 2, 3] if not rev else [3, 2, 1, 0]
    blocks = [("ctx", 0)] + [("lat", b) for b in lat_blocks]
    carry_col = 0 if not rev else NBM
    first = True
    for kind, bi in blocks:
        nb = 32 if kind == "ctx" else NBM
        if kind == "ctx":
            for rho in range(2):
                P.dma(utile[rho * 16:(rho + 1) * 16, :, :],
                      ST[rho * TOKR + LATR:rho * TOKR + LATR + 128, c0:c0 + GC].rearrange("(c j) ch -> c j ch", j=8))
        else:
            rho, hb = bi // 2, bi % 2
            r0 = rho * TOKR + hb * 1024
            P.dma(utile[:, :, :], ST[r0:r0 + 1024, c0:c0 + GC].rearrange("(c j) ch -> c j ch", j=8))
        xoff = 1 if not rev else 0
        ccol = 0 if not rev else nb
        if first:
            P.memset(XE[:, :, :, ccol], 0.0)
            first = False
        for g0 in range(0, G, 4):
            for gi in range(4):
                g = g0 + gi
                pu = ps_u[g % 2]
                ug = ugs[g % 2]
                P.copy(ug[0:nb, :, :], utile[0:nb, :, g * 16:(g + 1) * 16], eng="pool")
                P.mm(pu[:, 0:nb], ug[0:nb, :, :].rearrange("c j p -> c (j p)"), Iid[0:nb, 0:nb])
                P.copy(Ubf[:, g, 0:nb], pu[:, 0:nb], eng="act" if g % 2 else "dve")
                P.mm(ps_xr[:, gi, 0:nb], W2[:, g, 0:64], Ubf[:, g, 0:nb])
                P.mm(ps_xi[:, gi, 0:nb], W2[:, g, 64:128], Ubf[:, g, 0:nb])
            P.copy(XE[:, 0, g0:g0 + 4, xoff:xoff + nb], ps_xr[:, :, 0:nb], eng="act")
            P.copy(XE[:, 1, g0:g0 + 4, xoff:xoff + nb], ps_xi[:, :, 0:nb], eng="dve")
        order = range(nb) if not rev else range(nb - 1, -1, -1)
        for c in order:
            pc = c if not rev else c + 1
            cc_ = c + xoff
            P.tt(r1, MU1, XE[:, :, :, pc], ALU.mult)
            P.tt(r2[:, 0, :], MUa, XE[:, 1, :, pc], ALU.mult)
            P.tt(r2[:, 1, :], MUb, XE[:, 0, :, pc], ALU.mult)
            P.tt(r1, r1, r2, ALU.add)
            P.tt(XE[:, :, :, cc_], XE[:, :, :, cc_], r1, ALU.add)
        sh = 0 if not rev else 1
        P.copy(Ebf[:, :, :, 0:nb], XE[:, :, :, sh:sh + nb], eng="act")
        for g0 in range(0, G, 4):
            py = ps_y[(g0 // 4) % 2]
            for gi in range(4):
                g = g0 + gi
                P.mm(py[0:nb, gi, :], Ubf[:, g, 0:nb], Tm[:, g, :], start=True, stop=False)
                P.mm(py[0:nb, gi, :], Ebf[:, 0, g, 0:nb], Vre[:, g, :], start=False, stop=False)
                P.mm(py[0:nb, gi, :], Ebf[:, 1, g, 0:nb], Vim[:, g, :], start=False, stop=True)
            P.copy(ybuf[0:nb, :, g0 * 16:(g0 + 4) * 16].rearrange("c j (g p) -> c g j p", p=16),
                   py[0:nb, :, :].rearrange("c g (j p) -> c g j p", p=16), eng="act" if (g0 // 4) % 2 else "dve")
        if kind == "ctx":
            for rho in range(2):
                P.dma(Od[rho * TOKR + LATR:rho * TOKR + LATR + 128, :].rearrange("(c j) ch -> c j ch", j=8),
                      ybuf[rho * 16:(rho + 1) * 16, :, :], q="pool")
        else:
            P.dma(Od[r0:r0 + 1024, :].rearrange("(c j) ch -> c j ch", j=8), ybuf[:, :, :], q="pool")
        last_col = nb if not rev else 0
        nxt_nb = NBM
        nxt_ccol = 0 if not rev else nxt_nb
        if (kind, bi) != blocks[-1]:
            if last_col != nxt_ccol:
                P.copy(XE[:, :, :, nxt_ccol], XE[:, :, :, last_col])
    return P


def build_s5(G, GB, NB, NBLK, name="s5"):
    P = Prog(name)
    emit_s5(P, G, GB, NB, NBLK)
    return P


EPS = 1e-6


class TokCtx:
    def __init__(self, P, D, DFF, TMAX, nwst=3, nwbf=4, look=3):
        self.P, self.D, self.DFF, self.T = P, D, DFF, TMAX
        self.KC = D // 128
        self.JC = DFF // 128
        KC, JC, T = self.KC, self.JC, TMAX
        self.ones = P.sb([128, 128], F32, "ones")
        P.memset(self.ones, 1.0)
        self.ones_bf = P.sb([128, 128], BF16, "ones_bf")
        P.memset(self.ones_bf, 1.0)
        self.XB = P.sb([128, KC, T], F32, "XB")
        self.h = P.sb([128, KC, T], BF16, "h")
        self.hid = P.sb([128, JC, T], BF16, "hid")
        self.sq = [P.sb([128, T], F32, "sq%d" % i) for i in range(2)]
        self.rstd = P.sb([128, T], F32, "rstd")
        self.tmp = [P.sb([128, T], F32, "tmp%d" % i) for i in range(4)]
        self.sg = [P.sb([128, T], F32, "sg%d" % i) for i in range(2)]
        self.xr = [P.sb([128, T], F32, "xr%d" % i) for i in range(4)]
        self.WSZ = KC * 128
        self.wst = [P.sb([128, self.WSZ], F32, "wst%d" % i) for i in range(nwst)]
        self.wbf = [P.sb([128, self.WSZ], BF16, "wbf%d" % i) for i in range(nwbf)]
        self.look = look
        self.wi = 0
        self.ps_g = [P.ps([128, T], F32, "psg%d" % i) for i in range(2)]
        self.ps_u = [P.ps([128, T], F32, "psu%d" % i) for i in range(2)]
        self.ps_y = [P.ps([128, T], F32, "psy%d" % i) for i in range(2)]
        self.ps_s = P.ps([128, T], F32, "pss")
        self.ps_m = P.ps([128, 512], F32, "psm")
        self.cast_i = 0
        self.eps_col = P.sb([128, 1], F32, "epsc")
        P.memset(self.eps_col, EPS)

    def stream(self, items, look=None):
        loaded = {}
        look = look or self.look

        def get(i):
            for k in range(i, min(i + look, len(items))):
                if k not in loaded:
                    loaded[k] = self.load_w(*items[k])
            return loaded[i]
        return get

    def load_w(self, src_ap, nrow_chunks, ncols):
        P = self.P
        i = self.wi % len(self.wst)
        ib = self.wi % len(self.wbf)
        self.wi += 1
        n = nrow_chunks * ncols
        assert n <= self.WSZ
        st = self.wst[i][:, 0:n].re("p (a b) -> p a b", b=ncols)
        bf = self.wbf[ib][:, 0:n].re("p (a b) -> p a b", b=ncols)
        P.dma(st, src_ap, q="sp")
        self.cast_i += 1
        if self.cast_i % 3 == 0:
            P.copy(bf, st, eng="act")
        else:
            P.copy(bf, st, eng="dve")
        return bf


def rms_rstd(C, chunks, T, Dtot, out_rstd, ps=None, big=None):
    P = C.P
    ps = ps or C.ps_s
    n = len(chunks)
    if big is not None and n >= 4 and C.JC >= n:
        sqb = C.hid[:, 0:n, 0:T]
        h1 = n // 2
        P.act(sqb[:, 0:h1, :], big[:, 0:h1, :], AF.Square)
        P.tt(sqb[:, h1:n, :], big[:, h1:n, :], big[:, h1:n, :], ALU.mult)
        for i in range(n):
            P.mm(ps[:, 0:T], C.ones_bf, sqb[:, i, :], start=(i == 0), stop=(i == n - 1))
    else:
        for i, src in enumerate(chunks):
            sq = C.sq[i % 2][:, 0:T]
            P.act(sq, src, AF.Square)
            P.mm(ps[:, 0:T], C.ones, sq, start=(i == 0), stop=(i == n - 1))
    P.act(out_rstd[:, 0:T], ps[:, 0:T], AF.Sqrt, scale=1.0 / Dtot, bias=C.eps_col)
    P.recip(out_rstd[:, 0:T], out_rstd[:, 0:T])


def sublayer_in(C, T, Ain, shift, col):
    P = C.P
    KC = C.KC
    rms_rstd(C, [C.XB[:, kc, 0:T] for kc in range(KC)], T, C.D, C.rstd, big=C.XB[:, :, 0:T])
    for kc in range(KC):
        t = C.tmp[kc % 4][:, 0:T]
        P.stt(t, C.XB[:, kc, 0:T], Ain[:, kc, col:col + 1], C.rstd[:, 0:T], ALU.mult, ALU.mult)
        P.act(C.h[:, kc, 0:T], t, AF.Identity, bias=shift[:, kc, col:col + 1])


def sublayer_out(C, T, Gout, col, resid):
    P = C.P
    KC = C.KC
    rms_rstd(C, [C.XB[:, kc, 0:T] for kc in range(KC)], T, C.D, C.rstd, big=C.XB[:, :, 0:T])
    for kc in range(KC):
        xr = C.xr[kc % 4][:, 0:T]
        P.dma(xr, resid(kc), q="sp")
        t = C.tmp[kc % 4][:, 0:T]
        P.stt(t, C.XB[:, kc, 0:T], Gout[:, kc, col:col + 1], C.rstd[:, 0:T], ALU.mult, ALU.mult)
        P.tt(C.XB[:, kc, 0:T], xr, t, ALU.add, eng="pool")


def split_parts(n, maxp):
    k = (n + maxp - 1) // maxp
    base, rem = divmod(n, k)
    out, s = [], 0
    for i in range(k):
        sz = base + (1 if i < rem else 0)
        out.append((s, s + sz))
        s += sz
    return out


def ffn_core(C, T, wg, wu, wd, tiled=False):
    P = C.P
    KC, JC = C.KC, C.JC
    parts = split_parts(JC, C.WSZ // 128)
    items = []
    if tiled:
        for j in range(JC):
            items.append((wg[j], KC, 128))
            items.append((wu[j], KC, 128))
        for m in range(KC):
            for (j0, j1) in parts:
                items.append((wd[m][:, j0:j1, :], j1 - j0, 128))
    else:
        wg_v = wg.rearrange("(kc p) n -> p kc n", p=128)
        wu_v = wu.rearrange("(kc p) n -> p kc n", p=128)
        wd_v = wd.rearrange("(jc p) n -> p jc n", p=128)
        for j in range(JC):
            items.append((wg_v[:, :, j * 128:(j + 1) * 128], KC, 128))
            items.append((wu_v[:, :, j * 128:(j + 1) * 128], KC, 128))
        for m in range(KC):
            for (j0, j1) in parts:
                items.append((wd_v[:, j0:j1, m * 128:(m + 1) * 128], j1 - j0, 128))
    get = C.stream(items)
    for j in range(JC):
        wgb = get(2 * j)
        wub = get(2 * j + 1)
        pg = C.ps_g[j % 2][:, 0:T]
        pu = C.ps_u[j % 2][:, 0:T]
        for kc in range(KC):
            P.mm(pg, wgb[:, kc, :], C.h[:, kc, 0:T], start=(kc == 0), stop=(kc == KC - 1))
        for kc in range(KC):
            P.mm(pu, wub[:, kc, :], C.h[:, kc, 0:T], start=(kc == 0), stop=(kc == KC - 1))
        sg = C.sg[j % 2][:, 0:T]
        P.act(sg, pg, AF.Silu)
        P.tt(C.hid[:, j, 0:T], sg, pu, ALU.mult)
    npart = len(parts)
    for m in range(KC):
        py = C.ps_y[m % 2][:, 0:T]
        for hi, (j0, j1) in enumerate(parts):
            wdb = get(2 * JC + npart * m + hi)
            for j in range(j0, j1):
                P.mm(py, wdb[:, j - j0, :], C.hid[:, j, 0:T], start=(j == 0), stop=(j == JC - 1))
        P.copy(C.XB[:, m, 0:T], py, eng="act")


def proj(C, T, w, nk, ncol, rhs, sink, bias_fn=None, tiled=False):
    P = C.P
    nch = (ncol + 127) // 128
    if tiled:
        items = [(w[j], nk, 128) for j in range(nch)]
    else:
        w_v = w.rearrange("(kc p) n -> p kc n", p=128)
        items = [(w_v[:, :, j * 128:min(ncol, (j + 1) * 128)], nk, min(128, ncol - j * 128)) for j in range(nch)]
    get = C.stream(items)
    for j in range(nch):
        cw = min(128, ncol - j * 128)
        wb = get(j)
        pg = C.ps_g[j % 2][0:cw, 0:T]
        for kc in range(nk):
            P.mm(pg, wb[:, kc, :], rhs(kc), start=(kc == 0), stop=(kc == nk - 1))
        sink(j, cw, pg)


def in_proj(C, T, w_in, ncol, sT_out, t0):
    P = C.P

    def sink(j, cw, pg):
        so = C.sg[j % 2][0:cw, 0:T]
        P.copy(so, pg, eng="act")
        P.dma(sT_out[j * 128:j * 128 + cw, t0:t0 + T], so, q="pool", is_output=True)
    proj(C, T, w_in, C.KC, ncol, lambda kc: C.h[:, kc, 0:T], sink)


def load_consts(C, norm_pre, norm_post):
    P = C.P
    C.npre = P.sb([128, 3, C.KC], F32, "npre")
    C.npost = P.sb([128, 3, C.KC], F32, "npost")
    with P.nc.allow_non_contiguous_dma(reason="small const loads"):
        P.dma(C.npre, norm_pre.rearrange("s (kc p) -> p s kc", p=128))
        P.dma(C.npost, norm_post.rearrange("s (kc p) -> p s kc", p=128))


def compute_mod(C, cT, w_mod, b_mod, mod_out=None, tiled=False):
    P = C.P
    KC = C.KC
    NJ = 9 * KC
    C.mod = P.sb([128, NJ, 2], F32, "mod")
    cs = P.sb([128, KC, 2], F32, "csilu")
    bm = P.sb([128, NJ], F32, "bmod")
    with P.nc.allow_non_contiguous_dma(reason="small const loads"):
        P.dma(cs, cT.rearrange("(kc p) c -> p kc c", p=128))
        P.dma(bm, b_mod.rearrange("(j p) -> p j", p=128))
    P.act(cs, cs, AF.Silu)
    w_v = None if tiled else w_mod.rearrange("(kc p) n -> p kc n", p=128)
    assert 2 * NJ <= 512
    psm = C.ps_m[:, 0:2 * NJ].re("p (j c) -> p j c", c=2)
    for j in range(NJ):
        i = C.wi % len(C.wst)
        C.wi += 1
        st = C.wst[i][:, 0:KC * 128].re("p (a b) -> p a b", b=128)
        P.dma(st, w_mod[j] if tiled else w_v[:, :, j * 128:(j + 1) * 128], q="sp")
        for kc in range(KC):
            P.mm(psm[:, j, :], st[:, kc, :], cs[:, kc, :], start=(kc == 0), stop=(kc == KC - 1))
    for c in range(2):
        P.tt(C.mod[:, :, c], psm[:, :, c], bm, ALU.add)
    if mod_out is not None:
        P.dma(mod_out, C.mod, q="pool", is_output=True)


def load_mod(C, modi):
    P = C.P
    C.mod = P.sb([128, 9 * C.KC, 2], F32, "mod")
    P.dma(C.mod, modi)


def derive_sub(C, s, weight):
    P = C.P
    KC = C.KC
    Ain = P.sb([128, KC, 2], F32, "Ain%d" % s)
    Gout = P.sb([128, KC, 2], F32, "Gout%d" % s)
    shift = C.mod[:, (3 * s) * KC:(3 * s + 1) * KC, :]
    scale = C.mod[:, (3 * s + 1) * KC:(3 * s + 2) * KC, :]
    gate = C.mod[:, (3 * s + 2) * KC:(3 * s + 3) * KC, :]
    for c in range(2):
        P.stt(Ain[:, :, c], scale[:, :, c], 1.0, C.npre[:, s, :], ALU.add, ALU.mult)
        P.stt(Gout[:, :, c], gate[:, :, c], float(weight), C.npost[:, s, :], ALU.mult, ALU.mult)
    return Ain, shift, Gout


def build_stageA(D, DFF, NCOL, tiles, Ttot, name="stageA"):
    P = Prog(name)
    xT = P.dram("xT", [D, Ttot])
    cT = P.dram("cT", [D, 2])
    w_mod = P.dram("w_mod", [D, 9 * D])
    b_mod = P.dram("b_mod", [9 * D])
    norm_pre = P.dram("norm_pre", [3, D])
    norm_post = P.dram("norm_post", [3, D])
    wg = P.dram("wg", [D, DFF])
    wu = P.dram("wu", [D, DFF])
    wd = P.dram("wd", [DFF, D])
    w_in = P.dram("w_in", [D, NCOL])
    x1T = P.dram("x1T", [D, Ttot], kind="ExternalOutput")
    sT = P.dram("sT", [NCOL, Ttot], kind="ExternalOutput")
    KC = D // 128
    modo = P.dram("modo", [128, 9 * KC, 2], kind="ExternalOutput")
    TMAX = max(t[2] for t in tiles)
    C = TokCtx(P, D, DFF, TMAX)
    load_consts(C, norm_pre, norm_post)
    compute_mod(C, cT, w_mod, b_mod, modo)
    A0, sh0, G0 = derive_sub(C, 0, 0.5)
    A1, sh1, G1 = derive_sub(C, 1, 1.0)
    xv = xT.rearrange("(kc p) t -> p kc t", p=128)
    x1v = x1T.rearrange("(kc p) t -> p kc t", p=128)
    for (col, t0, T) in tiles:
        P.dma(C.XB[:, :, 0:T], xv[:, :, t0:t0 + T], q="sp")
        sublayer_in(C, T, A0, sh0, col)
        ffn_core(C, T, wg, wu, wd)
        sublayer_out(C, T, G0, col, lambda kc: xv[:, kc, t0:t0 + T])
        P.dma(x1v[:, :, t0:t0 + T], C.XB[:, :, 0:T], q="pool", is_output=True)
        sublayer_in(C, T, A1, sh1, col)
        in_proj(C, T, w_in, NCOL, sT, t0)
    return P


def head_norm_chunks(C, T, osum, gvec, gcols, gact, center, dst_chunks, HD):
    P = C.P
    n = len(dst_chunks)
    if center:
        for i in range(n):
            P.mm(C.ps_m[:, 0:T], C.ones, osum[:, i, 0:T], start=(i == 0), stop=(i == n - 1))
        for i in range(n):
            sq = C.sq[i % 2][:, 0:T]
            P.act(sq, osum[:, i, 0:T], AF.Square)
            P.mm(C.ps_s[:, 0:T], C.ones, sq, start=(i == 0), stop=(i == n - 1))
        mean = C.xr[0][:, 0:T]
        var = C.xr[1][:, 0:T]
        P.ts(mean, C.ps_m[:, 0:T], 1.0 / HD, ALU.mult)
        P.tt(var, mean, mean, ALU.mult)
        P.stt(var, C.ps_s[:, 0:T], 1.0 / HD, var, ALU.mult, ALU.subtract)
        P.act(C.rstd[:, 0:T], var, AF.Sqrt, bias=C.eps_col)
        P.recip(C.rstd[:, 0:T], C.rstd[:, 0:T])
        for i in range(n):
            t = C.tmp[i % 2][:, 0:T]
            P.tt(t, osum[:, i, 0:T], mean, ALU.subtract)
            P.tt(t, t, C.rstd[:, 0:T], ALU.mult)
            P.stt(dst_chunks[i], t, gvec[:, gcols[i]:gcols[i] + 1], gact[i], ALU.mult, ALU.mult)
    else:
        rms_rstd(C, [osum[:, i, 0:T] for i in range(n)], T, HD, C.rstd)
        for i in range(n):
            t = C.tmp[i % 2][:, 0:T]
            P.tt(t, osum[:, i, 0:T], C.rstd[:, 0:T], ALU.mult)
            P.stt(dst_chunks[i], t, gvec[:, gcols[i]:gcols[i] + 1], gact[i], ALU.mult, ALU.mult)


def build_stageC(D, DFF, tiles, Ttot, parity, HD=256, name="stageC"):
    P = Prog(name)
    KC = D // 128
    H2 = D // 2
    KH = KC // 2
    x1T = P.dram("x1T", [D, Ttot])
    modi = P.dram("modi", [128, 9 * KC, 2])
    norm_pre = P.dram("norm_pre", [3, D])
    norm_post = P.dram("norm_post", [3, D])
    w_out = P.dram("w_out", [D, D])
    wg = P.dram("wg", [D, DFF])
    wu = P.dram("wu", [D, DFF])
    wd = P.dram("wd", [DFF, D])
    oAf = P.dram("oAf", [H2, Ttot])
    oAb = P.dram("oAb", [H2, Ttot])
    oBf = P.dram("oBf", [H2, Ttot])
    oBb = P.dram("oBb", [H2, Ttot])
    gB = P.dram("gB", [H2, Ttot])
    nB = P.dram("nB", [H2])
    if parity == 0:
        gA = P.dram("gA", [H2, Ttot])
        nA = P.dram("nA", [H2])
    else:
        uT = P.dram("uT", [H2, Ttot])
        s5d = P.dram("s5d", [H2])
        w_glu = P.dram("w_glu", [H2, H2])
        b_glu = P.dram("b_glu", [H2])
    x3T = P.dram("x3T", [D, Ttot], kind="ExternalOutput")
    x2s = V(P.dram("x2s", [D, Ttot], kind="Internal"), "x2s")
    TMAX = max(t[2] for t in tiles)
    C = TokCtx(P, D, DFF, TMAX)
    load_consts(C, norm_pre, norm_post)
    load_mod(C, modi)
    A1, sh1, G1 = derive_sub(C, 1, 1.0)
    A2, sh2, G2 = derive_sub(C, 2, 0.5)
    gv = P.sb([128, 4, KH], F32, "gv")
    with P.nc.allow_non_contiguous_dma(reason="small const loads"):
        P.dma(gv[:, 1, :], nB.rearrange("(kc p) -> p kc", p=128))
        if parity == 0:
            P.dma(gv[:, 0, :], nA.rearrange("(kc p) -> p kc", p=128))
        else:
            P.dma(gv[:, 0, :], s5d.rearrange("(kc p) -> p kc", p=128))
            P.dma(gv[:, 2, :], b_glu.rearrange("(kc p) -> p kc", p=128))
    NH = HD // 128
    ld = [[P.sb([128, TMAX], F32, "ld%d_%d" % (a, b)) for b in range(2)] for a in range(3)]
    osum = P.sb([128, NH, TMAX], F32, "osum")
    gact = [P.sb([128, TMAX], F32, "gact%d" % i) for i in range(NH)]
    x1v = x1T.rearrange("(kc p) t -> p kc t", p=128)
    x3v = x3T.rearrange("(kc p) t -> p kc t", p=128)
    x2v = x2s.re("(kc p) t -> p kc t", p=128)
    li = [0]

    def normed_half(T, t0, of_, ob_, g_, gvrow, func, center, kc0):
        for hh in range(H2 // HD):
            for i in range(NH):
                r0 = (hh * NH + i) * 128
                b = li[0] % 2
                li[0] += 1
                P.dma(ld[0][b][:, 0:T], of_[r0:r0 + 128, t0:t0 + T])
                P.dma(ld[1][b][:, 0:T], ob_[r0:r0 + 128, t0:t0 + T])
                P.dma(ld[2][b][:, 0:T], g_[r0:r0 + 128, t0:t0 + T])
                P.tt(osum[:, i, 0:T], ld[0][b][:, 0:T], ld[1][b][:, 0:T], ALU.add, eng="pool")
                P.act(gact[i][:, 0:T], ld[2][b][:, 0:T], func)
            cols = [hh * NH + i for i in range(NH)]
            head_norm_chunks(C, T, osum, gv[:, gvrow, :], cols, [g[:, 0:T] for g in gact], center,
                             [C.h[:, kc0 + hh * NH + i, 0:T] for i in range(NH)], HD)

    for (col, t0, T) in tiles:
        if parity == 0:
            normed_half(T, t0, oAf, oAb, gA, 0, AF.Silu, False, 0)
            normed_half(T, t0, oBf, oBb, gB, 1, AF.Sigmoid, True, KH)
        else:
            for kc in range(KH):
                b = li[0] % 2
                li[0] += 1
                r0 = kc * 128
                P.dma(ld[0][b][:, 0:T], oAf[r0:r0 + 128, t0:t0 + T])
                P.dma(ld[1][b][:, 0:T], oAb[r0:r0 + 128, t0:t0 + T])
                P.dma(ld[2][b][:, 0:T], uT[r0:r0 + 128, t0:t0 + T])
                yv = C.tmp[0][:, 0:T]
                t2 = C.tmp[1][:, 0:T]
                P.tt(yv, ld[0][b][:, 0:T], ld[1][b][:, 0:T], ALU.add, eng="pool")
                P.stt(yv, ld[2][b][:, 0:T], gv[:, 0, kc:kc + 1], yv, ALU.mult, ALU.add)
                P.tt(t2, yv, yv, ALU.mult)
                P.ts(t2, t2, 0.044715, ALU.mult, 1.0, ALU.add)
                P.tt(t2, t2, yv, ALU.mult)
                P.act(t2, t2, AF.Sigmoid, scale=1.5957691216057308)
                P.tt(C.XB[:, kc, 0:T], t2, yv, ALU.mult)
                P.copy(C.h[:, KH + kc, 0:T], C.XB[:, kc, 0:T], eng="act")

            def sink(j, cw, pg):
                sgm = C.sg[j % 2][:, 0:T]
                P.act(sgm, pg, AF.Sigmoid, bias=gv[:, 2, j:j + 1])
                P.tt(C.h[:, j, 0:T], C.XB[:, j, 0:T], sgm, ALU.mult)
            proj(C, T, w_glu, KH, H2, lambda kc: C.h[:, KH + kc, 0:T], sink)
            normed_half(T, t0, oBf, oBb, gB, 1, AF.Silu, True, KH)

        def sink_y(j, cw, pg):
            P.copy(C.XB[:, j, 0:T], pg, eng="act")
        proj(C, T, w_out, KC, D, lambda kc: C.h[:, kc, 0:T], sink_y)
        sublayer_out(C, T, G1, col, lambda kc: x1v[:, kc, t0:t0 + T])
        P.dma(x2v[:, :, t0:t0 + T], C.XB[:, :, 0:T], q="pool")
        sublayer_in(C, T, A2, sh2, col)
        ffn_core(C, T, wg, wu, wd)
        sublayer_out(C, T, G2, col, lambda kc: x2v[:, kc, t0:t0 + T])
        P.dma(x3v[:, :, t0:t0 + T], C.XB[:, :, 0:T], q="pool", is_output=True)
    return P


def compute_mod_half(C, cT, w_half, b_half, Mown, Mall, groups):
    P = C.P
    KC = C.KC
    NJ = 9 * KC
    NH = NJ // 2
    mh = P.sb([128, NH, 2], F32, "modh")
    cs = P.sb([128, KC, 2], F32, "csilu")
    bm = P.sb([128, NH], F32, "bmod")
    with P.nc.allow_non_contiguous_dma(reason="small const loads"):
        P.dma(cs, cT.rearrange("(kc p) c -> p kc c", p=128))
        P.dma(bm, b_half.rearrange("(j p) -> p j", p=128))
    P.act(cs, cs, AF.Silu)
    psm = C.ps_m[:, 0:2 * NH].re("p (j c) -> p j c", c=2)
    for j in range(NH):
        i = C.wi % len(C.wst)
        C.wi += 1
        st = C.wst[i][:, 0:KC * 128].re("p (a b) -> p a b", b=128)
        P.dma(st, w_half[j], q="sp")
        for kc in range(KC):
            P.mm(psm[:, j, :], st[:, kc, :], cs[:, kc, :], start=(kc == 0), stop=(kc == KC - 1))
    for c in range(2):
        P.tt(mh[:, :, c], psm[:, :, c], bm, ALU.add)
    P.dma(Mown, mh.re("p j c -> p (j c)"), q="pool")
    P.collective("AllGather", Mall, Mown, groups)
    load_mod_pair(C, Mall)


def load_mod_pair(C, Mall):
    P = C.P
    NJ = 9 * C.KC
    NH = NJ // 2
    C.mod = P.sb([128, NJ, 2], F32, "mod")
    P.dma(C.mod[:, 0:NH, :], Mall[0:128, :].rearrange("p (j c) -> p j c", c=2))
    P.dma(C.mod[:, NH:NJ, :], Mall[128:256, :].rearrange("p (j c) -> p j c", c=2))


G2 = [[0, 1], [2, 3], [4, 5], [6, 7]]
TOKR = 2176
LATR = 2048
NSQ = 4352


def nat_block(c):
    if c < 2:
        return c * TOKR + LATR
    i = c - 2
    return (i // 16) * TOKR + (i % 16) * 128


def rev_chunk(c):
    return 1 - c if c < 2 else 2 + (33 - c)


class Relay:
    def __init__(self, P, maxc):
        self.P = P
        self.I = P.sb([128, 128], F32, "rI")
        self.J = P.sb([128, 128], F32, "rJ")
        P.memset(self.I, 1.0, eng="pool")
        P.memset(self.J, 1.0, eng="pool")
        P.aselect(self.I, self.I, [[-1, 128]], ALU.is_equal, 0.0, 0, 1)
        P.aselect(self.J, self.J, [[1, 128]], ALU.is_equal, 0.0, -127, 1)
        self.tl = [P.sb([128, maxc], F32, "rtl%d" % i) for i in range(2)]
        self.ob = [P.sb([128, 512], F32, "rob%d" % i) for i in range(4)]
        self.ps = [P.ps([128, 512], F32, "rps%d" % i) for i in range(4)]
        self.k = 0
        self.ti = 0

    def load(self, ST, c, c0, ncols, cm):
        P = self.P
        t = self.tl[self.ti % 2]
        self.ti += 1
        if c < 2 or not cm:
            r0 = nat_block(c)
            P.dma(t[:, 0:ncols], ST[r0:r0 + 128, c0:c0 + ncols])
        else:
            i = c - 2
            for cl in range(2):
                for rho in range(2):
                    src = ST[rho * TOKR:rho * TOKR + LATR, c0:c0 + ncols].rearrange("(r w) c -> r w c", w=64)[:, 2 * i + cl, :]
                    P.dma(t[cl * 64 + rho * 32:cl * 64 + rho * 32 + 32, 0:ncols], src)
        return t

    def fm(self, t, c0, w, flip, dst):
        P = self.P
        k = self.k % 4
        self.k += 1
        P.mm(self.ps[k][0:w, 0:128], t[:, c0:c0 + w], self.J if flip else self.I)
        if k % 2 == 0:
            P.copy(self.ob[k][0:w, 0:128], self.ps[k][0:w, 0:128], eng="act")
        else:
            P.copy(self.ob[k][0:w, 0:128], self.ps[k][0:w, 0:128], eng="dve")
        return self.ob[k]

    def tmflip(self, t, c0, w):
        P = self.P
        k = self.k % 4
        self.k += 1
        P.mm(self.ps[k][:, 0:w], self.J, t[:, c0:c0 + w])
        if k % 2 == 0:
            P.copy(self.ob[k][:, 0:w], self.ps[k][:, 0:w], eng="act")
        else:
            P.copy(self.ob[k][:, 0:w], self.ps[k][:, 0:w], eng="dve")
        return self.ob[k]


def relayout_gla(P, ST, c0, DK, cm, arr, lr=True):
    NDC = DK // 128
    ncols = 4 * DK + 512 + (32 if lr else 0)
    R = Relay(P, ncols)
    qo, ko, vo, lo = 0, 2 * DK, 4 * DK, 4 * DK + 512
    for c in range(34):
        t = R.load(ST, c, c0, ncols, cm)
        for d in range(2):
            cc = c if d == 0 else rev_chunk(c)
            ps_ = slice(cc * 128, cc * 128 + 128)
            fl = (d == 1)
            for hl in range(2):
                u = d * 2 + hl
                for dc in range(NDC):
                    o_ = R.fm(t, qo + hl * DK + dc * 128, 128, fl, None)
                    P.dma(arr["qT"][u, dc * 128:(dc + 1) * 128, ps_], o_[:, 0:128], q="sp")
                    o_ = R.fm(t, ko + hl * DK + dc * 128, 128, fl, None)
                    P.dma(arr["kT"][u, dc * 128:(dc + 1) * 128, ps_], o_[:, 0:128], q="sp")
            if lr:
                o_ = R.fm(t, lo + d * 16, 16, fl, None)
                for hl in range(2):
                    P.dma(arr["lrT"][d * 2 + hl, :, ps_], o_[0:16, 0:128], q="sp")
            if d == 0:
                for hl in range(2):
                    P.dma(arr["k"][hl, ps_, :], t[:, ko + hl * DK:ko + (hl + 1) * DK], q="sp")
                    P.dma(arr["v"][hl, ps_, :], t[:, vo + hl * 256:vo + (hl + 1) * 256], q="sp")
            else:
                for hl in range(2):
                    o_ = R.tmflip(t, ko + hl * DK, DK)
                    P.dma(arr["k"][2 + hl, ps_, :], o_[:, 0:DK], q="sp")
                o_ = R.tmflip(t, vo, 512)
                for hl in range(2):
                    P.dma(arr["v"][2 + hl, ps_, :], o_[:, hl * 256:(hl + 1) * 256], q="sp")


def relayout_mlstm(P, ST, c0, arr):
    ncols = 1544
    R = Relay(P, ncols)
    for c in range(34):
        t = R.load(ST, c, c0, ncols, True)
        for d in range(2):
            cc = c if d == 0 else rev_chunk(c)
            ps_ = slice(cc * 128, cc * 128 + 128)
            fl = (d == 1)
            for hl in range(2):
                u = d * 2 + hl
                for dc in range(2):
                    o_ = R.fm(t, hl * 256 + dc * 128, 128, fl, None)
                    P.dma(arr["qpT"][u, dc * 128:(dc + 1) * 128, ps_], o_[:, 0:128], q="sp")
                    o_ = R.fm(t, 512 + hl * 256 + dc * 128, 128, fl, None)
                    P.dma(arr["kpT"][u, dc * 128:(dc + 1) * 128, ps_], o_[:, 0:128], q="sp")
            o_ = R.fm(t, 1536 + d * 4, 4, fl, None)
            for hl in range(2):
                P.dma(arr["gf"][d * 2 + hl:d * 2 + hl + 1, ps_], o_[hl:hl + 1, 0:128], q="sp")
                P.dma(arr["gi"][d * 2 + hl:d * 2 + hl + 1, ps_], o_[2 + hl:3 + hl, 0:128], q="sp")
            if d == 0:
                for hl in range(2):
                    P.dma(arr["v"][hl, ps_, :], t[:, 1024 + hl * 256:1024 + (hl + 1) * 256], q="sp")
            else:
                o_ = R.tmflip(t, 1024, 512)
                for hl in range(2):
                    P.dma(arr["v"][2 + hl, ps_, :], o_[:, hl * 256:(hl + 1) * 256], q="sp")


def emit_seq_inproj(P, Hall, wtok, NT, ST, wfm, NFM, SG, KC=16):
    hT = P.sb([128, KC, TOKR], BF16, "sq_hT")
    wst = [P.sb([128, KC, 512], F32, "sq_wst%d" % i) for i in range(2)]
    wbf = [P.sb([128, KC, 512], BF16, "sq_wbf%d" % i) for i in range(2)]
    osb = [P.sb([128, 512], F32, "sq_o%d" % i) for i in range(3)]
    ps = [P.ps([128, 512], F32, "sq_ps%d" % i) for i in range(4)]
    wtv = wtok.rearrange("(kc p) n -> p kc n", p=128)
    wfv = wfm.rearrange("(kc p) n -> p kc n", p=128)
    wi = 0
    oi = 0
    for rho in range(2):
        for ti, (_c, t0_, T_) in enumerate(TILES_ALL):
            P.dma(hT[:, :, t0_:t0_ + T_], Hall[ti][rho * 2048:(rho + 1) * 2048, :].rearrange("(kc p) t -> p kc t", p=128))
        for cb0 in range(0, NT, 512):
            cw = min(512, NT - cb0)
            b = wi % 2
            wi += 1
            P.dma(wst[b][:, :, 0:cw], wtv[:, :, cb0:cb0 + cw])
            P.copy(wbf[b][:, :, 0:cw], wst[b][:, :, 0:cw], eng="dve" if wi % 2 else "act")
            for tt in range(TOKR // 128):
                pz = ps[oi % 4]
                for kc in range(KC):
                    P.mm(pz[:, 0:cw], hT[:, kc, tt * 128:(tt + 1) * 128], wbf[b][:, kc, 0:cw], start=(kc == 0), stop=(kc == KC - 1))
                ob = osb[oi % 3]
                P.copy(ob[:, 0:cw], pz[:, 0:cw], eng="act" if oi % 2 else "dve")
                oi += 1
                P.dma(ST[rho * TOKR + tt * 128:rho * TOKR + (tt + 1) * 128, cb0:cb0 + cw], ob[:, 0:cw], q="pool")
        for j in range(NFM // 128):
            b = wi % 2
            wi += 1
            P.dma(wst[b][:, :, 0:128], wfv[:, :, j * 128:(j + 1) * 128])
            P.copy(wbf[b][:, :, 0:128], wst[b][:, :, 0:128], eng="dve" if wi % 2 else "act")
            for (t0, T) in [(0, 512), (512, 512), (1024, 512), (1536, 512), (2048, 128)]:
                pz = ps[oi % 4]
                for kc in range(KC):
                    P.mm(pz[:, 0:T], wbf[b][:, kc, 0:128], hT[:, kc, t0:t0 + T], start=(kc == 0), stop=(kc == KC - 1))
                ob = osb[oi % 3]
                P.copy(ob[:, 0:T], pz[:, 0:T], eng="act" if oi % 2 else "dve")
                oi += 1
                P.dma(SG[j * 128:(j + 1) * 128, rho * TOKR + t0:rho * TOKR + t0 + T], ob[:, 0:T], q="pool")


class MergeIn:
    def __init__(self, P):
        self.P = P
        self.I = P.sb([128, 128], F32, "mI")
        self.J = P.sb([128, 128], F32, "mJ")
        P.memset(self.I, 1.0, eng="pool")
        P.memset(self.J, 1.0, eng="pool")
        P.aselect(self.I, self.I, [[-1, 128]], ALU.is_equal, 0.0, 0, 1)
        P.aselect(self.J, self.J, [[1, 128]], ALU.is_equal, 0.0, -127, 1)
        self.tl = [P.sb([128, 256], F32, "mtl%d" % i) for i in range(4)]
        self.ti = 0

    def load(self, O, u, d, cm, rho, t0):
        P = self.P
        t = self.tl[self.ti % 4]
        self.ti += 1
        if t0 >= LATR:
            c = rho if d == 0 else 1 - rho
            P.dma(t, O[u, c * 128:(c + 1) * 128, :])
            return t, (d == 1)
        if not cm:
            i = (rho * LATR + t0) // 128
            c = 2 + i if d == 0 else 2 + (31 - i)
            P.dma(t, O[u, c * 128:(c + 1) * 128, :])
            return t, (d == 1)
        r = (rho * LATR + t0) // 64
        lat = O[u, 256:256 + 4096, :].rearrange("(col row) c -> col row c", row=64)
        if d == 0:
            for rl in range(2):
                P.dma(t[rl * 64:(rl + 1) * 64, :], lat[:, r + rl, :])
            return t, False
        base = 62 - r
        for rl in range(2):
            P.dma(t[rl * 64:(rl + 1) * 64, :], lat[:, base + rl, :])
        return t, True


def emit_merge_layer0(P, C, M, O_gla, O_ml, SG, gvA, gvB, wout, Ypart, tiles_rho):
    T_ = C.T
    osum = P.sb([128, 2, T_], F32, "osum")
    gact = [P.sb([128, T_], F32, "gact%d" % i) for i in range(2)]
    gld = [P.sb([128, T_], F32, "gld%d" % i) for i in range(2)]
    gv = P.sb([128, 2, 4], F32, "gvm")
    with P.nc.allow_non_contiguous_dma(reason="small const loads"):
        P.dma(gv[:, 0, :], gvA.rearrange("(kc p) -> p kc", p=128))
        P.dma(gv[:, 1, :], gvB.rearrange("(kc p) -> p kc", p=128))
    pst = [C.ps_u[0], C.ps_u[1]]
    for rho in range(2):
        for ti, (t0, T) in enumerate(tiles_rho):
            for mix in range(2):
                O = O_gla if mix == 0 else O_ml
                for hl in range(2):
                    for i in range(2):
                        kcl = mix * 4 + hl * 2 + i
                        P.dma(gld[i][:, 0:T], SG[kcl * 128:(kcl + 1) * 128, rho * TOKR + t0:rho * TOKR + t0 + T])
                        P.act(gact[i][:, 0:T], gld[i][:, 0:T], AF.Silu if mix == 0 else AF.Sigmoid)
                    for sub in range(T // 128):
                        ssl = slice(sub * 128, (sub + 1) * 128)
                        tf, ff = M.load(O, hl, 0, mix == 1, rho, t0 + sub * 128)
                        tb, fb = M.load(O, 2 + hl, 1, mix == 1, rho, t0 + sub * 128)
                        for i in range(2):
                            P.mm(pst[0][:, 0:128], tf[:, i * 128:(i + 1) * 128], M.J if ff else M.I)
                            P.mm(pst[1][:, 0:128], tb[:, i * 128:(i + 1) * 128], M.J if fb else M.I)
                            P.copy(osum[:, i, ssl], pst[0][:, 0:128], eng="act")
                            P.tt(osum[:, i, ssl], osum[:, i, ssl], pst[1][:, 0:128], ALU.add)
                    cols = [hl * 2, hl * 2 + 1]
                    head_norm_chunks(C, T, osum, gv[:, mix, :], cols, [g[:, 0:T] for g in gact], mix == 1,
                                     [C.h[:, mix * 4 + hl * 2 + i, 0:T] for i in range(2)], 256)

            def sink(j, cw, pg, rho=rho, ti=ti, T=T):
                so = C.sg[j % 2][:, 0:T]
                P.copy(so, pg, eng="act")
                P.dma(Ypart[ti][j // 4][rho * 512 + (j % 4) * 128:rho * 512 + (j % 4 + 1) * 128, :], so, q="pool")
            proj(C, T, wout, 8, 2048, lambda kc: C.h[:, kc, 0:T], sink, tiled=True)


def emit_merge_layer1(P, C, M, O_s5, O_ret, SG, ins, Gf, Gown, Gall, wout, Ypart, tiles_rho):
    T_ = C.T
    osum = P.sb([128, 2, T_], F32, "osum")
    ysum = P.sb([128, 4, T_], F32, "ysum")
    gact = [P.sb([128, T_], F32, "gact%d" % i) for i in range(2)]
    gld = [P.sb([128, T_], F32, "gld%d" % i) for i in range(2)]
    otl = [P.sb([128, 512], F32, "otl%d" % i) for i in range(4)]
    gv = P.sb([128, 3, 4], F32, "gvm")
    with P.nc.allow_non_contiguous_dma(reason="small const loads"):
        P.dma(gv[:, 0, :], ins["s5d"].rearrange("(kc p) -> p kc", p=128))
        P.dma(gv[:, 1, :], ins["nB"].rearrange("(kc p) -> p kc", p=128))
        P.dma(gv[:, 2, :], ins["bglu"].rearrange("(kc p) -> p kc", p=128))
    pst = [C.ps_u[0], C.ps_u[1]]
    oi = 0
    for rho in range(2):
        for ti, (t0, T) in enumerate(tiles_rho):
            for sub in range(T // 128):
                ssl = slice(sub * 128, (sub + 1) * 128)
                r0 = rho * TOKR + t0 + sub * 128
                tf = otl[oi % 4]
                tb = otl[(oi + 1) % 4]
                oi += 2
                P.dma(tf, O_s5[0][r0:r0 + 128, :])
                P.dma(tb, O_s5[1][r0:r0 + 128, :])
                for kc in range(4):
                    P.mm(pst[0][:, 0:128], tf[:, kc * 128:(kc + 1) * 128], M.I)
                    P.mm(pst[1][:, 0:128], tb[:, kc * 128:(kc + 1) * 128], M.I)
                    P.copy(ysum[:, kc, ssl], pst[0][:, 0:128], eng="act")
                    P.tt(ysum[:, kc, ssl], ysum[:, kc, ssl], pst[1][:, 0:128], ALU.add)
            csl = slice(rho * TOKR + t0, rho * TOKR + t0 + T)
            for kc in range(4):
                u_ = gld[kc % 2][:, 0:T]
                P.dma(u_, SG[kc * 128:(kc + 1) * 128, csl])
                yv = C.tmp[0][:, 0:T]
                t2 = C.tmp[1][:, 0:T]
                P.stt(yv, u_, gv[:, 0, kc:kc + 1], ysum[:, kc, 0:T], ALU.mult, ALU.add)
                P.tt(t2, yv, yv, ALU.mult)
                P.ts(t2, t2, 0.044715, ALU.mult, 1.0, ALU.add)
                P.tt(t2, t2, yv, ALU.mult)
                P.act(t2, t2, AF.Sigmoid, scale=1.5957691216057308)
                P.tt(C.XB[:, kc, 0:T], t2, yv, ALU.mult)
                P.copy(C.h[:, kc, 0:T], C.XB[:, kc, 0:T], eng="act")
                P.dma(Gf[kc * 128:(kc + 1) * 128, csl], C.XB[:, kc, 0:T], q="pool")
                P.dma(Gown[rho][ti][kc * 128:(kc + 1) * 128, :], C.h[:, kc, 0:T], q="pool")
    for rho in range(2):
        for ti in range(len(tiles_rho)):
            P.collective("AllGather", Gall[rho][ti], Gown[rho][ti], G2)
    for rho in range(2):
        for ti, (t0, T) in enumerate(tiles_rho):
            csl = slice(rho * TOKR + t0, rho * TOKR + t0 + T)
            P.dma(C.hid[:, 0:8, 0:T], Gall[rho][ti].rearrange("(kc p) t -> p kc t", p=128))

            def sink_g(j, cw, pg, T=T, csl=csl):
                sgm = C.sg[j % 2][:, 0:T]
                P.act(sgm, pg, AF.Sigmoid, bias=gv[:, 2, j:j + 1])
                g_ = gld[j % 2][:, 0:T]
                P.dma(g_, Gf[j * 128:(j + 1) * 128, csl])
                P.tt(C.h[:, j, 0:T], g_, sgm, ALU.mult)
            proj(C, T, ins["wglu"], 8, 512, lambda kc: C.hid[:, kc, 0:T], sink_g, tiled=True)
            for hl in range(2):
                for i in range(2):
                    kcl = 4 + hl * 2 + i
                    P.dma(gld[i][:, 0:T], SG[kcl * 128:(kcl + 1) * 128, csl])
                    P.act(gact[i][:, 0:T], gld[i][:, 0:T], AF.Silu)
                for sub in range(T // 128):
                    ssl = slice(sub * 128, (sub + 1) * 128)
                    tf, ff = M.load(O_ret, hl, 0, True, rho, t0 + sub * 128)
                    tb, fb = M.load(O_ret, 2 + hl, 1, True, rho, t0 + sub * 128)
                    for i in range(2):
                        P.mm(pst[0][:, 0:128], tf[:, i * 128:(i + 1) * 128], M.J if ff else M.I)
                        P.mm(pst[1][:, 0:128], tb[:, i * 128:(i + 1) * 128], M.J if fb else M.I)
                        P.copy(osum[:, i, ssl], pst[0][:, 0:128], eng="act")
                        P.tt(osum[:, i, ssl], osum[:, i, ssl], pst[1][:, 0:128], ALU.add)
                cols = [hl * 2, hl * 2 + 1]
                head_norm_chunks(C, T, osum, gv[:, 1, :], cols, [g[:, 0:T] for g in gact], True,
                                 [C.h[:, 4 + hl * 2 + i, 0:T] for i in range(2)], 256)

            def sink(j, cw, pg, rho=rho, ti=ti, T=T):
                so = C.sg[j % 2][:, 0:T]
                P.copy(so, pg, eng="act")
                P.dma(Ypart[ti][j // 4][rho * 512 + (j % 4) * 128:rho * 512 + (j % 4 + 1) * 128, :], so, q="pool")
            proj(C, T, wout, 8, 2048, lambda kc: C.h[:, kc, 0:T], sink, tiled=True)


D_, DFF_M = 2048, 5504
TILES_ALL = [(0, i * 512, 512) for i in range(4)] + [(1, 2048, 128)]
TILES_LAT = [(0, i * 512, 512) for i in range(4)]


def emit_P1(P, l, xin, E, X1, Hown, modo):
    C = TokCtx(P, D_, DFF_M, 512, nwst=5, nwbf=7, look=5)
    load_consts(C, E["npre%d" % l], E["npost%d" % l])
    compute_mod_half(C, E["cT"], E["w_mod%d" % l], E["b_mod%d" % l], modo[0], modo[1], G2)
    A0, sh0, G0 = derive_sub(C, 0, 0.5)
    A1, sh1, G1 = derive_sub(C, 1, 1.0)
    xv = xin.rearrange("(kc p) t -> p kc t", p=128)
    x1v = X1.rearrange("(kc p) t -> p kc t", p=128)
    for ti, (col, t0, T) in enumerate(TILES_ALL):
        P.dma(C.XB[:, :, 0:T], xv[:, :, t0:t0 + T], q="sp")
        sublayer_in(C, T, A0, sh0, col)
        ffn_core(C, T, E["wg%da" % l], E["wu%da" % l], E["wd%da" % l], tiled=True)
        sublayer_out(C, T, G0, col, lambda kc, t0=t0, T=T: xv[:, kc, t0:t0 + T])
        P.dma(x1v[:, :, t0:t0 + T], C.XB[:, :, 0:T], q="pool")
        sublayer_in(C, T, A1, sh1, col)
        P.dma(Hown[ti].rearrange("(kc p) t -> p kc t", p=128), C.h[:, :, 0:T], q="pool")


def emit_P3(P, l, E, X1, X2, Yown, modo, xout, tiles, pre=None):
    C = TokCtx(P, D_, DFF_M, 512, nwst=5, nwbf=7, look=5)
    load_consts(C, E["npre%d" % l], E["npost%d" % l])
    load_mod_pair(C, modo[1])
    A1, sh1, G1 = derive_sub(C, 1, 1.0)
    A2, sh2, G2_ = derive_sub(C, 2, 0.5)
    x1v = X1.rearrange("(kc p) t -> p kc t", p=128)
    x2v = X2.rearrange("(kc p) t -> p kc t", p=128)
    xov = xout.rearrange("(kc p) t -> p kc t", p=128)
    if pre is not None:
        pre(0)
    for ti, (col, t0, T) in enumerate(tiles):
        for q_ in range(4):
            P.dma(C.XB[:, q_ * 4:(q_ + 1) * 4, 0:T], Yown[ti][q_].rearrange("(kc p) t -> p kc t", p=128), q="sp")
        sublayer_out(C, T, G1, col, lambda kc, t0=t0, T=T: x1v[:, kc, t0:t0 + T])
        if pre is not None and ti + 1 < len(tiles):
            pre(ti + 1)
        P.dma(x2v[:, :, t0:t0 + T], C.XB[:, :, 0:T], q="pool")
        sublayer_in(C, T, A2, sh2, col)
        ffn_core(C, T, E["wg%db" % l], E["wu%db" % l], E["wd%db" % l], tiled=True)
        sublayer_out(C, T, G2_, col, lambda kc, t0=t0, T=T: x2v[:, kc, t0:t0 + T])
        P.dma(xov[:, :, t0:t0 + T], C.XB[:, :, 0:T], q="pool", is_output=True)


EXT_SHAPES = {
    "xT": [2048, TOKR], "cT": [2048, 2],
    "gwg": [4, 17, 128], "mcwq": [4, 256, 3], "mcwk": [4, 256, 3], "mcbq": [4, 256, 1], "mcbk": [4, 256, 1],
    "mbi": [4, 1], "mbf": [4, 1], "nA0": [512], "nB0": [512],
    "rdec": [4, 128, 1], "s5d": [512], "wglu": [4, 128, 8, 128], "bglu": [512], "nB1": [512],
    "wtok0": [2048, 2600], "wtok1": [2048, 2048],
}
for _l in range(2):
    EXT_SHAPES.update({"w_mod%d" % _l: [72, 128, 16, 128], "b_mod%d" % _l: [9216], "npre%d" % _l: [3, 2048], "npost%d" % _l: [3, 2048],
                       "wfm%d" % _l: [2048, 1024], "wout%d" % _l: [16, 128, 8, 128]})
    for _s in "ab":
        EXT_SHAPES.update({"wg%d%s" % (_l, _s): [43, 128, 16, 128], "wu%d%s" % (_l, _s): [43, 128, 16, 128], "wd%d%s" % (_l, _s): [16, 128, 43, 128]})
for _d in range(2):
    for _k, _sh in (("lamre", [64, 32]), ("lamim", [64, 32]), ("lstep", [64, 32]), ("Bre", [64, 32, 16]), ("Bim", [64, 32, 16]),
                    ("Cre", [64, 32, 16]), ("Cim", [64, 32, 16])):
        EXT_SHAPES["s5%d_%s" % (_d, _k)] = _sh


class _Stop(Exception):
    pass


def build_mega(stop=None, dump=()):
    P = Prog("mega")
    try:
        _build_mega(P, stop, dump)
    except _Stop:
        pass
    return P


def _build_mega(P, stop, dump):
    cnt = [0]

    def chk(tag, tensors=()):
        cnt[0] += 1
        if stop is not None and cnt[0] == stop:
            for nm, v in tensors:
                if nm in dump:
                    o = P.dram("dbg_" + nm, list(v.shape), v.ap.dtype, kind="ExternalOutput")
                    P.dma(o, v, q="sp")
            print("STOP at", cnt[0], tag)
            raise _Stop()

    class _LazyE(dict):
        def __missing__(self, k):
            v = P.dram(k, EXT_SHAPES[k])
            self[k] = v
            return v
    E = _LazyE()
    P.ext = E
    x3T = P.dram("x3T", [2048, LATR], kind="ExternalOutput")
    N = NSQ
    X1 = P.dram_i("X1", [2048, TOKR])
    X2 = P.dram_i("X2", [2048, TOKR])
    X3 = P.dram_i("X3", [2048, TOKR])
    Hown = [P.dram_i("Hown%d" % i, [2048, T], BF16) for i, (_, _t, T) in enumerate(TILES_ALL)]
    Hall = [P.dram_i("Hall%d" % i, [4096, T], BF16) for i, (_, _t, T) in enumerate(TILES_ALL)]
    ST = P.dram_i("ST", [N, 2600])
    SG = P.dram_i("SG", [1024, N])
    modo = (P.dram_i("Mown", [128, 144]), P.dram_i("Mall", [256, 144]))
    Ypart = [[P.dram_i("Yp%d_%d" % (i, q), [1024, T]) for q in range(4)] for i, (_, _t, T) in enumerate(TILES_ALL)]
    Yown = [[P.dram_i("Yo%d_%d" % (i, q), [512, T]) for q in range(4)] for i, (_, _t, T) in enumerate(TILES_ALL)]
    for l in range(2):
        xin = E["xT"] if l == 0 else X3
        P.push_scope()
        emit_P1(P, l, xin, E, X1, Hown, modo)
        P.pop_scope()
        chk("P1_%d" % l, [("X1", X1), ("Hown", Hown[0])])
        for ti in range(len(TILES_ALL)):
            P.collective("AllGather", Hall[ti], Hown[ti], G2)
        chk("AG_%d" % l, [("Hall", Hall[0])])
        P.push_scope()
        emit_seq_inproj(P, Hall, E["wtok%d" % l], 2600 if l == 0 else 2048, ST, E["wfm%d" % l], 1024, SG)
        P.pop_scope()
        chk("inproj_%d" % l, [("ST", ST), ("SG", SG)])
        if l == 0:
            ga = {"qT": P.dram_i("g_qT", [4, 128, N]), "kT": P.dram_i("g_kT", [4, 128, N]), "k": P.dram_i("g_k", [4, N, 128]),
                  "v": P.dram_i("g_v", [4, N, 256]), "lrT": P.dram_i("g_lrT", [4, 16, N]), "o": P.dram_i("g_o", [4, N, 256])}
            P.push_scope()
            relayout_gla(P, ST, 0, 128, False, ga, lr=True)
            P.pop_scope()
            chk("relay_gla", [("g_qT", ga["qT"]), ("g_k", ga["k"]), ("g_v", ga["v"]), ("g_lrT", ga["lrT"])])
            P.push_scope()
            P.bind = dict(ga)
            P.bind["wg"] = E["gwg"]
            emit_gla(P, 4, 34, 128, 256, "gla", 128 ** -0.5, 1.0)
            P.bind = None
            P.pop_scope()
            chk("gla", [("g_o", ga["o"])])
            ma = {"qpT": P.dram_i("m_qpT", [4, 256, N]), "kpT": P.dram_i("m_kpT", [4, 256, N]), "v": P.dram_i("m_v", [4, N, 256]),
                  "gi": P.dram_i("m_gi", [4, N]), "gf": P.dram_i("m_gf", [4, N]), "h": P.dram_i("m_h", [4, N, 256])}
            P.push_scope()
            relayout_mlstm(P, ST, 1056, ma)
            P.pop_scope()
            chk("relay_ml", [("m_qpT", ma["qpT"]), ("m_gi", ma["gi"]), ("m_v", ma["v"])])
            P.push_scope()
            P.bind = dict(ma)
            P.bind.update({"cwq": E["mcwq"], "cwk": E["mcwk"], "cbq": E["mcbq"], "cbk": E["mcbk"], "bi": E["mbi"], "bf": E["mbf"]})
            emit_mlstm(P, 4, [2, 32], 256, 256 ** -0.5)
            P.bind = None
            P.pop_scope()
            chk("mlstm", [("m_h", ma["h"])])
            P.push_scope()
            C = TokCtx(P, D_, DFF_M, 512)
            M = MergeIn(P)
            emit_merge_layer0(P, C, M, ga["o"], ma["h"], SG, E["nA0"], E["nB0"], E["wout0"], Ypart,
                              [(t0, T) for (_, t0, T) in TILES_ALL])
            P.pop_scope()
            chk("merge0", [("Ypart", Ypart[0][0])])
        else:
            O_s5 = [P.dram_i("s5_o%d" % d, [N, 512]) for d in range(2)]
            for d in range(2):
                P.push_scope()
                prm = {k: E["s5%d_%s" % (d, k)] for k in ("lamre", "lamim", "lstep", "Bre", "Bim", "Cre", "Cim")}
                emit_s5v2(P, 32, 8, d == 1, ST, 0, O_s5[d], prm)
                P.pop_scope()
                chk("s5_%d" % d, [("s5o", O_s5[d])])
            ra = {"qT": P.dram_i("r_qT", [4, 256, N]), "kT": P.dram_i("r_kT", [4, 256, N]), "k": P.dram_i("r_k", [4, N, 256]),
                  "v": P.dram_i("r_v", [4, N, 256]), "o": P.dram_i("r_o", [4, N, 256])}
            P.push_scope()
            relayout_gla(P, ST, 512, 256, True, ra, lr=False)
            P.pop_scope()
            chk("relay_ret", [("r_qT", ra["qT"])])
            P.push_scope()
            P.bind = dict(ra)
            P.bind["dec"] = E["rdec"]
            emit_gla(P, 4, 34, 256, 256, "ret", 1.0, 256 ** -0.5)
            P.bind = None
            P.pop_scope()
            chk("ret", [("r_o", ra["o"])])
            Gf = P.dram_i("Gf", [512, N])
            Gown = [[P.dram_i("Gown%d_%d" % (r_, i), [512, 512], BF16) for i in range(4)] for r_ in range(2)]
            Gall = [[P.dram_i("Gall%d_%d" % (r_, i), [1024, 512], BF16) for i in range(4)] for r_ in range(2)]
            P.push_scope()
            C = TokCtx(P, D_, DFF_M, 512)
            M = MergeIn(P)
            emit_merge_layer1(P, C, M, O_s5, ra["o"], SG, {"s5d": E["s5d"], "wglu": E["wglu"], "bglu": E["bglu"], "nB": E["nB1"]},
                              Gf, Gown, Gall, E["wout1"], Ypart, [(t0, T) for (_, t0, T) in TILES_LAT])
            P.pop_scope()
            chk("merge1", [("Ypart", Ypart[0][0])])
        def rs_tile(ti):
            for q_ in range(4):
                P.collective("ReduceScatter", Yown[ti][q_], Ypart[ti][q_], G2, op=ALU.add)
        chk("RS_%d" % l, [("Yown", Yown[0][0])])
        P.push_scope()
        if l == 0:
            emit_P3(P, l, E, X1, X2, Yown, modo, X3, TILES_ALL, pre=rs_tile)
            P.pop_scope()
            chk("P3_0", [("X3", X3)])
            P.push_scope()
        else:
            emit_P3(P, l, E, X1, X2, Yown, modo, x3T, TILES_LAT, pre=rs_tile)
        P.pop_scope()
    return P

_C = np.ascontiguousarray
_NCORE = 8


def _tile_w(w):
    K, N = w.shape
    return _C(w.reshape(K // 128, 128, N // 128, 128).transpose(2, 1, 0, 3))


def _core_inputs(inp, r):
    b, hf = r // 2, r % 2
    hs = [2 * hf, 2 * hf + 1]
    m = {}
    x = inp["x"][b]
    ctx = inp["ctx"][b]
    m["xT"] = _C(np.concatenate([x[hf * 2048:(hf + 1) * 2048], ctx[hf * 128:(hf + 1) * 128]], axis=0).T)
    m["cT"] = _C(np.stack([inp["c"][b], inp["c_ctx"]], axis=1))
    for l in range(2):
        m["w_mod%d" % l] = _C(_tile_w(inp["w_mod"][l])[hf * 72:(hf + 1) * 72])
        m["b_mod%d" % l] = _C(inp["b_mod"][l][hf * 9216:(hf + 1) * 9216])
        m["npre%d" % l] = _C(inp["norm_pre"][l])
        m["npost%d" % l] = _C(inp["norm_post"][l])
        for si, s in enumerate("ab"):
            m["wg%d%s" % (l, s)] = _tile_w(inp["ffn_w_gate"][l, si])
            m["wu%d%s" % (l, s)] = _tile_w(inp["ffn_w_up"][l, si])
            m["wd%d%s" % (l, s)] = _tile_w(inp["ffn_w_down"][l, si])
    ar = np.arange
    cols = []
    for off, w in ((0, 128), (512, 128), (1024, 256)):
        for h in hs:
            cols.append(off + h * w + ar(w))
    cols.append(3072 + ar(32))
    for off in (3104, 4128, 5152):
        for h in hs:
            cols.append(off + h * 256 + ar(256))
    for d in range(2):
        cols.append(np.array([7200 + d * 8 + 4 + hs[0], 7200 + d * 8 + 4 + hs[1], 7200 + d * 8 + hs[0], 7200 + d * 8 + hs[1]]))
    cols = np.concatenate(cols)
    w_in0 = inp["ev_w_in"][0]
    m["wtok0"] = _C(w_in0[:, cols])
    own512 = np.concatenate([h * 256 + ar(256) for h in hs])
    m["wfm0"] = _C(w_in0[:, np.concatenate([2048 + own512, 6176 + own512])])
    m["wout0"] = _tile_w(inp["ev_w_out"][0][np.concatenate([own512, 1024 + own512])])
    m["nA0"] = _C(inp["gla_norm"][0][own512])
    m["nB0"] = _C(inp["ml_norm"][0][own512])
    gwg, cwq, cwk, cbq, cbk, bi, bf = [], [], [], [], [], [], []
    cw = inp["ml_conv_w"][0]
    cb = inp["ml_conv_b"][0]
    bg = inp["ml_b_gates"][0]
    for d in range(2):
        for h in hs:
            gwg.append(np.concatenate([inp["gla_w_gate"][0, d][:, h * 128:(h + 1) * 128],
                                       inp["gla_b_gate"][0, d][None, h * 128:(h + 1) * 128]], axis=0))
            wq = cw[:, h * 256:(h + 1) * 256].T
            wk = cw[:, 1024 + h * 256:1024 + (h + 1) * 256].T
            if d == 1:
                wq, wk = wq[:, ::-1], wk[:, ::-1]
            cwq.append(wq)
            cwk.append(wk)
            cbq.append(cb[h * 256:(h + 1) * 256][:, None])
            cbk.append(cb[1024 + h * 256:1024 + (h + 1) * 256][:, None])
            bi.append([bg[d, 0, h]])
            bf.append([bg[d, 1, h]])
    m["gwg"] = _C(np.stack(gwg)).astype(np.float32)
    m["mcwq"] = _C(np.stack(cwq)).astype(np.float32)
    m["mcwk"] = _C(np.stack(cwk)).astype(np.float32)
    m["mcbq"] = _C(np.stack(cbq)).astype(np.float32)
    m["mcbk"] = _C(np.stack(cbk)).astype(np.float32)
    m["mbi"] = np.array(bi, np.float32)
    m["mbf"] = np.array(bf, np.float32)
    w_in1 = inp["od_w_in"][0]
    ch512 = hf * 512 + ar(512)
    cols1 = [ch512]
    for off in (1024, 2048, 3072):
        for h in hs:
            cols1.append(off + h * 256 + ar(256))
    m["wtok1"] = _C(w_in1[:, np.concatenate(cols1)])
    m["wfm1"] = _C(w_in1[:, np.concatenate([ch512, 4096 + own512])])
    m["wout1"] = _tile_w(inp["od_w_out"][0][np.concatenate([ch512, 1024 + own512])])
    m["nB1"] = _C(inp["ret_norm"][0][own512])
    m["s5d"] = _C(inp["s5_d"][0][ch512])
    m["wglu"] = _tile_w(inp["s5_w_glu"][0][:, ch512])
    m["bglu"] = _C(inp["s5_b_glu"][0][ch512])
    gs = slice(hf * 32, (hf + 1) * 32)
    for d in range(2):
        pre = "s5%d_" % d
        m[pre + "lamre"] = _C(inp["s5_lam_re"][0, d][gs].T)
        m[pre + "lamim"] = _C(inp["s5_lam_im"][0, d][gs].T)
        m[pre + "lstep"] = _C(np.broadcast_to(inp["s5_log_step"][0, d][gs][None, :], (64, 32)))
        m[pre + "Bre"] = _C(inp["s5_b_re"][0, d][gs].transpose(1, 0, 2))
        m[pre + "Bim"] = _C(inp["s5_b_im"][0, d][gs].transpose(1, 0, 2))
        m[pre + "Cre"] = _C(inp["s5_c_re"][0, d][gs].transpose(2, 0, 1))
        m[pre + "Cim"] = _C(inp["s5_c_im"][0, d][gs].transpose(2, 0, 1))
    m["rdec"] = _C(np.stack([np.full((128, 1), inp["ret_log_decay"][0, d, h], np.float32) for d in range(2) for h in hs]))
    return m


def kernel(**inp):
    inp = {k: np.asarray(v, dtype=np.float32) for k, v in inp.items()}
    P = build_mega()
    nc = P.finish()
    maps = [{k: v for k, v in _core_inputs(inp, r).items() if k in P.ext} for r in range(_NCORE)]
    res = run_bass_kernel_spmd(nc, maps, core_ids=list(range(_NCORE))).results
    out = np.empty((4, 4096, 2048), np.float32)
    for r in range(_NCORE):
        b, hf = r // 2, r % 2
        out[b, hf * 2048:(hf + 1) * 2048] = res[r]["x3T"].T
    return out
```
